# Optimizing a Trainium2 kernel written in Bass

```python
import math
import jax, jax.numpy as jnp
from jax import lax
import numpy as np

D_MODEL = 1024
BATCH = 1
SEQ = 16384
DEPTH = 1
DEC_BATCH = 32
DEC_SEQ = 2048
PAST_LEN = 128

ATT_H = 8
ATT_KV = 2
ATT_G = ATT_H // ATT_KV
ATT_HD = 64
ATT_W = ATT_H * ATT_HD
ATT_KV_W = ATT_KV * ATT_HD
WINDOW = 128
BLK = 128
ATT_SCALE = 1.0 / math.sqrt(ATT_HD)
RW_H = 8
RW_N = 64
RW_W = RW_H * RW_N
DECAY_LORA = 64
AAA_LORA = 64
GATE_LORA = 160
RW_MIX_W = 3 * RW_W + 2 * DECAY_LORA + 2 * AAA_LORA + GATE_LORA
GATE_W = 2 * D_MODEL
IN_W = ATT_W + 2 * ATT_KV_W + RW_MIX_W + GATE_W
D_FF = 2816
NORM_EPS = 1e-6
GN_EPS = 64e-5

kernel_name = "hybrid_bidir_window_gqa_rwkv7_convglu"


def rms_norm(x, g):
    xf = x.astype(jnp.float32)
    y = xf * lax.rsqrt(jnp.mean(xf * xf, axis=-1, keepdims=True) + NORM_EPS)
    return (y * g.astype(jnp.float32)).astype(x.dtype)


def banded_gqa_alibi_sink(q, k, v, sink):
    B, T = q.shape[0], q.shape[1]
    nb = T // BLK
    kp = jnp.pad(k, ((0, 0), (BLK, BLK), (0, 0), (0, 0)))
    vp = jnp.pad(v, ((0, 0), (BLK, BLK), (0, 0), (0, 0)))
    slopes = jnp.exp2(-8.0 / ATT_H * jnp.arange(1, ATT_H + 1, dtype=jnp.float32)).reshape(ATT_KV, ATT_G)
    offs_q = jnp.arange(BLK)
    offs_k = jnp.arange(3 * BLK) - BLK
    dist = jnp.abs(offs_q[:, None] - offs_k[None, :])
    penalty = slopes[:, :, None, None] * dist.astype(jnp.float32)
    sink_b = sink.astype(jnp.float32).reshape(ATT_KV, ATT_G)[None, :, :, None, None]

    def block(i):
        start = i * BLK
        qb = lax.dynamic_slice_in_dim(q, start, BLK, axis=1)
        kb = lax.dynamic_slice_in_dim(kp, start, 3 * BLK, axis=1)
        vb = lax.dynamic_slice_in_dim(vp, start, 3 * BLK, axis=1)
        kpos = start - BLK + offs_k
        valid = (dist <= WINDOW) & ((kpos >= 0) & (kpos < T))[None, :]
        s = jnp.einsum('bqkgd,bskd->bkgqs', qb, kb).astype(jnp.float32) * ATT_SCALE - penalty
        s = jnp.where(valid, s, -jnp.inf)
        mx = jnp.maximum(jnp.max(s, axis=-1, keepdims=True), sink_b)
        p = jnp.exp(s - mx)
        den = jnp.sum(p, axis=-1, keepdims=True) + jnp.exp(sink_b - mx)
        return jnp.einsum('bkgqs,bskd->bqkgd', (p / den).astype(vb.dtype), vb)

    o = lax.map(block, jnp.arange(nb))
    return jnp.moveaxis(o, 0, 1).reshape(B, T, ATT_W)


def _heads(t):
    return t.reshape(t.shape[:-1] + (RW_H, RW_N))


def _scan_shared(t):
    tt = jnp.moveaxis(t, 1, 0)
    return jnp.stack([tt, tt[::-1]], axis=1)


def _scan_dir(t):
    tt = jnp.transpose(t, (1, 2, 0, 3, 4))
    return jnp.stack([tt[:, 0], tt[::-1, 1]], axis=1)


def _rwkv_step(S, inp):
    r, w, k, v, aa, bb = inp
    sa = jnp.einsum('dbhij,dbhj->dbhi', S, aa)
    S = S * w[..., None, :] + sa[..., :, None] * bb[..., None, :] + v[..., :, None] * k[..., None, :]
    y = jnp.einsum('dbhij,dbhj->dbhi', S, r)
    return S, y


def rwkv7_bidir(z, mu_prev, mu_next, w0, w2, a0, a2, g2, k_k, k_a, r_k, ln_w, ln_b):
    B, T, _ = z.shape
    z = z.astype(jnp.float32)
    zp = jnp.pad(z, ((0, 0), (1, 1), (0, 0)))
    z = z + mu_prev * (zp[:, :-2] - z) + mu_next * (zp[:, 2:] - z)
    o1 = RW_W; o2 = 2 * RW_W; o3 = 3 * RW_W
    o4 = o3 + 2 * DECAY_LORA; o5 = o4 + 2 * AAA_LORA
    r, k, v, wd, ad, gd = jnp.split(z, [o1, o2, o3, o4, o5], axis=-1)
    wd = wd.reshape(B, T, 2, DECAY_LORA)
    ad = ad.reshape(B, T, 2, AAA_LORA)
    w_log = -jax.nn.softplus(-(w0 + jnp.einsum('btdl,dlc->btdc', jnp.tanh(wd), w2))) - 0.5
    decay = jnp.exp(-jnp.exp(w_log))
    a = jax.nn.sigmoid(a0 + jnp.einsum('btdl,dlc->btdc', ad, a2))
    g = jnp.matmul(jax.nn.sigmoid(gd), g2)
    kk = _heads(k * k_k)
    kk = kk / jnp.maximum(jnp.sqrt(jnp.sum(kk * kk, axis=-1, keepdims=True)), 1e-12)
    k_dir = k[:, :, None, :] * (1.0 + (a - 1.0) * k_a)
    r_h, v_h = _heads(r), _heads(v)
    k_dir_h, decay_h, a_h = _heads(k_dir), _heads(decay), _heads(a)
    xs = (_scan_shared(r_h), _scan_dir(decay_h), _scan_dir(k_dir_h), _scan_shared(v_h),
          _scan_shared(-kk), _scan_dir(kk[:, :, None] * a_h))
    S0 = jnp.zeros((2, B, RW_H, RW_N, RW_N), jnp.float32)
    _, ys = lax.scan(_rwkv_step, S0, xs)
    y = jnp.moveaxis(ys[:, 0] + ys[::-1, 1], 0, 1)
    mu = jnp.mean(y, axis=-1, keepdims=True)
    var = jnp.mean(jnp.square(y - mu), axis=-1, keepdims=True)
    y = (y - mu) * lax.rsqrt(var + GN_EPS) * _heads(ln_w) + _heads(ln_b)
    bonus = jnp.sum(jnp.sum(r_h[:, :, None] * k_dir_h * r_k, axis=-1, keepdims=True), axis=2) * v_h
    return (y + bonus).reshape(B, T, RW_W) * g


def conv_glu_ffn(u, w_up, conv_w, conv_b, w_down):
    h = u @ w_up
    hp = jnp.pad(h, ((0, 0), (1, 1), (0, 0)))
    h = hp[:, :-2] * conv_w[0] + hp[:, 1:-1] * conv_w[1] + hp[:, 2:] * conv_w[2] + conv_b
    gate, up = jnp.split(h, 2, axis=-1)
    return (jax.nn.gelu(gate, approximate=True) * up) @ w_down


def encoder_layer(x, g_mix_pre, g_mix_post, g_ffn_pre, g_ffn_post, w_in, attn_sink,
                  rw_mu_prev, rw_mu_next, rw_w0, rw_w2, rw_a0, rw_a2, rw_g2, rw_k_k, rw_k_a,
                  rw_r_k, rw_ln_w, rw_ln_b, w_branch_attn, w_branch_rwkv, w_out,
                  w_ffn_up, ffn_conv_w, ffn_conv_b, w_ffn_down):
    B, T, _ = x.shape
    u = rms_norm(x, g_mix_pre)
    proj = u @ w_in
    c1 = ATT_W; c2 = c1 + ATT_KV_W; c3 = c2 + ATT_KV_W; c4 = c3 + RW_MIX_W
    q, k, v, z, gates = jnp.split(proj, [c1, c2, c3, c4], axis=-1)
    q = q.reshape(B, T, ATT_KV, ATT_G, ATT_HD)
    k = k.reshape(B, T, ATT_KV, ATT_HD)
    v = v.reshape(B, T, ATT_KV, ATT_HD)
    o_attn = banded_gqa_alibi_sink(q, k, v, attn_sink)
    o_rwkv = rwkv7_bidir(z, rw_mu_prev, rw_mu_next, rw_w0, rw_w2, rw_a0, rw_a2, rw_g2,
                         rw_k_k, rw_k_a, rw_r_k, rw_ln_w, rw_ln_b).astype(x.dtype)
    g_attn, g_rwkv = jnp.split(jax.nn.sigmoid(gates), 2, axis=-1)
    merged = g_attn * (o_attn @ w_branch_attn) + g_rwkv * (o_rwkv @ w_branch_rwkv)
    h = x + rms_norm(merged @ w_out, g_mix_post)
    f = conv_glu_ffn(rms_norm(h, g_ffn_pre), w_ffn_up, ffn_conv_w, ffn_conv_b, w_ffn_down)
    return h + rms_norm(f, g_ffn_post)


def setup_inputs(seed: int = 0) -> dict:
    key = jax.random.key(seed)
    ks = jax.random.split(key, 32)
    L = DEPTH
    f32 = jnp.float32

    def nrm(k, shape, scale):
        return jax.random.normal(k, shape, f32) * scale

    def unif(k, shape, lo, hi):
        return jax.random.uniform(k, shape, f32, minval=lo, maxval=hi)

    return {
        "x_prompt": nrm(ks[0], (BATCH, SEQ, D_MODEL), 1.0),
        "x_sample": nrm(ks[1], (DEC_BATCH, DEC_SEQ, D_MODEL), 1.0),
        "norm_mix_pre": 1.0 + nrm(ks[2], (L, D_MODEL), 0.02),
        "norm_mix_post": 1.0 + nrm(ks[3], (L, D_MODEL), 0.02),
        "norm_ffn_pre": 1.0 + nrm(ks[4], (L, D_MODEL), 0.02),
        "norm_ffn_post": 1.0 + nrm(ks[5], (L, D_MODEL), 0.02),
        "w_in": nrm(ks[6], (L, D_MODEL, IN_W), D_MODEL ** -0.5),
        "attn_sink": nrm(ks[7], (L, ATT_H), 0.5),
        "rw_mu_prev": unif(ks[8], (L, RW_MIX_W), 0.0, 0.5),
        "rw_mu_next": unif(ks[9], (L, RW_MIX_W), 0.0, 0.5),
        "rw_w0": unif(ks[10], (L, 2, RW_W), -5.0, -0.5),
        "rw_w2": nrm(ks[11], (L, 2, DECAY_LORA, RW_W), 0.1),
        "rw_a0": nrm(ks[12], (L, 2, RW_W), 0.1),
        "rw_a2": nrm(ks[13], (L, 2, AAA_LORA, RW_W), 0.5 * AAA_LORA ** -0.5),
        "rw_g2": nrm(ks[14], (L, GATE_LORA, RW_W), GATE_LORA ** -0.5),
        "rw_k_k": 0.85 + nrm(ks[15], (L, RW_W), 0.02),
        "rw_k_a": 1.0 + nrm(ks[16], (L, RW_W), 0.02),
        "rw_r_k": nrm(ks[17], (L, RW_H, RW_N), 0.1),
        "rw_ln_w": 1.0 + nrm(ks[18], (L, RW_W), 0.02),
        "rw_ln_b": nrm(ks[19], (L, RW_W), 0.02),
        "w_branch_attn": nrm(ks[20], (L, ATT_W, D_MODEL), ATT_W ** -0.5),
        "w_branch_rwkv": nrm(ks[21], (L, RW_W, D_MODEL), RW_W ** -0.5),
        "w_out": nrm(ks[22], (L, D_MODEL, D_MODEL), D_MODEL ** -0.5),
        "w_ffn_up": nrm(ks[23], (L, D_MODEL, 2 * D_FF), D_MODEL ** -0.5),
        "ffn_conv_w": nrm(ks[24], (L, 3, 2 * D_FF), 3.0 ** -0.5),
        "ffn_conv_b": nrm(ks[25], (L, 2 * D_FF), 0.02),
        "w_ffn_down": nrm(ks[26], (L, D_FF, D_MODEL), D_FF ** -0.5),
    }


def reference(x_prompt, x_sample, norm_mix_pre, norm_mix_post, norm_ffn_pre, norm_ffn_post,
              w_in, attn_sink, rw_mu_prev, rw_mu_next, rw_w0, rw_w2, rw_a0, rw_a2, rw_g2,
              rw_k_k, rw_k_a, rw_r_k, rw_ln_w, rw_ln_b, w_branch_attn, w_branch_rwkv, w_out,
              w_ffn_up, ffn_conv_w, ffn_conv_b, w_ffn_down):
    def run(x):
        h = x
        for l in range(DEPTH):
            h = encoder_layer(h, norm_mix_pre[l], norm_mix_post[l], norm_ffn_pre[l], norm_ffn_post[l],
                              w_in[l], attn_sink[l], rw_mu_prev[l], rw_mu_next[l], rw_w0[l], rw_w2[l],
                              rw_a0[l], rw_a2[l], rw_g2[l], rw_k_k[l], rw_k_a[l], rw_r_k[l],
                              rw_ln_w[l], rw_ln_b[l], w_branch_attn[l], w_branch_rwkv[l], w_out[l],
                              w_ffn_up[l], ffn_conv_w[l], ffn_conv_b[l], w_ffn_down[l])
        return h

    y_prompt = run(x_prompt)
    y_sample = run(x_sample)
    return (y_prompt, y_sample)
```

```python
import math
from contextlib import ExitStack

import numpy as np
import concourse.bass as bass
import concourse.mybir as mybir
from concourse.bass_utils import run_bass_kernel_spmd

F32 = mybir.dt.float32
BF16 = mybir.dt.bfloat16
AF = mybir.ActivationFunctionType
ALU = mybir.AluOpType

D = 1024
SEG = 512
HALO = 128
NTH = SEG + 2 * HALO
NBLK = NTH // 128
C = 64
GRP = 4
NGRP = SEG // (C * GRP)
GT = C * GRP
KAPPA = math.exp(-0.5)
NORM_EPS = 1e-6
GN_EPS = 64e-5
NW = 3
NCORES = 8
SLOTS_FULL = 32
D_FF = 2816
NFT = D_FF // 128

V_G1, V_G2, V_MUP, V_MUN, V_A0, V_KK, V_KA, V_RK, V_LNW, V_LNB, V_CW, V_CB = (
    0, 8, 16, 32, 48, 56, 60, 64, 68, 72, 76, 76 + 132)
NV = V_CB + 44
C_ID, C_BONES, C_M1, C_M2, C_TRI, C_I2, C_PEN = 0, 128, 256, 512, 640, 1408, 1472
NCST = C_PEN + 6 * 512

SEM_LIMIT = 12000


class Op:
    __slots__ = ("eng", "fn", "is_dma", "semkey", "waits", "needs_inc", "sem", "count")

    def __init__(self, eng, fn, is_dma, semkey):
        self.eng = eng
        self.fn = fn
        self.is_dma = is_dma
        self.semkey = semkey
        self.waits = []
        self.needs_inc = is_dma
        self.sem = None
        self.count = 0


class Prog:
    ENGS = ("pe", "act", "dve", "pool", "sp")

    def __init__(self):
        self.ops = {e: [] for e in self.ENGS}
        self.last_w = {}
        self.readers = {}
        self.dma_cnt = {}
        self.n = 0
        self.alias = {}

    def add(self, eng, fn, reads=(), writes=(), dma=False, semkey=None):
        op = Op(eng, fn, dma, semkey if dma else None)
        self.n += 1
        if self.alias:
            reads = list(reads) + [a for k in reads for a in self.alias.get(k, ())]
            writes = list(writes) + [a for k in writes for a in self.alias.get(k, ())]
        deps = []
        for k in reads:
            lw = self.last_w.get(k)
            if lw is not None:
                deps.append((lw, "raw"))
        for k in writes:
            lw = self.last_w.get(k)
            if lw is not None:
                deps.append((lw, "waw"))
            for r in self.readers.get(k, ()):
                deps.append((r, "war"))
        seen = set()
        for d, kind in deps:
            if d is op or id(d) in seen:
                continue
            if not d.is_dma and not dma and d.eng == eng:
                if eng == "pe":
                    continue
            seen.add(id(d))
            if d.is_dma:
                op.waits.append((d.sem, self.dma_cnt[d.semkey]))
            else:
                d.needs_inc = True
                op.waits.append(d)
        if dma:
            self.dma_cnt[semkey] = self.dma_cnt.get(semkey, 0) + 16
            op.sem = "d_" + str(semkey)
            op.count = self.dma_cnt[semkey]
        for k in reads:
            self.readers.setdefault(k, []).append(op)
        for k in writes:
            self.last_w[k] = op
            self.readers[k] = []
        self.ops[eng].append(op)
        return op

    def assign(self, nc, es):
        sems = {}
        for e in self.ENGS:
            epoch, cnt = 0, 0
            for op in self.ops[e]:
                if not op.is_dma and op.needs_inc:
                    if cnt >= SEM_LIMIT:
                        epoch += 1
                        cnt = 0
                    cnt += 1
                    op.sem = "c_%s_%d" % (e, epoch)
                    op.count = cnt
        for e in self.ENGS:
            for op in self.ops[e]:
                if op.sem is not None and op.sem not in sems:
                    sems[op.sem] = es.enter_context(nc.semaphore(op.sem))
        return sems

    def run_engine(self, ename, eng, sems):
        seen = {}
        for op in self.ops[ename]:
            for d in op.waits:
                sname, cnt = d if isinstance(d, tuple) else (d.sem, d.count)
                if seen.get(sname, 0) >= cnt:
                    continue
                seen[sname] = cnt
                eng.wait_ge(sems[sname], cnt)
            if op.fn is None:
                continue
            ins = op.fn(eng)
            if op.is_dma:
                ins.then_inc(sems[op.sem], 16)
            elif op.needs_inc:
                ins.then_inc(sems[op.sem], 1)


def _fm(vec, ntile):
    return np.ascontiguousarray(np.asarray(vec, np.float32).reshape(ntile, 128).T)


def _constants():
    cst = np.zeros((128, NCST), np.float32)
    p = np.arange(128)
    cst[:, C_ID:C_ID + 128] = np.eye(128, dtype=np.float32)
    cst[:, C_BONES:C_BONES + 128] = (p[:, None] // 64 == p[None, :] // 64).astype(np.float32)
    s = (p % 64)[:, None]
    t = np.arange(64)[None, :]
    m1f = np.concatenate([(t > s), (t >= s)], axis=1).astype(np.float32)
    m1b = np.concatenate([(t < s), (t <= s)], axis=1).astype(np.float32)
    cst[:, C_M1:C_M1 + 128] = m1f
    cst[:, C_M1 + 128:C_M1 + 256] = m1b
    cst[:, C_M2:C_M2 + 64] = (t < s).astype(np.float32)
    cst[:, C_M2 + 64:C_M2 + 128] = (t > s).astype(np.float32)
    ps_, pt_ = p[:, None], p[None, :]
    same = (ps_ // 64 == pt_ // 64)
    for d in range(2):
        if d == 0:
            incl, excl, rest = (ps_ <= pt_), (ps_ < pt_), (ps_ > pt_)
        else:
            incl, excl, rest = (ps_ >= pt_), (ps_ > pt_), (ps_ < pt_)
        base = C_TRI + d * 384
        cst[:, base:base + 128] = (incl & same)
        cst[:, base + 128:base + 256] = (excl & same)
        cst[:, base + 256:base + 384] = (rest & same)
    cst[:, C_I2:C_I2 + 64] = (t == s).astype(np.float32)
    for g in range(2):
        for j in range(3):
            blk = np.zeros((128, 4, 128), np.float32)
            sk = p[:, None] + (j - 1) * 128
            qq = p[None, :]
            dist = np.abs(qq - sk).astype(np.float32)
            for i in range(4):
                h = g * 4 + i
                slope = 2.0 ** (-(h + 1))
                v = -8.0 * slope * dist
                v = np.where(dist <= 128, v, -240000.0)
                blk[:, i, :] = v
            base = C_PEN + (g * 3 + j) * 512
            cst[:, base:base + 512] = blk.reshape(128, 512)
    return cst


def _layout_weights(inp):
    L = 0
    w_in = np.asarray(inp["w_in"][L], np.float32)
    cols = []
    for i in range(4):
        cols += list(range(i * 64, (i + 1) * 64)) + list(range((4 + i) * 64, (5 + i) * 64))
    cols += list(range(512, 768))
    zc = list(range(768, 768 + 1952))
    w_in_p = np.zeros((1024, 38 * 128), np.float32)
    w_in_p[:, 0:768] = w_in[:, cols]
    w_in_p[:, 768:768 + 1952] = w_in[:, zc]
    w_in_p[:, 22 * 128:38 * 128] = w_in[:, 2720:4768]
    wba = np.asarray(inp["w_branch_attn"][L], np.float32)
    rows = []
    for i in range(4):
        rows += list(range(i * 64, (i + 1) * 64)) + list(range((4 + i) * 64, (5 + i) * 64))
    wba_p = np.ascontiguousarray(wba[rows, :])
    vecs = np.zeros((128, NV), np.float32)
    vecs[:, V_G1:V_G1 + 8] = _fm(inp["norm_mix_pre"][L], 8)
    vecs[:, V_G2:V_G2 + 8] = _fm(inp["norm_ffn_pre"][L], 8)
    mup = np.zeros(2048, np.float32)
    mun = np.zeros(2048, np.float32)
    mup[:1952] = np.asarray(inp["rw_mu_prev"][L])
    mun[:1952] = np.asarray(inp["rw_mu_next"][L])
    vecs[:, V_MUP:V_MUP + 16] = _fm(mup, 16)
    vecs[:, V_MUN:V_MUN + 16] = _fm(mun, 16)
    a0 = np.asarray(inp["rw_a0"][L], np.float32)
    for d in range(2):
        vecs[:, V_A0 + d * 4:V_A0 + d * 4 + 4] = _fm(a0[d], 4)
    vecs[:, V_KK:V_KK + 4] = _fm(inp["rw_k_k"][L], 4)
    vecs[:, V_KA:V_KA + 4] = _fm(inp["rw_k_a"][L], 4)
    vecs[:, V_RK:V_RK + 4] = _fm(np.asarray(inp["rw_r_k"][L]).reshape(512), 4)
    vecs[:, V_LNW:V_LNW + 4] = _fm(inp["rw_ln_w"][L], 4)
    vecs[:, V_LNB:V_LNB + 4] = _fm(inp["rw_ln_b"][L], 4)
    cw = np.asarray(inp["ffn_conv_w"][L], np.float32)
    for j in range(3):
        vecs[:, V_CW + j * 44:V_CW + (j + 1) * 44] = _fm(cw[j], 44)
    vecs[:, V_CB:V_CB + 44] = _fm(inp["ffn_conv_b"][L], 44)
    rows128 = np.zeros((128, 2, 1024), np.float32)
    rows128[:, 0, :] = np.asarray(inp["norm_mix_post"][L], np.float32)[None, :]
    rows128[:, 1, :] = np.asarray(inp["norm_ffn_post"][L], np.float32)[None, :]
    w0rows = np.ascontiguousarray(np.asarray(inp["rw_w0"][L], np.float32).reshape(1, 2, 512))
    sink = np.asarray(inp["attn_sink"][L], np.float32)
    sinkrows = np.zeros((1, 2, 4, 128), np.float32)
    for g in range(2):
        for i in range(4):
            sinkrows[0, g, i, :] = sink[g * 4 + i]
    sinkrows = sinkrows.reshape(1, 2, 512)
    lora = np.zeros((128, 4, 512), np.float32)
    lora[:, 0, :] = np.asarray(inp["rw_w2"][L], np.float32).reshape(128, 512)
    lora[:, 1, :] = np.asarray(inp["rw_a2"][L], np.float32).reshape(128, 512)
    g2 = np.asarray(inp["rw_g2"][L], np.float32)
    lora[:, 2, :] = g2[0:128]
    lora[0:32, 3, :] = g2[128:160]
    return {
        "w_in": w_in_p, "wba": wba_p,
        "wbr": np.ascontiguousarray(np.asarray(inp["w_branch_rwkv"][L], np.float32)),
        "wout": np.ascontiguousarray(np.asarray(inp["w_out"][L], np.float32)),
        "wup": np.ascontiguousarray(np.asarray(inp["w_ffn_up"][L], np.float32)),
        "wdn": np.ascontiguousarray(np.asarray(inp["w_ffn_down"][L], np.float32)),
        "vecs": vecs, "rows": rows128, "w0rows": w0rows, "sinkrows": sinkrows, "lora": lora,
        "cst": _constants(),
    }


def _layout_core(seqs, nslot):
    xh = np.zeros((nslot, NTH, D), np.float32)
    flags = np.zeros((nslot, 2), np.float32)
    s = 0
    place = []
    for x in seqs:
        T = x.shape[0]
        n = T // SEG
        xp = np.zeros((T + 2 * HALO, D), np.float32)
        xp[HALO:HALO + T] = x
        for k in range(n):
            xh[s + k] = xp[k * SEG:k * SEG + NTH]
            flags[s + k, 0] = 1.0 if k > 0 else 0.0
            flags[s + k, 1] = 1.0 if k < n - 1 else 0.0
        place.append((s, n))
        s += n
    fl = np.ascontiguousarray(np.broadcast_to(flags.reshape(1, nslot * 2), (128, nslot * 2))).astype(np.float32)
    return xh, fl, place


def weight_schedule(nslot, passes):
    q = []
    if "A" in passes:
        for s in reversed(range(nslot)):
            q += [("ZG0", s), ("ZG1", s), ("ZG2", s), ("ZG3", s)]
    if "B" in passes:
        for s in range(nslot):
            q += [("QG", s), ("KVG", s), ("ZG0", s), ("ZG1", s), ("ZG2", s), ("ZG3", s)]
            for h in range(2):
                q += [("GA%d" % h, s), ("GR%d" % h, s), ("BAR%d" % h, s)]
            q += [("WO0", s), ("WO1", s)]
    if "C" in passes:
        for s in range(nslot):
            for g in range(6):
                q += [("UG%d" % g, s), ("UU%d" % g, s)]
            for g in range(6):
                q += [("DN%d" % g, s)]
    return q


def build(nslot, passes="ABC", dbg=()):
    nc = bass.Bass("TRN2", target_bir_lowering=False)

    def din(name, shape, dt=F32):
        return nc.dram_tensor(name, list(shape), dt, kind="ExternalInput").ap()

    def dout(name, shape, dt=F32):
        return nc.dram_tensor(name, list(shape), dt, kind="ExternalOutput").ap()

    def dscr(name, shape, dt):
        return nc.dram_tensor(name, list(shape), dt).ap()

    xh = din("xh", [nslot, NTH, D])
    flags_d = din("flags", [128, 2 * nslot])
    w_in_d = din("w_in", [1024, 4864])
    wba_d = din("wba", [512, 1024])
    wbr_d = din("wbr", [512, 1024])
    wout_d = din("wout", [1024, 1024])
    wup_d = din("wup", [1024, 5632])
    wdn_d = din("wdn", [2816, 1024])
    vecs_d = din("vecs", [128, NV])
    rows_d = din("rows", [128, 2, 1024])
    w0_d = din("w0rows", [1, 2, 512])
    sink_d = din("sinkrows", [1, 2, 512])
    lora_d = din("lora", [128, 4, 512])
    cst_d = din("cst", [128, NCST])
    y_out = dout("y", [nslot * SEG, D])
    dbg_out = {}
    for name, shape in dbg:
        dbg_out[name] = dout("dbg_" + name, shape)

    w_in_b = dscr("w_in_b", [1024, 4864], BF16)
    wba_b = dscr("wba_b", [512, 1024], BF16)
    wbr_b = dscr("wbr_b", [512, 1024], BF16)
    wout_b = dscr("wout_b", [1024, 1024], BF16)
    wup_b = dscr("wup_b", [1024, 5632], BF16)
    wdn_b = dscr("wdn_b", [2816, 1024], BF16)
    ybwd_d = dscr("ybwd", [nslot, 128, 4 * SEG], F32)
    hbuf = dscr("hbuf", [nslot * SEG + 2, D], F32)

    P = Prog()
    es = ExitStack()

    def sb(name, shape, dt):
        return es.enter_context(nc.sbuf_tensor("s_" + name, list(shape), dt))

    psum = es.enter_context(nc.psum_tensor("psum", [128, 4096], F32))

    def PB(b, lo=0, hi=512):
        return psum[:, b * 512 + lo:b * 512 + hi]

    def bk(*banks):
        r = []
        for b in banks:
            r += ["pb%da" % b, "pb%db" % b]
        return r

    pbT = psum[:, 7 * 512:8 * 512].bitcast(BF16)

    ident_b = sb("ident_b", [128, 128], BF16)
    ones_b = sb("ones_b", [128, 128], BF16)
    bones_f = sb("bones_f", [128, 128], F32)
    bones_b = sb("bones_b", [128, 128], BF16)
    M1 = sb("M1", [128, 2, 128], F32)
    M2 = sb("M2", [128, 2, 64], F32)
    TRI = sb("TRI", [128, 2, 384], F32)
    I2 = sb("I2", [128, 64], BF16)
    PEN = sb("PEN", [128, 6, 512], BF16)
    vecs = sb("vecs", [128, NV], F32)
    c0 = sb("c0", [128, 16], F32)
    omka = sb("omka", [128, 4], F32)
    rows = sb("rows", [128, 1024], F32)
    gB = sb("gB", [128, 8, 128], F32)
    flags = sb("flags", [128, 2 * nslot], F32)
    rb = sb("rb", [2, 4, 512], BF16)
    w2b = sb("w2b", [128, 512], BF16)
    a2b = sb("a2b", [128, 512], BF16)
    g2b = sb("g2b", [128, 2, 512], BF16)
    onesf = sb("onesf", [128, 128], F32)

    xt = [sb("xt%d" % i, [128, 1024], F32) for i in range(3)]
    ub = [sb("ub%d" % i, [128, 1024], BF16) for i in range(2)]
    st_ssq = [sb("ssq%d" % i, [128, 2], F32) for i in range(4)]
    st_rstd = [sb("rstd%d" % i, [128, 1], F32) for i in range(4)]
    junk = sb("junk", [128, 1024], BF16)
    uT = sb("uT", [128, 8, NTH], BF16)
    wbuf = [sb("wbuf%d" % i, [128, 4096], BF16) for i in range(NW)]

    cnt = {"x": 0, "st": 0}

    def dma(eng, out, in_, reads, writes, semkey):
        P.add(eng, lambda e: e.dma_start(out=out, in_=in_), reads=reads, writes=writes, dma=True, semkey=semkey + "_" + eng)

    def act_copy(out, in_, reads, writes):
        P.add("act", lambda e: e.copy(out, in_), reads=reads, writes=writes)

    def dve_copy(out, in_, reads, writes):
        P.add("dve", lambda e: e.tensor_copy(out, in_), reads=reads, writes=writes)

    def mm(out, lhsT, rhs, start, stop, r, w, tp=None):
        if tp is None:
            P.add("pe", lambda e: e.matmul(out, lhsT, rhs, start=start, stop=stop), reads=r, writes=w)
        else:
            P.add("pe", lambda e: e.matmul(out, lhsT, rhs, start=start, stop=stop, tile_position=tp), reads=r, writes=w)

    def actf(out, in_, func, r, w, bias=None, scale=None, accum=None):
        kw = {}
        if bias is not None:
            kw["bias"] = bias
        if scale is not None:
            kw["scale"] = scale
        if accum is not None:
            kw["accum_out"] = accum
        P.add("act", lambda e: e.activation(out=out, in_=in_, func=func, **kw), reads=r, writes=w)

    def tt(eng, out, in0, in1, op, r, w):
        P.add(eng, lambda e: e.tensor_tensor(out=out, in0=in0, in1=in1, op=op), reads=r, writes=w)

    def stt(out, in0, scalar, in1, op0, op1, r, w):
        P.add("dve", lambda e: e.scalar_tensor_tensor(out=out, in0=in0, scalar=scalar, in1=in1, op0=op0, op1=op1), reads=r, writes=w)

    def tsc(eng, out, in0, s1, s2, op0, op1, r, w):
        P.add(eng, lambda e: e.tensor_scalar(out, in0, s1, s2, op0, op1), reads=r, writes=w)

    def tsmul(eng, out, in0, s, r, w):
        P.add(eng, lambda e: e.tensor_scalar_mul(out, in0, s), reads=r, writes=w)

    def recip(out, in_, r, w):
        P.add("dve", lambda e: e.reciprocal(out, in_), reads=r, writes=w)

    def load_gains(which):
        dma("pool", rows[:], rows_d[:, which, :], [], ["rows"], "ld_rows")
        for c in range(8):
            col = (V_G1 if which == 0 else V_G2) + c
            actf(gB[:, c, :], onesf[:], AF.Copy, ["vecs", "onesf"], ["gB"], scale=vecs[:, col:col + 1])

    def setup():
        stg = xt[0]
        dma("pool", vecs[:], vecs_d, [], ["vecs"], "ld_vecs")
        dma("pool", flags[:], flags_d, [], ["flags"], "ld_flags")
        dma("pool", stg[:, 0:1024], cst_d[:, 0:1024], [], ["xt0"], "xin0")
        dve_copy(ident_b[:], stg[:, C_ID:C_ID + 128], ["xt0"], ["ident_b"])
        dve_copy(bones_f[:], stg[:, C_BONES:C_BONES + 128], ["xt0"], ["bones_f"])
        dve_copy(bones_b[:], stg[:, C_BONES:C_BONES + 128], ["xt0"], ["bones_b"])
        dve_copy(M1[:].rearrange("p a b -> p (a b)"), stg[:, C_M1:C_M1 + 256], ["xt0"], ["M1"])
        dve_copy(M2[:].rearrange("p a b -> p (a b)"), stg[:, C_M2:C_M2 + 128], ["xt0"], ["M2"])
        dve_copy(TRI[:, 0, :], stg[:, C_TRI:C_TRI + 384], ["xt0"], ["TRI"])
        dma("pool", xt[1][:, 0:448], cst_d[:, 1024:1472], [], ["xt1"], "xin1")
        dve_copy(TRI[:, 1, :], xt[1][:, 0:384], ["xt1"], ["TRI"])
        dve_copy(I2[:], xt[1][:, 384:448], ["xt1"], ["I2"])
        for k in range(3):
            t = xt[(k + 2) % 3]
            key = "xt%d" % ((k + 2) % 3)
            dma("pool", t[:], cst_d[:, C_PEN + k * 1024:C_PEN + (k + 1) * 1024], [], [key], "xin%d" % ((k + 2) % 3))
            dve_copy(PEN[:, 2 * k:2 * k + 2, :].rearrange("p a b -> p (a b)"), t[:], [key], ["PEN"])
        P.add("dve", lambda e: e.memset(ones_b[:], 1.0), writes=["ones_b"])
        P.add("dve", lambda e: e.memset(onesf[:], 1.0), writes=["onesf"])
        P.add("dve", lambda e: e.memset(epsc[:], GN_EPS), writes=["epsc"])
        dma("pool", xt[0][:, 0:1024], lora_d[:, 0:2, :].rearrange("p a b -> p (a b)"), [], ["xt0"], "xin0")
        dve_copy(w2b[:], xt[0][:, 0:512], ["xt0"], ["w2b"])
        dve_copy(a2b[:], xt[0][:, 512:1024], ["xt0"], ["a2b"])
        dma("pool", xt[1][:, 0:1024], lora_d[:, 2:4, :].rearrange("p a b -> p (a b)"), [], ["xt1"], "xin1")
        dve_copy(g2b[:].rearrange("p a b -> p (a b)"), xt[1][:, 0:1024], ["xt1"], ["g2b"])
        tt("dve", c0[:], vecs[:, V_MUP:V_MUP + 16], vecs[:, V_MUN:V_MUN + 16], ALU.add, ["vecs"], ["c0"])
        tsc("dve", c0[:], c0[:], -1.0, 1.0, ALU.mult, ALU.add, ["c0"], ["c0"])
        tsc("dve", omka[:], vecs[:, V_KA:V_KA + 4], -1.0, 1.0, ALU.mult, ALU.add, ["vecs"], ["omka"])
        w0f = xt[2][0:1, 0:1024].rearrange("p (a b) -> p a b", a=2)
        w0t = xt[0][0:1, 0:1024].rearrange("p (a b) -> p a b", a=2)
        sinkf = xt[1][0:1, 0:1024].rearrange("p (a b) -> p a b", a=2)
        lo_b = ub[0][0:1, 0:1024].rearrange("p (a b) -> p a b", a=2)
        dma("pool", w0f, w0_d, [], ["xt2"], "xin2")
        dma("pool", sinkf, sink_d, [], ["xt1"], "xin1")
        dve_copy(rb[0:1, 0:2, :], w0f, ["xt2"], ["rb"])
        dve_copy(w0t, rb[0:1, 0:2, :], ["rb"], ["xt0"])
        tt("dve", w0t, w0f, w0t, ALU.subtract, ["xt2", "xt0"], ["xt0"])
        dve_copy(lo_b, w0t, ["xt0"], ["ub0"])
        dma("pool", rb[1:2, 0:2, :], lo_b, ["ub0"], ["rb"], "ubst0")
        actf(rb[0:1, 2:4, :], sinkf, AF.Exp, ["xt1"], ["rb"])
        P.add("dve", lambda e: e.memset(xt[2][0:1, :], 0.0), reads=["xt2"], writes=["xt2"])
        dma("pool", hbuf[0:1, :], xt[2][0:1, :], ["xt2"], ["hb_first"], "xst2")
        dma("pool", hbuf[nslot * SEG + 1:nslot * SEG + 2, :], xt[2][0:1, :], ["xt2"], ["hb_last"], "xst2")

    def prepass():
        jobs = []
        for (src, dst, R, Cc) in ((w_in_d, w_in_b, 1024, 4864), (wba_d, wba_b, 512, 1024), (wbr_d, wbr_b, 512, 1024),
                                  (wout_d, wout_b, 1024, 1024), (wup_d, wup_b, 1024, 5632), (wdn_d, wdn_b, 2816, 1024)):
            for rc in range(R // 128):
                for c0_ in range(0, Cc, 1024):
                    w = min(1024, Cc - c0_)
                    jobs.append((src[rc * 128:(rc + 1) * 128, c0_:c0_ + w], dst[rc * 128:(rc + 1) * 128, c0_:c0_ + w], w))
        for i, (s_ap, d_ap, w) in enumerate(jobs):
            a = i % 3
            b = i % 2
            dma("sp", xt[a][:, 0:w], s_ap, [], ["xt%d" % a], "xin%d" % a)
            if i % 2 == 0:
                dve_copy(ub[b][:, 0:w], xt[a][:, 0:w], ["xt%d" % a], ["ub%d" % b])
            else:
                act_copy(ub[b][:, 0:w], xt[a][:, 0:w], ["xt%d" % a], ["ub%d" % b])
            dma("pool", d_ap, ub[b][:, 0:w], ["ub%d" % b], ["wscr"], "ubst%d" % b)

    wq = weight_schedule(nslot, passes)
    wstate = {"issued": 0, "taken": 0, "released": 0}
    wfm = lambda t: t.rearrange("(c p) n -> p c n", p=128)

    def wsrc(tag):
        if tag == "QG":
            return wfm(w_in_b)[:, :, 0:512], 8, 512
        if tag == "KVG":
            return wfm(w_in_b)[:, :, 512:768], 8, 256
        if tag.startswith("ZG"):
            g = int(tag[2])
            return wfm(w_in_b)[:, :, 768 + g * 512:768 + (g + 1) * 512], 8, 512
        if tag.startswith("GA"):
            h = int(tag[2])
            return wfm(w_in_b)[:, :, 2816 + h * 512:2816 + (h + 1) * 512], 8, 512
        if tag.startswith("GR"):
            h = int(tag[2])
            return wfm(w_in_b)[:, :, 3840 + h * 512:3840 + (h + 1) * 512], 8, 512
        if tag.startswith("WO"):
            h = int(tag[2])
            return wfm(wout_b)[:, :, h * 512:(h + 1) * 512], 8, 512
        if tag.startswith("UG"):
            g = int(tag[2])
            n = 512 if g < 5 else 256
            return wfm(wup_b)[:, :, g * 512:g * 512 + n], 8, n
        if tag.startswith("UU"):
            g = int(tag[2])
            n = 512 if g < 5 else 256
            return wfm(wup_b)[:, :, 2816 + g * 512:2816 + g * 512 + n], 8, n
        if tag.startswith("DN"):
            g = int(tag[2])
            kc = 4 if g < 5 else 2
            return wfm(wdn_b)[:, g * 4:g * 4 + kc, :], kc, 1024
        raise KeyError(tag)

    def w_issue():
        i = wstate["issued"]
        tag, _ = wq[i]
        b = i % NW
        if tag.startswith("BAR"):
            h = int(tag[3])
            v = wbuf[b][:, :].rearrange("p (a c n) -> p a c n", a=2, c=4)
            dma("sp", v[:, 0, :, :], wfm(wba_b)[:, :, h * 512:(h + 1) * 512], ["wscr"], ["wbuf%d" % b], "w%d" % b)
            dma("sp", v[:, 1, :, :], wfm(wbr_b)[:, :, h * 512:(h + 1) * 512], ["wscr"], ["wbuf%d" % b], "w%d" % b)
        else:
            src, kc, n = wsrc(tag)
            v = wbuf[b][:, 0:kc * n].rearrange("p (c n) -> p c n", c=kc)
            dma("sp", v, src, ["wscr"], ["wbuf%d" % b], "w%d" % b)
        wstate["issued"] += 1

    def w_pump():
        while wstate["issued"] < len(wq) and wstate["issued"] - wstate["released"] < NW:
            w_issue()

    def w_take(tag, slot):
        i = wstate["taken"]
        assert wq[i] == (tag, slot), (wq[i], tag, slot)
        if wstate["issued"] <= i:
            assert wstate["issued"] - wstate["released"] < NW, "too many weight groups held"
            w_issue()
        wstate["taken"] += 1
        b = i % NW
        return wbuf[b], "wbuf%d" % b

    def w_done(k=1):
        wstate["released"] += k
        assert wstate["released"] <= wstate["taken"]
        w_pump()

    def norm_block(src_ap, nrow, which, dst, dst_key, col0, src_reads=()):
        i = cnt["x"]
        cnt["x"] += 1
        a, b, q = i % 3, i % 2, i % 4
        xk, uk = "xt%d" % a, "ub%d" % b
        dma("pool", xt[a][0:nrow, :], src_ap, list(src_reads), [xk], "xin%d" % a)
        P.add("act", lambda e: e.activation(out=junk[0:nrow, :], in_=xt[a][0:nrow, :], func=AF.Square, accum_out=st_ssq[q][0:nrow, 0:1]),
              reads=[xk], writes=["junk", "ssq%d" % q])
        P.add("dve", lambda e: e.tensor_scalar(st_rstd[q][0:nrow, :], st_ssq[q][0:nrow, 0:1], 1.0 / D, NORM_EPS, ALU.mult, ALU.add),
              reads=["ssq%d" % q], writes=["rstd%d" % q])
        P.add("act", lambda e: e.activation(out=st_rstd[q][0:nrow, :], in_=st_rstd[q][0:nrow, :], func=AF.Sqrt),
              reads=["rstd%d" % q], writes=["rstd%d" % q])
        P.add("dve", lambda e: e.reciprocal(st_rstd[q][0:nrow, :], st_rstd[q][0:nrow, :]), reads=["rstd%d" % q], writes=["rstd%d" % q])
        P.add("dve", lambda e: e.tensor_scalar_mul(ub[b][0:nrow, :], xt[a][0:nrow, :], st_rstd[q][0:nrow, :]),
              reads=[xk, "rstd%d" % q], writes=[uk])
        for c in range(8):
            P.add("pe", lambda e, c=c: e.transpose(pbT[:, c * 128:c * 128 + nrow], ub[b][0:nrow, c * 128:(c + 1) * 128], ident_b[0:nrow, 0:nrow]),
                  reads=[uk, "ident_b"], writes=bk(7))
        P.add("dve", lambda e: e.tensor_tensor(out=dst[:, :, col0:col0 + nrow],
                                               in0=pbT[:, :].rearrange("p (c n) -> p c n", c=8)[:, :, 0:nrow],
                                               in1=gB[:, :, 0:nrow], op=ALU.mult),
              reads=bk(7) + ["gB"], writes=[dst_key])

    ARENA_BYTES = 73 * 1024
    arena = sb("arena", [128, ARENA_BYTES // 4], F32)
    carve_state = {}

    def carve(group, name, shape, dt):
        off = carve_state.get(group, 0)
        nel = 1
        for d_ in shape[1:]:
            nel *= d_
        nbytes = nel * (4 if dt == F32 else 2)
        nbytes_al = (nbytes + 31) // 32 * 32
        assert off + nbytes_al <= ARENA_BYTES, (group, name, off, nbytes_al)
        carve_state[group] = off + nbytes_al
        v = arena[0:shape[0], off // 4:(off + nbytes) // 4]
        if dt != F32:
            v = v.bitcast(dt)
        if len(shape) == 3:
            v = v.rearrange("p (a b) -> p a b", a=shape[1])
        elif len(shape) == 4:
            v = v.rearrange("p (a b c) -> p a b c", a=shape[1], b=shape[2])
        elif len(shape) == 5:
            v = v.rearrange("p (a b c d) -> p a b c d", a=shape[1], b=shape[2], c=shape[3])
        groups.setdefault(group, []).append(name)
        return v

    groups = {}
    tmp4 = [sb("tmp%d" % i, [128, 512], F32) for i in range(4)]
    carve_state["att"] = 40 * 1024
    qT = carve("att", "qT", [128, 4, SEG], BF16)
    kT = carve("att", "kT", [128, NTH], BF16)
    vtok = carve("att", "vtok", [128, NBLK, 128], BF16)
    ptb = [carve("att", "ptb%d" % i, [128, 512], BF16) for i in range(3)]
    rden = carve("att", "rden", [128, 512], F32)
    fones = sb("fones", [128, 2, 64], BF16)
    oattn = sb("oattn", [128, 4, SEG], BF16)
    zs = sb("zs", [128, 16, SEG], BF16)
    ztmp = [tmp4[0], tmp4[1]]
    ztmp2 = [tmp4[2], tmp4[3]]
    ps_rot = {"d": 0}

    def nbank():
        b = ps_rot["d"] % 2
        ps_rot["d"] += 1
        return b

    def slot_load_norm(S):
        for b in range(NBLK):
            norm_block(xh[S, b * 128:(b + 1) * 128, :], 128, 0, uT, "uT", b * 128)

    def proj_fm_tile(wb, wkey, kc, n, col, src, skey, tok_lo, ntok, bank):
        wv = wb[:, 0:kc * n].rearrange("p (c n) -> p c n", c=kc)
        for c in range(kc):
            mm(PB(bank, 0, ntok), wv[:, c, col:col + 128], src[:, c, tok_lo:tok_lo + ntok], c == 0, c == kc - 1,
               [wkey, skey], bk(bank))

    def qkv_project(S):
        wb, wkey = w_take("QG", S)
        for i in range(4):
            bank = nbank()
            proj_fm_tile(wb, wkey, 8, 512, i * 128, uT, "uT", HALO, SEG, bank)
            act_copy(qT[:, i, :], PB(bank), bk(bank), ["qT"])
        w_done()
        wb, wkey = w_take("KVG", S)
        for (lo, n) in ((0, 512), (512, 256)):
            bank = nbank()
            proj_fm_tile(wb, wkey, 8, 256, 0, uT, "uT", lo, n, bank)
            act_copy(kT[:, lo:lo + n], PB(bank, 0, n), bk(bank), ["kT"])
        wv = wb[:, 0:8 * 256].rearrange("p (c n) -> p c n", c=8)
        for half in range(2):
            bank = nbank()
            for bb in range(3):
                b = half * 3 + bb
                for c in range(8):
                    mm(PB(bank, bb * 128, (bb + 1) * 128), uT[:, c, b * 128:(b + 1) * 128], wv[:, c, 128:256], c == 0, c == 7,
                       [wkey, "uT"], bk(bank))
            dve_copy(vtok[:, half * 3:half * 3 + 3, :].rearrange("p a b -> p (a b)"), PB(bank, 0, 384), bk(bank), ["vtok"])
        w_done()

    def attention(S):
        fp = flags[:, 2 * S:2 * S + 1]
        tsmul("dve", fones[:, 0, :], ones_b[:, 0:64], fp, ["ones_b", "flags"], ["fones"])
        tsmul("dve", vtok[:, 1, :], vtok[:, 1, :], fp, ["vtok", "flags"], ["vtok"])
        nqb = SEG // 128
        for qb in range(nqb):
            nb, db = 3 + (qb % 2), 5 + (qb % 2)
            for g in range(2):
                pr = slice(g * 64, (g + 1) * 64)
                for j in range(3):
                    kb = qb + j
                    sbk = (qb * 6 + g * 3 + j) % 3
                    pk = "ptb%d" % sbk
                    mm(PB(sbk), kT[pr, kb * 128:(kb + 1) * 128], qT[pr, :, qb * 128:(qb + 1) * 128], True, False,
                       ["kT", "qT"], bk(sbk), tp=(g * 64, 0))
                    mm(PB(sbk), ident_b[:], PEN[:, g * 3 + j, :], False, True, ["ident_b", "PEN"], bk(sbk))
                    actf(ptb[sbk][:], PB(sbk), AF.Exp, bk(sbk), [pk], scale=0.125)
                    mm(PB(nb)[pr, :], vtok[:, kb, pr], ptb[sbk][:], j == 0, j == 2, ["vtok", pk], bk(nb), tp=(0, g * 64))
                    if kb <= 1:
                        dl, dk = fones[:, 0, :], "fones"
                    else:
                        dl, dk = ones_b[:, 0:64], "ones_b"
                    mm(PB(db)[pr, :], dl, ptb[sbk][:], j == 0, False, [dk, pk], bk(db), tp=(0, g * 64))
                mm(PB(db)[pr, :], ones_b[0:1, 0:64], rb[0:1, 2 + g, :], False, True, ["ones_b", "rb"], bk(db), tp=(0, g * 64))
            recip(rden[:], PB(db), bk(db), ["rden"])
            tt("dve", oattn[:, :, qb * 128:(qb + 1) * 128], PB(nb).rearrange("p (a b) -> p a b", a=4),
               rden[:].rearrange("p (a b) -> p a b", a=4), ALU.mult, bk(nb) + ["rden"], ["oattn"])

    ZSUB = ((0, 510), (510, 2))

    def z_project(S, tiles):
        cur = None
        for zt in tiles:
            g = zt // 4
            if cur is None or cur[0] != g:
                if cur is not None:
                    w_done()
                wb, wkey = w_take("ZG%d" % g, S)
                cur = (g, wb, wkey)
            _, wb, wkey = cur
            for (o, n) in ZSUB:
                bank = nbank()
                ti = bank
                proj_fm_tile(wb, wkey, 8, 512, (zt % 4) * 128, uT, "uT", HALO + o - 1, n + 2, bank)
                actf(ztmp[ti][:, 0:n], PB(bank, 1, n + 1), AF.Copy, bk(bank) + ["c0"], ["ztmp%d" % ti], scale=c0[:, zt:zt + 1])
                stt(ztmp2[ti][:, 0:n], PB(bank, 0, n), vecs[:, V_MUP + zt:V_MUP + zt + 1], ztmp[ti][:, 0:n], ALU.mult, ALU.add,
                    bk(bank) + ["vecs", "ztmp%d" % ti], ["ztmq%d" % ti])
                stt(zs[:, zt, o:o + n], PB(bank, 2, n + 2), vecs[:, V_MUN + zt:V_MUN + zt + 1], ztmp2[ti][:, 0:n], ALU.mult, ALU.add,
                    bk(bank) + ["vecs", "ztmq%d" % ti], ["zs%d" % zt])
        w_done()

    def rw(name, shape, dt):
        return carve("rw", name, shape, dt)

    twd = rw("twd", [128, GT], BF16)
    sigtok = [rw("sigtok%d" % i, [128, 512], F32) for i in range(2)]
    E = [rw("E%d" % i, [128, 4, GT], F32) for i in range(2)]
    kq = rw("kq", [128, GT], F32)
    ksq = rw("ksq", [128, GT], F32)
    nrm = rw("nrm", [128, GT], F32)
    kk = rw("kk", [128, GT], F32)
    asig = [rw("asig%d" % i, [128, GT], F32) for i in range(2)]
    kdir = [rw("kdir%d" % i, [128, GT], F32) for i in range(2)]
    t1 = rw("t1", [128, GT], F32)
    bbv = rw("bbv", [128, GT], F32)
    bbar = rw("bbar", [128, GT], BF16)
    kbar = rw("kbar", [128, GT], BF16)
    AR = [rw("AR%d" % i, [128, 4, GRP, 2, C], BF16) for i in range(2)]
    BT = [rw("BT%d" % i, [128, 4, GT], BF16) for i in range(2)]
    KTl = [rw("KTl%d" % i, [128, 4, GT], BF16) for i in range(2)]
    bbt = [rw("bbt%d" % i, [128, 4, GRP, C], BF16) for i in range(2)]
    kbt = [rw("kbt%d" % i, [128, 4, GRP, C], BF16) for i in range(2)]
    vt = [rw("vt%d" % i, [128, 4, GRP, C], BF16) for i in range(2)]
    GC = [rw("GC%d" % i, [128, 4, GRP], F32) for i in range(2)]
    GB1 = rw("GB1", [128, 16, 128], BF16)
    GB2 = rw("GB2", [128, 16, 128], BF16)
    Pk = [rw("Pk%d" % i, [128, 16, 2, C], BF16) for i in range(2)]
    Zk = [rw("Zk%d" % i, [128, 16, C], BF16) for i in range(2)]
    Wb = rw("Wb", [128, 4, C], BF16)
    Ub = rw("Ub", [128, 4, C], BF16)
    Tf = sb("Tf", [128, 4, C], F32)
    Tst = sb("Tst", [128, 4, C], BF16)
    yT = sb("yT", [128, 4, SEG], F32)
    bp = sb("bp", [128, 4, SEG], BF16)

    def flat(ap3):
        return ap3.rearrange("p a b -> p (a b)")

    def rwkv_state_init():
        P.add("dve", lambda e: e.memset(flat(Tf[:]), 0.0), writes=["Tf"])

    def rwkv_slot_begin(S, d):
        col = 2 * S + (0 if d == 0 else 1)
        tsmul("dve", flat(Tf[:]), flat(Tf[:]), flags[:, col:col + 1], ["Tf", "flags"], ["Tf"])
        act_copy(flat(Tst[:]), flat(Tf[:]), ["Tf"], ["Tst"])

    def rwkv_group(S, d, gi, passB):
        gb = gi % 2
        t0 = gi * GT
        dr = slice(d * 64, (d + 1) * 64)
        ARk, BTk, KTk, bbtk, kbtk, vtk, GCk = ("AR%d" % gb, "BT%d" % gb, "KTl%d" % gb, "bbt%d" % gb, "kbt%d" % gb,
                                               "vt%d" % gb, "GC%d" % gb)
        actf(twd[dr, :], zs[dr, 12, t0:t0 + GT], AF.Tanh, ["zs12"], ["twd"])
        for blk in range(2):
            mm(PB(blk), twd[dr, blk * 128:(blk + 1) * 128], w2b[dr, :], True, False, ["twd", "w2b"], bk(blk), tp=(d * 64, 0))
            mm(PB(blk), ones_b[0:2, 0:128], rb[0:2, d, :], False, True, ["ones_b", "rb"], bk(blk))
            actf(sigtok[blk][:], PB(blk), AF.Sigmoid, bk(blk), ["sigtok%d" % blk])
        for ct in range(4):
            eb = ct % 2
            Et, Ek = E[eb], "E%d" % eb
            for blk in range(2):
                bank = 2 + blk
                mm(PB(bank, 0, 384), sigtok[blk][:, ct * 128:(ct + 1) * 128], TRI[:, d, :], True, True,
                   ["sigtok%d" % blk, "TRI"], bk(bank))
                actf(Et[:, 0:3, blk * 128:(blk + 1) * 128], PB(bank, 0, 384).rearrange("p (a b) -> p a b", a=3), AF.Exp,
                     bk(bank), [Ek], scale=-KAPPA)
                actf(Et[:, 3, blk * 128:(blk + 1) * 128], PB(bank, 0, 128), AF.Exp, bk(bank), [Ek], scale=KAPPA)
            kz, kzk = zs[:, 4 + ct, t0:t0 + GT], "zs%d" % (4 + ct)
            tsmul("dve", kq[:], kz, vecs[:, V_KK + ct:V_KK + ct + 1], [kzk, "vecs"], ["kq"])
            actf(ksq[:], kq[:], AF.Square, ["kq"], ["ksq"])
            mm(PB(4, 0, 256), bones_f[:], ksq[:], True, True, ["bones_f", "ksq"], bk(4))
            actf(nrm[:], PB(4, 0, 256), AF.Sqrt, bk(4), ["nrm"])
            P.add("dve", lambda e: e.tensor_scalar_max(nrm[:], nrm[:], 1e-12), reads=["nrm"], writes=["nrm"])
            recip(nrm[:], nrm[:], ["nrm"], ["nrm"])
            tt("dve", kk[:], kq[:], nrm[:], ALU.mult, ["kq", "nrm"], ["kk"])
            dirs = (0, 1) if passB else (d,)
            for dd in dirs:
                ddr = slice(dd * 64, (dd + 1) * 64)
                psa = PB(5, 0, 256) if dd == d else PB(5, 256, 512)
                mm(psa, a2b[ddr, ct * 128:(ct + 1) * 128], zs[ddr, 13, t0:t0 + GT], True, True, ["a2b", "zs13"], bk(5), tp=(dd * 64, 0))
                actf(asig[dd][:], psa, AF.Sigmoid, bk(5) + ["vecs"], ["asig%d" % dd],
                     bias=vecs[:, V_A0 + dd * 4 + ct:V_A0 + dd * 4 + ct + 1])
                tsc("dve", t1[:], asig[dd][:], vecs[:, V_KA + ct:V_KA + ct + 1], omka[:, ct:ct + 1], ALU.mult, ALU.add,
                    ["asig%d" % dd, "vecs", "omka"], ["t1"])
                tt("dve", kdir[dd][:], kz, t1[:], ALU.mult, [kzk, "t1"], ["kdir%d" % dd])
            if passB:
                tt("pool", t1[:], kdir[0][:], kdir[1][:], ALU.add, ["kdir0", "kdir1"], ["t1"])
                stt(bp[:, ct, t0:t0 + GT], zs[:, ct, t0:t0 + GT], vecs[:, V_RK + ct:V_RK + ct + 1], t1[:], ALU.mult, ALU.mult,
                    ["zs%d" % ct, "vecs", "t1"], ["bp"])
            tt("dve", bbv[:], kk[:], asig[d][:], ALU.mult, ["kk", "asig%d" % d], ["bbv"])
            v4 = lambda ap: ap.rearrange("p (c t) -> p c t", c=GRP)
            tt("dve", AR[gb][:, ct, :, 1, :], v4(zs[:, ct, t0:t0 + GT]), v4(Et[:, 0, :]), ALU.mult, ["zs%d" % ct, Ek], [ARk])
            stt(AR[gb][:, ct, :, 0, :], v4(kk[:]), -1.0, v4(Et[:, 1, :]), ALU.mult, ALU.mult, ["kk", Ek], [ARk])
            tt("pool", BT[gb][:, ct, :], bbv[:], Et[:, 3, :], ALU.mult, ["bbv", Ek], [BTk])
            tt("pool", KTl[gb][:, ct, :], kdir[d][:], Et[:, 3, :], ALU.mult, ["kdir%d" % d, Ek], [KTk])
            tt("pool", bbar[:], bbv[:], Et[:, 2, :], ALU.mult, ["bbv", Ek], ["bbar"])
            tt("pool", kbar[:], kdir[d][:], Et[:, 2, :], ALU.mult, ["kdir%d" % d, Ek], ["kbar"])
            tend = (C - 1) if d == 0 else 0
            dve_copy(GC[gb][:, ct, :], v4(Et[:, 0, :])[:, :, tend], [Ek], [GCk])
            vz, vzk = zs[:, 8 + ct, t0:t0 + GT], "zs%d" % (8 + ct)
            for e_ in range(2):
                er = slice(e_ * 64, (e_ + 1) * 64)
                tp = (e_ * 64, e_ * 64)
                for c in range(GRP):
                    cs = slice(c * C, (c + 1) * C)
                    mm(PB(6)[er, c * C:(c + 1) * C], bbar[er, cs], ident_b[er, er], True, True, ["bbar", "ident_b"], bk(6), tp=tp)
                    mm(PB(6)[er, 256 + c * C:256 + (c + 1) * C], kbar[er, cs], ident_b[er, er], True, True, ["kbar", "ident_b"], bk(6), tp=tp)
                    mm(PB(7)[er, c * C:(c + 1) * C], vz[er, cs], ident_b[er, er], True, True, [vzk, "ident_b"], bk(7), tp=tp)
            act_copy(flat(bbt[gb][:, ct, :, :]), PB(6, 0, 256), bk(6), [bbtk])
            act_copy(flat(kbt[gb][:, ct, :, :]), PB(6, 256, 512), bk(6), [kbtk])
            dve_copy(flat(vt[gb][:, ct, :, :]), PB(7, 0, 256), bk(7), [vtk])
        import os
        kcut = int(os.environ.get("K_CUT", "9")) if d == 1 else 9
        if kcut <= 1:
            return
        m1 = M1[:, d, :].unsqueeze(1).to_broadcast([128, 4, 128])
        m2 = M2[:, d, :].unsqueeze(1).to_broadcast([128, 4, C])
        for c in range(GRP):
            b1, b2 = c % 2, 2 + c % 2
            b3 = 4 + c % 2
            cs = slice(c * C, (c + 1) * C)
            for hp in range(4):
                for e_ in range(2):
                    er = slice(e_ * 64, (e_ + 1) * 64)
                    tp = (e_ * 64, e_ * 64)
                    arv = AR[gb][er, hp, c, :, :]
                    mm(PB(b1)[er, hp * 128:(hp + 1) * 128], BT[gb][er, hp, cs], arv, True, True, [BTk, ARk], bk(b1), tp=tp)
                    mm(PB(b2)[er, hp * 128:(hp + 1) * 128], KTl[gb][er, hp, cs], arv, True, True, [KTk, ARk], bk(b2), tp=tp)
                    mm(PB(b3)[er, hp * C:(hp + 1) * C], AR[gb][er, hp, c, 0, :], BT[gb][er, hp, cs], True, True,
                       [ARk, BTk], bk(b3), tp=tp)
            tt("dve", GB1[:, c * 4:(c + 1) * 4, :], PB(b1).rearrange("p (a b) -> p a b", a=4), m1, ALU.mult, bk(b1) + ["M1"], ["GB1"])
            tt("dve", GB2[:, c * 4:(c + 1) * 4, :], PB(b2).rearrange("p (a b) -> p a b", a=4), m1, ALU.mult, bk(b2) + ["M1"], ["GB2"])
            tt("dve", Pk[0][:, c * 4:(c + 1) * 4, 0, :], PB(b3, 0, 256).rearrange("p (a b) -> p a b", a=4), m2, ALU.mult,
               bk(b3) + ["M2"], ["Pk0"])
        tt("dve", Zk[0][:], GB1[:, :, 0:C], I2[:].unsqueeze(1).to_broadcast([128, 16, C]), ALU.add, ["GB1", "I2"], ["Zk0"])
        if kcut <= 2:
            return
        psP = psum[:, 0:2048]
        psZ = psum[:, 2048:3072]
        for k in range(5):
            cur, nxt = k % 2, (k + 1) % 2
            for slot in range(16):
                for e_ in range(2):
                    er = slice(e_ * 64, (e_ + 1) * 64)
                    tp = (e_ * 64, e_ * 64)
                    Pv = Pk[cur][er, slot, 0, :]
                    if k == 0:
                        PTv, ptk = GB1[er, slot, 0:C], "GB1"
                    else:
                        PTv, ptk = Pk[cur][er, slot, 1, :], "Pk%d" % cur
                    mm(psP[er, slot * 128:slot * 128 + C], PTv, Pv, True, True, [ptk, "Pk%d" % cur], bk(0, 1, 2, 3), tp=tp)
                    if k < 4:
                        mm(psP[er, slot * 128 + C:slot * 128 + 2 * C], Pv, PTv, True, True, [ptk, "Pk%d" % cur], bk(0, 1, 2, 3), tp=tp)
            if k < 4:
                act_copy(Pk[nxt][:].rearrange("p s o n -> p (s o n)"), psP, bk(0, 1, 2, 3), ["Pk%d" % nxt])
            else:
                act_copy(Pk[nxt][:, :, 0, :], psP.rearrange("p (s o n) -> p s o n", s=16, o=2)[:, :, 0, :], bk(0, 1, 2, 3), ["Pk%d" % nxt])
            for slot in range(16):
                for e_ in range(2):
                    er = slice(e_ * 64, (e_ + 1) * 64)
                    tp = (e_ * 64, e_ * 64)
                    mm(psZ[er, slot * C:(slot + 1) * C], Pk[nxt][er, slot, 0, :], Zk[cur][er, slot, :], True, True,
                       ["Pk%d" % nxt, "Zk%d" % cur], bk(4, 5), tp=tp)
            tt("dve", flat(Zk[nxt][:]), psZ, flat(Zk[cur][:]), ALU.add, bk(4, 5) + ["Zk%d" % cur], ["Zk%d" % nxt])
        Zf, Zfk = Zk[1], "Zk1"
        if kcut <= 3:
            return
        order = range(GRP) if d == 0 else range(GRP - 1, -1, -1)
        for c in order:
            tok = t0 + c * C
            for hp in range(4):
                for e_ in range(2):
                    er = slice(e_ * 64, (e_ + 1) * 64)
                    tp = (e_ * 64, e_ * 64)
                    slot = c * 4 + hp
                    o = PB(6)[er, hp * C:(hp + 1) * C]
                    mm(o, AR[gb][er, hp, c, 0, :], Tst[er, hp, :], True, False, [ARk, "Tst"], bk(6), tp=tp)
                    mm(o, GB2[er, slot, 0:C], vt[gb][er, hp, c, :], False, True, ["GB2", vtk], bk(6), tp=tp)
            act_copy(flat(Wb[:]), PB(6, 0, 256), bk(6), ["Wb"])
            for hp in range(4):
                for e_ in range(2):
                    er = slice(e_ * 64, (e_ + 1) * 64)
                    tp = (e_ * 64, e_ * 64)
                    slot = c * 4 + hp
                    mm(PB(6)[er, 256 + hp * C:256 + (hp + 1) * C], Zf[er, slot, :], Wb[er, hp, :], True, True, [Zfk, "Wb"], bk(6), tp=tp)
            dve_copy(flat(Ub[:]), PB(6, 256, 512), bk(6), ["Ub"])
            for hp in range(4):
                for e_ in range(2):
                    er = slice(e_ * 64, (e_ + 1) * 64)
                    tp = (e_ * 64, e_ * 64)
                    slot = c * 4 + hp
                    oy = PB(7)[er, hp * C:(hp + 1) * C]
                    mm(oy, Tst[er, hp, :], AR[gb][er, hp, c, 1, :], True, False, ["Tst", ARk], bk(7), tp=tp)
                    mm(oy, Ub[er, hp, :], GB1[er, slot, C:2 * C], False, False, ["Ub", "GB1"], bk(7), tp=tp)
                    mm(oy, vt[gb][er, hp, c, :], GB2[er, slot, C:2 * C], False, True, [vtk, "GB2"], bk(7), tp=tp)
                    ot = PB(7)[er, 256 + hp * C:256 + (hp + 1) * C]
                    mm(ot, bbt[gb][er, hp, c, :], Ub[er, hp, :], True, False, [bbtk, "Ub"], bk(7), tp=tp)
                    mm(ot, kbt[gb][er, hp, c, :], vt[gb][er, hp, c, :], False, True, [kbtk, vtk], bk(7), tp=tp)
            py = PB(7, 0, 256).rearrange("p (a b) -> p a b", a=4)
            if passB:
                tt("dve", yT[:, :, tok:tok + C], py, yT[:, :, tok:tok + C], ALU.add, bk(7) + ["yT"], ["yT"])
            else:
                dve_copy(yT[:, :, tok:tok + C], py, bk(7), ["yT"])
            tt("dve", Tf[:], Tf[:], GC[gb][:, :, c:c + 1].to_broadcast([128, 4, C]), ALU.mult, ["Tf", GCk], ["Tf"])
            tt("dve", flat(Tf[:]), flat(Tf[:]), PB(7, 256, 512), ALU.add, bk(7) + ["Tf"], ["Tf"])
            act_copy(flat(Tst[:]), flat(Tf[:]), ["Tf"], ["Tst"])

    yc, ysq, sd, bon = tmp4
    sg = sb("sg", [128, 2, SEG], BF16)
    orw = sb("orw", [128, 4, SEG], BF16)
    epsc = sb("epsc", [128, 1], F32)

    def rwkv_epilogue(S):
        actf(sg[:, 0, :], zs[:, 14, :], AF.Sigmoid, ["zs14"], ["sg"])
        actf(sg[0:32, 1, :], zs[0:32, 15, :], AF.Sigmoid, ["zs15"], ["sg"])
        for ct in range(4):
            b0 = nbank()
            mm(PB(b0), bones_f[:], yT[:, ct, :], True, True, ["bones_f", "yT"], bk(b0))
            stt(yc[:], PB(b0), -1.0 / C, yT[:, ct, :], ALU.mult, ALU.add, bk(b0) + ["yT"], ["yc"])
            actf(ysq[:], yc[:], AF.Square, ["yc"], ["ysq"])
            b1 = nbank()
            mm(PB(b1), bones_f[:], ysq[:], True, True, ["bones_f", "ysq"], bk(b1))
            actf(sd[:], PB(b1), AF.Sqrt, bk(b1) + ["epsc"], ["sd"], bias=epsc[:, 0:1], scale=1.0 / C)
            recip(sd[:], sd[:], ["sd"], ["sd"])
            tt("dve", yc[:], yc[:], sd[:], ALU.mult, ["yc", "sd"], ["yc"])
            tsc("dve", yc[:], yc[:], vecs[:, V_LNW + ct:V_LNW + ct + 1], vecs[:, V_LNB + ct:V_LNB + ct + 1], ALU.mult, ALU.add,
                ["yc", "vecs"], ["yc"])
            b2 = nbank()
            mm(PB(b2), bones_b[:], bp[:, ct, :], True, True, ["bones_b", "bp"], bk(b2))
            tt("dve", bon[:], PB(b2), zs[:, 8 + ct, :], ALU.mult, bk(b2) + ["zs%d" % (8 + ct)], ["bon"])
            tt("pool", yc[:], yc[:], bon[:], ALU.add, ["yc", "bon"], ["yc"])
            b3 = nbank()
            mm(PB(b3), g2b[:, 0, ct * 128:(ct + 1) * 128], sg[:, 0, :], True, False, ["g2b", "sg"], bk(b3))
            mm(PB(b3), g2b[0:32, 1, ct * 128:(ct + 1) * 128], sg[0:32, 1, :], False, True, ["g2b", "sg"], bk(b3))
            tt("dve", orw[:, ct, :], PB(b3), yc[:], ALU.mult, bk(b3) + ["yc"], ["orw"])

    mergedT = zs
    sga, sgr, tma, tmr = tmp4
    for grp_ in (("ztmp0", "yc", "sga", "hrA"), ("ztmp1", "ysq", "sgr", "hrB"), ("ztmq0", "sd", "tma"), ("ztmq1", "bon", "tmr")):
        for k_ in grp_:
            P.alias[k_] = tuple(x for x in grp_ if x != k_)

    def merge_branches(S):
        for h in range(2):
            wga, kga = w_take("GA%d" % h, S)
            wgr, kgr = w_take("GR%d" % h, S)
            wbr_, kbr = w_take("BAR%d" % h, S)
            wbv = wbr_[:, :].rearrange("p (a c n) -> p a c n", a=2, c=4)
            for mi in range(4):
                m = h * 4 + mi
                proj_fm_tile(wga, kga, 8, 512, mi * 128, uT, "uT", HALO, SEG, 0)
                actf(sga[:], PB(0), AF.Sigmoid, bk(0), ["sga"])
                proj_fm_tile(wgr, kgr, 8, 512, mi * 128, uT, "uT", HALO, SEG, 1)
                actf(sgr[:], PB(1), AF.Sigmoid, bk(1), ["sgr"])
                for c in range(4):
                    mm(PB(2), wbv[:, 0, c, mi * 128:(mi + 1) * 128], oattn[:, c, :], c == 0, c == 3, [kbr, "oattn"], bk(2))
                for c in range(4):
                    mm(PB(3), wbv[:, 1, c, mi * 128:(mi + 1) * 128], orw[:, c, :], c == 0, c == 3, [kbr, "orw"], bk(3))
                tt("dve", tma[:], PB(2), sga[:], ALU.mult, bk(2) + ["sga"], ["tma"])
                tt("dve", tmr[:], PB(3), sgr[:], ALU.mult, bk(3) + ["sgr"], ["tmr"])
                tt("pool", mergedT[:, m, :], tma[:], tmr[:], ALU.add, ["tma", "tmr"], ["zs%d" % m])
            w_done(3)

    def out_proj(S):
        w0_, k0_ = w_take("WO0", S)
        w1_, k1_ = w_take("WO1", S)
        wv = [w0_[:, :].rearrange("p (c n) -> p c n", c=8), w1_[:, :].rearrange("p (c n) -> p c n", c=8)]
        wk = [k0_, k1_]
        for tb in range(SEG // 128):
            i = cnt["st"]
            cnt["st"] += 1
            q = i % 4
            a = cnt["x"] % 3
            cnt["x"] += 1
            xk = "xt%d" % a
            dma("pool", xt[a][:], xh[S, HALO + tb * 128:HALO + (tb + 1) * 128, :], [], [xk], "xin%d" % a)
            for n2 in range(2):
                bank = 4 + n2
                for c in range(8):
                    mm(PB(bank), mergedT[:, c, tb * 128:(tb + 1) * 128], wv[n2][:, c, :], c == 0, c == 7, ["zs%d" % c, wk[n2]], bk(bank))
            post_norm_residual(q, (4, 5), xt[a], xk)
            dma("pool", hbuf[1 + S * SEG + tb * 128:1 + S * SEG + (tb + 1) * 128, :], xt[a][:], [xk], ["hb%d" % S], "xst%d" % a)
        w_done(2)

    def post_norm_residual(q, banks, res, rkey):
        sk, rk_ = "ssq%d" % q, "rstd%d" % q
        for n2 in range(2):
            actf(junk[:, n2 * 512:(n2 + 1) * 512], PB(banks[n2]), AF.Square, bk(banks[n2]), ["junk", sk], accum=st_ssq[q][:, n2:n2 + 1])
        tt("dve", st_rstd[q][:], st_ssq[q][:, 0:1], st_ssq[q][:, 1:2], ALU.add, [sk], [rk_])
        tsc("dve", st_rstd[q][:], st_rstd[q][:], 1.0 / D, NORM_EPS, ALU.mult, ALU.add, [rk_], [rk_])
        actf(st_rstd[q][:], st_rstd[q][:], AF.Sqrt, [rk_], [rk_])
        recip(st_rstd[q][:], st_rstd[q][:], [rk_], [rk_])
        for n2 in range(2):
            hk = "hrA" if n2 == 0 else "hrB"
            stt(tmp4[n2][:], PB(banks[n2]), st_rstd[q][:, 0:1], rows[:, n2 * 512:(n2 + 1) * 512], ALU.mult, ALU.mult,
                bk(banks[n2]) + [rk_, "rows"], [hk])
            tt("pool", res[:, n2 * 512:(n2 + 1) * 512], res[:, n2 * 512:(n2 + 1) * 512], tmp4[n2][:], ALU.add, [hk, rkey], [rkey])

    uT2 = carve("ffn", "uT2", [128, 8, SEG + 2], BF16)
    actT = [carve("ffn", "actT%d" % i, [128, NFT, 256], BF16) for i in range(2)]
    cg = [carve("ffn", "cg%d" % i, [128, 256], F32) for i in range(2)]
    cu = [carve("ffn", "cu%d" % i, [128, 256], F32) for i in range(2)]
    gl = [carve("ffn", "gl%d" % i, [128, 256], F32) for i in range(2)]
    for g_ in groups:
        others = tuple(k_ for g2_ in groups if g2_ != g_ for k_ in groups[g2_])
        for k_ in groups[g_]:
            P.alias[k_] = P.alias.get(k_, ()) + others

    def ffn_slot(S):
        base = S * SEG
        for (r0, n) in ((0, 128), (128, 128), (256, 128), (384, 128), (512, 2)):
            norm_block(hbuf[base + r0:base + r0 + n, :], n, 1, uT2, "uT2", r0,
                       src_reads=["hb_first", "hb_last"] + ["hb%d" % k for k in (S - 1, S, S + 1) if 0 <= k < nslot])
        for side, col in ((0, 0), (1, SEG + 1)):
            tsmul("dve", uT2[:, :, col:col + 1], uT2[:, :, col:col + 1], flags[:, 2 * S + side:2 * S + side + 1],
                  ["uT2", "flags"], ["uT2"])
        ci = 0
        for g in range(6):
            wg_, kg_ = w_take("UG%d" % g, S)
            wu_, ku_ = w_take("UU%d" % g, S)
            nt = 4 if g < 5 else 2
            n = nt * 128
            for ti in range(nt):
                f = g * 4 + ti
                for st in range(2):
                    x = ci % 2
                    ci += 1
                    bg, bu = 2 * x, 2 * x + 1
                    proj_fm_tile(wg_, kg_, 8, n, ti * 128, uT2, "uT2", st * 256, 258, bg)
                    proj_fm_tile(wu_, ku_, 8, n, ti * 128, uT2, "uT2", st * 256, 258, bu)
                    for (bank, dst, dk, ft) in ((bg, cg[x], "cg%d" % x, f), (bu, cu[x], "cu%d" % x, NFT + f)):
                        actf(dst[:], PB(bank, 1, 257), AF.Identity, bk(bank) + ["vecs"], [dk],
                             bias=vecs[:, V_CB + ft:V_CB + ft + 1], scale=vecs[:, V_CW + 44 + ft:V_CW + 44 + ft + 1])
                        stt(dst[:], PB(bank, 0, 256), vecs[:, V_CW + ft:V_CW + ft + 1], dst[:], ALU.mult, ALU.add,
                            bk(bank) + ["vecs", dk], [dk])
                        stt(dst[:], PB(bank, 2, 258), vecs[:, V_CW + 88 + ft:V_CW + 88 + ft + 1], dst[:], ALU.mult, ALU.add,
                            bk(bank) + ["vecs", dk], [dk])
                    actf(gl[x][:], cg[x][:], AF.Gelu_apprx_tanh, ["cg%d" % x], ["gl%d" % x])
                    tt("pool", actT[st][:, f, :], gl[x][:], cu[x][:], ALU.mult, ["gl%d" % x, "cu%d" % x], ["actT%d" % st])
            w_done(2)
        for g in range(6):
            wd_, kd_ = w_take("DN%d" % g, S)
            kc = 4 if g < 5 else 2
            wv = wd_[:, 0:kc * 1024].rearrange("p (c n) -> p c n", c=kc)
            for c in range(kc):
                f = g * 4 + c
                for tb in range(4):
                    st, o = tb // 2, (tb % 2) * 128
                    for n2 in range(2):
                        bank = tb * 2 + n2
                        mm(PB(bank), actT[st][:, f, o:o + 128], wv[:, c, n2 * 512:(n2 + 1) * 512], f == 0, f == NFT - 1,
                           ["actT%d" % st, kd_], bk(bank))
            w_done()
        for tb in range(4):
            i = cnt["st"]
            cnt["st"] += 1
            q = i % 4
            a = cnt["x"] % 3
            cnt["x"] += 1
            xk = "xt%d" % a
            dma("pool", xt[a][:], hbuf[1 + base + tb * 128:1 + base + (tb + 1) * 128, :], ["hb%d" % S], [xk], "xin%d" % a)
            post_norm_residual(q, (tb * 2, tb * 2 + 1), xt[a], xk)
            dma("pool", y_out[S * SEG + tb * 128:S * SEG + (tb + 1) * 128, :], xt[a][:], [xk], ["yout"], "xst%d" % a)

    def dump(name, ap, keys):
        if name in dbg_out:
            dma("pool", dbg_out[name], ap, keys, ["dbg_" + name], "dbg_" + name)

    setup()
    prepass()
    load_gains(0)
    if "A" in passes:
        rwkv_state_init()
        for S in reversed(range(nslot)):
            import os
            slot_load_norm(S)
            if "noz" in os.environ.get("K_DBG", ""):
                for g_ in range(4):
                    w_take("ZG%d" % g_, S)
                    w_done()
            else:
                z_project(S, range(14))
            if "nobegin" not in os.environ.get("K_DBG", ""):
                rwkv_slot_begin(S, 1)
            for gi in reversed(range(NGRP)):
                if "noscan" not in os.environ.get("K_DBG", ""):
                    rwkv_group(S, 1, gi, False)
            if os.environ.get("K_DBG", "") != "nostore":
                dma("pool", ybwd_d[S], flat(yT[:]), ["yT"], ["ybwd%d" % S], "yTst")
            if S == 0:
                dump("ybwd", flat(yT[:]), ["yT"])
    if "B" in passes:
        rwkv_state_init()
        for S in range(nslot):
            slot_load_norm(S)
            qkv_project(S)
            attention(S)
            z_project(S, range(16))
            if "A" in passes:
                dma("pool", flat(yT[:]), ybwd_d[S], ["ybwd%d" % S], ["yT"], "yTld")
            else:
                P.add("dve", lambda e: e.memset(flat(yT[:]), 0.0), writes=["yT"])
            rwkv_slot_begin(S, 0)
            for gi in range(NGRP):
                rwkv_group(S, 0, gi, True)
            if S == 0:
                dump("uT", flat(uT[:]), ["uT"])
                dump("oattn", flat(oattn[:]), ["oattn"])
                dump("zs", flat(zs[:]), ["zs%d" % i for i in range(16)])
                dump("yT", flat(yT[:]), ["yT"])
            rwkv_epilogue(S)
            merge_branches(S)
            if S == 0:
                dump("orw", flat(orw[:]), ["orw"])
                dump("mergedT", flat(mergedT[:, 0:8, :]), ["zs%d" % i for i in range(8)])
            out_proj(S)
    if "C" in passes:
        load_gains(1)
        for S in range(nslot):
            ffn_slot(S)
    P.add("pool", None, reads=["yout", "wscr"] + ["hb%d" % k for k in range(nslot)] + ["dbg_" + n for n in dbg_out]
          + ["ybwd%d" % k for k in range(nslot)])
    assert wstate["taken"] == len(wq) == wstate["released"], (wstate, len(wq))

    sems = P.assign(nc, es)
    with nc.Block() as block:
        @block.sync
        def _(e):
            P.run_engine("sp", e, sems)

        @block.tensor
        def _(e):
            P.run_engine("pe", e, sems)

        @block.scalar
        def _(e):
            P.run_engine("act", e, sems)

        @block.vector
        def _(e):
            P.run_engine("dve", e, sems)

        @block.gpsimd
        def _(e):
            P.run_engine("pool", e, sems)
    es.close()
    nc._n_ops = P.n
    return nc


def _assign_sequences():
    plan = [[("p", 0)]]
    counts = [5, 5, 5, 5, 4, 4, 4]
    k = 0
    for c in counts:
        plan.append([("s", k + i) for i in range(c)])
        k += c
    return plan


def kernel(**inputs):
    xp = np.asarray(inputs["x_prompt"], np.float32)
    xs = np.asarray(inputs["x_sample"], np.float32)
    wl = _layout_weights(inputs)
    plan = _assign_sequences()
    in_maps, places = [], []
    for core in range(NCORES):
        seqs = [xp[0] if kind == "p" else xs[i] for (kind, i) in plan[core]]
        xh, fl, place = _layout_core(seqs, SLOTS_FULL)
        m = dict(wl)
        m["xh"] = xh
        m["flags"] = fl
        in_maps.append(m)
        places.append(place)
    nc = build(SLOTS_FULL, "ABC")
    res = run_bass_kernel_spmd(nc, in_maps, core_ids=list(range(NCORES)))
    y_prompt = np.zeros_like(xp)
    y_sample = np.zeros_like(xs)
    for core in range(NCORES):
        y = np.asarray(res.results[core]["y"], np.float32)
        for (kind, i), (s0, n) in zip(plan[core], places[core]):
            blk = y[s0 * SEG:(s0 + n) * SEG]
            if kind == "p":
                y_prompt[0] = blk
            else:
                y_sample[i] = blk
    return (y_prompt, y_sample)
```

```python
import math
from contextlib import ExitStack

import numpy as np
import concourse.bass as bass
import concourse.mybir as mybir
from concourse.bass_utils import run_bass_kernel_spmd

F32 = mybir.dt.float32
BF16 = mybir.dt.bfloat16
AF = mybir.ActivationFunctionType
ALU = mybir.AluOpType

D = 1024
SEG = 512
HALO = 128
NTH = SEG + 2 * HALO
NBLK = NTH // 128
C = 64
GRP = 4
NGRP = SEG // (C * GRP)
GT = C * GRP
KAPPA = math.exp(-0.5)
NORM_EPS = 1e-6
GN_EPS = 64e-5
NW = 3
NCORES = 8
SLOTS_FULL = 32
D_FF = 2816
NFT = D_FF // 128

V_G1, V_G2, V_MUP, V_MUN, V_A0, V_KK, V_KA, V_RK, V_LNW, V_LNB, V_CW, V_CB = (
    0, 8, 16, 32, 48, 56, 60, 64, 68, 72, 76, 76 + 132)
NV = V_CB + 44
C_ID, C_BONES, C_M1, C_M2, C_TRI, C_I2, C_PEN = 0, 128, 256, 512, 640, 1408, 1472
NCST = C_PEN + 6 * 512

SEM_LIMIT = 12000


class Op:
    __slots__ = ("eng", "fn", "is_dma", "semkey", "waits", "needs_inc", "sem", "count")

    def __init__(self, eng, fn, is_dma, semkey):
        self.eng = eng
        self.fn = fn
        self.is_dma = is_dma
        self.semkey = semkey
        self.waits = []
        self.needs_inc = is_dma
        self.sem = None
        self.count = 0


class Prog:
    ENGS = ("pe", "act", "dve", "pool", "sp")

    def __init__(self):
        self.ops = {e: [] for e in self.ENGS}
        self.last_w = {}
        self.readers = {}
        self.dma_cnt = {}
        self.n = 0
        self.alias = {}

    def add(self, eng, fn, reads=(), writes=(), dma=False, semkey=None):
        op = Op(eng, fn, dma, semkey if dma else None)
        self.n += 1
        if self.alias:
            reads = list(reads) + [a for k in reads for a in self.alias.get(k, ())]
            writes = list(writes) + [a for k in writes for a in self.alias.get(k, ())]
        deps = []
        for k in reads:
            lw = self.last_w.get(k)
            if lw is not None:
                deps.append((lw, "raw"))
        for k in writes:
            lw = self.last_w.get(k)
            if lw is not None:
                deps.append((lw, "waw"))
            for r in self.readers.get(k, ()):
                deps.append((r, "war"))
        seen = set()
        for d, kind in deps:
            if d is op or id(d) in seen:
                continue
            if not d.is_dma and not dma and d.eng == eng:
                if eng == "pe":
                    continue
            seen.add(id(d))
            if d.is_dma:
                op.waits.append((d.sem, self.dma_cnt[d.semkey]))
            else:
                d.needs_inc = True
                op.waits.append(d)
        if dma:
            self.dma_cnt[semkey] = self.dma_cnt.get(semkey, 0) + 16
            op.sem = "d_" + str(semkey)
            op.count = self.dma_cnt[semkey]
        for k in reads:
            self.readers.setdefault(k, []).append(op)
        for k in writes:
            self.last_w[k] = op
            self.readers[k] = []
        self.ops[eng].append(op)
        return op

    def barrier(self, keys, engines=("pe", "act", "dve", "pool")):
        deps, seen = [], set()
        for k in keys:
            cand = list(self.readers.get(k, ()))
            if self.last_w.get(k) is not None:
                cand.append(self.last_w[k])
            for d in cand:
                if id(d) not in seen and d.fn is not None:
                    seen.add(id(d))
                    deps.append(d)
        for e in engines:
            op = Op(e, None, False, None)
            for d in deps:
                if d.is_dma:
                    op.waits.append((d.sem, self.dma_cnt[d.semkey]))
                elif not (d.eng == e and e == "pe"):
                    d.needs_inc = True
                    op.waits.append(d)
            self.ops[e].append(op)
        for k in keys:
            self.last_w.pop(k, None)
            self.readers[k] = []

    def assign(self, nc, es):
        sems = {}
        for e in self.ENGS:
            epoch, cnt = 0, 0
            for op in self.ops[e]:
                if not op.is_dma and op.needs_inc:
                    if cnt >= SEM_LIMIT:
                        epoch += 1
                        cnt = 0
                    cnt += 1
                    op.sem = "c_%s_%d" % (e, epoch)
                    op.count = cnt
        for e in self.ENGS:
            for op in self.ops[e]:
                if op.sem is not None and op.sem not in sems:
                    sems[op.sem] = es.enter_context(nc.semaphore(op.sem))
        return sems

    def run_engine(self, ename, eng, sems):
        seen = {}
        for op in self.ops[ename]:
            for d in op.waits:
                sname, cnt = d if isinstance(d, tuple) else (d.sem, d.count)
                if seen.get(sname, 0) >= cnt:
                    continue
                seen[sname] = cnt
                eng.wait_ge(sems[sname], cnt)
            if op.fn is None:
                continue
            ins = op.fn(eng)
            if op.is_dma:
                ins.then_inc(sems[op.sem], 16)
            elif op.needs_inc:
                ins.then_inc(sems[op.sem], 1)


def _fm(vec, ntile):
    return np.ascontiguousarray(np.asarray(vec, np.float32).reshape(ntile, 128).T)


def _constants():
    cst = np.zeros((128, NCST), np.float32)
    p = np.arange(128)
    cst[:, C_ID:C_ID + 128] = np.eye(128, dtype=np.float32)
    cst[:, C_BONES:C_BONES + 128] = (p[:, None] // 64 == p[None, :] // 64).astype(np.float32)
    s = (p % 64)[:, None]
    t = np.arange(64)[None, :]
    m1f = np.concatenate([(t > s), (t >= s)], axis=1).astype(np.float32)
    m1b = np.concatenate([(t < s), (t <= s)], axis=1).astype(np.float32)
    cst[:, C_M1:C_M1 + 128] = m1f
    cst[:, C_M1 + 128:C_M1 + 256] = m1b
    cst[:, C_M2:C_M2 + 64] = (t < s).astype(np.float32)
    cst[:, C_M2 + 64:C_M2 + 128] = (t > s).astype(np.float32)
    ps_, pt_ = p[:, None], p[None, :]
    same = (ps_ // 64 == pt_ // 64)
    for d in range(2):
        if d == 0:
            incl, excl, rest = (ps_ <= pt_), (ps_ < pt_), (ps_ > pt_)
        else:
            incl, excl, rest = (ps_ >= pt_), (ps_ > pt_), (ps_ < pt_)
        base = C_TRI + d * 384
        cst[:, base:base + 128] = (incl & same)
        cst[:, base + 128:base + 256] = (excl & same)
        cst[:, base + 256:base + 384] = (rest & same)
    cst[:, C_I2:C_I2 + 64] = (t == s).astype(np.float32)
    for g in range(2):
        for j in range(3):
            blk = np.zeros((128, 4, 128), np.float32)
            sk = p[:, None] + (j - 1) * 128
            qq = p[None, :]
            dist = np.abs(qq - sk).astype(np.float32)
            for i in range(4):
                h = g * 4 + i
                slope = 2.0 ** (-(h + 1))
                v = -8.0 * slope * dist
                v = np.where(dist <= 128, v, -240000.0)
                blk[:, i, :] = v
            base = C_PEN + (g * 3 + j) * 512
            cst[:, base:base + 512] = blk.reshape(128, 512)
    return cst


def _layout_weights(inp):
    L = 0
    w_in = np.asarray(inp["w_in"][L], np.float32)
    cols = []
    for i in range(4):
        cols += list(range(i * 64, (i + 1) * 64)) + list(range((4 + i) * 64, (5 + i) * 64))
    cols += list(range(512, 768))
    zc = list(range(768, 768 + 1952))
    w_in_p = np.zeros((1024, 38 * 128), np.float32)
    w_in_p[:, 0:768] = w_in[:, cols]
    w_in_p[:, 768:768 + 1952] = w_in[:, zc]
    w_in_p[:, 22 * 128:38 * 128] = w_in[:, 2720:4768]
    wba = np.asarray(inp["w_branch_attn"][L], np.float32)
    rows = []
    for i in range(4):
        rows += list(range(i * 64, (i + 1) * 64)) + list(range((4 + i) * 64, (5 + i) * 64))
    wba_p = np.ascontiguousarray(wba[rows, :])
    vecs = np.zeros((128, NV), np.float32)
    vecs[:, V_G1:V_G1 + 8] = _fm(inp["norm_mix_pre"][L], 8)
    vecs[:, V_G2:V_G2 + 8] = _fm(inp["norm_ffn_pre"][L], 8)
    mup = np.zeros(2048, np.float32)
    mun = np.zeros(2048, np.float32)
    mup[:1952] = np.asarray(inp["rw_mu_prev"][L])
    mun[:1952] = np.asarray(inp["rw_mu_next"][L])
    vecs[:, V_MUP:V_MUP + 16] = _fm(mup, 16)
    vecs[:, V_MUN:V_MUN + 16] = _fm(mun, 16)
    a0 = np.asarray(inp["rw_a0"][L], np.float32)
    for d in range(2):
        vecs[:, V_A0 + d * 4:V_A0 + d * 4 + 4] = _fm(a0[d], 4)
    vecs[:, V_KK:V_KK + 4] = _fm(inp["rw_k_k"][L], 4)
    vecs[:, V_KA:V_KA + 4] = _fm(inp["rw_k_a"][L], 4)
    vecs[:, V_RK:V_RK + 4] = _fm(np.asarray(inp["rw_r_k"][L]).reshape(512), 4)
    vecs[:, V_LNW:V_LNW + 4] = _fm(inp["rw_ln_w"][L], 4)
    vecs[:, V_LNB:V_LNB + 4] = _fm(inp["rw_ln_b"][L], 4)
    cw = np.asarray(inp["ffn_conv_w"][L], np.float32)
    for j in range(3):
        vecs[:, V_CW + j * 44:V_CW + (j + 1) * 44] = _fm(cw[j], 44)
    vecs[:, V_CB:V_CB + 44] = _fm(inp["ffn_conv_b"][L], 44)
    rows128 = np.zeros((128, 2, 1024), np.float32)
    rows128[:, 0, :] = np.asarray(inp["norm_mix_post"][L], np.float32)[None, :]
    rows128[:, 1, :] = np.asarray(inp["norm_ffn_post"][L], np.float32)[None, :]
    w0rows = np.ascontiguousarray(np.asarray(inp["rw_w0"][L], np.float32).reshape(1, 2, 512))
    sink = np.asarray(inp["attn_sink"][L], np.float32)
    sinkrows = np.zeros((1, 2, 4, 128), np.float32)
    for g in range(2):
        for i in range(4):
            sinkrows[0, g, i, :] = sink[g * 4 + i]
    sinkrows = sinkrows.reshape(1, 2, 512)
    lora = np.zeros((128, 4, 512), np.float32)
    lora[:, 0, :] = np.asarray(inp["rw_w2"][L], np.float32).reshape(128, 512)
    lora[:, 1, :] = np.asarray(inp["rw_a2"][L], np.float32).reshape(128, 512)
    g2 = np.asarray(inp["rw_g2"][L], np.float32)
    lora[:, 2, :] = g2[0:128]
    lora[0:32, 3, :] = g2[128:160]
    return {
        "w_in": w_in_p, "wba": wba_p,
        "wbr": np.ascontiguousarray(np.asarray(inp["w_branch_rwkv"][L], np.float32)),
        "wout": np.ascontiguousarray(np.asarray(inp["w_out"][L], np.float32)),
        "wup": np.ascontiguousarray(np.asarray(inp["w_ffn_up"][L], np.float32)),
        "wdn": np.ascontiguousarray(np.asarray(inp["w_ffn_down"][L], np.float32)),
        "vecs": vecs, "rows": rows128, "w0rows": w0rows, "sinkrows": sinkrows, "lora": lora,
        "cst": _constants(),
    }


def _layout_core(seqs, nslot):
    xh = np.zeros((nslot, NTH, D), np.float32)
    flags = np.zeros((nslot, 2), np.float32)
    s = 0
    place = []
    for x in seqs:
        T = x.shape[0]
        n = T // SEG
        xp = np.zeros((T + 2 * HALO, D), np.float32)
        xp[HALO:HALO + T] = x
        for k in range(n):
            xh[s + k] = xp[k * SEG:k * SEG + NTH]
            flags[s + k, 0] = 1.0 if k > 0 else 0.0
            flags[s + k, 1] = 1.0 if k < n - 1 else 0.0
        place.append((s, n))
        s += n
    fl = np.ascontiguousarray(np.broadcast_to(flags.reshape(1, nslot * 2), (128, nslot * 2))).astype(np.float32)
    return xh, fl, place


def weight_schedule(nslot, passes):
    q = []
    if "A" in passes:
        for s in reversed(range(nslot)):
            q += [("ZG0", s), ("ZG1", s), ("ZG2", s), ("ZG3", s)]
    if "B" in passes:
        for s in range(nslot):
            q += [("QG", s), ("KVG", s), ("ZG0", s), ("ZG1", s), ("ZG2", s), ("ZG3", s)]
            for h in range(2):
                q += [("GA%d" % h, s), ("GR%d" % h, s), ("BAR%d" % h, s)]
            q += [("WO0", s), ("WO1", s)]
    if "C" in passes:
        for s in range(nslot):
            for g in range(6):
                q += [("UG%d" % g, s), ("UU%d" % g, s)]
            for g in range(6):
                q += [("DN%d" % g, s)]
    return q


def build(nslot, passes="ABC", dbg=()):
    nc = bass.Bass("TRN2", target_bir_lowering=False)

    def din(name, shape, dt=F32):
        return nc.dram_tensor(name, list(shape), dt, kind="ExternalInput").ap()

    def dout(name, shape, dt=F32):
        return nc.dram_tensor(name, list(shape), dt, kind="ExternalOutput").ap()

    def dscr(name, shape, dt):
        return nc.dram_tensor(name, list(shape), dt).ap()

    xh = din("xh", [nslot, NTH, D])
    flags_d = din("flags", [128, 2 * nslot])
    w_in_d = din("w_in", [1024, 4864])
    wba_d = din("wba", [512, 1024])
    wbr_d = din("wbr", [512, 1024])
    wout_d = din("wout", [1024, 1024])
    wup_d = din("wup", [1024, 5632])
    wdn_d = din("wdn", [2816, 1024])
    vecs_d = din("vecs", [128, NV])
    rows_d = din("rows", [128, 2, 1024])
    w0_d = din("w0rows", [1, 2, 512])
    sink_d = din("sinkrows", [1, 2, 512])
    lora_d = din("lora", [128, 4, 512])
    cst_d = din("cst", [128, NCST])
    y_out = dout("y", [nslot * SEG, D])
    dbg_out = {}
    for name, shape in dbg:
        dbg_out[name] = dout("dbg_" + name, shape)

    w_in_b = dscr("w_in_b", [1024, 4864], BF16)
    wba_b = dscr("wba_b", [512, 1024], BF16)
    wbr_b = dscr("wbr_b", [512, 1024], BF16)
    wout_b = dscr("wout_b", [1024, 1024], BF16)
    wup_b = dscr("wup_b", [1024, 5632], BF16)
    wdn_b = dscr("wdn_b", [2816, 1024], BF16)
    ybwd_d = dscr("ybwd", [nslot, 128, 4 * SEG], F32)
    hbuf = dscr("hbuf", [nslot * SEG + 2, D], F32)

    P = Prog()
    es = ExitStack()

    def sb(name, shape, dt):
        return es.enter_context(nc.sbuf_tensor("s_" + name, list(shape), dt))

    psum = es.enter_context(nc.psum_tensor("psum", [128, 4096], F32))

    def PB(b, lo=0, hi=512):
        return psum[:, b * 512 + lo:b * 512 + hi]

    def bk(*banks):
        r = []
        for b in banks:
            r += ["pb%da" % b, "pb%db" % b]
        return r

    pbT = psum[:, 7 * 512:8 * 512].bitcast(BF16)

    ident_b = sb("ident_b", [128, 128], BF16)
    ones_b = sb("ones_b", [128, 128], BF16)
    bones_f = sb("bones_f", [128, 128], F32)
    bones_b = sb("bones_b", [128, 128], BF16)
    M1 = sb("M1", [128, 2, 128], F32)
    M2 = sb("M2", [128, 2, 64], F32)
    TRI = sb("TRI", [128, 2, 384], F32)
    I2 = sb("I2", [128, 64], BF16)
    PEN = sb("PEN", [128, 6, 512], BF16)
    vecs = sb("vecs", [128, NV], F32)
    c0 = sb("c0", [128, 16], F32)
    omka = sb("omka", [128, 4], F32)
    rows = sb("rows", [128, 1024], F32)
    gB = sb("gB", [128, 8, 128], F32)
    flags = sb("flags", [128, 2 * nslot], F32)
    rb = sb("rb", [2, 4, 512], BF16)
    w2b = sb("w2b", [128, 512], BF16)
    a2b = sb("a2b", [128, 512], BF16)
    g2b = sb("g2b", [128, 2, 512], BF16)
    onesf = sb("onesf", [128, 128], F32)

    xt = [sb("xt%d" % i, [128, 1024], F32) for i in range(3)]
    ub = [sb("ub%d" % i, [128, 1024], BF16) for i in range(2)]
    st_ssq = [sb("ssq%d" % i, [128, 2], F32) for i in range(4)]
    st_rstd = [sb("rstd%d" % i, [128, 1], F32) for i in range(4)]
    junk = sb("junk", [128, 1024], BF16)
    uT = sb("uT", [128, 8, NTH], BF16)
    wbuf = [sb("wbuf%d" % i, [128, 4096], BF16) for i in range(NW)]

    cnt = {"x": 0, "st": 0}

    def dma(eng, out, in_, reads, writes, semkey):
        P.add(eng, lambda e: e.dma_start(out=out, in_=in_), reads=reads, writes=writes, dma=True, semkey=semkey + "_" + eng)

    def act_copy(out, in_, reads, writes):
        P.add("act", lambda e: e.copy(out, in_), reads=reads, writes=writes)

    def dve_copy(out, in_, reads, writes):
        P.add("dve", lambda e: e.tensor_copy(out, in_), reads=reads, writes=writes)

    def mm(out, lhsT, rhs, start, stop, r, w, tp=None):
        if tp is None:
            P.add("pe", lambda e: e.matmul(out, lhsT, rhs, start=start, stop=stop), reads=r, writes=w)
        else:
            P.add("pe", lambda e: e.matmul(out, lhsT, rhs, start=start, stop=stop, tile_position=tp), reads=r, writes=w)

    def actf(out, in_, func, r, w, bias=None, scale=None, accum=None):
        kw = {}
        if bias is not None:
            kw["bias"] = bias
        if scale is not None:
            kw["scale"] = scale
        if accum is not None:
            kw["accum_out"] = accum
        P.add("act", lambda e: e.activation(out=out, in_=in_, func=func, **kw), reads=r, writes=w)

    def tt(eng, out, in0, in1, op, r, w):
        P.add(eng, lambda e: e.tensor_tensor(out=out, in0=in0, in1=in1, op=op), reads=r, writes=w)

    def stt(out, in0, scalar, in1, op0, op1, r, w):
        P.add("dve", lambda e: e.scalar_tensor_tensor(out=out, in0=in0, scalar=scalar, in1=in1, op0=op0, op1=op1), reads=r, writes=w)

    def tsc(eng, out, in0, s1, s2, op0, op1, r, w):
        P.add(eng, lambda e: e.tensor_scalar(out, in0, s1, s2, op0, op1), reads=r, writes=w)

    def tsmul(eng, out, in0, s, r, w):
        P.add(eng, lambda e: e.tensor_scalar_mul(out, in0, s), reads=r, writes=w)

    def recip(out, in_, r, w):
        P.add("dve", lambda e: e.reciprocal(out, in_), reads=r, writes=w)

    def load_gains(which):
        dma("pool", rows[:], rows_d[:, which, :], [], ["rows"], "ld_rows")
        for c in range(8):
            col = (V_G1 if which == 0 else V_G2) + c
            actf(gB[:, c, :], onesf[:], AF.Copy, ["vecs", "onesf"], ["gB"], scale=vecs[:, col:col + 1])

    def setup():
        stg = xt[0]
        dma("pool", vecs[:], vecs_d, [], ["vecs"], "ld_vecs")
        dma("pool", flags[:], flags_d, [], ["flags"], "ld_flags")
        dma("pool", stg[:, 0:1024], cst_d[:, 0:1024], [], ["xt0"], "xin0")
        dve_copy(ident_b[:], stg[:, C_ID:C_ID + 128], ["xt0"], ["ident_b"])
        dve_copy(bones_f[:], stg[:, C_BONES:C_BONES + 128], ["xt0"], ["bones_f"])
        dve_copy(bones_b[:], stg[:, C_BONES:C_BONES + 128], ["xt0"], ["bones_b"])
        dve_copy(M1[:].rearrange("p a b -> p (a b)"), stg[:, C_M1:C_M1 + 256], ["xt0"], ["M1"])
        dve_copy(M2[:].rearrange("p a b -> p (a b)"), stg[:, C_M2:C_M2 + 128], ["xt0"], ["M2"])
        dve_copy(TRI[:, 0, :], stg[:, C_TRI:C_TRI + 384], ["xt0"], ["TRI"])
        dma("pool", xt[1][:, 0:448], cst_d[:, 1024:1472], [], ["xt1"], "xin1")
        dve_copy(TRI[:, 1, :], xt[1][:, 0:384], ["xt1"], ["TRI"])
        dve_copy(I2[:], xt[1][:, 384:448], ["xt1"], ["I2"])
        for k in range(3):
            t = xt[(k + 2) % 3]
            key = "xt%d" % ((k + 2) % 3)
            dma("pool", t[:], cst_d[:, C_PEN + k * 1024:C_PEN + (k + 1) * 1024], [], [key], "xin%d" % ((k + 2) % 3))
            dve_copy(PEN[:, 2 * k:2 * k + 2, :].rearrange("p a b -> p (a b)"), t[:], [key], ["PEN"])
        P.add("dve", lambda e: e.memset(ones_b[:], 1.0), writes=["ones_b"])
        P.add("dve", lambda e: e.memset(onesf[:], 1.0), writes=["onesf"])
        P.add("dve", lambda e: e.memset(epsc[:], GN_EPS), writes=["epsc"])
        dma("pool", xt[0][:, 0:1024], lora_d[:, 0:2, :].rearrange("p a b -> p (a b)"), [], ["xt0"], "xin0")
        dve_copy(w2b[:], xt[0][:, 0:512], ["xt0"], ["w2b"])
        dve_copy(a2b[:], xt[0][:, 512:1024], ["xt0"], ["a2b"])
        dma("pool", xt[1][:, 0:1024], lora_d[:, 2:4, :].rearrange("p a b -> p (a b)"), [], ["xt1"], "xin1")
        dve_copy(g2b[:].rearrange("p a b -> p (a b)"), xt[1][:, 0:1024], ["xt1"], ["g2b"])
        tt("dve", c0[:], vecs[:, V_MUP:V_MUP + 16], vecs[:, V_MUN:V_MUN + 16], ALU.add, ["vecs"], ["c0"])
        tsc("dve", c0[:], c0[:], -1.0, 1.0, ALU.mult, ALU.add, ["c0"], ["c0"])
        tsc("dve", omka[:], vecs[:, V_KA:V_KA + 4], -1.0, 1.0, ALU.mult, ALU.add, ["vecs"], ["omka"])
        w0f = xt[2][0:1, 0:1024].rearrange("p (a b) -> p a b", a=2)
        w0t = xt[0][0:1, 0:1024].rearrange("p (a b) -> p a b", a=2)
        sinkf = xt[1][0:1, 0:1024].rearrange("p (a b) -> p a b", a=2)
        lo_b = ub[0][0:1, 0:1024].rearrange("p (a b) -> p a b", a=2)
        dma("pool", w0f, w0_d, [], ["xt2"], "xin2")
        dma("pool", sinkf, sink_d, [], ["xt1"], "xin1")
        dve_copy(rb[0:1, 0:2, :], w0f, ["xt2"], ["rb"])
        dve_copy(w0t, rb[0:1, 0:2, :], ["rb"], ["xt0"])
        tt("dve", w0t, w0f, w0t, ALU.subtract, ["xt2", "xt0"], ["xt0"])
        dve_copy(lo_b, w0t, ["xt0"], ["ub0"])
        dma("pool", rb[1:2, 0:2, :], lo_b, ["ub0"], ["rb"], "ubst0")
        actf(rb[0:1, 2:4, :], sinkf, AF.Exp, ["xt1"], ["rb"])
        P.add("dve", lambda e: e.memset(xt[2][0:1, :], 0.0), reads=["xt2"], writes=["xt2"])
        dma("pool", hbuf[0:1, :], xt[2][0:1, :], ["xt2"], ["hb_first"], "xst2")
        dma("pool", hbuf[nslot * SEG + 1:nslot * SEG + 2, :], xt[2][0:1, :], ["xt2"], ["hb_last"], "xst2")

    def prepass():
        jobs = []
        for (src, dst, R, Cc) in ((w_in_d, w_in_b, 1024, 4864), (wba_d, wba_b, 512, 1024), (wbr_d, wbr_b, 512, 1024),
                                  (wout_d, wout_b, 1024, 1024), (wup_d, wup_b, 1024, 5632), (wdn_d, wdn_b, 2816, 1024)):
            for rc in range(R // 128):
                for c0_ in range(0, Cc, 1024):
                    w = min(1024, Cc - c0_)
                    jobs.append((src[rc * 128:(rc + 1) * 128, c0_:c0_ + w], dst[rc * 128:(rc + 1) * 128, c0_:c0_ + w], w))
        for i, (s_ap, d_ap, w) in enumerate(jobs):
            a = i % 3
            b = i % 2
            dma("sp", xt[a][:, 0:w], s_ap, [], ["xt%d" % a], "xin%d" % a)
            if i % 2 == 0:
                dve_copy(ub[b][:, 0:w], xt[a][:, 0:w], ["xt%d" % a], ["ub%d" % b])
            else:
                act_copy(ub[b][:, 0:w], xt[a][:, 0:w], ["xt%d" % a], ["ub%d" % b])
            dma("pool", d_ap, ub[b][:, 0:w], ["ub%d" % b], ["wscr"], "ubst%d" % b)

    wq = weight_schedule(nslot, passes)
    wstate = {"issued": 0, "taken": 0, "released": 0}
    wfm = lambda t: t.rearrange("(c p) n -> p c n", p=128)

    def wsrc(tag):
        if tag == "QG":
            return wfm(w_in_b)[:, :, 0:512], 8, 512
        if tag == "KVG":
            return wfm(w_in_b)[:, :, 512:768], 8, 256
        if tag.startswith("ZG"):
            g = int(tag[2])
            return wfm(w_in_b)[:, :, 768 + g * 512:768 + (g + 1) * 512], 8, 512
        if tag.startswith("GA"):
            h = int(tag[2])
            return wfm(w_in_b)[:, :, 2816 + h * 512:2816 + (h + 1) * 512], 8, 512
        if tag.startswith("GR"):
            h = int(tag[2])
            return wfm(w_in_b)[:, :, 3840 + h * 512:3840 + (h + 1) * 512], 8, 512
        if tag.startswith("WO"):
            h = int(tag[2])
            return wfm(wout_b)[:, :, h * 512:(h + 1) * 512], 8, 512
        if tag.startswith("UG"):
            g = int(tag[2])
            n = 512 if g < 5 else 256
            return wfm(wup_b)[:, :, g * 512:g * 512 + n], 8, n
        if tag.startswith("UU"):
            g = int(tag[2])
            n = 512 if g < 5 else 256
            return wfm(wup_b)[:, :, 2816 + g * 512:2816 + g * 512 + n], 8, n
        if tag.startswith("DN"):
            g = int(tag[2])
            kc = 4 if g < 5 else 2
            return wfm(wdn_b)[:, g * 4:g * 4 + kc, :], kc, 1024
        raise KeyError(tag)

    def w_issue():
        i = wstate["issued"]
        tag, _ = wq[i]
        b = i % NW
        if tag.startswith("BAR"):
            h = int(tag[3])
            v = wbuf[b][:, :].rearrange("p (a c n) -> p a c n", a=2, c=4)
            dma("sp", v[:, 0, :, :], wfm(wba_b)[:, :, h * 512:(h + 1) * 512], ["wscr"], ["wbuf%d" % b], "w%d" % b)
            dma("sp", v[:, 1, :, :], wfm(wbr_b)[:, :, h * 512:(h + 1) * 512], ["wscr"], ["wbuf%d" % b], "w%d" % b)
        else:
            src, kc, n = wsrc(tag)
            v = wbuf[b][:, 0:kc * n].rearrange("p (c n) -> p c n", c=kc)
            dma("sp", v, src, ["wscr"], ["wbuf%d" % b], "w%d" % b)
        wstate["issued"] += 1

    def w_pump():
        while wstate["issued"] < len(wq) and wstate["issued"] - wstate["released"] < NW:
            w_issue()

    def w_take(tag, slot):
        i = wstate["taken"]
        assert wq[i] == (tag, slot), (wq[i], tag, slot)
        if wstate["issued"] <= i:
            assert wstate["issued"] - wstate["released"] < NW, "too many weight groups held"
            w_issue()
        wstate["taken"] += 1
        b = i % NW
        return wbuf[b], "wbuf%d" % b

    def w_done(k=1):
        wstate["released"] += k
        assert wstate["released"] <= wstate["taken"]
        w_pump()

    def norm_block(src_ap, nrow, which, dst, dst_key, col0, src_reads=()):
        i = cnt["x"]
        cnt["x"] += 1
        a, b, q = i % 3, i % 2, i % 4
        xk, uk = "xt%d" % a, "ub%d" % b
        dma("pool", xt[a][0:nrow, :], src_ap, list(src_reads), [xk], "xin%d" % a)
        P.add("act", lambda e: e.activation(out=junk[0:nrow, :], in_=xt[a][0:nrow, :], func=AF.Square, accum_out=st_ssq[q][0:nrow, 0:1]),
              reads=[xk], writes=["junk", "ssq%d" % q])
        P.add("dve", lambda e: e.tensor_scalar(st_rstd[q][0:nrow, :], st_ssq[q][0:nrow, 0:1], 1.0 / D, NORM_EPS, ALU.mult, ALU.add),
              reads=["ssq%d" % q], writes=["rstd%d" % q])
        P.add("act", lambda e: e.activation(out=st_rstd[q][0:nrow, :], in_=st_rstd[q][0:nrow, :], func=AF.Sqrt),
              reads=["rstd%d" % q], writes=["rstd%d" % q])
        P.add("dve", lambda e: e.reciprocal(st_rstd[q][0:nrow, :], st_rstd[q][0:nrow, :]), reads=["rstd%d" % q], writes=["rstd%d" % q])
        P.add("dve", lambda e: e.tensor_scalar_mul(ub[b][0:nrow, :], xt[a][0:nrow, :], st_rstd[q][0:nrow, :]),
              reads=[xk, "rstd%d" % q], writes=[uk])
        for c in range(8):
            P.add("pe", lambda e, c=c: e.transpose(pbT[:, c * 128:c * 128 + nrow], ub[b][0:nrow, c * 128:(c + 1) * 128], ident_b[0:nrow, 0:nrow]),
                  reads=[uk, "ident_b"], writes=bk(7))
        P.add("dve", lambda e: e.tensor_tensor(out=dst[:, :, col0:col0 + nrow],
                                               in0=pbT[:, :].rearrange("p (c n) -> p c n", c=8)[:, :, 0:nrow],
                                               in1=gB[:, :, 0:nrow], op=ALU.mult),
              reads=bk(7) + ["gB"], writes=[dst_key])

    ARENA_BYTES = 73 * 1024
    arena = sb("arena", [128, ARENA_BYTES // 4], F32)
    carve_state = {}

    def carve(group, name, shape, dt):
        off = carve_state.get(group, 0)
        nel = 1
        for d_ in shape[1:]:
            nel *= d_
        nbytes = nel * (4 if dt == F32 else 2)
        nbytes_al = (nbytes + 31) // 32 * 32
        assert off + nbytes_al <= ARENA_BYTES, (group, name, off, nbytes_al)
        carve_state[group] = off + nbytes_al
        v = arena[0:shape[0], off // 4:(off + nbytes) // 4]
        if dt != F32:
            v = v.bitcast(dt)
        if len(shape) == 3:
            v = v.rearrange("p (a b) -> p a b", a=shape[1])
        elif len(shape) == 4:
            v = v.rearrange("p (a b c) -> p a b c", a=shape[1], b=shape[2])
        elif len(shape) == 5:
            v = v.rearrange("p (a b c d) -> p a b c d", a=shape[1], b=shape[2], c=shape[3])
        groups.setdefault(group, []).append(name)
        return v

    groups = {}
    tmp4 = [sb("tmp%d" % i, [128, 512], F32) for i in range(4)]
    carve_state["att"] = 40 * 1024
    qT = carve("att", "qT", [128, 4, SEG], BF16)
    kT = carve("att", "kT", [128, NTH], BF16)
    vtok = carve("att", "vtok", [128, NBLK, 128], BF16)
    ptb = [carve("att", "ptb%d" % i, [128, 512], BF16) for i in range(3)]
    rden = carve("att", "rden", [128, 512], F32)
    fones = sb("fones", [128, 2, 64], BF16)
    oattn = sb("oattn", [128, 4, SEG], BF16)
    zs = sb("zs", [128, 16, SEG], BF16)
    ztmp = [tmp4[0], tmp4[1]]
    ztmp2 = [tmp4[2], tmp4[3]]
    ps_rot = {"d": 0}

    def nbank():
        b = ps_rot["d"] % 2
        ps_rot["d"] += 1
        return b

    def slot_load_norm(S):
        for b in range(NBLK):
            norm_block(xh[S, b * 128:(b + 1) * 128, :], 128, 0, uT, "uT", b * 128)

    def proj_fm_tile(wb, wkey, kc, n, col, src, skey, tok_lo, ntok, bank):
        wv = wb[:, 0:kc * n].rearrange("p (c n) -> p c n", c=kc)
        for c in range(kc):
            mm(PB(bank, 0, ntok), wv[:, c, col:col + 128], src[:, c, tok_lo:tok_lo + ntok], c == 0, c == kc - 1,
               [wkey, skey], bk(bank))

    def qkv_project(S):
        wb, wkey = w_take("QG", S)
        for i in range(4):
            bank = nbank()
            proj_fm_tile(wb, wkey, 8, 512, i * 128, uT, "uT", HALO, SEG, bank)
            act_copy(qT[:, i, :], PB(bank), bk(bank), ["qT"])
        w_done()
        wb, wkey = w_take("KVG", S)
        for (lo, n) in ((0, 512), (512, 256)):
            bank = nbank()
            proj_fm_tile(wb, wkey, 8, 256, 0, uT, "uT", lo, n, bank)
            act_copy(kT[:, lo:lo + n], PB(bank, 0, n), bk(bank), ["kT"])
        wv = wb[:, 0:8 * 256].rearrange("p (c n) -> p c n", c=8)
        for half in range(2):
            bank = nbank()
            for bb in range(3):
                b = half * 3 + bb
                for c in range(8):
                    mm(PB(bank, bb * 128, (bb + 1) * 128), uT[:, c, b * 128:(b + 1) * 128], wv[:, c, 128:256], c == 0, c == 7,
                       [wkey, "uT"], bk(bank))
            dve_copy(vtok[:, half * 3:half * 3 + 3, :].rearrange("p a b -> p (a b)"), PB(bank, 0, 384), bk(bank), ["vtok"])
        w_done()

    def attention(S):
        fp = flags[:, 2 * S:2 * S + 1]
        tsmul("dve", fones[:, 0, :], ones_b[:, 0:64], fp, ["ones_b", "flags"], ["fones"])
        tsmul("dve", vtok[:, 1, :], vtok[:, 1, :], fp, ["vtok", "flags"], ["vtok"])
        nqb = SEG // 128
        for qb in range(nqb):
            nb, db = 3 + (qb % 2), 5 + (qb % 2)
            for g in range(2):
                pr = slice(g * 64, (g + 1) * 64)
                for j in range(3):
                    kb = qb + j
                    sbk = (qb * 6 + g * 3 + j) % 3
                    pk = "ptb%d" % sbk
                    mm(PB(sbk), kT[pr, kb * 128:(kb + 1) * 128], qT[pr, :, qb * 128:(qb + 1) * 128], True, False,
                       ["kT", "qT"], bk(sbk), tp=(g * 64, 0))
                    mm(PB(sbk), ident_b[:], PEN[:, g * 3 + j, :], False, True, ["ident_b", "PEN"], bk(sbk))
                    actf(ptb[sbk][:], PB(sbk), AF.Exp, bk(sbk), [pk], scale=0.125)
                    mm(PB(nb)[pr, :], vtok[:, kb, pr], ptb[sbk][:], j == 0, j == 2, ["vtok", pk], bk(nb), tp=(0, g * 64))
                    if kb <= 1:
                        dl, dk = fones[:, 0, :], "fones"
                    else:
                        dl, dk = ones_b[:, 0:64], "ones_b"
                    mm(PB(db)[pr, :], dl, ptb[sbk][:], j == 0, False, [dk, pk], bk(db), tp=(0, g * 64))
                mm(PB(db)[pr, :], ones_b[0:1, 0:64], rb[0:1, 2 + g, :], False, True, ["ones_b", "rb"], bk(db), tp=(0, g * 64))
            recip(rden[:], PB(db), bk(db), ["rden"])
            tt("dve", oattn[:, :, qb * 128:(qb + 1) * 128], PB(nb).rearrange("p (a b) -> p a b", a=4),
               rden[:].rearrange("p (a b) -> p a b", a=4), ALU.mult, bk(nb) + ["rden"], ["oattn"])

    ZSUB = ((0, 510), (510, 2))

    def z_project(S, tiles):
        tasks = [(zt, o, n) for zt in tiles for (o, n) in ZSUB]
        zb = [tmp4[0], tmp4[1], tmp4[2]]
        zk = ["ztmp0", "ztmp1", "ztmq0"]
        cur = [None]

        def stage_mm(i):
            zt, o, n = tasks[i]
            g = zt // 4
            if cur[0] is None or cur[0][0] != g:
                if cur[0] is not None:
                    w_done()
                wb, wkey = w_take("ZG%d" % g, S)
                cur[0] = (g, wb, wkey)
            _, wb, wkey = cur[0]
            bank = i % 2
            proj_fm_tile(wb, wkey, 8, 512, (zt % 4) * 128, uT, "uT", HALO + o - 1, n + 2, bank)
            actf(zb[i % 3][:, 0:n], PB(bank, 1, n + 1), AF.Copy, bk(bank) + ["c0"], [zk[i % 3]], scale=c0[:, zt:zt + 1])

        def stage_taps(i):
            zt, o, n = tasks[i]
            bank = i % 2
            stt(tmp4[3][:, 0:n], PB(bank, 0, n), vecs[:, V_MUP + zt:V_MUP + zt + 1], zb[i % 3][:, 0:n], ALU.mult, ALU.add,
                bk(bank) + ["vecs", zk[i % 3]], ["ztmq1"])
            stt(zs[:, zt, o:o + n], PB(bank, 2, n + 2), vecs[:, V_MUN + zt:V_MUN + zt + 1], tmp4[3][:, 0:n], ALU.mult, ALU.add,
                bk(bank) + ["vecs", "ztmq1"], ["zs%d" % zt])

        nt_ = len(tasks)
        for i in range(nt_ + 1):
            if i < nt_:
                stage_mm(i)
            if i >= 1:
                stage_taps(i - 1)
        w_done()

    def rw(name, shape, dt):
        return carve("rw", name, shape, dt)

    twd = rw("twd", [128, GT], BF16)
    sigtok = [rw("sigtok%d" % i, [128, 512], F32) for i in range(2)]
    E = [rw("E%d" % i, [128, 4, GT], F32) for i in range(2)]
    kq = rw("kq", [128, GT], F32)
    ksq = rw("ksq", [128, GT], F32)
    nrm = rw("nrm", [128, GT], F32)
    kk = rw("kk", [128, GT], F32)
    asig = [rw("asig%d" % i, [128, GT], F32) for i in range(2)]
    kdir = [rw("kdir%d" % i, [128, GT], F32) for i in range(2)]
    t1 = rw("t1", [128, GT], F32)
    bbv = rw("bbv", [128, GT], F32)
    bbar = rw("bbar", [128, GT], BF16)
    kbar = rw("kbar", [128, GT], BF16)
    AR = [rw("AR%d" % i, [128, 4, GRP, 2, C], BF16) for i in range(2)]
    BT = [rw("BT%d" % i, [128, 4, GT], BF16) for i in range(2)]
    KTl = [rw("KTl%d" % i, [128, 4, GT], BF16) for i in range(2)]
    bbt = [rw("bbt%d" % i, [128, 4, GRP, C], BF16) for i in range(2)]
    kbt = [rw("kbt%d" % i, [128, 4, GRP, C], BF16) for i in range(2)]
    vt = [rw("vt%d" % i, [128, 4, GRP, C], BF16) for i in range(2)]
    GC = [rw("GC%d" % i, [128, 4, GRP], F32) for i in range(2)]
    GB1 = rw("GB1", [128, 16, 128], BF16)
    GB2 = rw("GB2", [128, 16, 128], BF16)
    Pk = [rw("Pk%d" % i, [128, 16, 2, C], BF16) for i in range(2)]
    Zk = [rw("Zk%d" % i, [128, 16, C], BF16) for i in range(2)]
    Wb = rw("Wb", [128, 4, C], BF16)
    Ub = rw("Ub", [128, 4, C], BF16)
    Tf = sb("Tf", [128, 4, C], F32)
    Tst = sb("Tst", [128, 4, C], BF16)
    yT = sb("yT", [128, 4, SEG], F32)
    bp = sb("bp", [128, 4, SEG], BF16)

    def flat(ap3):
        return ap3.rearrange("p a b -> p (a b)")

    def rwkv_state_init():
        P.add("dve", lambda e: e.memset(flat(Tf[:]), 0.0), writes=["Tf"])

    def rwkv_slot_begin(S, d):
        col = 2 * S + (0 if d == 0 else 1)
        tsmul("dve", flat(Tf[:]), flat(Tf[:]), flags[:, col:col + 1], ["Tf", "flags"], ["Tf"])
        act_copy(flat(Tst[:]), flat(Tf[:]), ["Tf"], ["Tst"])

    def rwkv_group(S, d, gi, passB):
        gb = gi % 2
        t0 = gi * GT
        dr = slice(d * 64, (d + 1) * 64)
        ARk, BTk, KTk, bbtk, kbtk, vtk, GCk = ("AR%d" % gb, "BT%d" % gb, "KTl%d" % gb, "bbt%d" % gb, "kbt%d" % gb,
                                               "vt%d" % gb, "GC%d" % gb)
        actf(twd[dr, :], zs[dr, 12, t0:t0 + GT], AF.Tanh, ["zs12"], ["twd"])
        for blk in range(2):
            mm(PB(blk), twd[dr, blk * 128:(blk + 1) * 128], w2b[dr, :], True, False, ["twd", "w2b"], bk(blk), tp=(d * 64, 0))
            mm(PB(blk), ones_b[0:2, 0:128], rb[0:2, d, :], False, True, ["ones_b", "rb"], bk(blk))
            actf(sigtok[blk][:], PB(blk), AF.Sigmoid, bk(blk), ["sigtok%d" % blk])
        for ct in range(4):
            eb = ct % 2
            Et, Ek = E[eb], "E%d" % eb
            for blk in range(2):
                bank = 2 + blk
                mm(PB(bank, 0, 384), sigtok[blk][:, ct * 128:(ct + 1) * 128], TRI[:, d, :], True, True,
                   ["sigtok%d" % blk, "TRI"], bk(bank))
                actf(Et[:, 0:3, blk * 128:(blk + 1) * 128], PB(bank, 0, 384).rearrange("p (a b) -> p a b", a=3), AF.Exp,
                     bk(bank), [Ek], scale=-KAPPA)
                actf(Et[:, 3, blk * 128:(blk + 1) * 128], PB(bank, 0, 128), AF.Exp, bk(bank), [Ek], scale=KAPPA)
            kz, kzk = zs[:, 4 + ct, t0:t0 + GT], "zs%d" % (4 + ct)
            tsmul("dve", kq[:], kz, vecs[:, V_KK + ct:V_KK + ct + 1], [kzk, "vecs"], ["kq"])
            actf(ksq[:], kq[:], AF.Square, ["kq"], ["ksq"])
            mm(PB(4, 0, 256), bones_f[:], ksq[:], True, True, ["bones_f", "ksq"], bk(4))
            actf(nrm[:], PB(4, 0, 256), AF.Sqrt, bk(4), ["nrm"])
            P.add("dve", lambda e: e.tensor_scalar_max(nrm[:], nrm[:], 1e-12), reads=["nrm"], writes=["nrm"])
            recip(nrm[:], nrm[:], ["nrm"], ["nrm"])
            tt("dve", kk[:], kq[:], nrm[:], ALU.mult, ["kq", "nrm"], ["kk"])
            dirs = (0, 1) if passB else (d,)
            for dd in dirs:
                ddr = slice(dd * 64, (dd + 1) * 64)
                psa = PB(5, 0, 256) if dd == d else PB(5, 256, 512)
                mm(psa, a2b[ddr, ct * 128:(ct + 1) * 128], zs[ddr, 13, t0:t0 + GT], True, True, ["a2b", "zs13"], bk(5), tp=(dd * 64, 0))
                actf(asig[dd][:], psa, AF.Sigmoid, bk(5) + ["vecs"], ["asig%d" % dd],
                     bias=vecs[:, V_A0 + dd * 4 + ct:V_A0 + dd * 4 + ct + 1])
                tsc("dve", t1[:], asig[dd][:], vecs[:, V_KA + ct:V_KA + ct + 1], omka[:, ct:ct + 1], ALU.mult, ALU.add,
                    ["asig%d" % dd, "vecs", "omka"], ["t1"])
                tt("dve", kdir[dd][:], kz, t1[:], ALU.mult, [kzk, "t1"], ["kdir%d" % dd])
            if passB:
                tt("pool", t1[:], kdir[0][:], kdir[1][:], ALU.add, ["kdir0", "kdir1"], ["t1"])
                stt(bp[:, ct, t0:t0 + GT], zs[:, ct, t0:t0 + GT], vecs[:, V_RK + ct:V_RK + ct + 1], t1[:], ALU.mult, ALU.mult,
                    ["zs%d" % ct, "vecs", "t1"], ["bp"])
            tt("dve", bbv[:], kk[:], asig[d][:], ALU.mult, ["kk", "asig%d" % d], ["bbv"])
            v4 = lambda ap: ap.rearrange("p (c t) -> p c t", c=GRP)
            tt("dve", AR[gb][:, ct, :, 1, :], v4(zs[:, ct, t0:t0 + GT]), v4(Et[:, 0, :]), ALU.mult, ["zs%d" % ct, Ek], [ARk])
            stt(AR[gb][:, ct, :, 0, :], v4(kk[:]), -1.0, v4(Et[:, 1, :]), ALU.mult, ALU.mult, ["kk", Ek], [ARk])
            tt("pool", BT[gb][:, ct, :], bbv[:], Et[:, 3, :], ALU.mult, ["bbv", Ek], [BTk])
            tt("pool", KTl[gb][:, ct, :], kdir[d][:], Et[:, 3, :], ALU.mult, ["kdir%d" % d, Ek], [KTk])
            tt("pool", bbar[:], bbv[:], Et[:, 2, :], ALU.mult, ["bbv", Ek], ["bbar"])
            tt("pool", kbar[:], kdir[d][:], Et[:, 2, :], ALU.mult, ["kdir%d" % d, Ek], ["kbar"])
            tend = (C - 1) if d == 0 else 0
            dve_copy(GC[gb][:, ct, :], v4(Et[:, 0, :])[:, :, tend], [Ek], [GCk])
            vz, vzk = zs[:, 8 + ct, t0:t0 + GT], "zs%d" % (8 + ct)
            for e_ in range(2):
                er = slice(e_ * 64, (e_ + 1) * 64)
                tp = (e_ * 64, e_ * 64)
                for c in range(GRP):
                    cs = slice(c * C, (c + 1) * C)
                    mm(PB(6)[er, c * C:(c + 1) * C], bbar[er, cs], ident_b[er, er], True, True, ["bbar", "ident_b"], bk(6), tp=tp)
                    mm(PB(6)[er, 256 + c * C:256 + (c + 1) * C], kbar[er, cs], ident_b[er, er], True, True, ["kbar", "ident_b"], bk(6), tp=tp)
                    mm(PB(7)[er, c * C:(c + 1) * C], vz[er, cs], ident_b[er, er], True, True, [vzk, "ident_b"], bk(7), tp=tp)
            act_copy(flat(bbt[gb][:, ct, :, :]), PB(6, 0, 256), bk(6), [bbtk])
            act_copy(flat(kbt[gb][:, ct, :, :]), PB(6, 256, 512), bk(6), [kbtk])
            dve_copy(flat(vt[gb][:, ct, :, :]), PB(7, 0, 256), bk(7), [vtk])
        import os
        kcut = int(os.environ.get("K_CUT", "9")) if d == 1 else 9
        if kcut <= 1:
            return
        m1 = M1[:, d, :].unsqueeze(1).to_broadcast([128, 4, 128])
        m2 = M2[:, d, :].unsqueeze(1).to_broadcast([128, 4, C])
        for c in range(GRP):
            b1, b2 = c % 2, 2 + c % 2
            b3 = 4 + c % 2
            cs = slice(c * C, (c + 1) * C)
            for hp in range(4):
                for e_ in range(2):
                    er = slice(e_ * 64, (e_ + 1) * 64)
                    tp = (e_ * 64, e_ * 64)
                    arv = AR[gb][er, hp, c, :, :]
                    mm(PB(b1)[er, hp * 128:(hp + 1) * 128], BT[gb][er, hp, cs], arv, True, True, [BTk, ARk], bk(b1), tp=tp)
                    mm(PB(b2)[er, hp * 128:(hp + 1) * 128], KTl[gb][er, hp, cs], arv, True, True, [KTk, ARk], bk(b2), tp=tp)
                    mm(PB(b3)[er, hp * C:(hp + 1) * C], AR[gb][er, hp, c, 0, :], BT[gb][er, hp, cs], True, True,
                       [ARk, BTk], bk(b3), tp=tp)
            tt("dve", GB1[:, c * 4:(c + 1) * 4, :], PB(b1).rearrange("p (a b) -> p a b", a=4), m1, ALU.mult, bk(b1) + ["M1"], ["GB1"])
            tt("dve", GB2[:, c * 4:(c + 1) * 4, :], PB(b2).rearrange("p (a b) -> p a b", a=4), m1, ALU.mult, bk(b2) + ["M1"], ["GB2"])
            tt("dve", Pk[0][:, c * 4:(c + 1) * 4, 0, :], PB(b3, 0, 256).rearrange("p (a b) -> p a b", a=4), m2, ALU.mult,
               bk(b3) + ["M2"], ["Pk0"])
        tt("dve", Zk[0][:], GB1[:, :, 0:C], I2[:].unsqueeze(1).to_broadcast([128, 16, C]), ALU.add, ["GB1", "I2"], ["Zk0"])
        if kcut <= 2:
            return
        psP = psum[:, 0:2048]
        psZ = psum[:, 2048:3072]
        for k in range(5):
            cur, nxt = k % 2, (k + 1) % 2
            for slot in range(16):
                for e_ in range(2):
                    er = slice(e_ * 64, (e_ + 1) * 64)
                    tp = (e_ * 64, e_ * 64)
                    Pv = Pk[cur][er, slot, 0, :]
                    if k == 0:
                        PTv, ptk = GB1[er, slot, 0:C], "GB1"
                    else:
                        PTv, ptk = Pk[cur][er, slot, 1, :], "Pk%d" % cur
                    mm(psP[er, slot * 128:slot * 128 + C], PTv, Pv, True, True, [ptk, "Pk%d" % cur], bk(0, 1, 2, 3), tp=tp)
                    if k < 4:
                        mm(psP[er, slot * 128 + C:slot * 128 + 2 * C], Pv, PTv, True, True, [ptk, "Pk%d" % cur], bk(0, 1, 2, 3), tp=tp)
            if k < 4:
                act_copy(Pk[nxt][:].rearrange("p s o n -> p (s o n)"), psP, bk(0, 1, 2, 3), ["Pk%d" % nxt])
            else:
                act_copy(Pk[nxt][:, :, 0, :], psP.rearrange("p (s o n) -> p s o n", s=16, o=2)[:, :, 0, :], bk(0, 1, 2, 3), ["Pk%d" % nxt])
            for slot in range(16):
                for e_ in range(2):
                    er = slice(e_ * 64, (e_ + 1) * 64)
                    tp = (e_ * 64, e_ * 64)
                    mm(psZ[er, slot * C:(slot + 1) * C], Pk[nxt][er, slot, 0, :], Zk[cur][er, slot, :], True, True,
                       ["Pk%d" % nxt, "Zk%d" % cur], bk(4, 5), tp=tp)
            tt("dve", flat(Zk[nxt][:]), psZ, flat(Zk[cur][:]), ALU.add, bk(4, 5) + ["Zk%d" % cur], ["Zk%d" % nxt])
        Zf, Zfk = Zk[1], "Zk1"
        if kcut <= 3:
            return
        order = range(GRP) if d == 0 else range(GRP - 1, -1, -1)
        for c in order:
            tok = t0 + c * C
            for hp in range(4):
                for e_ in range(2):
                    er = slice(e_ * 64, (e_ + 1) * 64)
                    tp = (e_ * 64, e_ * 64)
                    slot = c * 4 + hp
                    o = PB(6)[er, hp * C:(hp + 1) * C]
                    mm(o, AR[gb][er, hp, c, 0, :], Tst[er, hp, :], True, False, [ARk, "Tst"], bk(6), tp=tp)
                    mm(o, GB2[er, slot, 0:C], vt[gb][er, hp, c, :], False, True, ["GB2", vtk], bk(6), tp=tp)
            act_copy(flat(Wb[:]), PB(6, 0, 256), bk(6), ["Wb"])
            for hp in range(4):
                for e_ in range(2):
                    er = slice(e_ * 64, (e_ + 1) * 64)
                    tp = (e_ * 64, e_ * 64)
                    slot = c * 4 + hp
                    mm(PB(6)[er, 256 + hp * C:256 + (hp + 1) * C], Zf[er, slot, :], Wb[er, hp, :], True, True, [Zfk, "Wb"], bk(6), tp=tp)
            dve_copy(flat(Ub[:]), PB(6, 256, 512), bk(6), ["Ub"])
            for hp in range(4):
                for e_ in range(2):
                    er = slice(e_ * 64, (e_ + 1) * 64)
                    tp = (e_ * 64, e_ * 64)
                    slot = c * 4 + hp
                    oy = PB(7)[er, hp * C:(hp + 1) * C]
                    mm(oy, Tst[er, hp, :], AR[gb][er, hp, c, 1, :], True, False, ["Tst", ARk], bk(7), tp=tp)
                    mm(oy, Ub[er, hp, :], GB1[er, slot, C:2 * C], False, False, ["Ub", "GB1"], bk(7), tp=tp)
                    mm(oy, vt[gb][er, hp, c, :], GB2[er, slot, C:2 * C], False, True, [vtk, "GB2"], bk(7), tp=tp)
                    ot = PB(7)[er, 256 + hp * C:256 + (hp + 1) * C]
                    mm(ot, bbt[gb][er, hp, c, :], Ub[er, hp, :], True, False, [bbtk, "Ub"], bk(7), tp=tp)
                    mm(ot, kbt[gb][er, hp, c, :], vt[gb][er, hp, c, :], False, True, [kbtk, vtk], bk(7), tp=tp)
            py = PB(7, 0, 256).rearrange("p (a b) -> p a b", a=4)
            if passB:
                tt("dve", yT[:, :, tok:tok + C], py, yT[:, :, tok:tok + C], ALU.add, bk(7) + ["yT"], ["yT"])
            else:
                dve_copy(yT[:, :, tok:tok + C], py, bk(7), ["yT"])
            tt("dve", Tf[:], Tf[:], GC[gb][:, :, c:c + 1].to_broadcast([128, 4, C]), ALU.mult, ["Tf", GCk], ["Tf"])
            tt("dve", flat(Tf[:]), flat(Tf[:]), PB(7, 256, 512), ALU.add, bk(7) + ["Tf"], ["Tf"])
            act_copy(flat(Tst[:]), flat(Tf[:]), ["Tf"], ["Tst"])

    yc, ysq, sd, bon = tmp4
    sg = sb("sg", [128, 2, SEG], BF16)
    orw = sb("orw", [128, 4, SEG], BF16)
    epsc = sb("epsc", [128, 1], F32)

    def rwkv_epilogue(S):
        actf(sg[:, 0, :], zs[:, 14, :], AF.Sigmoid, ["zs14"], ["sg"])
        actf(sg[0:32, 1, :], zs[0:32, 15, :], AF.Sigmoid, ["zs15"], ["sg"])
        for ct in range(4):
            b0 = nbank()
            mm(PB(b0), bones_f[:], yT[:, ct, :], True, True, ["bones_f", "yT"], bk(b0))
            stt(yc[:], PB(b0), -1.0 / C, yT[:, ct, :], ALU.mult, ALU.add, bk(b0) + ["yT"], ["yc"])
            actf(ysq[:], yc[:], AF.Square, ["yc"], ["ysq"])
            b1 = nbank()
            mm(PB(b1), bones_f[:], ysq[:], True, True, ["bones_f", "ysq"], bk(b1))
            actf(sd[:], PB(b1), AF.Sqrt, bk(b1) + ["epsc"], ["sd"], bias=epsc[:, 0:1], scale=1.0 / C)
            recip(sd[:], sd[:], ["sd"], ["sd"])
            tt("dve", yc[:], yc[:], sd[:], ALU.mult, ["yc", "sd"], ["yc"])
            tsc("dve", yc[:], yc[:], vecs[:, V_LNW + ct:V_LNW + ct + 1], vecs[:, V_LNB + ct:V_LNB + ct + 1], ALU.mult, ALU.add,
                ["yc", "vecs"], ["yc"])
            b2 = nbank()
            mm(PB(b2), bones_b[:], bp[:, ct, :], True, True, ["bones_b", "bp"], bk(b2))
            tt("dve", bon[:], PB(b2), zs[:, 8 + ct, :], ALU.mult, bk(b2) + ["zs%d" % (8 + ct)], ["bon"])
            tt("pool", yc[:], yc[:], bon[:], ALU.add, ["yc", "bon"], ["yc"])
            b3 = nbank()
            mm(PB(b3), g2b[:, 0, ct * 128:(ct + 1) * 128], sg[:, 0, :], True, False, ["g2b", "sg"], bk(b3))
            mm(PB(b3), g2b[0:32, 1, ct * 128:(ct + 1) * 128], sg[0:32, 1, :], False, True, ["g2b", "sg"], bk(b3))
            tt("dve", orw[:, ct, :], PB(b3), yc[:], ALU.mult, bk(b3) + ["yc"], ["orw"])

    mergedT = zs
    sga, sgr, tma, tmr = tmp4
    for grp_ in (("ztmp0", "yc", "sga", "hrA"), ("ztmp1", "ysq", "sgr", "hrB"), ("ztmq0", "sd", "tma"), ("ztmq1", "bon", "tmr")):
        for k_ in grp_:
            P.alias[k_] = tuple(x for x in grp_ if x != k_)

    def merge_branches(S):
        for h in range(2):
            wga, kga = w_take("GA%d" % h, S)
            wgr, kgr = w_take("GR%d" % h, S)
            wbr_, kbr = w_take("BAR%d" % h, S)
            wbv = wbr_[:, :].rearrange("p (a c n) -> p a c n", a=2, c=4)
            for mi in range(4):
                m = h * 4 + mi
                proj_fm_tile(wga, kga, 8, 512, mi * 128, uT, "uT", HALO, SEG, 0)
                actf(sga[:], PB(0), AF.Sigmoid, bk(0), ["sga"])
                proj_fm_tile(wgr, kgr, 8, 512, mi * 128, uT, "uT", HALO, SEG, 1)
                actf(sgr[:], PB(1), AF.Sigmoid, bk(1), ["sgr"])
                for c in range(4):
                    mm(PB(2), wbv[:, 0, c, mi * 128:(mi + 1) * 128], oattn[:, c, :], c == 0, c == 3, [kbr, "oattn"], bk(2))
                for c in range(4):
                    mm(PB(3), wbv[:, 1, c, mi * 128:(mi + 1) * 128], orw[:, c, :], c == 0, c == 3, [kbr, "orw"], bk(3))
                tt("dve", tma[:], PB(2), sga[:], ALU.mult, bk(2) + ["sga"], ["tma"])
                tt("dve", tmr[:], PB(3), sgr[:], ALU.mult, bk(3) + ["sgr"], ["tmr"])
                tt("pool", mergedT[:, m, :], tma[:], tmr[:], ALU.add, ["tma", "tmr"], ["zs%d" % m])
            w_done(3)

    def out_proj(S):
        w0_, k0_ = w_take("WO0", S)
        w1_, k1_ = w_take("WO1", S)
        wv = [w0_[:, :].rearrange("p (c n) -> p c n", c=8), w1_[:, :].rearrange("p (c n) -> p c n", c=8)]
        wk = [k0_, k1_]
        for tb in range(SEG // 128):
            i = cnt["st"]
            cnt["st"] += 1
            q = i % 4
            a = cnt["x"] % 3
            cnt["x"] += 1
            xk = "xt%d" % a
            dma("pool", xt[a][:], xh[S, HALO + tb * 128:HALO + (tb + 1) * 128, :], [], [xk], "xin%d" % a)
            for n2 in range(2):
                bank = 4 + n2
                for c in range(8):
                    mm(PB(bank), mergedT[:, c, tb * 128:(tb + 1) * 128], wv[n2][:, c, :], c == 0, c == 7, ["zs%d" % c, wk[n2]], bk(bank))
            post_norm_residual(q, (4, 5), xt[a], xk)
            dma("pool", hbuf[1 + S * SEG + tb * 128:1 + S * SEG + (tb + 1) * 128, :], xt[a][:], [xk], ["hb%d" % S], "xst%d" % a)
        w_done(2)

    def post_norm_residual(q, banks, res, rkey):
        sk, rk_ = "ssq%d" % q, "rstd%d" % q
        for n2 in range(2):
            actf(junk[:, n2 * 512:(n2 + 1) * 512], PB(banks[n2]), AF.Square, bk(banks[n2]), ["junk", sk], accum=st_ssq[q][:, n2:n2 + 1])
        tt("dve", st_rstd[q][:], st_ssq[q][:, 0:1], st_ssq[q][:, 1:2], ALU.add, [sk], [rk_])
        tsc("dve", st_rstd[q][:], st_rstd[q][:], 1.0 / D, NORM_EPS, ALU.mult, ALU.add, [rk_], [rk_])
        actf(st_rstd[q][:], st_rstd[q][:], AF.Sqrt, [rk_], [rk_])
        recip(st_rstd[q][:], st_rstd[q][:], [rk_], [rk_])
        for n2 in range(2):
            hk = "hrA" if n2 == 0 else "hrB"
            stt(tmp4[n2][:], PB(banks[n2]), st_rstd[q][:, 0:1], rows[:, n2 * 512:(n2 + 1) * 512], ALU.mult, ALU.mult,
                bk(banks[n2]) + [rk_, "rows"], [hk])
            tt("pool", res[:, n2 * 512:(n2 + 1) * 512], res[:, n2 * 512:(n2 + 1) * 512], tmp4[n2][:], ALU.add, [hk, rkey], [rkey])

    uT2 = carve("ffn", "uT2", [128, 8, SEG + 2], BF16)
    actT = [carve("ffn", "actT%d" % i, [128, NFT, 256], BF16) for i in range(2)]
    cg = [carve("ffn", "cg%d" % i, [128, 256], F32) for i in range(3)]
    cu = [carve("ffn", "cu%d" % i, [128, 256], F32) for i in range(3)]
    gl = [carve("ffn", "gl%d" % i, [128, 256], F32) for i in range(2)]
    def phase_barrier(G):
        others = [k_ for g2_ in groups if g2_ != G for k_ in groups[g2_]]
        P.barrier(others)

    def ffn_slot(S):
        base = S * SEG
        for (r0, n) in ((0, 128), (128, 128), (256, 128), (384, 128), (512, 2)):
            norm_block(hbuf[base + r0:base + r0 + n, :], n, 1, uT2, "uT2", r0,
                       src_reads=["hb_first", "hb_last"] + ["hb%d" % k for k in (S - 1, S, S + 1) if 0 <= k < nslot])
        for side, col in ((0, 0), (1, SEG + 1)):
            tsmul("dve", uT2[:, :, col:col + 1], uT2[:, :, col:col + 1], flags[:, 2 * S + side:2 * S + side + 1],
                  ["uT2", "flags"], ["uT2"])
        tasks = []
        for g in range(6):
            nt = 4 if g < 5 else 2
            for ti in range(nt):
                for st in range(2):
                    tasks.append((g, ti, st, nt))
        held = [None]

        def st_mm(i):
            g, ti, st, nt = tasks[i]
            if held[0] is None or held[0][0] != g:
                if held[0] is not None:
                    w_done(2)
                wg_, kg_ = w_take("UG%d" % g, S)
                wu_, ku_ = w_take("UU%d" % g, S)
                held[0] = (g, wg_, kg_, wu_, ku_)
            _, wg_, kg_, wu_, ku_ = held[0]
            n = nt * 128
            f = g * 4 + ti
            x, y = i % 2, i % 3
            bg, bu = 2 * x, 2 * x + 1
            proj_fm_tile(wg_, kg_, 8, n, ti * 128, uT2, "uT2", st * 256, 258, bg)
            proj_fm_tile(wu_, ku_, 8, n, ti * 128, uT2, "uT2", st * 256, 258, bu)
            for (bank, dst, dk, ft) in ((bg, cg[y], "cg%d" % y, f), (bu, cu[y], "cu%d" % y, NFT + f)):
                actf(dst[:], PB(bank, 1, 257), AF.Identity, bk(bank) + ["vecs"], [dk],
                     bias=vecs[:, V_CB + ft:V_CB + ft + 1], scale=vecs[:, V_CW + 44 + ft:V_CW + 44 + ft + 1])

        def st_taps(i):
            g, ti, st, nt = tasks[i]
            f = g * 4 + ti
            x, y = i % 2, i % 3
            bg, bu = 2 * x, 2 * x + 1
            for (lo, hi, wo) in ((0, 256, 0), (2, 258, 88)):
                for (bank, dst, dk, ft) in ((bg, cg[y], "cg%d" % y, f), (bu, cu[y], "cu%d" % y, NFT + f)):
                    stt(dst[:], PB(bank, lo, hi), vecs[:, V_CW + wo + ft:V_CW + wo + ft + 1], dst[:], ALU.mult, ALU.add,
                        bk(bank) + ["vecs", dk], [dk])

        def st_glu(i):
            g, ti, st, nt = tasks[i]
            f = g * 4 + ti
            x, y = i % 2, i % 3
            actf(gl[x][:], cg[y][:], AF.Gelu_apprx_tanh, ["cg%d" % y], ["gl%d" % x])
            tt("pool", actT[st][:, f, :], gl[x][:], cu[y][:], ALU.mult, ["gl%d" % x, "cu%d" % y], ["actT%d" % st])

        nt_ = len(tasks)
        import os
        kffn = os.environ.get("K_FFN", "")
        for i in range(nt_ + 2):
            if i < nt_:
                st_mm(i)
            if 0 <= i - 1 < nt_ and "notaps" not in kffn:
                st_taps(i - 1)
            if 0 <= i - 2 < nt_ and "noglu" not in kffn:
                st_glu(i - 2)
        w_done(2)
        for g in range(6):
            wd_, kd_ = w_take("DN%d" % g, S)
            kc = 4 if g < 5 else 2
            wv = wd_[:, 0:kc * 1024].rearrange("p (c n) -> p c n", c=kc)
            for c in range(kc):
                f = g * 4 + c
                for tb in range(4 if "nodown" not in kffn else 0):
                    st, o = tb // 2, (tb % 2) * 128
                    for n2 in range(2):
                        bank = tb * 2 + n2
                        mm(PB(bank), actT[st][:, f, o:o + 128], wv[:, c, n2 * 512:(n2 + 1) * 512], f == 0, f == NFT - 1,
                           ["actT%d" % st, kd_], bk(bank))
            w_done()
        for tb in range(4):
            i = cnt["st"]
            cnt["st"] += 1
            q = i % 4
            a = cnt["x"] % 3
            cnt["x"] += 1
            xk = "xt%d" % a
            dma("pool", xt[a][:], hbuf[1 + base + tb * 128:1 + base + (tb + 1) * 128, :], ["hb%d" % S], [xk], "xin%d" % a)
            post_norm_residual(q, (tb * 2, tb * 2 + 1), xt[a], xk)
            dma("pool", y_out[S * SEG + tb * 128:S * SEG + (tb + 1) * 128, :], xt[a][:], [xk], ["yout"], "xst%d" % a)

    def dump(name, ap, keys):
        if name in dbg_out:
            dma("pool", dbg_out[name], ap, keys, ["dbg_" + name], "dbg_" + name)

    setup()
    prepass()
    load_gains(0)
    if "A" in passes:
        rwkv_state_init()
        for S in reversed(range(nslot)):
            import os
            slot_load_norm(S)
            if "noz" in os.environ.get("K_DBG", ""):
                for g_ in range(4):
                    w_take("ZG%d" % g_, S)
                    w_done()
            else:
                z_project(S, range(14))
            if "nobegin" not in os.environ.get("K_DBG", ""):
                rwkv_slot_begin(S, 1)
            for gi in reversed(range(NGRP)):
                if "noscan" not in os.environ.get("K_DBG", ""):
                    rwkv_group(S, 1, gi, False)
            if os.environ.get("K_DBG", "") != "nostore":
                dma("pool", ybwd_d[S], flat(yT[:]), ["yT"], ["ybwd%d" % S], "yTst")
            if S == 0:
                dump("ybwd", flat(yT[:]), ["yT"])
    if "B" in passes:
        rwkv_state_init()
        import os
        kstop = int(os.environ.get("K_STOP", "99"))

        def drain(tags, S):
            for t_ in tags:
                w_take(t_, S)
                w_done()

        for S in range(nslot):
            slot_load_norm(S)
            if kstop <= 1:
                drain(["QG", "KVG", "ZG0", "ZG1", "ZG2", "ZG3", "GA0", "GR0", "BAR0", "GA1", "GR1", "BAR1", "WO0", "WO1"], S)
                continue
            phase_barrier("att")
            qkv_project(S)
            if kstop <= 2:
                drain(["ZG0", "ZG1", "ZG2", "ZG3", "GA0", "GR0", "BAR0", "GA1", "GR1", "BAR1", "WO0", "WO1"], S)
                continue
            attention(S)
            if kstop <= 3:
                drain(["ZG0", "ZG1", "ZG2", "ZG3", "GA0", "GR0", "BAR0", "GA1", "GR1", "BAR1", "WO0", "WO1"], S)
                continue
            z_project(S, range(16))
            if kstop <= 4:
                drain(["GA0", "GR0", "BAR0", "GA1", "GR1", "BAR1", "WO0", "WO1"], S)
                continue
            if "A" in passes:
                dma("pool", flat(yT[:]), ybwd_d[S], ["ybwd%d" % S], ["yT"], "yTld")
            else:
                P.add("dve", lambda e: e.memset(flat(yT[:]), 0.0), writes=["yT"])
            rwkv_slot_begin(S, 0)
            phase_barrier("rw")
            for gi in range(NGRP):
                rwkv_group(S, 0, gi, True)
            if S == 0:
                dump("uT", flat(uT[:]), ["uT"])
                dump("oattn", flat(oattn[:]), ["oattn"])
                dump("zs", flat(zs[:]), ["zs%d" % i for i in range(16)])
                dump("yT", flat(yT[:]), ["yT"])
            if kstop <= 5:
                drain(["GA0", "GR0", "BAR0", "GA1", "GR1", "BAR1", "WO0", "WO1"], S)
                continue
            rwkv_epilogue(S)
            if kstop <= 6:
                drain(["GA0", "GR0", "BAR0", "GA1", "GR1", "BAR1", "WO0", "WO1"], S)
                continue
            merge_branches(S)
            if S == 0:
                dump("orw", flat(orw[:]), ["orw"])
                dump("mergedT", flat(mergedT[:, 0:8, :]), ["zs%d" % i for i in range(8)])
            if kstop <= 7:
                drain(["WO0", "WO1"], S)
                continue
            out_proj(S)
    if "C" in passes:
        load_gains(1)
        phase_barrier("ffn")
        for S in range(nslot):
            ffn_slot(S)
    P.add("pool", None, reads=["yout", "wscr"] + ["hb%d" % k for k in range(nslot)] + ["dbg_" + n for n in dbg_out]
          + ["ybwd%d" % k for k in range(nslot)])
    assert wstate["taken"] == len(wq) == wstate["released"], (wstate, len(wq))

    sems = P.assign(nc, es)
    with nc.Block() as block:
        @block.sync
        def _(e):
            P.run_engine("sp", e, sems)

        @block.tensor
        def _(e):
            P.run_engine("pe", e, sems)

        @block.scalar
        def _(e):
            P.run_engine("act", e, sems)

        @block.vector
        def _(e):
            P.run_engine("dve", e, sems)

        @block.gpsimd
        def _(e):
            P.run_engine("pool", e, sems)
    es.close()
    nc._n_ops = P.n
    return nc


def _assign_sequences():
    plan = [[("p", 0)]]
    counts = [5, 5, 5, 5, 4, 4, 4]
    k = 0
    for c in counts:
        plan.append([("s", k + i) for i in range(c)])
        k += c
    return plan


def kernel(**inputs):
    xp = np.asarray(inputs["x_prompt"], np.float32)
    xs = np.asarray(inputs["x_sample"], np.float32)
    wl = _layout_weights(inputs)
    plan = _assign_sequences()
    in_maps, places = [], []
    for core in range(NCORES):
        seqs = [xp[0] if kind == "p" else xs[i] for (kind, i) in plan[core]]
        xh, fl, place = _layout_core(seqs, SLOTS_FULL)
        m = dict(wl)
        m["xh"] = xh
        m["flags"] = fl
        in_maps.append(m)
        places.append(place)
    nc = build(SLOTS_FULL, "ABC")
    res = run_bass_kernel_spmd(nc, in_maps, core_ids=list(range(NCORES)))
    y_prompt = np.zeros_like(xp)
    y_sample = np.zeros_like(xs)
    for core in range(NCORES):
        y = np.asarray(res.results[core]["y"], np.float32)
        for (kind, i), (s0, n) in zip(plan[core], places[core]):
            blk = y[s0 * SEG:(s0 + n) * SEG]
            if kind == "p":
                y_prompt[0] = blk
            else:
                y_sample[i] = blk
    return (y_prompt, y_sample)
```

```python
import math
from contextlib import ExitStack

import numpy as np
import concourse.bass as bass
import concourse.mybir as mybir
from concourse.bass_utils import run_bass_kernel_spmd

F32 = mybir.dt.float32
BF16 = mybir.dt.bfloat16
AF = mybir.ActivationFunctionType
ALU = mybir.AluOpType

D = 1024
SEG = 512
HALO = 128
NTH = SEG + 2 * HALO
NBLK = NTH // 128
C = 64
GRP = 4
NGRP = SEG // (C * GRP)
GT = C * GRP
KAPPA = math.exp(-0.5)
NORM_EPS = 1e-6
GN_EPS = 64e-5
NW = 3
NCORES = 8
SLOTS_FULL = 32
D_FF = 2816
NFT = D_FF // 128

V_G1, V_G2, V_MUP, V_MUN, V_A0, V_KK, V_KA, V_RK, V_LNW, V_LNB, V_CW, V_CB = (
    0, 8, 16, 32, 48, 56, 60, 64, 68, 72, 76, 76 + 132)
NV = V_CB + 44
C_ID, C_BONES, C_M1, C_M2, C_TRI, C_I2, C_PEN = 0, 128, 256, 512, 640, 1408, 1472
NCST = C_PEN + 6 * 512

SEM_LIMIT = 12000


class Op:
    __slots__ = ("eng", "fn", "is_dma", "semkey", "waits", "needs_inc", "sem", "count")

    def __init__(self, eng, fn, is_dma, semkey):
        self.eng = eng
        self.fn = fn
        self.is_dma = is_dma
        self.semkey = semkey
        self.waits = []
        self.needs_inc = is_dma
        self.sem = None
        self.count = 0


class Prog:
    ENGS = ("pe", "act", "dve", "pool", "sp")

    def __init__(self):
        self.ops = {e: [] for e in self.ENGS}
        self.last_w = {}
        self.readers = {}
        self.dma_cnt = {}
        self.n = 0
        self.alias = {}

    def begin_capture(self):
        self._cap = []

    def end_capture(self):
        c, self._cap = self._cap, None
        return c

    def replay(self, items):
        for it in items:
            self.add(*it)

    def interleave(self, a, b):
        ia = ib = 0
        na, nb = len(a), len(b)
        while ia < na or ib < nb:
            if ib >= nb or (ia < na and ia * nb <= ib * na):
                self.add(*a[ia])
                ia += 1
            else:
                self.add(*b[ib])
                ib += 1

    def add(self, eng, fn, reads=(), writes=(), dma=False, semkey=None):
        if getattr(self, "_cap", None) is not None:
            self._cap.append((eng, fn, tuple(reads), tuple(writes), dma, semkey))
            return None
        op = Op(eng, fn, dma, semkey if dma else None)
        self.n += 1
        if self.alias:
            reads = list(reads) + [a for k in reads for a in self.alias.get(k, ())]
            writes = list(writes) + [a for k in writes for a in self.alias.get(k, ())]
        deps = []
        for k in reads:
            lw = self.last_w.get(k)
            if lw is not None:
                deps.append((lw, "raw"))
        for k in writes:
            lw = self.last_w.get(k)
            if lw is not None:
                deps.append((lw, "waw"))
            for r in self.readers.get(k, ()):
                deps.append((r, "war"))
        seen = set()
        for d, kind in deps:
            if d is op or id(d) in seen:
                continue
            if not d.is_dma and not dma and d.eng == eng:
                if eng == "pe":
                    continue
            seen.add(id(d))
            if d.is_dma:
                op.waits.append((d.sem, self.dma_cnt[d.semkey]))
            else:
                d.needs_inc = True
                op.waits.append(d)
        if dma:
            self.dma_cnt[semkey] = self.dma_cnt.get(semkey, 0) + 16
            op.sem = "d_" + str(semkey)
            op.count = self.dma_cnt[semkey]
        for k in reads:
            self.readers.setdefault(k, []).append(op)
        for k in writes:
            self.last_w[k] = op
            self.readers[k] = []
        self.ops[eng].append(op)
        return op

    def barrier(self, keys, engines=("pe", "act", "dve", "pool")):
        deps, seen = [], set()
        for k in keys:
            cand = list(self.readers.get(k, ()))
            if self.last_w.get(k) is not None:
                cand.append(self.last_w[k])
            for d in cand:
                if id(d) not in seen and d.fn is not None:
                    seen.add(id(d))
                    deps.append(d)
        for e in engines:
            op = Op(e, None, False, None)
            for d in deps:
                if d.is_dma:
                    op.waits.append((d.sem, self.dma_cnt[d.semkey]))
                elif not (d.eng == e and e == "pe"):
                    d.needs_inc = True
                    op.waits.append(d)
            self.ops[e].append(op)
        for k in keys:
            self.last_w.pop(k, None)
            self.readers[k] = []

    def assign(self, nc, es):
        sems = {}
        for e in self.ENGS:
            epoch, cnt = 0, 0
            for op in self.ops[e]:
                if not op.is_dma and op.needs_inc:
                    if cnt >= SEM_LIMIT:
                        epoch += 1
                        cnt = 0
                    cnt += 1
                    op.sem = "c_%s_%d" % (e, epoch)
                    op.count = cnt
        for e in self.ENGS:
            for op in self.ops[e]:
                if op.sem is not None and op.sem not in sems:
                    sems[op.sem] = es.enter_context(nc.semaphore(op.sem))
        return sems

    def run_engine(self, ename, eng, sems):
        seen = {}
        for op in self.ops[ename]:
            for d in op.waits:
                sname, cnt = d if isinstance(d, tuple) else (d.sem, d.count)
                if seen.get(sname, 0) >= cnt:
                    continue
                seen[sname] = cnt
                eng.wait_ge(sems[sname], cnt)
            if op.fn is None:
                continue
            ins = op.fn(eng)
            if op.is_dma:
                ins.then_inc(sems[op.sem], 16)
            elif op.needs_inc:
                ins.then_inc(sems[op.sem], 1)


def _fm(vec, ntile):
    return np.ascontiguousarray(np.asarray(vec, np.float32).reshape(ntile, 128).T)


def _constants():
    cst = np.zeros((128, NCST), np.float32)
    p = np.arange(128)
    cst[:, C_ID:C_ID + 128] = np.eye(128, dtype=np.float32)
    cst[:, C_BONES:C_BONES + 128] = (p[:, None] // 64 == p[None, :] // 64).astype(np.float32)
    s = (p % 64)[:, None]
    t = np.arange(64)[None, :]
    m1f = np.concatenate([(t > s), (t >= s)], axis=1).astype(np.float32)
    m1b = np.concatenate([(t < s), (t <= s)], axis=1).astype(np.float32)
    cst[:, C_M1:C_M1 + 128] = m1f
    cst[:, C_M1 + 128:C_M1 + 256] = m1b
    cst[:, C_M2:C_M2 + 64] = (t < s).astype(np.float32)
    cst[:, C_M2 + 64:C_M2 + 128] = (t > s).astype(np.float32)
    ps_, pt_ = p[:, None], p[None, :]
    same = (ps_ // 64 == pt_ // 64)
    for d in range(2):
        if d == 0:
            incl, excl, rest = (ps_ <= pt_), (ps_ < pt_), (ps_ > pt_)
        else:
            incl, excl, rest = (ps_ >= pt_), (ps_ > pt_), (ps_ < pt_)
        base = C_TRI + d * 384
        cst[:, base:base + 128] = (incl & same)
        cst[:, base + 128:base + 256] = (excl & same)
        cst[:, base + 256:base + 384] = (rest & same)
    cst[:, C_I2:C_I2 + 64] = (t == s).astype(np.float32)
    for g in range(2):
        for j in range(3):
            blk = np.zeros((128, 4, 128), np.float32)
            sk = p[:, None] + (j - 1) * 128
            qq = p[None, :]
            dist = np.abs(qq - sk).astype(np.float32)
            for i in range(4):
                h = g * 4 + i
                slope = 2.0 ** (-(h + 1))
                v = -8.0 * slope * dist
                v = np.where(dist <= 128, v, -240000.0)
                blk[:, i, :] = v
            base = C_PEN + (g * 3 + j) * 512
            cst[:, base:base + 512] = blk.reshape(128, 512)
    return cst


def _layout_weights(inp):
    L = 0
    w_in = np.asarray(inp["w_in"][L], np.float32)
    cols = []
    for i in range(4):
        cols += list(range(i * 64, (i + 1) * 64)) + list(range((4 + i) * 64, (5 + i) * 64))
    cols += list(range(512, 768))
    zc = list(range(768, 768 + 1952))
    w_in_p = np.zeros((1024, 38 * 128), np.float32)
    w_in_p[:, 0:768] = w_in[:, cols]
    w_in_p[:, 768:768 + 1952] = w_in[:, zc]
    w_in_p[:, 22 * 128:38 * 128] = w_in[:, 2720:4768]
    wba = np.asarray(inp["w_branch_attn"][L], np.float32)
    rows = []
    for i in range(4):
        rows += list(range(i * 64, (i + 1) * 64)) + list(range((4 + i) * 64, (5 + i) * 64))
    wba_p = np.ascontiguousarray(wba[rows, :])
    vecs = np.zeros((128, NV), np.float32)
    vecs[:, V_G1:V_G1 + 8] = _fm(inp["norm_mix_pre"][L], 8)
    vecs[:, V_G2:V_G2 + 8] = _fm(inp["norm_ffn_pre"][L], 8)
    mup = np.zeros(2048, np.float32)
    mun = np.zeros(2048, np.float32)
    mup[:1952] = np.asarray(inp["rw_mu_prev"][L])
    mun[:1952] = np.asarray(inp["rw_mu_next"][L])
    vecs[:, V_MUP:V_MUP + 16] = _fm(mup, 16)
    vecs[:, V_MUN:V_MUN + 16] = _fm(mun, 16)
    a0 = np.asarray(inp["rw_a0"][L], np.float32)
    for d in range(2):
        vecs[:, V_A0 + d * 4:V_A0 + d * 4 + 4] = _fm(a0[d], 4)
    vecs[:, V_KK:V_KK + 4] = _fm(inp["rw_k_k"][L], 4)
    vecs[:, V_KA:V_KA + 4] = _fm(inp["rw_k_a"][L], 4)
    vecs[:, V_RK:V_RK + 4] = _fm(np.asarray(inp["rw_r_k"][L]).reshape(512), 4)
    vecs[:, V_LNW:V_LNW + 4] = _fm(inp["rw_ln_w"][L], 4)
    vecs[:, V_LNB:V_LNB + 4] = _fm(inp["rw_ln_b"][L], 4)
    cw = np.asarray(inp["ffn_conv_w"][L], np.float32)
    for j in range(3):
        vecs[:, V_CW + j * 44:V_CW + (j + 1) * 44] = _fm(cw[j], 44)
    vecs[:, V_CB:V_CB + 44] = _fm(inp["ffn_conv_b"][L], 44)
    rows128 = np.zeros((128, 2, 1024), np.float32)
    rows128[:, 0, :] = np.asarray(inp["norm_mix_post"][L], np.float32)[None, :]
    rows128[:, 1, :] = np.asarray(inp["norm_ffn_post"][L], np.float32)[None, :]
    w0rows = np.ascontiguousarray(np.asarray(inp["rw_w0"][L], np.float32).reshape(1, 2, 512))
    sink = np.asarray(inp["attn_sink"][L], np.float32)
    sinkrows = np.zeros((1, 2, 4, 128), np.float32)
    for g in range(2):
        for i in range(4):
            sinkrows[0, g, i, :] = sink[g * 4 + i]
    sinkrows = sinkrows.reshape(1, 2, 512)
    lora = np.zeros((128, 4, 512), np.float32)
    lora[:, 0, :] = np.asarray(inp["rw_w2"][L], np.float32).reshape(128, 512)
    lora[:, 1, :] = np.asarray(inp["rw_a2"][L], np.float32).reshape(128, 512)
    g2 = np.asarray(inp["rw_g2"][L], np.float32)
    lora[:, 2, :] = g2[0:128]
    lora[0:32, 3, :] = g2[128:160]
    return {
        "w_in": w_in_p, "wba": wba_p,
        "wbr": np.ascontiguousarray(np.asarray(inp["w_branch_rwkv"][L], np.float32)),
        "wout": np.ascontiguousarray(np.asarray(inp["w_out"][L], np.float32)),
        "wup": np.ascontiguousarray(np.asarray(inp["w_ffn_up"][L], np.float32)),
        "wdn": np.ascontiguousarray(np.asarray(inp["w_ffn_down"][L], np.float32)),
        "vecs": vecs, "rows": rows128, "w0rows": w0rows, "sinkrows": sinkrows, "lora": lora,
        "cst": _constants(),
    }


def _layout_core(seqs, nslot):
    xh = np.zeros((nslot, NTH, D), np.float32)
    flags = np.zeros((nslot, 2), np.float32)
    s = 0
    place = []
    for x in seqs:
        T = x.shape[0]
        n = T // SEG
        xp = np.zeros((T + 2 * HALO, D), np.float32)
        xp[HALO:HALO + T] = x
        for k in range(n):
            xh[s + k] = xp[k * SEG:k * SEG + NTH]
            flags[s + k, 0] = 1.0 if k > 0 else 0.0
            flags[s + k, 1] = 1.0 if k < n - 1 else 0.0
        place.append((s, n))
        s += n
    fl = np.ascontiguousarray(np.broadcast_to(flags.reshape(1, nslot * 2), (128, nslot * 2))).astype(np.float32)
    return xh, fl, place


def weight_schedule(nslot, passes):
    q = []
    if "A" in passes:
        for s in reversed(range(nslot)):
            q += [("ZG0", s), ("ZG1", s), ("ZG2", s), ("ZG3", s)]
    if "B" in passes:
        for s in range(nslot):
            q += [("QG", s), ("KVG", s)]
            if "A" not in passes:
                q += [("ZG0", s), ("ZG1", s), ("ZG2", s), ("ZG3", s)]
            for h in range(2):
                q += [("GA%d" % h, s), ("GR%d" % h, s), ("BAR%d" % h, s)]
            q += [("WO0", s), ("WO1", s)]
    if "C" in passes:
        for s in range(nslot):
            for g in range(6):
                q += [("UG%d" % g, s), ("UU%d" % g, s)]
            for g in range(6):
                q += [("DN%d" % g, s)]
    return q


def build(nslot, passes="ABC", dbg=()):
    nc = bass.Bass("TRN2", target_bir_lowering=False)

    def din(name, shape, dt=F32):
        return nc.dram_tensor(name, list(shape), dt, kind="ExternalInput").ap()

    def dout(name, shape, dt=F32):
        return nc.dram_tensor(name, list(shape), dt, kind="ExternalOutput").ap()

    def dscr(name, shape, dt):
        return nc.dram_tensor(name, list(shape), dt).ap()

    xh = din("xh", [nslot, NTH, D])
    flags_d = din("flags", [128, 2 * nslot])
    w_in_d = din("w_in", [1024, 4864])
    wba_d = din("wba", [512, 1024])
    wbr_d = din("wbr", [512, 1024])
    wout_d = din("wout", [1024, 1024])
    wup_d = din("wup", [1024, 5632])
    wdn_d = din("wdn", [2816, 1024])
    vecs_d = din("vecs", [128, NV])
    rows_d = din("rows", [128, 2, 1024])
    w0_d = din("w0rows", [1, 2, 512])
    sink_d = din("sinkrows", [1, 2, 512])
    lora_d = din("lora", [128, 4, 512])
    cst_d = din("cst", [128, NCST])
    y_out = dout("y", [nslot * SEG, D])
    dbg_out = {}
    for name, shape in dbg:
        dbg_out[name] = dout("dbg_" + name, shape)

    w_in_b = dscr("w_in_b", [1024, 4864], BF16)
    wba_b = dscr("wba_b", [512, 1024], BF16)
    wbr_b = dscr("wbr_b", [512, 1024], BF16)
    wout_b = dscr("wout_b", [1024, 1024], BF16)
    wup_b = dscr("wup_b", [1024, 5632], BF16)
    wdn_b = dscr("wdn_b", [2816, 1024], BF16)
    ybwd_d = dscr("ybwd", [nslot, 128, 4 * SEG], F32)
    us_d = dscr("us_scr", [nslot, 128, 8 * NTH], BF16)
    zs_d = dscr("zs_scr", [nslot, 128, 16 * SEG], BF16)
    hbuf = dscr("hbuf", [nslot * SEG + 2, D], F32)

    P = Prog()
    es = ExitStack()

    def sb(name, shape, dt):
        return es.enter_context(nc.sbuf_tensor("s_" + name, list(shape), dt))

    psum = es.enter_context(nc.psum_tensor("psum", [128, 4096], F32))

    def PB(b, lo=0, hi=512):
        return psum[:, b * 512 + lo:b * 512 + hi]

    def bk(*banks):
        r = []
        for b in banks:
            r += ["pb%da" % b, "pb%db" % b]
        return r

    pbT = psum[:, 7 * 512:8 * 512].bitcast(BF16)

    ident_b = sb("ident_b", [128, 128], BF16)
    ones_b = sb("ones_b", [128, 128], BF16)
    bones_f = sb("bones_f", [128, 128], F32)
    bones_b = sb("bones_b", [128, 128], BF16)
    M1 = sb("M1", [128, 2, 128], F32)
    M2 = sb("M2", [128, 2, 64], F32)
    TRI = sb("TRI", [128, 2, 384], F32)
    I2 = sb("I2", [128, 64], BF16)
    PEN = sb("PEN", [128, 6, 512], BF16)
    vecs = sb("vecs", [128, NV], F32)
    c0 = sb("c0", [128, 16], F32)
    omka = sb("omka", [128, 4], F32)
    rows = sb("rows", [128, 1024], F32)
    gB = sb("gB", [128, 8, 128], F32)
    flags = sb("flags", [128, 2 * nslot], F32)
    rb = sb("rb", [2, 4, 512], BF16)
    w2b = sb("w2b", [128, 512], BF16)
    a2b = sb("a2b", [128, 512], BF16)
    g2b = sb("g2b", [128, 2, 512], BF16)
    onesf = sb("onesf", [128, 128], F32)

    xt = [sb("xt%d" % i, [128, 1024], F32) for i in range(3)]
    ub = [sb("ub%d" % i, [128, 1024], BF16) for i in range(2)]
    st_ssq = [sb("ssq%d" % i, [128, 2], F32) for i in range(4)]
    st_rstd = [sb("rstd%d" % i, [128, 1], F32) for i in range(4)]
    junk = sb("junk", [128, 1024], BF16)
    uT = sb("uT", [128, 8, NTH], BF16)
    wbuf = [sb("wbuf%d" % i, [128, 4096], BF16) for i in range(NW)]

    cnt = {"x": 0, "st": 0}

    def dma(eng, out, in_, reads, writes, semkey):
        P.add(eng, lambda e: e.dma_start(out=out, in_=in_), reads=reads, writes=writes, dma=True, semkey=semkey + "_" + eng)

    def act_copy(out, in_, reads, writes):
        P.add("act", lambda e: e.copy(out, in_), reads=reads, writes=writes)

    def dve_copy(out, in_, reads, writes):
        P.add("dve", lambda e: e.tensor_copy(out, in_), reads=reads, writes=writes)

    def mm(out, lhsT, rhs, start, stop, r, w, tp=None):
        if tp is None:
            P.add("pe", lambda e: e.matmul(out, lhsT, rhs, start=start, stop=stop), reads=r, writes=w)
        else:
            P.add("pe", lambda e: e.matmul(out, lhsT, rhs, start=start, stop=stop, tile_position=tp), reads=r, writes=w)

    def actf(out, in_, func, r, w, bias=None, scale=None, accum=None):
        kw = {}
        if bias is not None:
            kw["bias"] = bias
        if scale is not None:
            kw["scale"] = scale
        if accum is not None:
            kw["accum_out"] = accum
        P.add("act", lambda e: e.activation(out=out, in_=in_, func=func, **kw), reads=r, writes=w)

    def tt(eng, out, in0, in1, op, r, w):
        P.add(eng, lambda e: e.tensor_tensor(out=out, in0=in0, in1=in1, op=op), reads=r, writes=w)

    def stt(out, in0, scalar, in1, op0, op1, r, w):
        P.add("dve", lambda e: e.scalar_tensor_tensor(out=out, in0=in0, scalar=scalar, in1=in1, op0=op0, op1=op1), reads=r, writes=w)

    def tsc(eng, out, in0, s1, s2, op0, op1, r, w):
        P.add(eng, lambda e: e.tensor_scalar(out, in0, s1, s2, op0, op1), reads=r, writes=w)

    def tsmul(eng, out, in0, s, r, w):
        P.add(eng, lambda e: e.tensor_scalar_mul(out, in0, s), reads=r, writes=w)

    def recip(out, in_, r, w):
        P.add("dve", lambda e: e.reciprocal(out, in_), reads=r, writes=w)

    def load_gains(which):
        dma("pool", rows[:], rows_d[:, which, :], [], ["rows"], "ld_rows")
        for c in range(8):
            col = (V_G1 if which == 0 else V_G2) + c
            actf(gB[:, c, :], onesf[:], AF.Copy, ["vecs", "onesf"], ["gB"], scale=vecs[:, col:col + 1])

    def setup():
        stg = xt[0]
        dma("pool", vecs[:], vecs_d, [], ["vecs"], "ld_vecs")
        dma("pool", flags[:], flags_d, [], ["flags"], "ld_flags")
        dma("pool", stg[:, 0:1024], cst_d[:, 0:1024], [], ["xt0"], "xin0")
        dve_copy(ident_b[:], stg[:, C_ID:C_ID + 128], ["xt0"], ["ident_b"])
        dve_copy(bones_f[:], stg[:, C_BONES:C_BONES + 128], ["xt0"], ["bones_f"])
        dve_copy(bones_b[:], stg[:, C_BONES:C_BONES + 128], ["xt0"], ["bones_b"])
        dve_copy(M1[:].rearrange("p a b -> p (a b)"), stg[:, C_M1:C_M1 + 256], ["xt0"], ["M1"])
        dve_copy(M2[:].rearrange("p a b -> p (a b)"), stg[:, C_M2:C_M2 + 128], ["xt0"], ["M2"])
        dve_copy(TRI[:, 0, :], stg[:, C_TRI:C_TRI + 384], ["xt0"], ["TRI"])
        dma("pool", xt[1][:, 0:448], cst_d[:, 1024:1472], [], ["xt1"], "xin1")
        dve_copy(TRI[:, 1, :], xt[1][:, 0:384], ["xt1"], ["TRI"])
        dve_copy(I2[:], xt[1][:, 384:448], ["xt1"], ["I2"])
        for k in range(3):
            t = xt[(k + 2) % 3]
            key = "xt%d" % ((k + 2) % 3)
            dma("pool", t[:], cst_d[:, C_PEN + k * 1024:C_PEN + (k + 1) * 1024], [], [key], "xin%d" % ((k + 2) % 3))
            dve_copy(PEN[:, 2 * k:2 * k + 2, :].rearrange("p a b -> p (a b)"), t[:], [key], ["PEN"])
        P.add("dve", lambda e: e.memset(ones_b[:], 1.0), writes=["ones_b"])
        P.add("dve", lambda e: e.memset(onesf[:], 1.0), writes=["onesf"])
        P.add("dve", lambda e: e.memset(epsc[:], GN_EPS), writes=["epsc"])
        dma("pool", xt[0][:, 0:1024], lora_d[:, 0:2, :].rearrange("p a b -> p (a b)"), [], ["xt0"], "xin0")
        dve_copy(w2b[:], xt[0][:, 0:512], ["xt0"], ["w2b"])
        dve_copy(a2b[:], xt[0][:, 512:1024], ["xt0"], ["a2b"])
        dma("pool", xt[1][:, 0:1024], lora_d[:, 2:4, :].rearrange("p a b -> p (a b)"), [], ["xt1"], "xin1")
        dve_copy(g2b[:].rearrange("p a b -> p (a b)"), xt[1][:, 0:1024], ["xt1"], ["g2b"])
        tt("dve", c0[:], vecs[:, V_MUP:V_MUP + 16], vecs[:, V_MUN:V_MUN + 16], ALU.add, ["vecs"], ["c0"])
        tsc("dve", c0[:], c0[:], -1.0, 1.0, ALU.mult, ALU.add, ["c0"], ["c0"])
        tsc("dve", omka[:], vecs[:, V_KA:V_KA + 4], -1.0, 1.0, ALU.mult, ALU.add, ["vecs"], ["omka"])
        w0f = xt[2][0:1, 0:1024].rearrange("p (a b) -> p a b", a=2)
        w0t = xt[0][0:1, 0:1024].rearrange("p (a b) -> p a b", a=2)
        sinkf = xt[1][0:1, 0:1024].rearrange("p (a b) -> p a b", a=2)
        lo_b = ub[0][0:1, 0:1024].rearrange("p (a b) -> p a b", a=2)
        dma("pool", w0f, w0_d, [], ["xt2"], "xin2")
        dma("pool", sinkf, sink_d, [], ["xt1"], "xin1")
        dve_copy(rb[0:1, 0:2, :], w0f, ["xt2"], ["rb"])
        dve_copy(w0t, rb[0:1, 0:2, :], ["rb"], ["xt0"])
        tt("dve", w0t, w0f, w0t, ALU.subtract, ["xt2", "xt0"], ["xt0"])
        dve_copy(lo_b, w0t, ["xt0"], ["ub0"])
        dma("pool", rb[1:2, 0:2, :], lo_b, ["ub0"], ["rb"], "ubst0")
        actf(rb[0:1, 2:4, :], sinkf, AF.Exp, ["xt1"], ["rb"])
        P.add("dve", lambda e: e.memset(xt[2][0:1, :], 0.0), reads=["xt2"], writes=["xt2"])
        dma("pool", hbuf[0:1, :], xt[2][0:1, :], ["xt2"], ["hb_first"], "xst2")
        dma("pool", hbuf[nslot * SEG + 1:nslot * SEG + 2, :], xt[2][0:1, :], ["xt2"], ["hb_last"], "xst2")

    def prepass():
        jobs = []
        for (src, dst, R, Cc) in ((w_in_d, w_in_b, 1024, 4864), (wba_d, wba_b, 512, 1024), (wbr_d, wbr_b, 512, 1024),
                                  (wout_d, wout_b, 1024, 1024), (wup_d, wup_b, 1024, 5632), (wdn_d, wdn_b, 2816, 1024)):
            for rc in range(R // 128):
                for c0_ in range(0, Cc, 1024):
                    w = min(1024, Cc - c0_)
                    jobs.append((src[rc * 128:(rc + 1) * 128, c0_:c0_ + w], dst[rc * 128:(rc + 1) * 128, c0_:c0_ + w], w))
        for i, (s_ap, d_ap, w) in enumerate(jobs):
            a = i % 3
            b = i % 2
            dma("sp", xt[a][:, 0:w], s_ap, [], ["xt%d" % a], "xin%d" % a)
            if i % 2 == 0:
                dve_copy(ub[b][:, 0:w], xt[a][:, 0:w], ["xt%d" % a], ["ub%d" % b])
            else:
                act_copy(ub[b][:, 0:w], xt[a][:, 0:w], ["xt%d" % a], ["ub%d" % b])
            dma("pool", d_ap, ub[b][:, 0:w], ["ub%d" % b], ["wscr"], "ubst%d" % b)

    wq = weight_schedule(nslot, passes)
    wstate = {"issued": 0, "taken": 0, "released": 0}
    wfm = lambda t: t.rearrange("(c p) n -> p c n", p=128)

    def wsrc(tag):
        if tag == "QG":
            return wfm(w_in_b)[:, :, 0:512], 8, 512
        if tag == "KVG":
            return wfm(w_in_b)[:, :, 512:768], 8, 256
        if tag.startswith("ZG"):
            g = int(tag[2])
            return wfm(w_in_b)[:, :, 768 + g * 512:768 + (g + 1) * 512], 8, 512
        if tag.startswith("GA"):
            h = int(tag[2])
            return wfm(w_in_b)[:, :, 2816 + h * 512:2816 + (h + 1) * 512], 8, 512
        if tag.startswith("GR"):
            h = int(tag[2])
            return wfm(w_in_b)[:, :, 3840 + h * 512:3840 + (h + 1) * 512], 8, 512
        if tag.startswith("WO"):
            h = int(tag[2])
            return wfm(wout_b)[:, :, h * 512:(h + 1) * 512], 8, 512
        if tag.startswith("UG"):
            g = int(tag[2])
            n = 512 if g < 5 else 256
            return wfm(wup_b)[:, :, g * 512:g * 512 + n], 8, n
        if tag.startswith("UU"):
            g = int(tag[2])
            n = 512 if g < 5 else 256
            return wfm(wup_b)[:, :, 2816 + g * 512:2816 + g * 512 + n], 8, n
        if tag.startswith("DN"):
            g = int(tag[2])
            kc = 4 if g < 5 else 2
            return wfm(wdn_b)[:, g * 4:g * 4 + kc, :], kc, 1024
        raise KeyError(tag)

    def w_issue():
        i = wstate["issued"]
        tag, _ = wq[i]
        b = i % NW
        if tag.startswith("BAR"):
            h = int(tag[3])
            v = wbuf[b][:, :].rearrange("p (a c n) -> p a c n", a=2, c=4)
            dma("sp", v[:, 0, :, :], wfm(wba_b)[:, :, h * 512:(h + 1) * 512], ["wscr"], ["wbuf%d" % b], "w%d" % b)
            dma("sp", v[:, 1, :, :], wfm(wbr_b)[:, :, h * 512:(h + 1) * 512], ["wscr"], ["wbuf%d" % b], "w%d" % b)
        else:
            src, kc, n = wsrc(tag)
            v = wbuf[b][:, 0:kc * n].rearrange("p (c n) -> p c n", c=kc)
            dma("sp", v, src, ["wscr"], ["wbuf%d" % b], "w%d" % b)
        wstate["issued"] += 1

    def w_pump():
        while wstate["issued"] < len(wq) and wstate["issued"] - wstate["released"] < NW:
            w_issue()

    def w_take(tag, slot):
        i = wstate["taken"]
        assert wq[i] == (tag, slot), (wq[i], tag, slot)
        if wstate["issued"] <= i:
            assert wstate["issued"] - wstate["released"] < NW, "too many weight groups held"
            w_issue()
        wstate["taken"] += 1
        b = i % NW
        return wbuf[b], "wbuf%d" % b

    def w_done(k=1):
        wstate["released"] += k
        assert wstate["released"] <= wstate["taken"]
        w_pump()

    def norm_block(src_ap, nrow, which, dst, dst_key, col0, src_reads=()):
        i = cnt["x"]
        cnt["x"] += 1
        a, b, q = i % 3, i % 2, i % 4
        xk, uk = "xt%d" % a, "ub%d" % b
        dma("pool", xt[a][0:nrow, :], src_ap, list(src_reads), [xk], "xin%d" % a)
        P.add("act", lambda e: e.activation(out=junk[0:nrow, :], in_=xt[a][0:nrow, :], func=AF.Square, accum_out=st_ssq[q][0:nrow, 0:1]),
              reads=[xk], writes=["junk", "ssq%d" % q])
        P.add("dve", lambda e: e.tensor_scalar(st_rstd[q][0:nrow, :], st_ssq[q][0:nrow, 0:1], 1.0 / D, NORM_EPS, ALU.mult, ALU.add),
              reads=["ssq%d" % q], writes=["rstd%d" % q])
        P.add("act", lambda e: e.activation(out=st_rstd[q][0:nrow, :], in_=st_rstd[q][0:nrow, :], func=AF.Sqrt),
              reads=["rstd%d" % q], writes=["rstd%d" % q])
        P.add("dve", lambda e: e.reciprocal(st_rstd[q][0:nrow, :], st_rstd[q][0:nrow, :]), reads=["rstd%d" % q], writes=["rstd%d" % q])
        P.add("dve", lambda e: e.tensor_scalar_mul(ub[b][0:nrow, :], xt[a][0:nrow, :], st_rstd[q][0:nrow, :]),
              reads=[xk, "rstd%d" % q], writes=[uk])
        for c in range(8):
            P.add("pe", lambda e, c=c: e.transpose(pbT[:, c * 128:c * 128 + nrow], ub[b][0:nrow, c * 128:(c + 1) * 128], ident_b[0:nrow, 0:nrow]),
                  reads=[uk, "ident_b"], writes=bk(7))
        P.add("dve", lambda e: e.tensor_tensor(out=dst[:, :, col0:col0 + nrow],
                                               in0=pbT[:, :].rearrange("p (c n) -> p c n", c=8)[:, :, 0:nrow],
                                               in1=gB[:, :, 0:nrow], op=ALU.mult),
              reads=bk(7) + ["gB"], writes=[dst_key])

    ARENA_BYTES = 73 * 1024
    arena = sb("arena", [128, ARENA_BYTES // 4], F32)
    carve_state = {}

    def carve(group, name, shape, dt):
        off = carve_state.get(group, 0)
        nel = 1
        for d_ in shape[1:]:
            nel *= d_
        nbytes = nel * (4 if dt == F32 else 2)
        nbytes_al = (nbytes + 31) // 32 * 32
        assert off + nbytes_al <= ARENA_BYTES, (group, name, off, nbytes_al)
        carve_state[group] = off + nbytes_al
        v = arena[0:shape[0], off // 4:(off + nbytes) // 4]
        if dt != F32:
            v = v.bitcast(dt)
        if len(shape) == 3:
            v = v.rearrange("p (a b) -> p a b", a=shape[1])
        elif len(shape) == 4:
            v = v.rearrange("p (a b c) -> p a b c", a=shape[1], b=shape[2])
        elif len(shape) == 5:
            v = v.rearrange("p (a b c d) -> p a b c d", a=shape[1], b=shape[2], c=shape[3])
        groups.setdefault(group, []).append(name)
        return v

    groups = {}
    tmp4 = [sb("tmp%d" % i, [128, 512], F32) for i in range(4)]
    carve_state["att"] = 40 * 1024
    qT = carve("att", "qT", [128, 4, SEG], BF16)
    kT = carve("att", "kT", [128, NTH], BF16)
    vtok = carve("att", "vtok", [128, NBLK, 128], BF16)
    ptb = [carve("att", "ptb%d" % i, [128, 512], BF16) for i in range(3)]
    rden = carve("att", "rden", [128, 512], F32)
    fones = sb("fones", [128, 2, 64], BF16)
    oattn = sb("oattn", [128, 4, SEG], BF16)
    zs = sb("zs", [128, 16, SEG], BF16)
    ztmp = [tmp4[0], tmp4[1]]
    ztmp2 = [tmp4[2], tmp4[3]]
    ps_rot = {"d": 0}

    def nbank():
        b = ps_rot["d"] % 2
        ps_rot["d"] += 1
        return b

    def slot_load_norm(S):
        for b in range(NBLK):
            norm_block(xh[S, b * 128:(b + 1) * 128, :], 128, 0, uT, "uT", b * 128)

    def proj_fm_tile(wb, wkey, kc, n, col, src, skey, tok_lo, ntok, bank):
        wv = wb[:, 0:kc * n].rearrange("p (c n) -> p c n", c=kc)
        for c in range(kc):
            mm(PB(bank, 0, ntok), wv[:, c, col:col + 128], src[:, c, tok_lo:tok_lo + ntok], c == 0, c == kc - 1,
               [wkey, skey], bk(bank))

    def qkv_project(S):
        wb, wkey = w_take("QG", S)
        for i in range(4):
            bank = nbank()
            proj_fm_tile(wb, wkey, 8, 512, i * 128, uT, "uT", HALO, SEG, bank)
            act_copy(qT[:, i, :], PB(bank), bk(bank), ["qT"])
        w_done()
        wb, wkey = w_take("KVG", S)
        for (lo, n) in ((0, 512), (512, 256)):
            bank = nbank()
            proj_fm_tile(wb, wkey, 8, 256, 0, uT, "uT", lo, n, bank)
            act_copy(kT[:, lo:lo + n], PB(bank, 0, n), bk(bank), ["kT"])
        wv = wb[:, 0:8 * 256].rearrange("p (c n) -> p c n", c=8)
        for half in range(2):
            bank = nbank()
            for bb in range(3):
                b = half * 3 + bb
                for c in range(8):
                    mm(PB(bank, bb * 128, (bb + 1) * 128), uT[:, c, b * 128:(b + 1) * 128], wv[:, c, 128:256], c == 0, c == 7,
                       [wkey, "uT"], bk(bank))
            dve_copy(vtok[:, half * 3:half * 3 + 3, :].rearrange("p a b -> p (a b)"), PB(bank, 0, 384), bk(bank), ["vtok"])
        w_done()

    def attention(S):
        fp = flags[:, 2 * S:2 * S + 1]
        tsmul("dve", fones[:, 0, :], ones_b[:, 0:64], fp, ["ones_b", "flags"], ["fones"])
        tsmul("dve", vtok[:, 1, :], vtok[:, 1, :], fp, ["vtok", "flags"], ["vtok"])
        nqb = SEG // 128
        for qb in range(nqb):
            nb, db = 3 + (qb % 2), 5 + (qb % 2)
            for g in range(2):
                pr = slice(g * 64, (g + 1) * 64)
                for j in range(3):
                    kb = qb + j
                    sbk = (qb * 6 + g * 3 + j) % 3
                    pk = "ptb%d" % sbk
                    mm(PB(sbk), kT[pr, kb * 128:(kb + 1) * 128], qT[pr, :, qb * 128:(qb + 1) * 128], True, False,
                       ["kT", "qT"], bk(sbk), tp=(g * 64, 0))
                    mm(PB(sbk), ident_b[:], PEN[:, g * 3 + j, :], False, True, ["ident_b", "PEN"], bk(sbk))
                    actf(ptb[sbk][:], PB(sbk), AF.Exp, bk(sbk), [pk], scale=0.125)
                    mm(PB(nb)[pr, :], vtok[:, kb, pr], ptb[sbk][:], j == 0, j == 2, ["vtok", pk], bk(nb), tp=(0, g * 64))
                    if kb <= 1:
                        dl, dk = fones[:, 0, :], "fones"
                    else:
                        dl, dk = ones_b[:, 0:64], "ones_b"
                    mm(PB(db)[pr, :], dl, ptb[sbk][:], j == 0, False, [dk, pk], bk(db), tp=(0, g * 64))
                mm(PB(db)[pr, :], ones_b[0:1, 0:64], rb[0:1, 2 + g, :], False, True, ["ones_b", "rb"], bk(db), tp=(0, g * 64))
            recip(rden[:], PB(db), bk(db), ["rden"])
            tt("dve", oattn[:, :, qb * 128:(qb + 1) * 128], PB(nb).rearrange("p (a b) -> p a b", a=4),
               rden[:].rearrange("p (a b) -> p a b", a=4), ALU.mult, bk(nb) + ["rden"], ["oattn"])

    ZSUB = ((0, 510), (510, 2))

    def z_project(S, tiles):
        tasks = [(zt, o, n) for zt in tiles for (o, n) in ZSUB]
        zb = [tmp4[0], tmp4[1], tmp4[2]]
        zk = ["ztmp0", "ztmp1", "ztmq0"]
        cur = [None]

        def stage_mm(i):
            zt, o, n = tasks[i]
            g = zt // 4
            if cur[0] is None or cur[0][0] != g:
                if cur[0] is not None:
                    w_done()
                wb, wkey = w_take("ZG%d" % g, S)
                cur[0] = (g, wb, wkey)
            _, wb, wkey = cur[0]
            bank = i % 2
            proj_fm_tile(wb, wkey, 8, 512, (zt % 4) * 128, uT, "uT", HALO + o - 1, n + 2, bank)
            actf(zb[i % 3][:, 0:n], PB(bank, 1, n + 1), AF.Copy, bk(bank) + ["c0"], [zk[i % 3]], scale=c0[:, zt:zt + 1])

        def stage_taps(i):
            zt, o, n = tasks[i]
            bank = i % 2
            stt(tmp4[3][:, 0:n], PB(bank, 0, n), vecs[:, V_MUP + zt:V_MUP + zt + 1], zb[i % 3][:, 0:n], ALU.mult, ALU.add,
                bk(bank) + ["vecs", zk[i % 3]], ["ztmq1"])
            stt(zs[:, zt, o:o + n], PB(bank, 2, n + 2), vecs[:, V_MUN + zt:V_MUN + zt + 1], tmp4[3][:, 0:n], ALU.mult, ALU.add,
                bk(bank) + ["vecs", "ztmq1"], ["zs%d" % zt])

        nt_ = len(tasks)
        for i in range(nt_ + 1):
            if i < nt_:
                stage_mm(i)
            if i >= 1:
                stage_taps(i - 1)
        w_done()

    def rw(name, shape, dt):
        return carve("rw", name, shape, dt)

    twd = rw("twd", [128, GT], BF16)
    sigtok = [rw("sigtok%d" % i, [128, 512], F32) for i in range(2)]
    E = [rw("E%d" % i, [128, 4, GT], F32) for i in range(2)]
    kq = rw("kq", [128, GT], F32)
    ksq = rw("ksq", [128, GT], F32)
    nrm = rw("nrm", [128, GT], F32)
    kk = rw("kk", [128, GT], F32)
    asig = [rw("asig%d" % i, [128, GT], F32) for i in range(2)]
    kdir = [rw("kdir%d" % i, [128, GT], F32) for i in range(2)]
    t1 = rw("t1", [128, GT], F32)
    bbv = rw("bbv", [128, GT], F32)
    bbar = rw("bbar", [128, GT], BF16)
    kbar = rw("kbar", [128, GT], BF16)
    AR = [rw("AR%d" % i, [128, 4, GRP, 2, C], BF16) for i in range(2)]
    BT = [rw("BT%d" % i, [128, 4, GT], BF16) for i in range(2)]
    KTl = [rw("KTl%d" % i, [128, 4, GT], BF16) for i in range(2)]
    bbt = [rw("bbt%d" % i, [128, 4, GRP, C], BF16) for i in range(2)]
    kbt = [rw("kbt%d" % i, [128, 4, GRP, C], BF16) for i in range(2)]
    vt = [rw("vt%d" % i, [128, 4, GRP, C], BF16) for i in range(2)]
    GC = [rw("GC%d" % i, [128, 4, GRP], F32) for i in range(2)]
    GB1 = rw("GB1", [128, 16, 128], BF16)
    GB2 = rw("GB2", [128, 16, 128], BF16)
    Pk = [rw("Pk%d" % i, [128, 16, 2, C], BF16) for i in range(2)]
    Zk = [rw("Zk%d" % i, [128, 16, C], BF16) for i in range(2)]
    Wb = rw("Wb", [128, 4, C], BF16)
    Ub = rw("Ub", [128, 4, C], BF16)
    Tf = sb("Tf", [128, 4, C], F32)
    Tst = sb("Tst", [128, 4, C], BF16)
    yT = sb("yT", [128, 4, SEG], F32)
    bp = sb("bp", [128, 4, SEG], BF16)

    def flat(ap3):
        return ap3.rearrange("p a b -> p (a b)")

    def rwkv_state_init():
        P.add("dve", lambda e: e.memset(flat(Tf[:]), 0.0), writes=["Tf"])

    def rwkv_slot_begin(S, d):
        col = 2 * S + (0 if d == 0 else 1)
        tsmul("dve", flat(Tf[:]), flat(Tf[:]), flags[:, col:col + 1], ["Tf", "flags"], ["Tf"])
        act_copy(flat(Tst[:]), flat(Tf[:]), ["Tf"], ["Tst"])

    def rwkv_group(S, d, gi, passB, parts=("pre", "minv", "scan")):
        gb = gi % 2
        t0 = gi * GT
        dr = slice(d * 64, (d + 1) * 64)
        ARk, BTk, KTk, bbtk, kbtk, vtk, GCk = ("AR%d" % gb, "BT%d" % gb, "KTl%d" % gb, "bbt%d" % gb, "kbt%d" % gb,
                                               "vt%d" % gb, "GC%d" % gb)
        Zf, Zfk = Zk[1], "Zk1"
        if "pre" in parts:
            actf(twd[dr, :], zs[dr, 12, t0:t0 + GT], AF.Tanh, ["zs12"], ["twd"])
            for blk in range(2):
                mm(PB(blk), twd[dr, blk * 128:(blk + 1) * 128], w2b[dr, :], True, False, ["twd", "w2b"], bk(blk), tp=(d * 64, 0))
                mm(PB(blk), ones_b[0:2, 0:128], rb[0:2, d, :], False, True, ["ones_b", "rb"], bk(blk))
                actf(sigtok[blk][:], PB(blk), AF.Sigmoid, bk(blk), ["sigtok%d" % blk])
            for ct in range(4):
                eb = ct % 2
                Et, Ek = E[eb], "E%d" % eb
                for blk in range(2):
                    bank = 2 + blk
                    mm(PB(bank, 0, 384), sigtok[blk][:, ct * 128:(ct + 1) * 128], TRI[:, d, :], True, True,
                       ["sigtok%d" % blk, "TRI"], bk(bank))
                    actf(Et[:, 0:3, blk * 128:(blk + 1) * 128], PB(bank, 0, 384).rearrange("p (a b) -> p a b", a=3), AF.Exp,
                         bk(bank), [Ek], scale=-KAPPA)
                    actf(Et[:, 3, blk * 128:(blk + 1) * 128], PB(bank, 0, 128), AF.Exp, bk(bank), [Ek], scale=KAPPA)
                kz, kzk = zs[:, 4 + ct, t0:t0 + GT], "zs%d" % (4 + ct)
                tsmul("dve", kq[:], kz, vecs[:, V_KK + ct:V_KK + ct + 1], [kzk, "vecs"], ["kq"])
                actf(ksq[:], kq[:], AF.Square, ["kq"], ["ksq"])
                mm(PB(4, 0, 256), bones_f[:], ksq[:], True, True, ["bones_f", "ksq"], bk(4))
                actf(nrm[:], PB(4, 0, 256), AF.Sqrt, bk(4), ["nrm"])
                P.add("dve", lambda e: e.tensor_scalar_max(nrm[:], nrm[:], 1e-12), reads=["nrm"], writes=["nrm"])
                recip(nrm[:], nrm[:], ["nrm"], ["nrm"])
                tt("dve", kk[:], kq[:], nrm[:], ALU.mult, ["kq", "nrm"], ["kk"])
                dirs = (0, 1) if passB else (d,)
                for dd in dirs:
                    ddr = slice(dd * 64, (dd + 1) * 64)
                    psa = PB(4, 256, 512)
                    mm(psa, a2b[ddr, ct * 128:(ct + 1) * 128], zs[ddr, 13, t0:t0 + GT], True, True, ["a2b", "zs13"], bk(4), tp=(dd * 64, 0))
                    actf(asig[dd][:], psa, AF.Sigmoid, bk(4) + ["vecs"], ["asig%d" % dd],
                         bias=vecs[:, V_A0 + dd * 4 + ct:V_A0 + dd * 4 + ct + 1])
                    tsc("dve", t1[:], asig[dd][:], vecs[:, V_KA + ct:V_KA + ct + 1], omka[:, ct:ct + 1], ALU.mult, ALU.add,
                        ["asig%d" % dd, "vecs", "omka"], ["t1"])
                    tt("dve", kdir[dd][:], kz, t1[:], ALU.mult, [kzk, "t1"], ["kdir%d" % dd])
                if passB:
                    tt("pool", t1[:], kdir[0][:], kdir[1][:], ALU.add, ["kdir0", "kdir1"], ["t1"])
                    stt(bp[:, ct, t0:t0 + GT], zs[:, ct, t0:t0 + GT], vecs[:, V_RK + ct:V_RK + ct + 1], t1[:], ALU.mult, ALU.mult,
                        ["zs%d" % ct, "vecs", "t1"], ["bp"])
                tt("dve", bbv[:], kk[:], asig[d][:], ALU.mult, ["kk", "asig%d" % d], ["bbv"])
                v4 = lambda ap: ap.rearrange("p (c t) -> p c t", c=GRP)
                tt("dve", AR[gb][:, ct, :, 1, :], v4(zs[:, ct, t0:t0 + GT]), v4(Et[:, 0, :]), ALU.mult, ["zs%d" % ct, Ek], [ARk])
                stt(AR[gb][:, ct, :, 0, :], v4(kk[:]), -1.0, v4(Et[:, 1, :]), ALU.mult, ALU.mult, ["kk", Ek], [ARk])
                tt("pool", BT[gb][:, ct, :], bbv[:], Et[:, 3, :], ALU.mult, ["bbv", Ek], [BTk])
                tt("pool", KTl[gb][:, ct, :], kdir[d][:], Et[:, 3, :], ALU.mult, ["kdir%d" % d, Ek], [KTk])
                tt("pool", bbar[:], bbv[:], Et[:, 2, :], ALU.mult, ["bbv", Ek], ["bbar"])
                tt("pool", kbar[:], kdir[d][:], Et[:, 2, :], ALU.mult, ["kdir%d" % d, Ek], ["kbar"])
                tend = (C - 1) if d == 0 else 0
                dve_copy(GC[gb][:, ct, :], v4(Et[:, 0, :])[:, :, tend], [Ek], [GCk])
                vz, vzk = zs[:, 8 + ct, t0:t0 + GT], "zs%d" % (8 + ct)
                for e_ in range(2):
                    er = slice(e_ * 64, (e_ + 1) * 64)
                    tp = (e_ * 64, e_ * 64)
                    for c in range(GRP):
                        cs = slice(c * C, (c + 1) * C)
                        mm(PB(0)[er, c * C:(c + 1) * C], bbar[er, cs], ident_b[er, er], True, True, ["bbar", "ident_b"], bk(0), tp=tp)
                        mm(PB(0)[er, 256 + c * C:256 + (c + 1) * C], kbar[er, cs], ident_b[er, er], True, True, ["kbar", "ident_b"], bk(0), tp=tp)
                        mm(PB(1)[er, c * C:(c + 1) * C], vz[er, cs], ident_b[er, er], True, True, [vzk, "ident_b"], bk(1), tp=tp)
                act_copy(flat(bbt[gb][:, ct, :, :]), PB(0, 0, 256), bk(0), [bbtk])
                act_copy(flat(kbt[gb][:, ct, :, :]), PB(0, 256, 512), bk(0), [kbtk])
                dve_copy(flat(vt[gb][:, ct, :, :]), PB(1, 0, 256), bk(1), [vtk])
        if "minv" in parts:
            m1 = M1[:, d, :].unsqueeze(1).to_broadcast([128, 4, 128])
            m2 = M2[:, d, :].unsqueeze(1).to_broadcast([128, 4, C])
            for c in range(GRP):
                b1, b2 = c % 2, 2 + c % 2
                b3 = 4 + c % 2
                cs = slice(c * C, (c + 1) * C)
                for hp in range(4):
                    for e_ in range(2):
                        er = slice(e_ * 64, (e_ + 1) * 64)
                        tp = (e_ * 64, e_ * 64)
                        arv = AR[gb][er, hp, c, :, :]
                        mm(PB(b1)[er, hp * 128:(hp + 1) * 128], BT[gb][er, hp, cs], arv, True, True, [BTk, ARk], bk(b1), tp=tp)
                        mm(PB(b2)[er, hp * 128:(hp + 1) * 128], KTl[gb][er, hp, cs], arv, True, True, [KTk, ARk], bk(b2), tp=tp)
                        mm(PB(b3)[er, hp * C:(hp + 1) * C], AR[gb][er, hp, c, 0, :], BT[gb][er, hp, cs], True, True,
                           [ARk, BTk], bk(b3), tp=tp)
                tt("dve", GB1[:, c * 4:(c + 1) * 4, :], PB(b1).rearrange("p (a b) -> p a b", a=4), m1, ALU.mult, bk(b1) + ["M1"], ["GB1"])
                tt("dve", GB2[:, c * 4:(c + 1) * 4, :], PB(b2).rearrange("p (a b) -> p a b", a=4), m1, ALU.mult, bk(b2) + ["M1"], ["GB2"])
                tt("dve", Pk[0][:, c * 4:(c + 1) * 4, 0, :], PB(b3, 0, 256).rearrange("p (a b) -> p a b", a=4), m2, ALU.mult,
                   bk(b3) + ["M2"], ["Pk0"])
            tt("dve", Zk[0][:], GB1[:, :, 0:C], I2[:].unsqueeze(1).to_broadcast([128, 16, C]), ALU.add, ["GB1", "I2"], ["Zk0"])
            psP = psum[:, 0:2048]
            psZ = psum[:, 2048:3072]
            for k in range(5):
                cur, nxt = k % 2, (k + 1) % 2
                for slot in range(16):
                    for e_ in range(2):
                        er = slice(e_ * 64, (e_ + 1) * 64)
                        tp = (e_ * 64, e_ * 64)
                        Pv = Pk[cur][er, slot, 0, :]
                        if k == 0:
                            PTv, ptk = GB1[er, slot, 0:C], "GB1"
                        else:
                            PTv, ptk = Pk[cur][er, slot, 1, :], "Pk%d" % cur
                        mm(psP[er, slot * 128:slot * 128 + C], PTv, Pv, True, True, [ptk, "Pk%d" % cur], bk(0, 1, 2, 3), tp=tp)
                        if k < 4:
                            mm(psP[er, slot * 128 + C:slot * 128 + 2 * C], Pv, PTv, True, True, [ptk, "Pk%d" % cur], bk(0, 1, 2, 3), tp=tp)
                if k < 4:
                    act_copy(Pk[nxt][:].rearrange("p s o n -> p (s o n)"), psP, bk(0, 1, 2, 3), ["Pk%d" % nxt])
                else:
                    act_copy(Pk[nxt][:, :, 0, :], psP.rearrange("p (s o n) -> p s o n", s=16, o=2)[:, :, 0, :], bk(0, 1, 2, 3), ["Pk%d" % nxt])
                for slot in range(16):
                    for e_ in range(2):
                        er = slice(e_ * 64, (e_ + 1) * 64)
                        tp = (e_ * 64, e_ * 64)
                        mm(psZ[er, slot * C:(slot + 1) * C], Pk[nxt][er, slot, 0, :], Zk[cur][er, slot, :], True, True,
                           ["Pk%d" % nxt, "Zk%d" % cur], bk(4, 5), tp=tp)
                tt("dve", flat(Zk[nxt][:]), psZ, flat(Zk[cur][:]), ALU.add, bk(4, 5) + ["Zk%d" % cur], ["Zk%d" % nxt])
        if "scan" in parts:
            order = range(GRP) if d == 0 else range(GRP - 1, -1, -1)
            for c in order:
                tok = t0 + c * C
                for hp in range(4):
                    for e_ in range(2):
                        er = slice(e_ * 64, (e_ + 1) * 64)
                        tp = (e_ * 64, e_ * 64)
                        slot = c * 4 + hp
                        o = PB(5)[er, hp * C:(hp + 1) * C]
                        mm(o, AR[gb][er, hp, c, 0, :], Tst[er, hp, :], True, False, [ARk, "Tst"], bk(5), tp=tp)
                        mm(o, GB2[er, slot, 0:C], vt[gb][er, hp, c, :], False, True, ["GB2", vtk], bk(5), tp=tp)
                act_copy(flat(Wb[:]), PB(5, 0, 256), bk(5), ["Wb"])
                for hp in range(4):
                    for e_ in range(2):
                        er = slice(e_ * 64, (e_ + 1) * 64)
                        tp = (e_ * 64, e_ * 64)
                        slot = c * 4 + hp
                        mm(PB(5)[er, 256 + hp * C:256 + (hp + 1) * C], Zf[er, slot, :], Wb[er, hp, :], True, True, [Zfk, "Wb"], bk(5), tp=tp)
                dve_copy(flat(Ub[:]), PB(5, 256, 512), bk(5), ["Ub"])
                for hp in range(4):
                    for e_ in range(2):
                        er = slice(e_ * 64, (e_ + 1) * 64)
                        tp = (e_ * 64, e_ * 64)
                        slot = c * 4 + hp
                        oy = PB(6)[er, hp * C:(hp + 1) * C]
                        mm(oy, Tst[er, hp, :], AR[gb][er, hp, c, 1, :], True, False, ["Tst", ARk], bk(6), tp=tp)
                        mm(oy, Ub[er, hp, :], GB1[er, slot, C:2 * C], False, False, ["Ub", "GB1"], bk(6), tp=tp)
                        mm(oy, vt[gb][er, hp, c, :], GB2[er, slot, C:2 * C], False, True, [vtk, "GB2"], bk(6), tp=tp)
                        ot = PB(6)[er, 256 + hp * C:256 + (hp + 1) * C]
                        mm(ot, bbt[gb][er, hp, c, :], Ub[er, hp, :], True, False, [bbtk, "Ub"], bk(6), tp=tp)
                        mm(ot, kbt[gb][er, hp, c, :], vt[gb][er, hp, c, :], False, True, [kbtk, vtk], bk(6), tp=tp)
                py = PB(6, 0, 256).rearrange("p (a b) -> p a b", a=4)
                if passB:
                    tt("dve", yT[:, :, tok:tok + C], py, yT[:, :, tok:tok + C], ALU.add, bk(6) + ["yT"], ["yT"])
                else:
                    dve_copy(yT[:, :, tok:tok + C], py, bk(6), ["yT"])
                tt("dve", Tf[:], Tf[:], GC[gb][:, :, c:c + 1].to_broadcast([128, 4, C]), ALU.mult, ["Tf", GCk], ["Tf"])
                tt("dve", flat(Tf[:]), flat(Tf[:]), PB(6, 256, 512), ALU.add, bk(6) + ["Tf"], ["Tf"])
                act_copy(flat(Tst[:]), flat(Tf[:]), ["Tf"], ["Tst"])

    yc, ysq, sd, bon = tmp4
    sg = sb("sg", [128, 2, SEG], BF16)
    orw = sb("orw", [128, 4, SEG], BF16)
    epsc = sb("epsc", [128, 1], F32)

    def rwkv_epilogue(S):
        actf(sg[:, 0, :], zs[:, 14, :], AF.Sigmoid, ["zs14"], ["sg"])
        actf(sg[0:32, 1, :], zs[0:32, 15, :], AF.Sigmoid, ["zs15"], ["sg"])
        for ct in range(4):
            b0 = nbank()
            mm(PB(b0), bones_f[:], yT[:, ct, :], True, True, ["bones_f", "yT"], bk(b0))
            stt(yc[:], PB(b0), -1.0 / C, yT[:, ct, :], ALU.mult, ALU.add, bk(b0) + ["yT"], ["yc"])
            actf(ysq[:], yc[:], AF.Square, ["yc"], ["ysq"])
            b1 = nbank()
            mm(PB(b1), bones_f[:], ysq[:], True, True, ["bones_f", "ysq"], bk(b1))
            actf(sd[:], PB(b1), AF.Sqrt, bk(b1) + ["epsc"], ["sd"], bias=epsc[:, 0:1], scale=1.0 / C)
            recip(sd[:], sd[:], ["sd"], ["sd"])
            tt("dve", yc[:], yc[:], sd[:], ALU.mult, ["yc", "sd"], ["yc"])
            tsc("dve", yc[:], yc[:], vecs[:, V_LNW + ct:V_LNW + ct + 1], vecs[:, V_LNB + ct:V_LNB + ct + 1], ALU.mult, ALU.add,
                ["yc", "vecs"], ["yc"])
            b2 = nbank()
            mm(PB(b2), bones_b[:], bp[:, ct, :], True, True, ["bones_b", "bp"], bk(b2))
            tt("dve", bon[:], PB(b2), zs[:, 8 + ct, :], ALU.mult, bk(b2) + ["zs%d" % (8 + ct)], ["bon"])
            tt("pool", yc[:], yc[:], bon[:], ALU.add, ["yc", "bon"], ["yc"])
            b3 = nbank()
            mm(PB(b3), g2b[:, 0, ct * 128:(ct + 1) * 128], sg[:, 0, :], True, False, ["g2b", "sg"], bk(b3))
            mm(PB(b3), g2b[0:32, 1, ct * 128:(ct + 1) * 128], sg[0:32, 1, :], False, True, ["g2b", "sg"], bk(b3))
            tt("dve", orw[:, ct, :], PB(b3), yc[:], ALU.mult, bk(b3) + ["yc"], ["orw"])

    mergedT = zs
    sga, sgr, tma, tmr = tmp4
    for grp_ in (("ztmp0", "yc", "sga", "hrA"), ("ztmp1", "ysq", "sgr", "hrB"), ("ztmq0", "sd", "tma"), ("ztmq1", "bon", "tmr")):
        for k_ in grp_:
            P.alias[k_] = tuple(x for x in grp_ if x != k_)

    def merge_branches(S):
        for h in range(2):
            wga, kga = w_take("GA%d" % h, S)
            wgr, kgr = w_take("GR%d" % h, S)
            wbr_, kbr = w_take("BAR%d" % h, S)
            wbv = wbr_[:, :].rearrange("p (a c n) -> p a c n", a=2, c=4)
            for mi in range(4):
                m = h * 4 + mi
                proj_fm_tile(wga, kga, 8, 512, mi * 128, uT, "uT", HALO, SEG, 0)
                actf(sga[:], PB(0), AF.Sigmoid, bk(0), ["sga"])
                proj_fm_tile(wgr, kgr, 8, 512, mi * 128, uT, "uT", HALO, SEG, 1)
                actf(sgr[:], PB(1), AF.Sigmoid, bk(1), ["sgr"])
                for c in range(4):
                    mm(PB(2), wbv[:, 0, c, mi * 128:(mi + 1) * 128], oattn[:, c, :], c == 0, c == 3, [kbr, "oattn"], bk(2))
                for c in range(4):
                    mm(PB(3), wbv[:, 1, c, mi * 128:(mi + 1) * 128], orw[:, c, :], c == 0, c == 3, [kbr, "orw"], bk(3))
                tt("dve", tma[:], PB(2), sga[:], ALU.mult, bk(2) + ["sga"], ["tma"])
                tt("dve", tmr[:], PB(3), sgr[:], ALU.mult, bk(3) + ["sgr"], ["tmr"])
                tt("pool", mergedT[:, m, :], tma[:], tmr[:], ALU.add, ["tma", "tmr"], ["zs%d" % m])
            w_done(3)

    def out_proj(S):
        w0_, k0_ = w_take("WO0", S)
        w1_, k1_ = w_take("WO1", S)
        wv = [w0_[:, :].rearrange("p (c n) -> p c n", c=8), w1_[:, :].rearrange("p (c n) -> p c n", c=8)]
        wk = [k0_, k1_]
        for tb in range(SEG // 128):
            i = cnt["st"]
            cnt["st"] += 1
            q = i % 4
            a = cnt["x"] % 3
            cnt["x"] += 1
            xk = "xt%d" % a
            dma("pool", xt[a][:], xh[S, HALO + tb * 128:HALO + (tb + 1) * 128, :], [], [xk], "xin%d" % a)
            for n2 in range(2):
                bank = 4 + n2
                for c in range(8):
                    mm(PB(bank), mergedT[:, c, tb * 128:(tb + 1) * 128], wv[n2][:, c, :], c == 0, c == 7, ["zs%d" % c, wk[n2]], bk(bank))
            post_norm_residual(q, (4, 5), xt[a], xk)
            dma("pool", hbuf[1 + S * SEG + tb * 128:1 + S * SEG + (tb + 1) * 128, :], xt[a][:], [xk], ["hb%d" % S], "xst%d" % a)
        w_done(2)

    def post_norm_residual(q, banks, res, rkey):
        sk, rk_ = "ssq%d" % q, "rstd%d" % q
        for n2 in range(2):
            actf(junk[:, n2 * 512:(n2 + 1) * 512], PB(banks[n2]), AF.Square, bk(banks[n2]), ["junk", sk], accum=st_ssq[q][:, n2:n2 + 1])
        tt("dve", st_rstd[q][:], st_ssq[q][:, 0:1], st_ssq[q][:, 1:2], ALU.add, [sk], [rk_])
        tsc("dve", st_rstd[q][:], st_rstd[q][:], 1.0 / D, NORM_EPS, ALU.mult, ALU.add, [rk_], [rk_])
        actf(st_rstd[q][:], st_rstd[q][:], AF.Sqrt, [rk_], [rk_])
        recip(st_rstd[q][:], st_rstd[q][:], [rk_], [rk_])
        for n2 in range(2):
            hk = "hrA" if n2 == 0 else "hrB"
            stt(tmp4[n2][:], PB(banks[n2]), st_rstd[q][:, 0:1], rows[:, n2 * 512:(n2 + 1) * 512], ALU.mult, ALU.mult,
                bk(banks[n2]) + [rk_, "rows"], [hk])
            tt("pool", res[:, n2 * 512:(n2 + 1) * 512], res[:, n2 * 512:(n2 + 1) * 512], tmp4[n2][:], ALU.add, [hk, rkey], [rkey])

    uT2 = carve("ffn", "uT2", [128, 8, SEG + 2], BF16)
    actT = [carve("ffn", "actT%d" % i, [128, NFT, 256], BF16) for i in range(2)]
    cg = [carve("ffn", "cg%d" % i, [128, 256], F32) for i in range(3)]
    cu = [carve("ffn", "cu%d" % i, [128, 256], F32) for i in range(3)]
    gl = [carve("ffn", "gl%d" % i, [128, 256], F32) for i in range(2)]
    def phase_barrier(G):
        others = [k_ for g2_ in groups if g2_ != G for k_ in groups[g2_]]
        P.barrier(others)

    def ffn_slot(S):
        base = S * SEG
        for (r0, n) in ((0, 128), (128, 128), (256, 128), (384, 128), (512, 2)):
            norm_block(hbuf[base + r0:base + r0 + n, :], n, 1, uT2, "uT2", r0,
                       src_reads=["hb_first", "hb_last"] + ["hb%d" % k for k in (S - 1, S, S + 1) if 0 <= k < nslot])
        for side, col in ((0, 0), (1, SEG + 1)):
            tsmul("dve", uT2[:, :, col:col + 1], uT2[:, :, col:col + 1], flags[:, 2 * S + side:2 * S + side + 1],
                  ["uT2", "flags"], ["uT2"])
        tasks = []
        for g in range(6):
            nt = 4 if g < 5 else 2
            for ti in range(nt):
                for st in range(2):
                    tasks.append((g, ti, st, nt))
        held = [None]

        def st_mm(i):
            g, ti, st, nt = tasks[i]
            if held[0] is None or held[0][0] != g:
                if held[0] is not None:
                    w_done(2)
                wg_, kg_ = w_take("UG%d" % g, S)
                wu_, ku_ = w_take("UU%d" % g, S)
                held[0] = (g, wg_, kg_, wu_, ku_)
            _, wg_, kg_, wu_, ku_ = held[0]
            n = nt * 128
            f = g * 4 + ti
            x, y = i % 2, i % 3
            bg, bu = 2 * x, 2 * x + 1
            proj_fm_tile(wg_, kg_, 8, n, ti * 128, uT2, "uT2", st * 256, 258, bg)
            proj_fm_tile(wu_, ku_, 8, n, ti * 128, uT2, "uT2", st * 256, 258, bu)
            for (bank, dst, dk, ft) in ((bg, cg[y], "cg%d" % y, f), (bu, cu[y], "cu%d" % y, NFT + f)):
                actf(dst[:], PB(bank, 1, 257), AF.Identity, bk(bank) + ["vecs"], [dk],
                     bias=vecs[:, V_CB + ft:V_CB + ft + 1], scale=vecs[:, V_CW + 44 + ft:V_CW + 44 + ft + 1])

        def st_taps(i):
            g, ti, st, nt = tasks[i]
            f = g * 4 + ti
            x, y = i % 2, i % 3
            bg, bu = 2 * x, 2 * x + 1
            for (lo, hi, wo) in ((0, 256, 0), (2, 258, 88)):
                for (bank, dst, dk, ft) in ((bg, cg[y], "cg%d" % y, f), (bu, cu[y], "cu%d" % y, NFT + f)):
                    stt(dst[:], PB(bank, lo, hi), vecs[:, V_CW + wo + ft:V_CW + wo + ft + 1], dst[:], ALU.mult, ALU.add,
                        bk(bank) + ["vecs", dk], [dk])

        def st_glu(i):
            g, ti, st, nt = tasks[i]
            f = g * 4 + ti
            x, y = i % 2, i % 3
            actf(gl[x][:], cg[y][:], AF.Gelu_apprx_tanh, ["cg%d" % y], ["gl%d" % x])
            tt("pool", actT[st][:, f, :], gl[x][:], cu[y][:], ALU.mult, ["gl%d" % x, "cu%d" % y], ["actT%d" % st])

        nt_ = len(tasks)
        import os
        kffn = os.environ.get("K_FFN", "")
        for i in range(nt_ + 2):
            if i < nt_:
                st_mm(i)
            if 0 <= i - 1 < nt_ and "notaps" not in kffn:
                st_taps(i - 1)
            if 0 <= i - 2 < nt_ and "noglu" not in kffn:
                st_glu(i - 2)
        w_done(2)
        for g in range(6):
            wd_, kd_ = w_take("DN%d" % g, S)
            kc = 4 if g < 5 else 2
            wv = wd_[:, 0:kc * 1024].rearrange("p (c n) -> p c n", c=kc)
            for c in range(kc):
                f = g * 4 + c
                for tb in range(4 if "nodown" not in kffn else 0):
                    st, o = tb // 2, (tb % 2) * 128
                    for n2 in range(2):
                        bank = tb * 2 + n2
                        mm(PB(bank), actT[st][:, f, o:o + 128], wv[:, c, n2 * 512:(n2 + 1) * 512], f == 0, f == NFT - 1,
                           ["actT%d" % st, kd_], bk(bank))
            w_done()
        for tb in range(4):
            i = cnt["st"]
            cnt["st"] += 1
            q = i % 4
            a = cnt["x"] % 3
            cnt["x"] += 1
            xk = "xt%d" % a
            dma("pool", xt[a][:], hbuf[1 + base + tb * 128:1 + base + (tb + 1) * 128, :], ["hb%d" % S], [xk], "xin%d" % a)
            post_norm_residual(q, (tb * 2, tb * 2 + 1), xt[a], xk)
            dma("pool", y_out[S * SEG + tb * 128:S * SEG + (tb + 1) * 128, :], xt[a][:], [xk], ["yout"], "xst%d" % a)

    def dump(name, ap, keys):
        if name in dbg_out:
            dma("pool", dbg_out[name], ap, keys, ["dbg_" + name], "dbg_" + name)

    setup()
    prepass()
    load_gains(0)
    def cap(fn):
        P.begin_capture()
        fn()
        return P.end_capture()

    if "A" in passes:
        rwkv_state_init()
        orderA = list(reversed(range(nslot)))

        def stage_x(S):
            slot_load_norm(S)
            z_project(S, range(16))
            dma("pool", us_d[S], flat(uT[:]), ["uT"], ["us%d" % S], "uTst")
            dma("pool", zs_d[S], flat(zs[:]), ["zs%d" % i for i in range(16)], ["zsd%d" % S], "zsst")

        stage_x(orderA[0])
        for idx, S in enumerate(orderA):
            rwkv_slot_begin(S, 1)
            rwkv_group(S, 1, 1, False, parts=("pre", "minv"))
            P.interleave(cap(lambda: rwkv_group(S, 1, 1, False, parts=("scan",))),
                         cap(lambda: rwkv_group(S, 1, 0, False, parts=("pre",))))
            rwkv_group(S, 1, 0, False, parts=("minv",))
            nxt = cap(lambda: stage_x(orderA[idx + 1])) if idx + 1 < len(orderA) else []
            P.interleave(cap(lambda: rwkv_group(S, 1, 0, False, parts=("scan",))), nxt)
            dma("pool", ybwd_d[S], flat(yT[:]), ["yT"], ["ybwd%d" % S], "yTst")
            if S == 0:
                dump("ybwd", flat(yT[:]), ["yT"])
    if "B" in passes:
        rwkv_state_init()
        import os
        kstop = int(os.environ.get("K_STOP", "99"))

        def drain(tags, S):
            for t_ in tags:
                if t_.startswith("ZG") and "A" in passes:
                    continue
                w_take(t_, S)
                w_done()

        for S in range(nslot):
            if "A" in passes:
                dma("pool", flat(uT[:]), us_d[S], ["us%d" % S], ["uT"], "uTld")
            else:
                slot_load_norm(S)
            if kstop <= 1:
                drain(["QG", "KVG", "ZG0", "ZG1", "ZG2", "ZG3", "GA0", "GR0", "BAR0", "GA1", "GR1", "BAR1", "WO0", "WO1"], S)
                continue
            phase_barrier("att")
            qkv_project(S)
            if kstop <= 2:
                drain(["ZG0", "ZG1", "ZG2", "ZG3", "GA0", "GR0", "BAR0", "GA1", "GR1", "BAR1", "WO0", "WO1"], S)
                continue
            attention(S)
            if kstop <= 3:
                drain(["ZG0", "ZG1", "ZG2", "ZG3", "GA0", "GR0", "BAR0", "GA1", "GR1", "BAR1", "WO0", "WO1"], S)
                continue
            if "A" in passes:
                dma("pool", flat(zs[:]), zs_d[S], ["zsd%d" % S], ["zs%d" % i for i in range(16)], "zsld")
            else:
                z_project(S, range(16))
            if kstop <= 4:
                drain(["GA0", "GR0", "BAR0", "GA1", "GR1", "BAR1", "WO0", "WO1"], S)
                continue
            if "A" in passes:
                dma("pool", flat(yT[:]), ybwd_d[S], ["ybwd%d" % S], ["yT"], "yTld")
            else:
                P.add("dve", lambda e: e.memset(flat(yT[:]), 0.0), writes=["yT"])
            rwkv_slot_begin(S, 0)
            phase_barrier("rw")
            rwkv_group(S, 0, 0, True, parts=("pre", "minv"))
            P.interleave(cap(lambda: rwkv_group(S, 0, 0, True, parts=("scan",))),
                         cap(lambda: rwkv_group(S, 0, 1, True, parts=("pre",))))
            rwkv_group(S, 0, 1, True, parts=("minv", "scan"))
            if S == 0:
                dump("uT", flat(uT[:]), ["uT"])
                dump("oattn", flat(oattn[:]), ["oattn"])
                dump("zs", flat(zs[:]), ["zs%d" % i for i in range(16)])
                dump("yT", flat(yT[:]), ["yT"])
            if kstop <= 5:
                drain(["GA0", "GR0", "BAR0", "GA1", "GR1", "BAR1", "WO0", "WO1"], S)
                continue
            rwkv_epilogue(S)
            if kstop <= 6:
                drain(["GA0", "GR0", "BAR0", "GA1", "GR1", "BAR1", "WO0", "WO1"], S)
                continue
            merge_branches(S)
            if S == 0:
                dump("orw", flat(orw[:]), ["orw"])
                dump("mergedT", flat(mergedT[:, 0:8, :]), ["zs%d" % i for i in range(8)])
            if kstop <= 7:
                drain(["WO0", "WO1"], S)
                continue
            out_proj(S)
    if "C" in passes:
        load_gains(1)
        phase_barrier("ffn")
        for S in range(nslot):
            ffn_slot(S)
    P.add("pool", None, reads=["yout", "wscr"] + ["hb%d" % k for k in range(nslot)] + ["dbg_" + n for n in dbg_out]
          + ["ybwd%d" % k for k in range(nslot)])
    assert wstate["taken"] == len(wq) == wstate["released"], (wstate, len(wq))

    sems = P.assign(nc, es)
    with nc.Block() as block:
        @block.sync
        def _(e):
            P.run_engine("sp", e, sems)

        @block.tensor
        def _(e):
            P.run_engine("pe", e, sems)

        @block.scalar
        def _(e):
            P.run_engine("act", e, sems)

        @block.vector
        def _(e):
            P.run_engine("dve", e, sems)

        @block.gpsimd
        def _(e):
            P.run_engine("pool", e, sems)
    es.close()
    nc._n_ops = P.n
    return nc


def _assign_sequences():
    plan = [[("p", 0)]]
    counts = [5, 5, 5, 5, 4, 4, 4]
    k = 0
    for c in counts:
        plan.append([("s", k + i) for i in range(c)])
        k += c
    return plan


def kernel(**inputs):
    xp = np.asarray(inputs["x_prompt"], np.float32)
    xs = np.asarray(inputs["x_sample"], np.float32)
    wl = _layout_weights(inputs)
    plan = _assign_sequences()
    in_maps, places = [], []
    for core in range(NCORES):
        seqs = [xp[0] if kind == "p" else xs[i] for (kind, i) in plan[core]]
        xh, fl, place = _layout_core(seqs, SLOTS_FULL)
        m = dict(wl)
        m["xh"] = xh
        m["flags"] = fl
        in_maps.append(m)
        places.append(place)
    nc = build(SLOTS_FULL, "ABC")
    res = run_bass_kernel_spmd(nc, in_maps, core_ids=list(range(NCORES)))
    y_prompt = np.zeros_like(xp)
    y_sample = np.zeros_like(xs)
    for core in range(NCORES):
        y = np.asarray(res.results[core]["y"], np.float32)
        for (kind, i), (s0, n) in zip(plan[core], places[core]):
            blk = y[s0 * SEG:(s0 + n) * SEG]
            if kind == "p":
                y_prompt[0] = blk
            else:
                y_sample[i] = blk
    return (y_prompt, y_sample)
```

```python
import math
from contextlib import ExitStack

import numpy as np
import concourse.bass as bass
import concourse.mybir as mybir
from concourse.bass_utils import run_bass_kernel_spmd

F32 = mybir.dt.float32
BF16 = mybir.dt.bfloat16
AF = mybir.ActivationFunctionType
ALU = mybir.AluOpType

D = 1024
SEG = 512
HALO = 128
NTH = SEG + 2 * HALO
NBLK = NTH // 128
C = 64
GRP = 4
NGRP = SEG // (C * GRP)
GT = C * GRP
KAPPA = math.exp(-0.5)
NORM_EPS = 1e-6
GN_EPS = 64e-5
NW = 3
NCORES = 8
SLOTS_FULL = 32
D_FF = 2816
NFT = D_FF // 128

V_G1, V_G2, V_MUP, V_MUN, V_A0, V_KK, V_KA, V_RK, V_LNW, V_LNB, V_CW, V_CB = (
    0, 8, 16, 32, 48, 56, 60, 64, 68, 72, 76, 76 + 132)
NV = V_CB + 44
C_ID, C_BONES, C_M1, C_M2, C_TRI, C_I2, C_PEN = 0, 128, 256, 512, 640, 1408, 1472
NCST = C_PEN + 6 * 512

SEM_LIMIT = 12000


class Op:
    __slots__ = ("eng", "fn", "is_dma", "semkey", "waits", "needs_inc", "sem", "count")

    def __init__(self, eng, fn, is_dma, semkey):
        self.eng = eng
        self.fn = fn
        self.is_dma = is_dma
        self.semkey = semkey
        self.waits = []
        self.needs_inc = is_dma
        self.sem = None
        self.count = 0


class Prog:
    ENGS = ("pe", "act", "dve", "pool", "sp")

    def __init__(self):
        self.ops = {e: [] for e in self.ENGS}
        self.last_w = {}
        self.readers = {}
        self.dma_cnt = {}
        self.n = 0
        self.alias = {}

    def begin_capture(self):
        self._cap = []

    def end_capture(self):
        c, self._cap = self._cap, None
        return c

    def replay(self, items):
        for it in items:
            self.add(*it)

    def interleave(self, a, b):
        ia = ib = 0
        na, nb = len(a), len(b)
        while ia < na or ib < nb:
            if ib >= nb or (ia < na and ia * nb <= ib * na):
                self.add(*a[ia])
                ia += 1
            else:
                self.add(*b[ib])
                ib += 1

    def add(self, eng, fn, reads=(), writes=(), dma=False, semkey=None):
        if getattr(self, "_cap", None) is not None:
            self._cap.append((eng, fn, tuple(reads), tuple(writes), dma, semkey))
            return None
        op = Op(eng, fn, dma, semkey if dma else None)
        self.n += 1
        if self.alias:
            reads = list(reads) + [a for k in reads for a in self.alias.get(k, ())]
            writes = list(writes) + [a for k in writes for a in self.alias.get(k, ())]
        deps = []
        for k in reads:
            lw = self.last_w.get(k)
            if lw is not None:
                deps.append((lw, "raw"))
        for k in writes:
            lw = self.last_w.get(k)
            if lw is not None:
                deps.append((lw, "waw"))
            for r in self.readers.get(k, ()):
                deps.append((r, "war"))
        seen = set()
        for d, kind in deps:
            if d is op or id(d) in seen:
                continue
            if not d.is_dma and not dma and d.eng == eng:
                if eng == "pe":
                    continue
            seen.add(id(d))
            if d.is_dma:
                op.waits.append((d.sem, self.dma_cnt[d.semkey]))
            else:
                d.needs_inc = True
                op.waits.append(d)
        if dma:
            self.dma_cnt[semkey] = self.dma_cnt.get(semkey, 0) + 16
            op.sem = "d_" + str(semkey)
            op.count = self.dma_cnt[semkey]
        for k in reads:
            self.readers.setdefault(k, []).append(op)
        for k in writes:
            self.last_w[k] = op
            self.readers[k] = []
        self.ops[eng].append(op)
        return op

    def barrier(self, keys, engines=("pe", "act", "dve", "pool")):
        deps, seen = [], set()
        for k in keys:
            cand = list(self.readers.get(k, ()))
            if self.last_w.get(k) is not None:
                cand.append(self.last_w[k])
            for d in cand:
                if id(d) not in seen and d.fn is not None:
                    seen.add(id(d))
                    deps.append(d)
        for e in engines:
            op = Op(e, None, False, None)
            for d in deps:
                if d.is_dma:
                    op.waits.append((d.sem, self.dma_cnt[d.semkey]))
                elif not (d.eng == e and e == "pe"):
                    d.needs_inc = True
                    op.waits.append(d)
            self.ops[e].append(op)
        for k in keys:
            self.last_w.pop(k, None)
            self.readers[k] = []

    def assign(self, nc, es):
        sems = {}
        for e in self.ENGS:
            epoch, cnt = 0, 0
            for op in self.ops[e]:
                if not op.is_dma and op.needs_inc:
                    if cnt >= SEM_LIMIT:
                        epoch += 1
                        cnt = 0
                    cnt += 1
                    op.sem = "c_%s_%d" % (e, epoch)
                    op.count = cnt
        for e in self.ENGS:
            for op in self.ops[e]:
                if op.sem is not None and op.sem not in sems:
                    sems[op.sem] = es.enter_context(nc.semaphore(op.sem))
        return sems

    def run_engine(self, ename, eng, sems):
        seen = {}
        for op in self.ops[ename]:
            for d in op.waits:
                sname, cnt = d if isinstance(d, tuple) else (d.sem, d.count)
                if seen.get(sname, 0) >= cnt:
                    continue
                seen[sname] = cnt
                eng.wait_ge(sems[sname], cnt)
            if op.fn is None:
                continue
            ins = op.fn(eng)
            if op.is_dma:
                ins.then_inc(sems[op.sem], 16)
            elif op.needs_inc:
                ins.then_inc(sems[op.sem], 1)


def _fm(vec, ntile):
    return np.ascontiguousarray(np.asarray(vec, np.float32).reshape(ntile, 128).T)


def _constants():
    cst = np.zeros((128, NCST), np.float32)
    p = np.arange(128)
    cst[:, C_ID:C_ID + 128] = np.eye(128, dtype=np.float32)
    cst[:, C_BONES:C_BONES + 128] = (p[:, None] // 64 == p[None, :] // 64).astype(np.float32)
    s = (p % 64)[:, None]
    t = np.arange(64)[None, :]
    m1f = np.concatenate([(t > s), (t >= s)], axis=1).astype(np.float32)
    m1b = np.concatenate([(t < s), (t <= s)], axis=1).astype(np.float32)
    cst[:, C_M1:C_M1 + 128] = m1f
    cst[:, C_M1 + 128:C_M1 + 256] = m1b
    cst[:, C_M2:C_M2 + 64] = (t < s).astype(np.float32)
    cst[:, C_M2 + 64:C_M2 + 128] = (t > s).astype(np.float32)
    ps_, pt_ = p[:, None], p[None, :]
    same = (ps_ // 64 == pt_ // 64)
    for d in range(2):
        if d == 0:
            incl, excl, rest = (ps_ <= pt_), (ps_ < pt_), (ps_ > pt_)
        else:
            incl, excl, rest = (ps_ >= pt_), (ps_ > pt_), (ps_ < pt_)
        base = C_TRI + d * 384
        cst[:, base:base + 128] = (incl & same)
        cst[:, base + 128:base + 256] = (excl & same)
        cst[:, base + 256:base + 384] = (rest & same)
    cst[:, C_I2:C_I2 + 64] = (t == s).astype(np.float32)
    for g in range(2):
        for j in range(3):
            blk = np.zeros((128, 4, 128), np.float32)
            sk = p[:, None] + (j - 1) * 128
            qq = p[None, :]
            dist = np.abs(qq - sk).astype(np.float32)
            for i in range(4):
                h = g * 4 + i
                slope = 2.0 ** (-(h + 1))
                v = -8.0 * slope * dist
                v = np.where(dist <= 128, v, -240000.0)
                blk[:, i, :] = v
            base = C_PEN + (g * 3 + j) * 512
            cst[:, base:base + 512] = blk.reshape(128, 512)
    return cst


def _layout_weights(inp):
    L = 0
    w_in = np.asarray(inp["w_in"][L], np.float32)
    cols = []
    for i in range(4):
        cols += list(range(i * 64, (i + 1) * 64)) + list(range((4 + i) * 64, (5 + i) * 64))
    cols += list(range(512, 768))
    zc = list(range(768, 768 + 1952))
    w_in_p = np.zeros((1024, 38 * 128), np.float32)
    w_in_p[:, 0:768] = w_in[:, cols]
    w_in_p[:, 768:768 + 1952] = w_in[:, zc]
    w_in_p[:, 22 * 128:38 * 128] = w_in[:, 2720:4768]
    wba = np.asarray(inp["w_branch_attn"][L], np.float32)
    rows = []
    for i in range(4):
        rows += list(range(i * 64, (i + 1) * 64)) + list(range((4 + i) * 64, (5 + i) * 64))
    wba_p = np.ascontiguousarray(wba[rows, :])
    vecs = np.zeros((128, NV), np.float32)
    vecs[:, V_G1:V_G1 + 8] = _fm(inp["norm_mix_pre"][L], 8)
    vecs[:, V_G2:V_G2 + 8] = _fm(inp["norm_ffn_pre"][L], 8)
    mup = np.zeros(2048, np.float32)
    mun = np.zeros(2048, np.float32)
    mup[:1952] = np.asarray(inp["rw_mu_prev"][L])
    mun[:1952] = np.asarray(inp["rw_mu_next"][L])
    vecs[:, V_MUP:V_MUP + 16] = _fm(mup, 16)
    vecs[:, V_MUN:V_MUN + 16] = _fm(mun, 16)
    a0 = np.asarray(inp["rw_a0"][L], np.float32)
    for d in range(2):
        vecs[:, V_A0 + d * 4:V_A0 + d * 4 + 4] = _fm(a0[d], 4)
    vecs[:, V_KK:V_KK + 4] = _fm(inp["rw_k_k"][L], 4)
    vecs[:, V_KA:V_KA + 4] = _fm(inp["rw_k_a"][L], 4)
    vecs[:, V_RK:V_RK + 4] = _fm(np.asarray(inp["rw_r_k"][L]).reshape(512), 4)
    vecs[:, V_LNW:V_LNW + 4] = _fm(inp["rw_ln_w"][L], 4)
    vecs[:, V_LNB:V_LNB + 4] = _fm(inp["rw_ln_b"][L], 4)
    cw = np.asarray(inp["ffn_conv_w"][L], np.float32)
    for j in range(3):
        vecs[:, V_CW + j * 44:V_CW + (j + 1) * 44] = _fm(cw[j], 44)
    vecs[:, V_CB:V_CB + 44] = _fm(inp["ffn_conv_b"][L], 44)
    rows128 = np.zeros((128, 2, 1024), np.float32)
    rows128[:, 0, :] = np.asarray(inp["norm_mix_post"][L], np.float32)[None, :]
    rows128[:, 1, :] = np.asarray(inp["norm_ffn_post"][L], np.float32)[None, :]
    w0rows = np.ascontiguousarray(np.asarray(inp["rw_w0"][L], np.float32).reshape(1, 2, 512))
    sink = np.asarray(inp["attn_sink"][L], np.float32)
    sinkrows = np.zeros((1, 2, 4, 128), np.float32)
    for g in range(2):
        for i in range(4):
            sinkrows[0, g, i, :] = sink[g * 4 + i]
    sinkrows = sinkrows.reshape(1, 2, 512)
    lora = np.zeros((128, 4, 512), np.float32)
    lora[:, 0, :] = np.asarray(inp["rw_w2"][L], np.float32).reshape(128, 512)
    lora[:, 1, :] = np.asarray(inp["rw_a2"][L], np.float32).reshape(128, 512)
    g2 = np.asarray(inp["rw_g2"][L], np.float32)
    lora[:, 2, :] = g2[0:128]
    lora[0:32, 3, :] = g2[128:160]
    return {
        "w_in": w_in_p, "wba": wba_p,
        "wbr": np.ascontiguousarray(np.asarray(inp["w_branch_rwkv"][L], np.float32)),
        "wout": np.ascontiguousarray(np.asarray(inp["w_out"][L], np.float32)),
        "wup": np.ascontiguousarray(np.asarray(inp["w_ffn_up"][L], np.float32)),
        "wdn": np.ascontiguousarray(np.asarray(inp["w_ffn_down"][L], np.float32)),
        "vecs": vecs, "rows": rows128, "w0rows": w0rows, "sinkrows": sinkrows, "lora": lora,
        "cst": _constants(),
    }


def _layout_core(seqs, nslot):
    xh = np.zeros((nslot, NTH, D), np.float32)
    flags = np.zeros((nslot, 2), np.float32)
    s = 0
    place = []
    for x in seqs:
        T = x.shape[0]
        n = T // SEG
        xp = np.zeros((T + 2 * HALO, D), np.float32)
        xp[HALO:HALO + T] = x
        for k in range(n):
            xh[s + k] = xp[k * SEG:k * SEG + NTH]
            flags[s + k, 0] = 1.0 if k > 0 else 0.0
            flags[s + k, 1] = 1.0 if k < n - 1 else 0.0
        place.append((s, n))
        s += n
    fl = np.ascontiguousarray(np.broadcast_to(flags.reshape(1, nslot * 2), (128, nslot * 2))).astype(np.float32)
    return xh, fl, place


def weight_schedule(nslot, passes):
    q = []
    if "A" in passes:
        for s in reversed(range(nslot)):
            q += [("ZG0", s), ("ZG1", s), ("ZG2", s), ("ZG3", s)]
    if "B" in passes:
        for s in range(nslot):
            q += [("QG", s), ("KVG", s)]
            if "A" not in passes:
                q += [("ZG0", s), ("ZG1", s), ("ZG2", s), ("ZG3", s)]
            for h in range(2):
                q += [("GA%d" % h, s), ("GR%d" % h, s), ("BAR%d" % h, s)]
            q += [("WO0", s), ("WO1", s)]
    if "C" in passes:
        for s in range(nslot):
            for g in range(6):
                q += [("UG%d" % g, s), ("UU%d" % g, s)]
            for g in range(6):
                q += [("DN%d" % g, s)]
    return q


def build(nslot, passes="ABC", dbg=()):
    nc = bass.Bass("TRN2", target_bir_lowering=False)

    def din(name, shape, dt=F32):
        return nc.dram_tensor(name, list(shape), dt, kind="ExternalInput").ap()

    def dout(name, shape, dt=F32):
        return nc.dram_tensor(name, list(shape), dt, kind="ExternalOutput").ap()

    def dscr(name, shape, dt):
        return nc.dram_tensor(name, list(shape), dt).ap()

    xh = din("xh", [nslot, NTH, D])
    flags_d = din("flags", [128, 2 * nslot])
    w_in_d = din("w_in", [1024, 4864])
    wba_d = din("wba", [512, 1024])
    wbr_d = din("wbr", [512, 1024])
    wout_d = din("wout", [1024, 1024])
    wup_d = din("wup", [1024, 5632])
    wdn_d = din("wdn", [2816, 1024])
    vecs_d = din("vecs", [128, NV])
    rows_d = din("rows", [128, 2, 1024])
    w0_d = din("w0rows", [1, 2, 512])
    sink_d = din("sinkrows", [1, 2, 512])
    lora_d = din("lora", [128, 4, 512])
    cst_d = din("cst", [128, NCST])
    y_out = dout("y", [nslot * SEG, D])
    dbg_out = {}
    for name, shape in dbg:
        dbg_out[name] = dout("dbg_" + name, shape)

    w_in_b = dscr("w_in_b", [1024, 4864], BF16)
    wba_b = dscr("wba_b", [512, 1024], BF16)
    wbr_b = dscr("wbr_b", [512, 1024], BF16)
    wout_b = dscr("wout_b", [1024, 1024], BF16)
    wup_b = dscr("wup_b", [1024, 5632], BF16)
    wdn_b = dscr("wdn_b", [2816, 1024], BF16)
    ybwd_d = dscr("ybwd", [nslot, 128, 4 * SEG], F32)
    us_d = dscr("us_scr", [nslot, 128, 8 * NTH], BF16)
    zs_d = dscr("zs_scr", [nslot, 128, 16 * SEG], BF16)
    hbuf = dscr("hbuf", [nslot * SEG + 2, D], F32)

    P = Prog()
    es = ExitStack()

    def sb(name, shape, dt):
        return es.enter_context(nc.sbuf_tensor("s_" + name, list(shape), dt))

    psum = es.enter_context(nc.psum_tensor("psum", [128, 4096], F32))

    def PB(b, lo=0, hi=512):
        return psum[:, b * 512 + lo:b * 512 + hi]

    def bk(*banks):
        r = []
        for b in banks:
            r += ["pb%da" % b, "pb%db" % b]
        return r

    pbT = psum[:, 7 * 512:8 * 512].bitcast(BF16)

    ident_b = sb("ident_b", [128, 128], BF16)
    ones_b = sb("ones_b", [128, 128], BF16)
    bones_f = sb("bones_f", [128, 128], F32)
    bones_b = sb("bones_b", [128, 128], BF16)
    M1 = sb("M1", [128, 2, 128], F32)
    M2 = sb("M2", [128, 2, 64], F32)
    TRI = sb("TRI", [128, 2, 384], F32)
    I2 = sb("I2", [128, 64], BF16)
    PEN = sb("PEN", [128, 6, 512], BF16)
    vecs = sb("vecs", [128, NV], F32)
    c0 = sb("c0", [128, 16], F32)
    omka = sb("omka", [128, 4], F32)
    rows = sb("rows", [128, 1024], F32)
    gB = sb("gB", [128, 8, 128], F32)
    flags = sb("flags", [128, 2 * nslot], F32)
    rb = sb("rb", [2, 4, 512], BF16)
    w2b = sb("w2b", [128, 512], BF16)
    a2b = sb("a2b", [128, 512], BF16)
    g2b = sb("g2b", [128, 2, 512], BF16)
    onesf = sb("onesf", [128, 128], F32)

    xt = [sb("xt%d" % i, [128, 1024], F32) for i in range(3)]
    ub = [sb("ub%d" % i, [128, 1024], BF16) for i in range(2)]
    st_ssq = [sb("ssq%d" % i, [128, 2], F32) for i in range(4)]
    st_rstd = [sb("rstd%d" % i, [128, 1], F32) for i in range(4)]
    junk = sb("junk", [128, 1024], BF16)
    uT = sb("uT", [128, 8, NTH], BF16)
    wbuf = [sb("wbuf%d" % i, [128, 4096], BF16) for i in range(NW)]

    cnt = {"x": 0, "st": 0}

    def dma(eng, out, in_, reads, writes, semkey):
        P.add(eng, lambda e: e.dma_start(out=out, in_=in_), reads=reads, writes=writes, dma=True, semkey=semkey + "_" + eng)

    def act_copy(out, in_, reads, writes):
        P.add("act", lambda e: e.copy(out, in_), reads=reads, writes=writes)

    def dve_copy(out, in_, reads, writes):
        P.add("dve", lambda e: e.tensor_copy(out, in_), reads=reads, writes=writes)

    def mm(out, lhsT, rhs, start, stop, r, w, tp=None):
        if tp is None:
            P.add("pe", lambda e: e.matmul(out, lhsT, rhs, start=start, stop=stop), reads=r, writes=w)
        else:
            P.add("pe", lambda e: e.matmul(out, lhsT, rhs, start=start, stop=stop, tile_position=tp), reads=r, writes=w)

    def actf(out, in_, func, r, w, bias=None, scale=None, accum=None):
        kw = {}
        if bias is not None:
            kw["bias"] = bias
        if scale is not None:
            kw["scale"] = scale
        if accum is not None:
            kw["accum_out"] = accum
        P.add("act", lambda e: e.activation(out=out, in_=in_, func=func, **kw), reads=r, writes=w)

    def tt(eng, out, in0, in1, op, r, w):
        P.add(eng, lambda e: e.tensor_tensor(out=out, in0=in0, in1=in1, op=op), reads=r, writes=w)

    def stt(out, in0, scalar, in1, op0, op1, r, w):
        P.add("dve", lambda e: e.scalar_tensor_tensor(out=out, in0=in0, scalar=scalar, in1=in1, op0=op0, op1=op1), reads=r, writes=w)

    def tsc(eng, out, in0, s1, s2, op0, op1, r, w):
        P.add(eng, lambda e: e.tensor_scalar(out, in0, s1, s2, op0, op1), reads=r, writes=w)

    def tsmul(eng, out, in0, s, r, w):
        P.add(eng, lambda e: e.tensor_scalar_mul(out, in0, s), reads=r, writes=w)

    def recip(out, in_, r, w):
        P.add("dve", lambda e: e.reciprocal(out, in_), reads=r, writes=w)

    def load_gains(which):
        dma("pool", rows[:], rows_d[:, which, :], [], ["rows"], "ld_rows")
        for c in range(8):
            col = (V_G1 if which == 0 else V_G2) + c
            actf(gB[:, c, :], onesf[:], AF.Copy, ["vecs", "onesf"], ["gB"], scale=vecs[:, col:col + 1])

    def setup():
        stg = xt[0]
        dma("pool", vecs[:], vecs_d, [], ["vecs"], "ld_vecs")
        dma("pool", flags[:], flags_d, [], ["flags"], "ld_flags")
        dma("pool", stg[:, 0:1024], cst_d[:, 0:1024], [], ["xt0"], "xin0")
        dve_copy(ident_b[:], stg[:, C_ID:C_ID + 128], ["xt0"], ["ident_b"])
        dve_copy(bones_f[:], stg[:, C_BONES:C_BONES + 128], ["xt0"], ["bones_f"])
        dve_copy(bones_b[:], stg[:, C_BONES:C_BONES + 128], ["xt0"], ["bones_b"])
        dve_copy(M1[:].rearrange("p a b -> p (a b)"), stg[:, C_M1:C_M1 + 256], ["xt0"], ["M1"])
        dve_copy(M2[:].rearrange("p a b -> p (a b)"), stg[:, C_M2:C_M2 + 128], ["xt0"], ["M2"])
        dve_copy(TRI[:, 0, :], stg[:, C_TRI:C_TRI + 384], ["xt0"], ["TRI"])
        dma("pool", xt[1][:, 0:448], cst_d[:, 1024:1472], [], ["xt1"], "xin1")
        dve_copy(TRI[:, 1, :], xt[1][:, 0:384], ["xt1"], ["TRI"])
        dve_copy(I2[:], xt[1][:, 384:448], ["xt1"], ["I2"])
        for k in range(3):
            t = xt[(k + 2) % 3]
            key = "xt%d" % ((k + 2) % 3)
            dma("pool", t[:], cst_d[:, C_PEN + k * 1024:C_PEN + (k + 1) * 1024], [], [key], "xin%d" % ((k + 2) % 3))
            dve_copy(PEN[:, 2 * k:2 * k + 2, :].rearrange("p a b -> p (a b)"), t[:], [key], ["PEN"])
        P.add("dve", lambda e: e.memset(ones_b[:], 1.0), writes=["ones_b"])
        P.add("dve", lambda e: e.memset(onesf[:], 1.0), writes=["onesf"])
        P.add("dve", lambda e: e.memset(epsc[:], GN_EPS), writes=["epsc"])
        dma("pool", xt[0][:, 0:1024], lora_d[:, 0:2, :].rearrange("p a b -> p (a b)"), [], ["xt0"], "xin0")
        dve_copy(w2b[:], xt[0][:, 0:512], ["xt0"], ["w2b"])
        dve_copy(a2b[:], xt[0][:, 512:1024], ["xt0"], ["a2b"])
        dma("pool", xt[1][:, 0:1024], lora_d[:, 2:4, :].rearrange("p a b -> p (a b)"), [], ["xt1"], "xin1")
        dve_copy(g2b[:].rearrange("p a b -> p (a b)"), xt[1][:, 0:1024], ["xt1"], ["g2b"])
        tt("dve", c0[:], vecs[:, V_MUP:V_MUP + 16], vecs[:, V_MUN:V_MUN + 16], ALU.add, ["vecs"], ["c0"])
        tsc("dve", c0[:], c0[:], -1.0, 1.0, ALU.mult, ALU.add, ["c0"], ["c0"])
        tsc("dve", omka[:], vecs[:, V_KA:V_KA + 4], -1.0, 1.0, ALU.mult, ALU.add, ["vecs"], ["omka"])
        w0f = xt[2][0:1, 0:1024].rearrange("p (a b) -> p a b", a=2)
        w0t = xt[0][0:1, 0:1024].rearrange("p (a b) -> p a b", a=2)
        sinkf = xt[1][0:1, 0:1024].rearrange("p (a b) -> p a b", a=2)
        lo_b = ub[0][0:1, 0:1024].rearrange("p (a b) -> p a b", a=2)
        dma("pool", w0f, w0_d, [], ["xt2"], "xin2")
        dma("pool", sinkf, sink_d, [], ["xt1"], "xin1")
        dve_copy(rb[0:1, 0:2, :], w0f, ["xt2"], ["rb"])
        dve_copy(w0t, rb[0:1, 0:2, :], ["rb"], ["xt0"])
        tt("dve", w0t, w0f, w0t, ALU.subtract, ["xt2", "xt0"], ["xt0"])
        dve_copy(lo_b, w0t, ["xt0"], ["ub0"])
        dma("pool", rb[1:2, 0:2, :], lo_b, ["ub0"], ["rb"], "ubst0")
        actf(rb[0:1, 2:4, :], sinkf, AF.Exp, ["xt1"], ["rb"])
        P.add("dve", lambda e: e.memset(xt[2][0:1, :], 0.0), reads=["xt2"], writes=["xt2"])
        dma("pool", hbuf[0:1, :], xt[2][0:1, :], ["xt2"], ["hb_first"], "xst2")
        dma("pool", hbuf[nslot * SEG + 1:nslot * SEG + 2, :], xt[2][0:1, :], ["xt2"], ["hb_last"], "xst2")

    def prepass():
        jobs = []
        for (src, dst, R, Cc) in ((w_in_d, w_in_b, 1024, 4864), (wba_d, wba_b, 512, 1024), (wbr_d, wbr_b, 512, 1024),
                                  (wout_d, wout_b, 1024, 1024), (wup_d, wup_b, 1024, 5632), (wdn_d, wdn_b, 2816, 1024)):
            for rc in range(R // 128):
                for c0_ in range(0, Cc, 1024):
                    w = min(1024, Cc - c0_)
                    jobs.append((src[rc * 128:(rc + 1) * 128, c0_:c0_ + w], dst[rc * 128:(rc + 1) * 128, c0_:c0_ + w], w))
        for i, (s_ap, d_ap, w) in enumerate(jobs):
            a = i % 3
            b = i % 2
            dma("sp", xt[a][:, 0:w], s_ap, [], ["xt%d" % a], "xin%d" % a)
            if i % 2 == 0:
                dve_copy(ub[b][:, 0:w], xt[a][:, 0:w], ["xt%d" % a], ["ub%d" % b])
            else:
                act_copy(ub[b][:, 0:w], xt[a][:, 0:w], ["xt%d" % a], ["ub%d" % b])
            dma("pool", d_ap, ub[b][:, 0:w], ["ub%d" % b], ["wscr"], "ubst%d" % b)

    wq = weight_schedule(nslot, passes)
    wstate = {"issued": 0, "taken": 0, "released": 0}
    wfm = lambda t: t.rearrange("(c p) n -> p c n", p=128)

    def wsrc(tag):
        if tag == "QG":
            return wfm(w_in_b)[:, :, 0:512], 8, 512
        if tag == "KVG":
            return wfm(w_in_b)[:, :, 512:768], 8, 256
        if tag.startswith("ZG"):
            g = int(tag[2])
            return wfm(w_in_b)[:, :, 768 + g * 512:768 + (g + 1) * 512], 8, 512
        if tag.startswith("GA"):
            h = int(tag[2])
            return wfm(w_in_b)[:, :, 2816 + h * 512:2816 + (h + 1) * 512], 8, 512
        if tag.startswith("GR"):
            h = int(tag[2])
            return wfm(w_in_b)[:, :, 3840 + h * 512:3840 + (h + 1) * 512], 8, 512
        if tag.startswith("WO"):
            h = int(tag[2])
            return wfm(wout_b)[:, :, h * 512:(h + 1) * 512], 8, 512
        if tag.startswith("UG"):
            g = int(tag[2])
            n = 512 if g < 5 else 256
            return wfm(wup_b)[:, :, g * 512:g * 512 + n], 8, n
        if tag.startswith("UU"):
            g = int(tag[2])
            n = 512 if g < 5 else 256
            return wfm(wup_b)[:, :, 2816 + g * 512:2816 + g * 512 + n], 8, n
        if tag.startswith("DN"):
            g = int(tag[2])
            kc = 4 if g < 5 else 2
            return wfm(wdn_b)[:, g * 4:g * 4 + kc, :], kc, 1024
        raise KeyError(tag)

    def w_issue():
        i = wstate["issued"]
        tag, _ = wq[i]
        b = i % NW
        if tag.startswith("BAR"):
            h = int(tag[3])
            v = wbuf[b][:, :].rearrange("p (a c n) -> p a c n", a=2, c=4)
            dma("sp", v[:, 0, :, :], wfm(wba_b)[:, :, h * 512:(h + 1) * 512], ["wscr"], ["wbuf%d" % b], "w%d" % b)
            dma("sp", v[:, 1, :, :], wfm(wbr_b)[:, :, h * 512:(h + 1) * 512], ["wscr"], ["wbuf%d" % b], "w%d" % b)
        else:
            src, kc, n = wsrc(tag)
            v = wbuf[b][:, 0:kc * n].rearrange("p (c n) -> p c n", c=kc)
            dma("sp", v, src, ["wscr"], ["wbuf%d" % b], "w%d" % b)
        wstate["issued"] += 1

    def w_pump():
        while wstate["issued"] < len(wq) and wstate["issued"] - wstate["released"] < NW:
            w_issue()

    def w_take(tag, slot):
        i = wstate["taken"]
        assert wq[i] == (tag, slot), (wq[i], tag, slot)
        if wstate["issued"] <= i:
            assert wstate["issued"] - wstate["released"] < NW, "too many weight groups held"
            w_issue()
        wstate["taken"] += 1
        b = i % NW
        return wbuf[b], "wbuf%d" % b

    def w_done(k=1):
        wstate["released"] += k
        assert wstate["released"] <= wstate["taken"]
        w_pump()

    def norm_block(src_ap, nrow, which, dst, dst_key, col0, src_reads=()):
        i = cnt["x"]
        cnt["x"] += 1
        a, b, q = i % 3, i % 2, i % 4
        xk, uk = "xt%d" % a, "ub%d" % b
        dma("pool", xt[a][0:nrow, :], src_ap, list(src_reads), [xk], "xin%d" % a)
        P.add("act", lambda e: e.activation(out=junk[0:nrow, :], in_=xt[a][0:nrow, :], func=AF.Square, accum_out=st_ssq[q][0:nrow, 0:1]),
              reads=[xk], writes=["junk", "ssq%d" % q])
        P.add("dve", lambda e: e.tensor_scalar(st_rstd[q][0:nrow, :], st_ssq[q][0:nrow, 0:1], 1.0 / D, NORM_EPS, ALU.mult, ALU.add),
              reads=["ssq%d" % q], writes=["rstd%d" % q])
        P.add("act", lambda e: e.activation(out=st_rstd[q][0:nrow, :], in_=st_rstd[q][0:nrow, :], func=AF.Sqrt),
              reads=["rstd%d" % q], writes=["rstd%d" % q])
        P.add("dve", lambda e: e.reciprocal(st_rstd[q][0:nrow, :], st_rstd[q][0:nrow, :]), reads=["rstd%d" % q], writes=["rstd%d" % q])
        P.add("dve", lambda e: e.tensor_scalar_mul(ub[b][0:nrow, :], xt[a][0:nrow, :], st_rstd[q][0:nrow, :]),
              reads=[xk, "rstd%d" % q], writes=[uk])
        for c in range(8):
            P.add("pe", lambda e, c=c: e.transpose(pbT[:, c * 128:c * 128 + nrow], ub[b][0:nrow, c * 128:(c + 1) * 128], ident_b[0:nrow, 0:nrow]),
                  reads=[uk, "ident_b"], writes=bk(7))
        P.add("dve", lambda e: e.tensor_tensor(out=dst[:, :, col0:col0 + nrow],
                                               in0=pbT[:, :].rearrange("p (c n) -> p c n", c=8)[:, :, 0:nrow],
                                               in1=gB[:, :, 0:nrow], op=ALU.mult),
              reads=bk(7) + ["gB"], writes=[dst_key])

    ARENA_BYTES = 73 * 1024
    arena = sb("arena", [128, ARENA_BYTES // 4], F32)
    carve_state = {}

    def carve(group, name, shape, dt):
        off = carve_state.get(group, 0)
        nel = 1
        for d_ in shape[1:]:
            nel *= d_
        nbytes = nel * (4 if dt == F32 else 2)
        nbytes_al = (nbytes + 31) // 32 * 32
        assert off + nbytes_al <= ARENA_BYTES, (group, name, off, nbytes_al)
        carve_state[group] = off + nbytes_al
        v = arena[0:shape[0], off // 4:(off + nbytes) // 4]
        if dt != F32:
            v = v.bitcast(dt)
        if len(shape) == 3:
            v = v.rearrange("p (a b) -> p a b", a=shape[1])
        elif len(shape) == 4:
            v = v.rearrange("p (a b c) -> p a b c", a=shape[1], b=shape[2])
        elif len(shape) == 5:
            v = v.rearrange("p (a b c d) -> p a b c d", a=shape[1], b=shape[2], c=shape[3])
        groups.setdefault(group, []).append(name)
        return v

    groups = {}
    tmp4 = [sb("tmp%d" % i, [128, 512], F32) for i in range(4)]
    carve_state["att"] = 40 * 1024
    qT = carve("att", "qT", [128, 4, SEG], BF16)
    kT = carve("att", "kT", [128, NTH], BF16)
    vtok = carve("att", "vtok", [128, NBLK, 128], BF16)
    ptb = [carve("att", "ptb%d" % i, [128, 512], BF16) for i in range(3)]
    rden = carve("att", "rden", [128, 512], F32)
    fones = sb("fones", [128, 2, 64], BF16)
    oattn = sb("oattn", [128, 4, SEG], BF16)
    zs = sb("zs", [128, 16, SEG], BF16)
    ztmp = [tmp4[0], tmp4[1]]
    ztmp2 = [tmp4[2], tmp4[3]]
    ps_rot = {"d": 0}

    def nbank():
        b = ps_rot["d"] % 2
        ps_rot["d"] += 1
        return b

    def slot_load_norm(S):
        for b in range(NBLK):
            norm_block(xh[S, b * 128:(b + 1) * 128, :], 128, 0, uT, "uT", b * 128)

    def proj_fm_tile(wb, wkey, kc, n, col, src, skey, tok_lo, ntok, bank):
        wv = wb[:, 0:kc * n].rearrange("p (c n) -> p c n", c=kc)
        for c in range(kc):
            mm(PB(bank, 0, ntok), wv[:, c, col:col + 128], src[:, c, tok_lo:tok_lo + ntok], c == 0, c == kc - 1,
               [wkey, skey], bk(bank))

    def qkv_project(S):
        wb, wkey = w_take("QG", S)
        for i in range(4):
            bank = nbank()
            proj_fm_tile(wb, wkey, 8, 512, i * 128, uT, "uT", HALO, SEG, bank)
            act_copy(qT[:, i, :], PB(bank), bk(bank), ["qT"])
        w_done()
        wb, wkey = w_take("KVG", S)
        for (lo, n) in ((0, 512), (512, 256)):
            bank = nbank()
            proj_fm_tile(wb, wkey, 8, 256, 0, uT, "uT", lo, n, bank)
            act_copy(kT[:, lo:lo + n], PB(bank, 0, n), bk(bank), ["kT"])
        wv = wb[:, 0:8 * 256].rearrange("p (c n) -> p c n", c=8)
        for half in range(2):
            bank = nbank()
            for bb in range(3):
                b = half * 3 + bb
                for c in range(8):
                    mm(PB(bank, bb * 128, (bb + 1) * 128), uT[:, c, b * 128:(b + 1) * 128], wv[:, c, 128:256], c == 0, c == 7,
                       [wkey, "uT"], bk(bank))
            dve_copy(vtok[:, half * 3:half * 3 + 3, :].rearrange("p a b -> p (a b)"), PB(bank, 0, 384), bk(bank), ["vtok"])
        w_done()

    def attention(S):
        fp = flags[:, 2 * S:2 * S + 1]
        tsmul("dve", fones[:, 0, :], ones_b[:, 0:64], fp, ["ones_b", "flags"], ["fones"])
        tsmul("dve", vtok[:, 1, :], vtok[:, 1, :], fp, ["vtok", "flags"], ["vtok"])
        nqb = SEG // 128
        for qb in range(nqb):
            nb, db = 3 + (qb % 2), 5 + (qb % 2)
            for g in range(2):
                pr = slice(g * 64, (g + 1) * 64)
                for j in range(3):
                    kb = qb + j
                    sbk = (qb * 6 + g * 3 + j) % 3
                    pk = "ptb%d" % sbk
                    mm(PB(sbk), kT[pr, kb * 128:(kb + 1) * 128], qT[pr, :, qb * 128:(qb + 1) * 128], True, False,
                       ["kT", "qT"], bk(sbk), tp=(g * 64, 0))
                    mm(PB(sbk), ident_b[:], PEN[:, g * 3 + j, :], False, True, ["ident_b", "PEN"], bk(sbk))
                    actf(ptb[sbk][:], PB(sbk), AF.Exp, bk(sbk), [pk], scale=0.125)
                    mm(PB(nb)[pr, :], vtok[:, kb, pr], ptb[sbk][:], j == 0, j == 2, ["vtok", pk], bk(nb), tp=(0, g * 64))
                    if kb <= 1:
                        dl, dk = fones[:, 0, :], "fones"
                    else:
                        dl, dk = ones_b[:, 0:64], "ones_b"
                    mm(PB(db)[pr, :], dl, ptb[sbk][:], j == 0, False, [dk, pk], bk(db), tp=(0, g * 64))
                mm(PB(db)[pr, :], ones_b[0:1, 0:64], rb[0:1, 2 + g, :], False, True, ["ones_b", "rb"], bk(db), tp=(0, g * 64))
            recip(rden[:], PB(db), bk(db), ["rden"])
            tt("dve", oattn[:, :, qb * 128:(qb + 1) * 128], PB(nb).rearrange("p (a b) -> p a b", a=4),
               rden[:].rearrange("p (a b) -> p a b", a=4), ALU.mult, bk(nb) + ["rden"], ["oattn"])

    ZSUB = ((0, 510), (510, 2))

    def z_project(S, tiles):
        tasks = [(zt, o, n) for zt in tiles for (o, n) in ZSUB]
        zb = [tmp4[0], tmp4[1], tmp4[2]]
        zk = ["ztmp0", "ztmp1", "ztmq0"]
        cur = [None]

        def stage_mm(i):
            zt, o, n = tasks[i]
            g = zt // 4
            if cur[0] is None or cur[0][0] != g:
                if cur[0] is not None:
                    w_done()
                wb, wkey = w_take("ZG%d" % g, S)
                cur[0] = (g, wb, wkey)
            _, wb, wkey = cur[0]
            bank = i % 2
            proj_fm_tile(wb, wkey, 8, 512, (zt % 4) * 128, uT, "uT", HALO + o - 1, n + 2, bank)
            actf(zb[i % 3][:, 0:n], PB(bank, 1, n + 1), AF.Copy, bk(bank) + ["c0"], [zk[i % 3]], scale=c0[:, zt:zt + 1])

        def stage_taps(i):
            zt, o, n = tasks[i]
            bank = i % 2
            stt(tmp4[3][:, 0:n], PB(bank, 0, n), vecs[:, V_MUP + zt:V_MUP + zt + 1], zb[i % 3][:, 0:n], ALU.mult, ALU.add,
                bk(bank) + ["vecs", zk[i % 3]], ["ztmq1"])
            stt(zs[:, zt, o:o + n], PB(bank, 2, n + 2), vecs[:, V_MUN + zt:V_MUN + zt + 1], tmp4[3][:, 0:n], ALU.mult, ALU.add,
                bk(bank) + ["vecs", "ztmq1"], ["zs%d" % zt])

        nt_ = len(tasks)
        for i in range(nt_ + 1):
            if i < nt_:
                stage_mm(i)
            if i >= 1:
                stage_taps(i - 1)
        w_done()

    def rw(name, shape, dt):
        return carve("rw", name, shape, dt)

    twd = rw("twd", [128, GT], BF16)
    sigtok = [rw("sigtok%d" % i, [128, 512], F32) for i in range(2)]
    E = [rw("E%d" % i, [128, 4, GT], F32) for i in range(2)]
    kq = rw("kq", [128, GT], F32)
    ksq = rw("ksq", [128, GT], F32)
    nrm = rw("nrm", [128, GT], F32)
    kk = rw("kk", [128, GT], F32)
    asig = [rw("asig%d" % i, [128, GT], F32) for i in range(2)]
    kdir = [rw("kdir%d" % i, [128, GT], F32) for i in range(2)]
    t1 = rw("t1", [128, GT], F32)
    bbv = rw("bbv", [128, GT], F32)
    bbar = rw("bbar", [128, GT], BF16)
    kbar = rw("kbar", [128, GT], BF16)
    AR = [rw("AR%d" % i, [128, 4, GRP, 2, C], BF16) for i in range(2)]
    BT = [rw("BT%d" % i, [128, 4, GT], BF16) for i in range(2)]
    KTl = [rw("KTl%d" % i, [128, 4, GT], BF16) for i in range(2)]
    bbt = [rw("bbt%d" % i, [128, 4, GRP, C], BF16) for i in range(2)]
    kbt = [rw("kbt%d" % i, [128, 4, GRP, C], BF16) for i in range(2)]
    vt = [rw("vt%d" % i, [128, 4, GRP, C], BF16) for i in range(2)]
    GC = [rw("GC%d" % i, [128, 4, GRP], F32) for i in range(2)]
    GB1 = rw("GB1", [128, 16, 128], BF16)
    GB2 = rw("GB2", [128, 16, 128], BF16)
    Pk = [rw("Pk%d" % i, [128, 16, 2, C], BF16) for i in range(2)]
    Zk = [rw("Zk%d" % i, [128, 16, C], BF16) for i in range(2)]
    groups["rw"] += ["Pk0a", "Pk0b", "Pk1a", "Pk1b", "Zk0a", "Zk0b", "Zk1a", "Zk1b"]
    Wb = rw("Wb", [128, 4, C], BF16)
    Ub = rw("Ub", [128, 4, C], BF16)
    Tf = sb("Tf", [128, 4, C], F32)
    Tst = sb("Tst", [128, 4, C], BF16)
    yT = sb("yT", [128, 4, SEG], F32)
    bp = sb("bp", [128, 4, SEG], BF16)

    def flat(ap3):
        return ap3.rearrange("p a b -> p (a b)")

    def rwkv_state_init():
        P.add("dve", lambda e: e.memset(flat(Tf[:]), 0.0), writes=["Tf"])

    def rwkv_slot_begin(S, d):
        col = 2 * S + (0 if d == 0 else 1)
        tsmul("dve", flat(Tf[:]), flat(Tf[:]), flags[:, col:col + 1], ["Tf", "flags"], ["Tf"])
        act_copy(flat(Tst[:]), flat(Tf[:]), ["Tf"], ["Tst"])

    def rwkv_group(S, d, gi, passB, parts=("pre", "minv", "scan")):
        gb = gi % 2
        t0 = gi * GT
        dr = slice(d * 64, (d + 1) * 64)
        ARk, BTk, KTk, bbtk, kbtk, vtk, GCk = ("AR%d" % gb, "BT%d" % gb, "KTl%d" % gb, "bbt%d" % gb, "kbt%d" % gb,
                                               "vt%d" % gb, "GC%d" % gb)
        Zf, Zfk = Zk[1], "Zk1"
        if "pre" in parts:
            actf(twd[dr, :], zs[dr, 12, t0:t0 + GT], AF.Tanh, ["zs12"], ["twd"])
            for blk in range(2):
                mm(PB(blk), twd[dr, blk * 128:(blk + 1) * 128], w2b[dr, :], True, False, ["twd", "w2b"], bk(blk), tp=(d * 64, 0))
                mm(PB(blk), ones_b[0:2, 0:128], rb[0:2, d, :], False, True, ["ones_b", "rb"], bk(blk))
                actf(sigtok[blk][:], PB(blk), AF.Sigmoid, bk(blk), ["sigtok%d" % blk])
            for ct in range(4):
                eb = ct % 2
                Et, Ek = E[eb], "E%d" % eb
                for blk in range(2):
                    bank = 2 + blk
                    mm(PB(bank, 0, 384), sigtok[blk][:, ct * 128:(ct + 1) * 128], TRI[:, d, :], True, True,
                       ["sigtok%d" % blk, "TRI"], bk(bank))
                    actf(Et[:, 0:3, blk * 128:(blk + 1) * 128], PB(bank, 0, 384).rearrange("p (a b) -> p a b", a=3), AF.Exp,
                         bk(bank), [Ek], scale=-KAPPA)
                    actf(Et[:, 3, blk * 128:(blk + 1) * 128], PB(bank, 0, 128), AF.Exp, bk(bank), [Ek], scale=KAPPA)
                kz, kzk = zs[:, 4 + ct, t0:t0 + GT], "zs%d" % (4 + ct)
                tsmul("dve", kq[:], kz, vecs[:, V_KK + ct:V_KK + ct + 1], [kzk, "vecs"], ["kq"])
                actf(ksq[:], kq[:], AF.Square, ["kq"], ["ksq"])
                mm(PB(4, 0, 256), bones_f[:], ksq[:], True, True, ["bones_f", "ksq"], bk(4))
                actf(nrm[:], PB(4, 0, 256), AF.Sqrt, bk(4), ["nrm"])
                P.add("dve", lambda e: e.tensor_scalar_max(nrm[:], nrm[:], 1e-12), reads=["nrm"], writes=["nrm"])
                recip(nrm[:], nrm[:], ["nrm"], ["nrm"])
                tt("dve", kk[:], kq[:], nrm[:], ALU.mult, ["kq", "nrm"], ["kk"])
                dirs = (0, 1) if passB else (d,)
                for dd in dirs:
                    ddr = slice(dd * 64, (dd + 1) * 64)
                    psa = PB(4, 256, 512)
                    mm(psa, a2b[ddr, ct * 128:(ct + 1) * 128], zs[ddr, 13, t0:t0 + GT], True, True, ["a2b", "zs13"], bk(4), tp=(dd * 64, 0))
                    actf(asig[dd][:], psa, AF.Sigmoid, bk(4) + ["vecs"], ["asig%d" % dd],
                         bias=vecs[:, V_A0 + dd * 4 + ct:V_A0 + dd * 4 + ct + 1])
                    tsc("dve", t1[:], asig[dd][:], vecs[:, V_KA + ct:V_KA + ct + 1], omka[:, ct:ct + 1], ALU.mult, ALU.add,
                        ["asig%d" % dd, "vecs", "omka"], ["t1"])
                    tt("dve", kdir[dd][:], kz, t1[:], ALU.mult, [kzk, "t1"], ["kdir%d" % dd])
                if passB:
                    tt("pool", t1[:], kdir[0][:], kdir[1][:], ALU.add, ["kdir0", "kdir1"], ["t1"])
                    stt(bp[:, ct, t0:t0 + GT], zs[:, ct, t0:t0 + GT], vecs[:, V_RK + ct:V_RK + ct + 1], t1[:], ALU.mult, ALU.mult,
                        ["zs%d" % ct, "vecs", "t1"], ["bp"])
                tt("dve", bbv[:], kk[:], asig[d][:], ALU.mult, ["kk", "asig%d" % d], ["bbv"])
                v4 = lambda ap: ap.rearrange("p (c t) -> p c t", c=GRP)
                tt("dve", AR[gb][:, ct, :, 1, :], v4(zs[:, ct, t0:t0 + GT]), v4(Et[:, 0, :]), ALU.mult, ["zs%d" % ct, Ek], [ARk])
                stt(AR[gb][:, ct, :, 0, :], v4(kk[:]), -1.0, v4(Et[:, 1, :]), ALU.mult, ALU.mult, ["kk", Ek], [ARk])
                tt("pool", BT[gb][:, ct, :], bbv[:], Et[:, 3, :], ALU.mult, ["bbv", Ek], [BTk])
                tt("pool", KTl[gb][:, ct, :], kdir[d][:], Et[:, 3, :], ALU.mult, ["kdir%d" % d, Ek], [KTk])
                tt("pool", bbar[:], bbv[:], Et[:, 2, :], ALU.mult, ["bbv", Ek], ["bbar"])
                tt("pool", kbar[:], kdir[d][:], Et[:, 2, :], ALU.mult, ["kdir%d" % d, Ek], ["kbar"])
                tend = (C - 1) if d == 0 else 0
                dve_copy(GC[gb][:, ct, :], v4(Et[:, 0, :])[:, :, tend], [Ek], [GCk])
                vz, vzk = zs[:, 8 + ct, t0:t0 + GT], "zs%d" % (8 + ct)
                for e_ in range(2):
                    er = slice(e_ * 64, (e_ + 1) * 64)
                    tp = (e_ * 64, e_ * 64)
                    for c in range(GRP):
                        cs = slice(c * C, (c + 1) * C)
                        mm(PB(0)[er, c * C:(c + 1) * C], bbar[er, cs], ident_b[er, er], True, True, ["bbar", "ident_b"], bk(0), tp=tp)
                        mm(PB(0)[er, 256 + c * C:256 + (c + 1) * C], kbar[er, cs], ident_b[er, er], True, True, ["kbar", "ident_b"], bk(0), tp=tp)
                        mm(PB(1)[er, c * C:(c + 1) * C], vz[er, cs], ident_b[er, er], True, True, [vzk, "ident_b"], bk(1), tp=tp)
                act_copy(flat(bbt[gb][:, ct, :, :]), PB(0, 0, 256), bk(0), [bbtk])
                act_copy(flat(kbt[gb][:, ct, :, :]), PB(0, 256, 512), bk(0), [kbtk])
                dve_copy(flat(vt[gb][:, ct, :, :]), PB(1, 0, 256), bk(1), [vtk])
        if "minv" in parts:
            m1 = M1[:, d, :].unsqueeze(1).to_broadcast([128, 4, 128])
            m2 = M2[:, d, :].unsqueeze(1).to_broadcast([128, 4, C])
            for c in range(GRP):
                b1, b2 = c % 2, 2 + c % 2
                b3 = 4 + c % 2
                cs = slice(c * C, (c + 1) * C)
                for hp in range(4):
                    for e_ in range(2):
                        er = slice(e_ * 64, (e_ + 1) * 64)
                        tp = (e_ * 64, e_ * 64)
                        arv = AR[gb][er, hp, c, :, :]
                        mm(PB(b1)[er, hp * 128:(hp + 1) * 128], BT[gb][er, hp, cs], arv, True, True, [BTk, ARk], bk(b1), tp=tp)
                        mm(PB(b2)[er, hp * 128:(hp + 1) * 128], KTl[gb][er, hp, cs], arv, True, True, [KTk, ARk], bk(b2), tp=tp)
                        mm(PB(b3)[er, hp * C:(hp + 1) * C], AR[gb][er, hp, c, 0, :], BT[gb][er, hp, cs], True, True,
                           [ARk, BTk], bk(b3), tp=tp)
                tt("dve", GB1[:, c * 4:(c + 1) * 4, :], PB(b1).rearrange("p (a b) -> p a b", a=4), m1, ALU.mult, bk(b1) + ["M1"], ["GB1"])
                tt("dve", GB2[:, c * 4:(c + 1) * 4, :], PB(b2).rearrange("p (a b) -> p a b", a=4), m1, ALU.mult, bk(b2) + ["M1"], ["GB2"])
                tt("dve", Pk[0][:, c * 4:(c + 1) * 4, 0, :], PB(b3, 0, 256).rearrange("p (a b) -> p a b", a=4), m2, ALU.mult,
                   bk(b3) + ["M2"], ["Pk0a" if c < 2 else "Pk0b"])
            tt("dve", Zk[0][:], GB1[:, :, 0:C], I2[:].unsqueeze(1).to_broadcast([128, 16, C]), ALU.add, ["GB1", "I2"], ["Zk0a", "Zk0b"])
            psP = psum[:, 0:2048]
            psZ = psum[:, 2048:3072]
            HS = ((0, "a"), (1, "b"))

            def mmP(k, h, hs):
                cur = k % 2
                for slot in range(h * 8, h * 8 + 8):
                    for e_ in range(2):
                        er = slice(e_ * 64, (e_ + 1) * 64)
                        tp = (e_ * 64, e_ * 64)
                        Pv = Pk[cur][er, slot, 0, :]
                        if k == 0:
                            PTv, ptk = GB1[er, slot, 0:C], "GB1"
                        else:
                            PTv, ptk = Pk[cur][er, slot, 1, :], "Pk%d%s" % (cur, hs)
                        rk_ = [ptk, "Pk%d%s" % (cur, hs)]
                        mm(psP[er, slot * 128:slot * 128 + C], PTv, Pv, True, True, rk_, bk(2 * h, 2 * h + 1), tp=tp)
                        if k < 4:
                            mm(psP[er, slot * 128 + C:slot * 128 + 2 * C], Pv, PTv, True, True, rk_, bk(2 * h, 2 * h + 1), tp=tp)

            def evP(k, h, hs):
                nxt = (k + 1) % 2
                src = psP[:, h * 1024:(h + 1) * 1024]
                if k < 4:
                    act_copy(Pk[nxt][:, h * 8:(h + 1) * 8, :, :].rearrange("p s o n -> p (s o n)"), src, bk(2 * h, 2 * h + 1),
                             ["Pk%d%s" % (nxt, hs)])
                else:
                    act_copy(Pk[nxt][:, h * 8:(h + 1) * 8, 0, :], src.rearrange("p (s o n) -> p s o n", s=8, o=2)[:, :, 0, :],
                             bk(2 * h, 2 * h + 1), ["Pk%d%s" % (nxt, hs)])

            def mmZ(k, h, hs):
                cur, nxt = k % 2, (k + 1) % 2
                for slot in range(h * 8, h * 8 + 8):
                    for e_ in range(2):
                        er = slice(e_ * 64, (e_ + 1) * 64)
                        tp = (e_ * 64, e_ * 64)
                        mm(psZ[er, slot * C:(slot + 1) * C], Pk[nxt][er, slot, 0, :], Zk[cur][er, slot, :], True, True,
                           ["Pk%d%s" % (nxt, hs), "Zk%d%s" % (cur, hs)], bk(4 + h), tp=tp)

            def evZ(k, h, hs):
                cur, nxt = k % 2, (k + 1) % 2
                tt("dve", flat(Zk[nxt][:, h * 8:(h + 1) * 8, :]), psZ[:, h * 512:(h + 1) * 512], flat(Zk[cur][:, h * 8:(h + 1) * 8, :]),
                   ALU.add, bk(4 + h) + ["Zk%d%s" % (cur, hs)], ["Zk%d%s" % (nxt, hs)])

            for k in range(5):
                for h, hs in HS:
                    mmP(k, h, hs)
                for h, hs in HS:
                    evP(k, h, hs)
                for h, hs in HS:
                    mmZ(k, h, hs)
                for h, hs in HS:
                    evZ(k, h, hs)
        if "scan" in parts:
            order = range(GRP) if d == 0 else range(GRP - 1, -1, -1)
            for c in order:
                tok = t0 + c * C
                for hp in range(4):
                    for e_ in range(2):
                        er = slice(e_ * 64, (e_ + 1) * 64)
                        tp = (e_ * 64, e_ * 64)
                        slot = c * 4 + hp
                        o = PB(5)[er, hp * C:(hp + 1) * C]
                        mm(o, AR[gb][er, hp, c, 0, :], Tst[er, hp, :], True, False, [ARk, "Tst"], bk(5), tp=tp)
                        mm(o, GB2[er, slot, 0:C], vt[gb][er, hp, c, :], False, True, ["GB2", vtk], bk(5), tp=tp)
                act_copy(flat(Wb[:]), PB(5, 0, 256), bk(5), ["Wb"])
                for hp in range(4):
                    for e_ in range(2):
                        er = slice(e_ * 64, (e_ + 1) * 64)
                        tp = (e_ * 64, e_ * 64)
                        slot = c * 4 + hp
                        mm(PB(5)[er, 256 + hp * C:256 + (hp + 1) * C], Zf[er, slot, :], Wb[er, hp, :], True, True,
                           ["Zk1a" if c < 2 else "Zk1b", "Wb"], bk(5), tp=tp)
                dve_copy(flat(Ub[:]), PB(5, 256, 512), bk(5), ["Ub"])
                for hp in range(4):
                    for e_ in range(2):
                        er = slice(e_ * 64, (e_ + 1) * 64)
                        tp = (e_ * 64, e_ * 64)
                        slot = c * 4 + hp
                        oy = PB(6)[er, hp * C:(hp + 1) * C]
                        mm(oy, Tst[er, hp, :], AR[gb][er, hp, c, 1, :], True, False, ["Tst", ARk], bk(6), tp=tp)
                        mm(oy, Ub[er, hp, :], GB1[er, slot, C:2 * C], False, False, ["Ub", "GB1"], bk(6), tp=tp)
                        mm(oy, vt[gb][er, hp, c, :], GB2[er, slot, C:2 * C], False, True, [vtk, "GB2"], bk(6), tp=tp)
                        ot = PB(6)[er, 256 + hp * C:256 + (hp + 1) * C]
                        mm(ot, bbt[gb][er, hp, c, :], Ub[er, hp, :], True, False, [bbtk, "Ub"], bk(6), tp=tp)
                        mm(ot, kbt[gb][er, hp, c, :], vt[gb][er, hp, c, :], False, True, [kbtk, vtk], bk(6), tp=tp)
                py = PB(6, 0, 256).rearrange("p (a b) -> p a b", a=4)
                if passB:
                    tt("dve", yT[:, :, tok:tok + C], py, yT[:, :, tok:tok + C], ALU.add, bk(6) + ["yT"], ["yT"])
                else:
                    dve_copy(yT[:, :, tok:tok + C], py, bk(6), ["yT"])
                tt("dve", Tf[:], Tf[:], GC[gb][:, :, c:c + 1].to_broadcast([128, 4, C]), ALU.mult, ["Tf", GCk], ["Tf"])
                tt("dve", flat(Tf[:]), flat(Tf[:]), PB(6, 256, 512), ALU.add, bk(6) + ["Tf"], ["Tf"])
                act_copy(flat(Tst[:]), flat(Tf[:]), ["Tf"], ["Tst"])

    yc, ysq, sd, bon = tmp4
    sg = sb("sg", [128, 2, SEG], BF16)
    orw = sb("orw", [128, 4, SEG], BF16)
    epsc = sb("epsc", [128, 1], F32)

    def rwkv_epilogue(S):
        actf(sg[:, 0, :], zs[:, 14, :], AF.Sigmoid, ["zs14"], ["sg"])
        actf(sg[0:32, 1, :], zs[0:32, 15, :], AF.Sigmoid, ["zs15"], ["sg"])
        for ct in range(4):
            b0 = nbank()
            mm(PB(b0), bones_f[:], yT[:, ct, :], True, True, ["bones_f", "yT"], bk(b0))
            stt(yc[:], PB(b0), -1.0 / C, yT[:, ct, :], ALU.mult, ALU.add, bk(b0) + ["yT"], ["yc"])
            actf(ysq[:], yc[:], AF.Square, ["yc"], ["ysq"])
            b1 = nbank()
            mm(PB(b1), bones_f[:], ysq[:], True, True, ["bones_f", "ysq"], bk(b1))
            actf(sd[:], PB(b1), AF.Sqrt, bk(b1) + ["epsc"], ["sd"], bias=epsc[:, 0:1], scale=1.0 / C)
            recip(sd[:], sd[:], ["sd"], ["sd"])
            tt("dve", yc[:], yc[:], sd[:], ALU.mult, ["yc", "sd"], ["yc"])
            tsc("dve", yc[:], yc[:], vecs[:, V_LNW + ct:V_LNW + ct + 1], vecs[:, V_LNB + ct:V_LNB + ct + 1], ALU.mult, ALU.add,
                ["yc", "vecs"], ["yc"])
            b2 = nbank()
            mm(PB(b2), bones_b[:], bp[:, ct, :], True, True, ["bones_b", "bp"], bk(b2))
            tt("dve", bon[:], PB(b2), zs[:, 8 + ct, :], ALU.mult, bk(b2) + ["zs%d" % (8 + ct)], ["bon"])
            tt("pool", yc[:], yc[:], bon[:], ALU.add, ["yc", "bon"], ["yc"])
            b3 = nbank()
            mm(PB(b3), g2b[:, 0, ct * 128:(ct + 1) * 128], sg[:, 0, :], True, False, ["g2b", "sg"], bk(b3))
            mm(PB(b3), g2b[0:32, 1, ct * 128:(ct + 1) * 128], sg[0:32, 1, :], False, True, ["g2b", "sg"], bk(b3))
            tt("dve", orw[:, ct, :], PB(b3), yc[:], ALU.mult, bk(b3) + ["yc"], ["orw"])

    mergedT = zs
    sga, sgr, tma, tmr = tmp4
    for grp_ in (("ztmp0", "yc", "sga", "hrA"), ("ztmp1", "ysq", "sgr", "hrB"), ("ztmq0", "sd", "tma"), ("ztmq1", "bon", "tmr")):
        for k_ in grp_:
            P.alias[k_] = tuple(x for x in grp_ if x != k_)

    def merge_branches(S):
        for h in range(2):
            wga, kga = w_take("GA%d" % h, S)
            wgr, kgr = w_take("GR%d" % h, S)
            wbr_, kbr = w_take("BAR%d" % h, S)
            wbv = wbr_[:, :].rearrange("p (a c n) -> p a c n", a=2, c=4)
            for mi in range(4):
                m = h * 4 + mi
                proj_fm_tile(wga, kga, 8, 512, mi * 128, uT, "uT", HALO, SEG, 0)
                actf(sga[:], PB(0), AF.Sigmoid, bk(0), ["sga"])
                proj_fm_tile(wgr, kgr, 8, 512, mi * 128, uT, "uT", HALO, SEG, 1)
                actf(sgr[:], PB(1), AF.Sigmoid, bk(1), ["sgr"])
                for c in range(4):
                    mm(PB(2), wbv[:, 0, c, mi * 128:(mi + 1) * 128], oattn[:, c, :], c == 0, c == 3, [kbr, "oattn"], bk(2))
                for c in range(4):
                    mm(PB(3), wbv[:, 1, c, mi * 128:(mi + 1) * 128], orw[:, c, :], c == 0, c == 3, [kbr, "orw"], bk(3))
                tt("dve", tma[:], PB(2), sga[:], ALU.mult, bk(2) + ["sga"], ["tma"])
                tt("dve", tmr[:], PB(3), sgr[:], ALU.mult, bk(3) + ["sgr"], ["tmr"])
                tt("pool", mergedT[:, m, :], tma[:], tmr[:], ALU.add, ["tma", "tmr"], ["zs%d" % m])
            w_done(3)

    def out_proj(S):
        w0_, k0_ = w_take("WO0", S)
        w1_, k1_ = w_take("WO1", S)
        wv = [w0_[:, :].rearrange("p (c n) -> p c n", c=8), w1_[:, :].rearrange("p (c n) -> p c n", c=8)]
        wk = [k0_, k1_]
        for tb in range(SEG // 128):
            i = cnt["st"]
            cnt["st"] += 1
            q = i % 4
            a = cnt["x"] % 3
            cnt["x"] += 1
            xk = "xt%d" % a
            dma("pool", xt[a][:], xh[S, HALO + tb * 128:HALO + (tb + 1) * 128, :], [], [xk], "xin%d" % a)
            for n2 in range(2):
                bank = 4 + n2
                for c in range(8):
                    mm(PB(bank), mergedT[:, c, tb * 128:(tb + 1) * 128], wv[n2][:, c, :], c == 0, c == 7, ["zs%d" % c, wk[n2]], bk(bank))
            post_norm_residual(q, (4, 5), xt[a], xk)
            dma("pool", hbuf[1 + S * SEG + tb * 128:1 + S * SEG + (tb + 1) * 128, :], xt[a][:], [xk], ["hb%d" % S], "xst%d" % a)
        w_done(2)

    def post_norm_residual(q, banks, res, rkey):
        sk, rk_ = "ssq%d" % q, "rstd%d" % q
        for n2 in range(2):
            actf(junk[:, n2 * 512:(n2 + 1) * 512], PB(banks[n2]), AF.Square, bk(banks[n2]), ["junk", sk], accum=st_ssq[q][:, n2:n2 + 1])
        tt("dve", st_rstd[q][:], st_ssq[q][:, 0:1], st_ssq[q][:, 1:2], ALU.add, [sk], [rk_])
        tsc("dve", st_rstd[q][:], st_rstd[q][:], 1.0 / D, NORM_EPS, ALU.mult, ALU.add, [rk_], [rk_])
        actf(st_rstd[q][:], st_rstd[q][:], AF.Sqrt, [rk_], [rk_])
        recip(st_rstd[q][:], st_rstd[q][:], [rk_], [rk_])
        for n2 in range(2):
            hk = "hrA" if n2 == 0 else "hrB"
            stt(tmp4[n2][:], PB(banks[n2]), st_rstd[q][:, 0:1], rows[:, n2 * 512:(n2 + 1) * 512], ALU.mult, ALU.mult,
                bk(banks[n2]) + [rk_, "rows"], [hk])
            tt("pool", res[:, n2 * 512:(n2 + 1) * 512], res[:, n2 * 512:(n2 + 1) * 512], tmp4[n2][:], ALU.add, [hk, rkey], [rkey])

    uT2 = carve("ffn", "uT2", [128, 8, SEG + 2], BF16)
    actT = [carve("ffn", "actT%d" % i, [128, NFT, 256], BF16) for i in range(2)]
    cg = [carve("ffn", "cg%d" % i, [128, 256], F32) for i in range(3)]
    cu = [carve("ffn", "cu%d" % i, [128, 256], F32) for i in range(3)]
    gl = [carve("ffn", "gl%d" % i, [128, 256], F32) for i in range(2)]
    def phase_barrier(G):
        others = [k_ for g2_ in groups if g2_ != G for k_ in groups[g2_]]
        P.barrier(others)

    def ffn_slot(S):
        base = S * SEG
        for (r0, n) in ((0, 128), (128, 128), (256, 128), (384, 128), (512, 2)):
            norm_block(hbuf[base + r0:base + r0 + n, :], n, 1, uT2, "uT2", r0,
                       src_reads=["hb_first", "hb_last"] + ["hb%d" % k for k in (S - 1, S, S + 1) if 0 <= k < nslot])
        for side, col in ((0, 0), (1, SEG + 1)):
            tsmul("dve", uT2[:, :, col:col + 1], uT2[:, :, col:col + 1], flags[:, 2 * S + side:2 * S + side + 1],
                  ["uT2", "flags"], ["uT2"])
        tasks = []
        for g in range(6):
            nt = 4 if g < 5 else 2
            for ti in range(nt):
                for st in range(2):
                    tasks.append((g, ti, st, nt))
        held = [None]

        def st_mm(i):
            g, ti, st, nt = tasks[i]
            if held[0] is None or held[0][0] != g:
                if held[0] is not None:
                    w_done(2)
                wg_, kg_ = w_take("UG%d" % g, S)
                wu_, ku_ = w_take("UU%d" % g, S)
                held[0] = (g, wg_, kg_, wu_, ku_)
            _, wg_, kg_, wu_, ku_ = held[0]
            n = nt * 128
            f = g * 4 + ti
            x, y = i % 2, i % 3
            bg, bu = 2 * x, 2 * x + 1
            proj_fm_tile(wg_, kg_, 8, n, ti * 128, uT2, "uT2", st * 256, 258, bg)
            proj_fm_tile(wu_, ku_, 8, n, ti * 128, uT2, "uT2", st * 256, 258, bu)
            for (bank, dst, dk, ft) in ((bg, cg[y], "cg%d" % y, f), (bu, cu[y], "cu%d" % y, NFT + f)):
                actf(dst[:], PB(bank, 1, 257), AF.Identity, bk(bank) + ["vecs"], [dk],
                     bias=vecs[:, V_CB + ft:V_CB + ft + 1], scale=vecs[:, V_CW + 44 + ft:V_CW + 44 + ft + 1])

        def st_taps(i):
            g, ti, st, nt = tasks[i]
            f = g * 4 + ti
            x, y = i % 2, i % 3
            bg, bu = 2 * x, 2 * x + 1
            for (lo, hi, wo) in ((0, 256, 0), (2, 258, 88)):
                for (bank, dst, dk, ft) in ((bg, cg[y], "cg%d" % y, f), (bu, cu[y], "cu%d" % y, NFT + f)):
                    stt(dst[:], PB(bank, lo, hi), vecs[:, V_CW + wo + ft:V_CW + wo + ft + 1], dst[:], ALU.mult, ALU.add,
                        bk(bank) + ["vecs", dk], [dk])

        def st_glu(i):
            g, ti, st, nt = tasks[i]
            f = g * 4 + ti
            x, y = i % 2, i % 3
            actf(gl[x][:], cg[y][:], AF.Gelu_apprx_tanh, ["cg%d" % y], ["gl%d" % x])
            tt("pool", actT[st][:, f, :], gl[x][:], cu[y][:], ALU.mult, ["gl%d" % x, "cu%d" % y], ["actT%d" % st])

        nt_ = len(tasks)
        import os
        kffn = os.environ.get("K_FFN", "")
        for i in range(nt_ + 2):
            if i < nt_:
                st_mm(i)
            if 0 <= i - 1 < nt_ and "notaps" not in kffn:
                st_taps(i - 1)
            if 0 <= i - 2 < nt_ and "noglu" not in kffn:
                st_glu(i - 2)
        w_done(2)
        for g in range(6):
            wd_, kd_ = w_take("DN%d" % g, S)
            kc = 4 if g < 5 else 2
            wv = wd_[:, 0:kc * 1024].rearrange("p (c n) -> p c n", c=kc)
            for c in range(kc):
                f = g * 4 + c
                for tb in range(4 if "nodown" not in kffn else 0):
                    st, o = tb // 2, (tb % 2) * 128
                    for n2 in range(2):
                        bank = tb * 2 + n2
                        mm(PB(bank), actT[st][:, f, o:o + 128], wv[:, c, n2 * 512:(n2 + 1) * 512], f == 0, f == NFT - 1,
                           ["actT%d" % st, kd_], bk(bank))
            w_done()
        for tb in range(4):
            i = cnt["st"]
            cnt["st"] += 1
            q = i % 4
            a = cnt["x"] % 3
            cnt["x"] += 1
            xk = "xt%d" % a
            dma("pool", xt[a][:], hbuf[1 + base + tb * 128:1 + base + (tb + 1) * 128, :], ["hb%d" % S], [xk], "xin%d" % a)
            post_norm_residual(q, (tb * 2, tb * 2 + 1), xt[a], xk)
            dma("pool", y_out[S * SEG + tb * 128:S * SEG + (tb + 1) * 128, :], xt[a][:], [xk], ["yout"], "xst%d" % a)

    def dump(name, ap, keys):
        if name in dbg_out:
            dma("pool", dbg_out[name], ap, keys, ["dbg_" + name], "dbg_" + name)

    setup()
    prepass()
    load_gains(0)
    def cap(fn):
        P.begin_capture()
        fn()
        return P.end_capture()

    if "A" in passes:
        rwkv_state_init()
        orderA = list(reversed(range(nslot)))

        def stage_x(S):
            slot_load_norm(S)
            z_project(S, range(16))
            dma("pool", us_d[S], flat(uT[:]), ["uT"], ["us%d" % S], "uTst")
            dma("pool", zs_d[S], flat(zs[:]), ["zs%d" % i for i in range(16)], ["zsd%d" % S], "zsst")

        stage_x(orderA[0])
        for idx, S in enumerate(orderA):
            rwkv_slot_begin(S, 1)
            rwkv_group(S, 1, 1, False, parts=("pre", "minv"))
            P.interleave(cap(lambda: rwkv_group(S, 1, 1, False, parts=("scan",))),
                         cap(lambda: rwkv_group(S, 1, 0, False, parts=("pre",))))
            rwkv_group(S, 1, 0, False, parts=("minv",))
            nxt = cap(lambda: stage_x(orderA[idx + 1])) if idx + 1 < len(orderA) else []
            P.interleave(cap(lambda: rwkv_group(S, 1, 0, False, parts=("scan",))), nxt)
            dma("pool", ybwd_d[S], flat(yT[:]), ["yT"], ["ybwd%d" % S], "yTst")
            if S == 0:
                dump("ybwd", flat(yT[:]), ["yT"])
    if "B" in passes:
        rwkv_state_init()
        import os
        kstop = int(os.environ.get("K_STOP", "99"))

        def drain(tags, S):
            for t_ in tags:
                if t_.startswith("ZG") and "A" in passes:
                    continue
                w_take(t_, S)
                w_done()

        for S in range(nslot):
            if "A" in passes:
                dma("pool", flat(uT[:]), us_d[S], ["us%d" % S], ["uT"], "uTld")
            else:
                slot_load_norm(S)
            if kstop <= 1:
                drain(["QG", "KVG", "ZG0", "ZG1", "ZG2", "ZG3", "GA0", "GR0", "BAR0", "GA1", "GR1", "BAR1", "WO0", "WO1"], S)
                continue
            phase_barrier("att")
            qkv_project(S)
            if kstop <= 2:
                drain(["ZG0", "ZG1", "ZG2", "ZG3", "GA0", "GR0", "BAR0", "GA1", "GR1", "BAR1", "WO0", "WO1"], S)
                continue
            attention(S)
            if kstop <= 3:
                drain(["ZG0", "ZG1", "ZG2", "ZG3", "GA0", "GR0", "BAR0", "GA1", "GR1", "BAR1", "WO0", "WO1"], S)
                continue
            if "A" in passes:
                dma("pool", flat(zs[:]), zs_d[S], ["zsd%d" % S], ["zs%d" % i for i in range(16)], "zsld")
            else:
                z_project(S, range(16))
            if kstop <= 4:
                drain(["GA0", "GR0", "BAR0", "GA1", "GR1", "BAR1", "WO0", "WO1"], S)
                continue
            if "A" in passes:
                dma("pool", flat(yT[:]), ybwd_d[S], ["ybwd%d" % S], ["yT"], "yTld")
            else:
                P.add("dve", lambda e: e.memset(flat(yT[:]), 0.0), writes=["yT"])
            rwkv_slot_begin(S, 0)
            phase_barrier("rw")
            rwkv_group(S, 0, 0, True, parts=("pre", "minv"))
            P.interleave(cap(lambda: rwkv_group(S, 0, 0, True, parts=("scan",))),
                         cap(lambda: rwkv_group(S, 0, 1, True, parts=("pre",))))
            rwkv_group(S, 0, 1, True, parts=("minv", "scan"))
            if S == 0:
                dump("uT", flat(uT[:]), ["uT"])
                dump("oattn", flat(oattn[:]), ["oattn"])
                dump("zs", flat(zs[:]), ["zs%d" % i for i in range(16)])
                dump("yT", flat(yT[:]), ["yT"])
            if kstop <= 5:
                drain(["GA0", "GR0", "BAR0", "GA1", "GR1", "BAR1", "WO0", "WO1"], S)
                continue
            rwkv_epilogue(S)
            if kstop <= 6:
                drain(["GA0", "GR0", "BAR0", "GA1", "GR1", "BAR1", "WO0", "WO1"], S)
                continue
            merge_branches(S)
            if S == 0:
                dump("orw", flat(orw[:]), ["orw"])
                dump("mergedT", flat(mergedT[:, 0:8, :]), ["zs%d" % i for i in range(8)])
            if kstop <= 7:
                drain(["WO0", "WO1"], S)
                continue
            out_proj(S)
    if "C" in passes:
        load_gains(1)
        phase_barrier("ffn")
        for S in range(nslot):
            ffn_slot(S)
    P.add("pool", None, reads=["yout", "wscr"] + ["hb%d" % k for k in range(nslot)] + ["dbg_" + n for n in dbg_out]
          + ["ybwd%d" % k for k in range(nslot)])
    assert wstate["taken"] == len(wq) == wstate["released"], (wstate, len(wq))

    sems = P.assign(nc, es)
    with nc.Block() as block:
        @block.sync
        def _(e):
            P.run_engine("sp", e, sems)

        @block.tensor
        def _(e):
            P.run_engine("pe", e, sems)

        @block.scalar
        def _(e):
            P.run_engine("act", e, sems)

        @block.vector
        def _(e):
            P.run_engine("dve", e, sems)

        @block.gpsimd
        def _(e):
            P.run_engine("pool", e, sems)
    es.close()
    nc._n_ops = P.n
    return nc


def _assign_sequences():
    plan = [[("p", 0)]]
    counts = [5, 5, 5, 5, 4, 4, 4]
    k = 0
    for c in counts:
        plan.append([("s", k + i) for i in range(c)])
        k += c
    return plan


def kernel(**inputs):
    xp = np.asarray(inputs["x_prompt"], np.float32)
    xs = np.asarray(inputs["x_sample"], np.float32)
    wl = _layout_weights(inputs)
    plan = _assign_sequences()
    in_maps, places = [], []
    for core in range(NCORES):
        seqs = [xp[0] if kind == "p" else xs[i] for (kind, i) in plan[core]]
        xh, fl, place = _layout_core(seqs, SLOTS_FULL)
        m = dict(wl)
        m["xh"] = xh
        m["flags"] = fl
        in_maps.append(m)
        places.append(place)
    nc = build(SLOTS_FULL, "ABC")
    res = run_bass_kernel_spmd(nc, in_maps, core_ids=list(range(NCORES)))
    y_prompt = np.zeros_like(xp)
    y_sample = np.zeros_like(xs)
    for core in range(NCORES):
        y = np.asarray(res.results[core]["y"], np.float32)
        for (kind, i), (s0, n) in zip(plan[core], places[core]):
            blk = y[s0 * SEG:(s0 + n) * SEG]
            if kind == "p":
                y_prompt[0] = blk
            else:
                y_sample[i] = blk
    return (y_prompt, y_sample)
```

```python
import math
from contextlib import ExitStack

import numpy as np
import concourse.bass as bass
import concourse.mybir as mybir
from concourse.bass_utils import run_bass_kernel_spmd

F32 = mybir.dt.float32
BF16 = mybir.dt.bfloat16
AF = mybir.ActivationFunctionType
ALU = mybir.AluOpType

D = 1024
SEG = 512
HALO = 128
NTH = SEG + 2 * HALO
NBLK = NTH // 128
C = 64
GRP = 4
NGRP = SEG // (C * GRP)
GT = C * GRP
KAPPA = math.exp(-0.5)
NORM_EPS = 1e-6
GN_EPS = 64e-5
NW = 3
NCORES = 8
SLOTS_FULL = 32
D_FF = 2816
NFT = D_FF // 128

V_G1, V_G2, V_MUP, V_MUN, V_A0, V_KK, V_KA, V_RK, V_LNW, V_LNB, V_CW, V_CB = (
    0, 8, 16, 32, 48, 56, 60, 64, 68, 72, 76, 76 + 132)
NV = V_CB + 44
C_ID, C_BONES, C_M1, C_M2, C_TRI, C_I2, C_PEN = 0, 128, 256, 512, 640, 1408, 1472
NCST = C_PEN + 6 * 512

SEM_LIMIT = 12000


class Op:
    __slots__ = ("eng", "fn", "is_dma", "semkey", "waits", "needs_inc", "sem", "count")

    def __init__(self, eng, fn, is_dma, semkey):
        self.eng = eng
        self.fn = fn
        self.is_dma = is_dma
        self.semkey = semkey
        self.waits = []
        self.needs_inc = is_dma
        self.sem = None
        self.count = 0


class Prog:
    ENGS = ("pe", "act", "dve", "pool", "sp")

    def __init__(self):
        self.ops = {e: [] for e in self.ENGS}
        self.last_w = {}
        self.readers = {}
        self.dma_cnt = {}
        self.n = 0
        self.alias = {}

    def begin_capture(self):
        self._cap = []

    def end_capture(self):
        c, self._cap = self._cap, None
        return c

    def replay(self, items):
        for it in items:
            self.add(*it)

    def interleave(self, a, b):
        ia = ib = 0
        na, nb = len(a), len(b)
        while ia < na or ib < nb:
            if ib >= nb or (ia < na and ia * nb <= ib * na):
                self.add(*a[ia])
                ia += 1
            else:
                self.add(*b[ib])
                ib += 1

    def add(self, eng, fn, reads=(), writes=(), dma=False, semkey=None):
        if getattr(self, "_cap", None) is not None:
            self._cap.append((eng, fn, tuple(reads), tuple(writes), dma, semkey))
            return None
        op = Op(eng, fn, dma, semkey if dma else None)
        self.n += 1
        if self.alias:
            reads = list(reads) + [a for k in reads for a in self.alias.get(k, ())]
            writes = list(writes) + [a for k in writes for a in self.alias.get(k, ())]
        deps = []
        for k in reads:
            lw = self.last_w.get(k)
            if lw is not None:
                deps.append((lw, "raw"))
        for k in writes:
            lw = self.last_w.get(k)
            if lw is not None:
                deps.append((lw, "waw"))
            for r in self.readers.get(k, ()):
                deps.append((r, "war"))
        seen = set()
        for d, kind in deps:
            if d is op or id(d) in seen:
                continue
            if not d.is_dma and not dma and d.eng == eng:
                if eng == "pe":
                    continue
            seen.add(id(d))
            if d.is_dma:
                op.waits.append((d.sem, self.dma_cnt[d.semkey]))
            else:
                d.needs_inc = True
                op.waits.append(d)
        if dma:
            self.dma_cnt[semkey] = self.dma_cnt.get(semkey, 0) + 16
            op.sem = "d_" + str(semkey)
            op.count = self.dma_cnt[semkey]
        for k in reads:
            self.readers.setdefault(k, []).append(op)
        for k in writes:
            self.last_w[k] = op
            self.readers[k] = []
        self.ops[eng].append(op)
        return op

    def barrier(self, keys, engines=("pe", "act", "dve", "pool")):
        deps, seen = [], set()
        for k in keys:
            cand = list(self.readers.get(k, ()))
            if self.last_w.get(k) is not None:
                cand.append(self.last_w[k])
            for d in cand:
                if id(d) not in seen and d.fn is not None:
                    seen.add(id(d))
                    deps.append(d)
        for e in engines:
            op = Op(e, None, False, None)
            for d in deps:
                if d.is_dma:
                    op.waits.append((d.sem, self.dma_cnt[d.semkey]))
                elif not (d.eng == e and e == "pe"):
                    d.needs_inc = True
                    op.waits.append(d)
            self.ops[e].append(op)
        for k in keys:
            self.last_w.pop(k, None)
            self.readers[k] = []

    def assign(self, nc, es):
        sems = {}
        for e in self.ENGS:
            epoch, cnt = 0, 0
            for op in self.ops[e]:
                if not op.is_dma and op.needs_inc:
                    if cnt >= SEM_LIMIT:
                        epoch += 1
                        cnt = 0
                    cnt += 1
                    op.sem = "c_%s_%d" % (e, epoch)
                    op.count = cnt
        for e in self.ENGS:
            for op in self.ops[e]:
                if op.sem is not None and op.sem not in sems:
                    sems[op.sem] = es.enter_context(nc.semaphore(op.sem))
        return sems

    def run_engine(self, ename, eng, sems):
        seen = {}
        for op in self.ops[ename]:
            for d in op.waits:
                sname, cnt = d if isinstance(d, tuple) else (d.sem, d.count)
                if seen.get(sname, 0) >= cnt:
                    continue
                seen[sname] = cnt
                eng.wait_ge(sems[sname], cnt)
            if op.fn is None:
                continue
            ins = op.fn(eng)
            if op.is_dma:
                ins.then_inc(sems[op.sem], 16)
            elif op.needs_inc:
                ins.then_inc(sems[op.sem], 1)


def _fm(vec, ntile):
    return np.ascontiguousarray(np.asarray(vec, np.float32).reshape(ntile, 128).T)


def _constants():
    cst = np.zeros((128, NCST), np.float32)
    p = np.arange(128)
    cst[:, C_ID:C_ID + 128] = np.eye(128, dtype=np.float32)
    cst[:, C_BONES:C_BONES + 128] = (p[:, None] // 64 == p[None, :] // 64).astype(np.float32)
    s = (p % 64)[:, None]
    t = np.arange(64)[None, :]
    m1f = np.concatenate([(t > s), (t >= s)], axis=1).astype(np.float32)
    m1b = np.concatenate([(t < s), (t <= s)], axis=1).astype(np.float32)
    cst[:, C_M1:C_M1 + 128] = m1f
    cst[:, C_M1 + 128:C_M1 + 256] = m1b
    cst[:, C_M2:C_M2 + 64] = (t < s).astype(np.float32)
    cst[:, C_M2 + 64:C_M2 + 128] = (t > s).astype(np.float32)
    ps_, pt_ = p[:, None], p[None, :]
    same = (ps_ // 64 == pt_ // 64)
    for d in range(2):
        if d == 0:
            incl, excl, rest = (ps_ <= pt_), (ps_ < pt_), (ps_ > pt_)
        else:
            incl, excl, rest = (ps_ >= pt_), (ps_ > pt_), (ps_ < pt_)
        base = C_TRI + d * 384
        cst[:, base:base + 128] = (incl & same)
        cst[:, base + 128:base + 256] = (excl & same)
        cst[:, base + 256:base + 384] = (rest & same)
    cst[:, C_I2:C_I2 + 64] = (t == s).astype(np.float32)
    for g in range(2):
        for j in range(3):
            blk = np.zeros((128, 4, 128), np.float32)
            sk = p[:, None] + (j - 1) * 128
            qq = p[None, :]
            dist = np.abs(qq - sk).astype(np.float32)
            for i in range(4):
                h = g * 4 + i
                slope = 2.0 ** (-(h + 1))
                v = -8.0 * slope * dist
                v = np.where(dist <= 128, v, -240000.0)
                blk[:, i, :] = v
            base = C_PEN + (g * 3 + j) * 512
            cst[:, base:base + 512] = blk.reshape(128, 512)
    return cst


def _layout_weights(inp):
    L = 0
    w_in = np.asarray(inp["w_in"][L], np.float32)
    cols = []
    for i in range(4):
        cols += list(range(i * 64, (i + 1) * 64)) + list(range((4 + i) * 64, (5 + i) * 64))
    cols += list(range(512, 768))
    zc = list(range(768, 768 + 1952))
    w_in_p = np.zeros((1024, 38 * 128), np.float32)
    w_in_p[:, 0:768] = w_in[:, cols]
    w_in_p[:, 768:768 + 1952] = w_in[:, zc]
    w_in_p[:, 22 * 128:38 * 128] = w_in[:, 2720:4768]
    wba = np.asarray(inp["w_branch_attn"][L], np.float32)
    rows = []
    for i in range(4):
        rows += list(range(i * 64, (i + 1) * 64)) + list(range((4 + i) * 64, (5 + i) * 64))
    wba_p = np.ascontiguousarray(wba[rows, :])
    vecs = np.zeros((128, NV), np.float32)
    vecs[:, V_G1:V_G1 + 8] = _fm(inp["norm_mix_pre"][L], 8)
    vecs[:, V_G2:V_G2 + 8] = _fm(inp["norm_ffn_pre"][L], 8)
    mup = np.zeros(2048, np.float32)
    mun = np.zeros(2048, np.float32)
    mup[:1952] = np.asarray(inp["rw_mu_prev"][L])
    mun[:1952] = np.asarray(inp["rw_mu_next"][L])
    vecs[:, V_MUP:V_MUP + 16] = _fm(mup, 16)
    vecs[:, V_MUN:V_MUN + 16] = _fm(mun, 16)
    a0 = np.asarray(inp["rw_a0"][L], np.float32)
    for d in range(2):
        vecs[:, V_A0 + d * 4:V_A0 + d * 4 + 4] = _fm(a0[d], 4)
    vecs[:, V_KK:V_KK + 4] = _fm(inp["rw_k_k"][L], 4)
    vecs[:, V_KA:V_KA + 4] = _fm(inp["rw_k_a"][L], 4)
    vecs[:, V_RK:V_RK + 4] = _fm(np.asarray(inp["rw_r_k"][L]).reshape(512), 4)
    vecs[:, V_LNW:V_LNW + 4] = _fm(inp["rw_ln_w"][L], 4)
    vecs[:, V_LNB:V_LNB + 4] = _fm(inp["rw_ln_b"][L], 4)
    cw = np.asarray(inp["ffn_conv_w"][L], np.float32)
    for j in range(3):
        vecs[:, V_CW + j * 44:V_CW + (j + 1) * 44] = _fm(cw[j], 44)
    vecs[:, V_CB:V_CB + 44] = _fm(inp["ffn_conv_b"][L], 44)
    rows128 = np.zeros((128, 2, 1024), np.float32)
    rows128[:, 0, :] = np.asarray(inp["norm_mix_post"][L], np.float32)[None, :]
    rows128[:, 1, :] = np.asarray(inp["norm_ffn_post"][L], np.float32)[None, :]
    w0rows = np.ascontiguousarray(np.asarray(inp["rw_w0"][L], np.float32).reshape(1, 2, 512))
    sink = np.asarray(inp["attn_sink"][L], np.float32)
    sinkrows = np.zeros((1, 2, 4, 128), np.float32)
    for g in range(2):
        for i in range(4):
            sinkrows[0, g, i, :] = sink[g * 4 + i]
    sinkrows = sinkrows.reshape(1, 2, 512)
    lora = np.zeros((128, 4, 512), np.float32)
    lora[:, 0, :] = np.asarray(inp["rw_w2"][L], np.float32).reshape(128, 512)
    lora[:, 1, :] = np.asarray(inp["rw_a2"][L], np.float32).reshape(128, 512)
    g2 = np.asarray(inp["rw_g2"][L], np.float32)
    lora[:, 2, :] = g2[0:128]
    lora[0:32, 3, :] = g2[128:160]
    return {
        "w_in": w_in_p, "wba": wba_p,
        "wbr": np.ascontiguousarray(np.asarray(inp["w_branch_rwkv"][L], np.float32)),
        "wout": np.ascontiguousarray(np.asarray(inp["w_out"][L], np.float32)),
        "wup": np.ascontiguousarray(np.asarray(inp["w_ffn_up"][L], np.float32)),
        "wdn": np.ascontiguousarray(np.asarray(inp["w_ffn_down"][L], np.float32)),
        "vecs": vecs, "rows": rows128, "w0rows": w0rows, "sinkrows": sinkrows, "lora": lora,
        "cst": _constants(),
    }


def _layout_core(seqs, nslot):
    xh = np.zeros((nslot, NTH, D), np.float32)
    flags = np.zeros((nslot, 2), np.float32)
    s = 0
    place = []
    for x in seqs:
        T = x.shape[0]
        n = T // SEG
        xp = np.zeros((T + 2 * HALO, D), np.float32)
        xp[HALO:HALO + T] = x
        for k in range(n):
            xh[s + k] = xp[k * SEG:k * SEG + NTH]
            flags[s + k, 0] = 1.0 if k > 0 else 0.0
            flags[s + k, 1] = 1.0 if k < n - 1 else 0.0
        place.append((s, n))
        s += n
    fl = np.ascontiguousarray(np.broadcast_to(flags.reshape(1, nslot * 2), (128, nslot * 2))).astype(np.float32)
    return xh, fl, place


def weight_schedule(nslot, passes):
    q = []
    if "A" in passes:
        for s in reversed(range(nslot)):
            q += [("ZG0", s), ("ZG1", s), ("ZG2", s), ("ZG3", s)]
    if "B" in passes:
        for s in range(nslot):
            q += [("QG", s), ("KVG", s)]
            if "A" not in passes:
                q += [("ZG0", s), ("ZG1", s), ("ZG2", s), ("ZG3", s)]
            for h in range(2):
                q += [("GA%d" % h, s), ("GR%d" % h, s), ("BAR%d" % h, s)]
            q += [("WO0", s), ("WO1", s)]
    if "C" in passes:
        for s in range(nslot):
            for g in range(11):
                q += [("UF%d" % g, s)]
            for g in range(6):
                q += [("DN%d" % g, s)]
    return q


def build(nslot, passes="ABC", dbg=()):
    nc = bass.Bass("TRN2", target_bir_lowering=False)

    def din(name, shape, dt=F32):
        return nc.dram_tensor(name, list(shape), dt, kind="ExternalInput").ap()

    def dout(name, shape, dt=F32):
        return nc.dram_tensor(name, list(shape), dt, kind="ExternalOutput").ap()

    def dscr(name, shape, dt):
        return nc.dram_tensor(name, list(shape), dt).ap()

    xh = din("xh", [nslot, NTH, D])
    flags_d = din("flags", [128, 2 * nslot])
    w_in_d = din("w_in", [1024, 4864])
    wba_d = din("wba", [512, 1024])
    wbr_d = din("wbr", [512, 1024])
    wout_d = din("wout", [1024, 1024])
    wup_d = din("wup", [1024, 5632])
    wdn_d = din("wdn", [2816, 1024])
    vecs_d = din("vecs", [128, NV])
    rows_d = din("rows", [128, 2, 1024])
    w0_d = din("w0rows", [1, 2, 512])
    sink_d = din("sinkrows", [1, 2, 512])
    lora_d = din("lora", [128, 4, 512])
    cst_d = din("cst", [128, NCST])
    y_out = dout("y", [nslot * SEG, D])
    dbg_out = {}
    for name, shape in dbg:
        dbg_out[name] = dout("dbg_" + name, shape)

    w_in_b = dscr("w_in_b", [1024, 4864], BF16)
    wba_b = dscr("wba_b", [512, 1024], BF16)
    wbr_b = dscr("wbr_b", [512, 1024], BF16)
    wout_b = dscr("wout_b", [1024, 1024], BF16)
    wup_b = dscr("wup_b", [1024, 5632], BF16)
    wdn_b = dscr("wdn_b", [2816, 1024], BF16)
    ybwd_d = dscr("ybwd", [nslot, 128, 4 * SEG], F32)
    us_d = dscr("us_scr", [nslot, 128, 8 * NTH], BF16)
    zs_d = dscr("zs_scr", [nslot, 128, 16 * SEG], BF16)
    hbuf = dscr("hbuf", [nslot * SEG + 2, D], F32)

    P = Prog()
    es = ExitStack()

    def sb(name, shape, dt):
        return es.enter_context(nc.sbuf_tensor("s_" + name, list(shape), dt))

    psum = es.enter_context(nc.psum_tensor("psum", [128, 4096], F32))

    def PB(b, lo=0, hi=512):
        return psum[:, b * 512 + lo:b * 512 + hi]

    def bk(*banks):
        r = []
        for b in banks:
            r += ["pb%da" % b, "pb%db" % b]
        return r

    pbT = psum[:, 7 * 512:8 * 512].bitcast(BF16)

    ident_b = sb("ident_b", [128, 128], BF16)
    ones_b = sb("ones_b", [128, 128], BF16)
    bones_f = sb("bones_f", [128, 128], F32)
    bones_b = sb("bones_b", [128, 128], BF16)
    M1 = sb("M1", [128, 2, 128], F32)
    M2 = sb("M2", [128, 2, 64], F32)
    TRI = sb("TRI", [128, 2, 384], F32)
    I2 = sb("I2", [128, 64], BF16)
    PEN = sb("PEN", [128, 6, 512], BF16)
    vecs = sb("vecs", [128, NV], F32)
    c0 = sb("c0", [128, 16], F32)
    omka = sb("omka", [128, 4], F32)
    rows = sb("rows", [128, 1024], F32)
    gB = sb("gB", [128, 8, 128], F32)
    flags = sb("flags", [128, 2 * nslot], F32)
    rb = sb("rb", [2, 4, 512], BF16)
    w2b = sb("w2b", [128, 512], BF16)
    a2b = sb("a2b", [128, 512], BF16)
    g2b = sb("g2b", [128, 2, 512], BF16)
    onesf = sb("onesf", [128, 128], F32)

    xt = [sb("xt%d" % i, [128, 1024], F32) for i in range(3)]
    ub = [sb("ub%d" % i, [128, 1024], BF16) for i in range(2)]
    st_ssq = [sb("ssq%d" % i, [128, 2], F32) for i in range(4)]
    st_rstd = [sb("rstd%d" % i, [128, 1], F32) for i in range(4)]
    junk = sb("junk", [128, 1024], BF16)
    uT = sb("uT", [128, 8, NTH], BF16)
    wbuf = [sb("wbuf%d" % i, [128, 4096], BF16) for i in range(NW)]

    cnt = {"x": 0, "st": 0}

    def dma(eng, out, in_, reads, writes, semkey):
        P.add(eng, lambda e: e.dma_start(out=out, in_=in_), reads=reads, writes=writes, dma=True, semkey=semkey + "_" + eng)

    def act_copy(out, in_, reads, writes):
        P.add("act", lambda e: e.copy(out, in_), reads=reads, writes=writes)

    def dve_copy(out, in_, reads, writes):
        P.add("dve", lambda e: e.tensor_copy(out, in_), reads=reads, writes=writes)

    def mm(out, lhsT, rhs, start, stop, r, w, tp=None):
        if tp is None:
            P.add("pe", lambda e: e.matmul(out, lhsT, rhs, start=start, stop=stop), reads=r, writes=w)
        else:
            P.add("pe", lambda e: e.matmul(out, lhsT, rhs, start=start, stop=stop, tile_position=tp), reads=r, writes=w)

    def actf(out, in_, func, r, w, bias=None, scale=None, accum=None):
        kw = {}
        if bias is not None:
            kw["bias"] = bias
        if scale is not None:
            kw["scale"] = scale
        if accum is not None:
            kw["accum_out"] = accum
        P.add("act", lambda e: e.activation(out=out, in_=in_, func=func, **kw), reads=r, writes=w)

    def tt(eng, out, in0, in1, op, r, w):
        P.add(eng, lambda e: e.tensor_tensor(out=out, in0=in0, in1=in1, op=op), reads=r, writes=w)

    def stt(out, in0, scalar, in1, op0, op1, r, w):
        P.add("dve", lambda e: e.scalar_tensor_tensor(out=out, in0=in0, scalar=scalar, in1=in1, op0=op0, op1=op1), reads=r, writes=w)

    def tsc(eng, out, in0, s1, s2, op0, op1, r, w):
        P.add(eng, lambda e: e.tensor_scalar(out, in0, s1, s2, op0, op1), reads=r, writes=w)

    def tsmul(eng, out, in0, s, r, w):
        P.add(eng, lambda e: e.tensor_scalar_mul(out, in0, s), reads=r, writes=w)

    def recip(out, in_, r, w):
        P.add("dve", lambda e: e.reciprocal(out, in_), reads=r, writes=w)

    def load_gains(which):
        dma("pool", rows[:], rows_d[:, which, :], [], ["rows"], "ld_rows")
        for c in range(8):
            col = (V_G1 if which == 0 else V_G2) + c
            actf(gB[:, c, :], onesf[:], AF.Copy, ["vecs", "onesf"], ["gB"], scale=vecs[:, col:col + 1])

    def setup():
        stg = xt[0]
        dma("pool", vecs[:], vecs_d, [], ["vecs"], "ld_vecs")
        dma("pool", flags[:], flags_d, [], ["flags"], "ld_flags")
        dma("pool", stg[:, 0:1024], cst_d[:, 0:1024], [], ["xt0"], "xin0")
        dve_copy(ident_b[:], stg[:, C_ID:C_ID + 128], ["xt0"], ["ident_b"])
        dve_copy(bones_f[:], stg[:, C_BONES:C_BONES + 128], ["xt0"], ["bones_f"])
        dve_copy(bones_b[:], stg[:, C_BONES:C_BONES + 128], ["xt0"], ["bones_b"])
        dve_copy(M1[:].rearrange("p a b -> p (a b)"), stg[:, C_M1:C_M1 + 256], ["xt0"], ["M1"])
        dve_copy(M2[:].rearrange("p a b -> p (a b)"), stg[:, C_M2:C_M2 + 128], ["xt0"], ["M2"])
        dve_copy(TRI[:, 0, :], stg[:, C_TRI:C_TRI + 384], ["xt0"], ["TRI"])
        dma("pool", xt[1][:, 0:448], cst_d[:, 1024:1472], [], ["xt1"], "xin1")
        dve_copy(TRI[:, 1, :], xt[1][:, 0:384], ["xt1"], ["TRI"])
        dve_copy(I2[:], xt[1][:, 384:448], ["xt1"], ["I2"])
        for k in range(3):
            t = xt[(k + 2) % 3]
            key = "xt%d" % ((k + 2) % 3)
            dma("pool", t[:], cst_d[:, C_PEN + k * 1024:C_PEN + (k + 1) * 1024], [], [key], "xin%d" % ((k + 2) % 3))
            dve_copy(PEN[:, 2 * k:2 * k + 2, :].rearrange("p a b -> p (a b)"), t[:], [key], ["PEN"])
        P.add("dve", lambda e: e.memset(ones_b[:], 1.0), writes=["ones_b"])
        P.add("dve", lambda e: e.memset(onesf[:], 1.0), writes=["onesf"])
        P.add("dve", lambda e: e.memset(epsc[:], GN_EPS), writes=["epsc"])
        dma("pool", xt[0][:, 0:1024], lora_d[:, 0:2, :].rearrange("p a b -> p (a b)"), [], ["xt0"], "xin0")
        dve_copy(w2b[:], xt[0][:, 0:512], ["xt0"], ["w2b"])
        dve_copy(a2b[:], xt[0][:, 512:1024], ["xt0"], ["a2b"])
        dma("pool", xt[1][:, 0:1024], lora_d[:, 2:4, :].rearrange("p a b -> p (a b)"), [], ["xt1"], "xin1")
        dve_copy(g2b[:].rearrange("p a b -> p (a b)"), xt[1][:, 0:1024], ["xt1"], ["g2b"])
        tt("dve", c0[:], vecs[:, V_MUP:V_MUP + 16], vecs[:, V_MUN:V_MUN + 16], ALU.add, ["vecs"], ["c0"])
        tsc("dve", c0[:], c0[:], -1.0, 1.0, ALU.mult, ALU.add, ["c0"], ["c0"])
        tsc("dve", omka[:], vecs[:, V_KA:V_KA + 4], -1.0, 1.0, ALU.mult, ALU.add, ["vecs"], ["omka"])
        w0f = xt[2][0:1, 0:1024].rearrange("p (a b) -> p a b", a=2)
        w0t = xt[0][0:1, 0:1024].rearrange("p (a b) -> p a b", a=2)
        sinkf = xt[1][0:1, 0:1024].rearrange("p (a b) -> p a b", a=2)
        lo_b = ub[0][0:1, 0:1024].rearrange("p (a b) -> p a b", a=2)
        dma("pool", w0f, w0_d, [], ["xt2"], "xin2")
        dma("pool", sinkf, sink_d, [], ["xt1"], "xin1")
        dve_copy(rb[0:1, 0:2, :], w0f, ["xt2"], ["rb"])
        dve_copy(w0t, rb[0:1, 0:2, :], ["rb"], ["xt0"])
        tt("dve", w0t, w0f, w0t, ALU.subtract, ["xt2", "xt0"], ["xt0"])
        dve_copy(lo_b, w0t, ["xt0"], ["ub0"])
        dma("pool", rb[1:2, 0:2, :], lo_b, ["ub0"], ["rb"], "ubst0")
        actf(rb[0:1, 2:4, :], sinkf, AF.Exp, ["xt1"], ["rb"])
        P.add("dve", lambda e: e.memset(xt[2][0:1, :], 0.0), reads=["xt2"], writes=["xt2"])
        dma("pool", hbuf[0:1, :], xt[2][0:1, :], ["xt2"], ["hb_first"], "xst2")
        dma("pool", hbuf[nslot * SEG + 1:nslot * SEG + 2, :], xt[2][0:1, :], ["xt2"], ["hb_last"], "xst2")

    def prepass():
        jobs = []
        for (src, dst, R, Cc) in ((w_in_d, w_in_b, 1024, 4864), (wba_d, wba_b, 512, 1024), (wbr_d, wbr_b, 512, 1024),
                                  (wout_d, wout_b, 1024, 1024), (wup_d, wup_b, 1024, 5632), (wdn_d, wdn_b, 2816, 1024)):
            for rc in range(R // 128):
                for c0_ in range(0, Cc, 1024):
                    w = min(1024, Cc - c0_)
                    jobs.append((src[rc * 128:(rc + 1) * 128, c0_:c0_ + w], dst[rc * 128:(rc + 1) * 128, c0_:c0_ + w], w))
        for i, (s_ap, d_ap, w) in enumerate(jobs):
            a = i % 3
            b = i % 2
            dma("sp", xt[a][:, 0:w], s_ap, [], ["xt%d" % a], "xin%d" % a)
            if i % 2 == 0:
                dve_copy(ub[b][:, 0:w], xt[a][:, 0:w], ["xt%d" % a], ["ub%d" % b])
            else:
                act_copy(ub[b][:, 0:w], xt[a][:, 0:w], ["xt%d" % a], ["ub%d" % b])
            dma("pool", d_ap, ub[b][:, 0:w], ["ub%d" % b], ["wscr"], "ubst%d" % b)

    wq = weight_schedule(nslot, passes)
    wstate = {"issued": 0, "taken": 0, "released": 0}
    wfm = lambda t: t.rearrange("(c p) n -> p c n", p=128)

    def wsrc(tag):
        if tag == "QG":
            return wfm(w_in_b)[:, :, 0:512], 8, 512
        if tag == "KVG":
            return wfm(w_in_b)[:, :, 512:768], 8, 256
        if tag.startswith("ZG"):
            g = int(tag[2])
            return wfm(w_in_b)[:, :, 768 + g * 512:768 + (g + 1) * 512], 8, 512
        if tag.startswith("GA"):
            h = int(tag[2])
            return wfm(w_in_b)[:, :, 2816 + h * 512:2816 + (h + 1) * 512], 8, 512
        if tag.startswith("GR"):
            h = int(tag[2])
            return wfm(w_in_b)[:, :, 3840 + h * 512:3840 + (h + 1) * 512], 8, 512
        if tag.startswith("WO"):
            h = int(tag[2])
            return wfm(wout_b)[:, :, h * 512:(h + 1) * 512], 8, 512
        if tag.startswith("UG"):
            g = int(tag[2])
            n = 512 if g < 5 else 256
            return wfm(wup_b)[:, :, g * 512:g * 512 + n], 8, n
        if tag.startswith("UU"):
            g = int(tag[2])
            n = 512 if g < 5 else 256
            return wfm(wup_b)[:, :, 2816 + g * 512:2816 + g * 512 + n], 8, n
        if tag.startswith("DN"):
            g = int(tag[2])
            kc = 4 if g < 5 else 2
            return wfm(wdn_b)[:, g * 4:g * 4 + kc, :], kc, 1024
        raise KeyError(tag)

    def w_issue():
        i = wstate["issued"]
        tag, _ = wq[i]
        b = i % NW
        if tag.startswith("UF"):
            g = int(tag[2:])
            v = wbuf[b][:, :].rearrange("p (a c n) -> p a c n", a=2, c=8)
            dma("sp", v[:, 0, :, :], wfm(wup_b)[:, :, g * 256:(g + 1) * 256], ["wscr"], ["wbuf%d" % b], "w%d" % b)
            dma("sp", v[:, 1, :, :], wfm(wup_b)[:, :, 2816 + g * 256:2816 + (g + 1) * 256], ["wscr"], ["wbuf%d" % b], "w%d" % b)
        elif tag.startswith("BAR"):
            h = int(tag[3])
            v = wbuf[b][:, :].rearrange("p (a c n) -> p a c n", a=2, c=4)
            dma("sp", v[:, 0, :, :], wfm(wba_b)[:, :, h * 512:(h + 1) * 512], ["wscr"], ["wbuf%d" % b], "w%d" % b)
            dma("sp", v[:, 1, :, :], wfm(wbr_b)[:, :, h * 512:(h + 1) * 512], ["wscr"], ["wbuf%d" % b], "w%d" % b)
        else:
            src, kc, n = wsrc(tag)
            v = wbuf[b][:, 0:kc * n].rearrange("p (c n) -> p c n", c=kc)
            dma("sp", v, src, ["wscr"], ["wbuf%d" % b], "w%d" % b)
        wstate["issued"] += 1

    def w_pump():
        while wstate["issued"] < len(wq) and wstate["issued"] - wstate["released"] < NW:
            w_issue()

    def w_take(tag, slot):
        i = wstate["taken"]
        assert wq[i] == (tag, slot), (wq[i], tag, slot)
        if wstate["issued"] <= i:
            assert wstate["issued"] - wstate["released"] < NW, "too many weight groups held"
            w_issue()
        wstate["taken"] += 1
        b = i % NW
        return wbuf[b], "wbuf%d" % b

    def w_done(k=1):
        wstate["released"] += k
        assert wstate["released"] <= wstate["taken"]
        w_pump()

    def norm_block(src_ap, nrow, which, dst, dst_key, col0, src_reads=()):
        i = cnt["x"]
        cnt["x"] += 1
        a, b, q = i % 3, i % 2, i % 4
        xk, uk = "xt%d" % a, "ub%d" % b
        dma("pool", xt[a][0:nrow, :], src_ap, list(src_reads), [xk], "xin%d" % a)
        P.add("act", lambda e: e.activation(out=junk[0:nrow, :], in_=xt[a][0:nrow, :], func=AF.Square, accum_out=st_ssq[q][0:nrow, 0:1]),
              reads=[xk], writes=["junk", "ssq%d" % q])
        P.add("dve", lambda e: e.tensor_scalar(st_rstd[q][0:nrow, :], st_ssq[q][0:nrow, 0:1], 1.0 / D, NORM_EPS, ALU.mult, ALU.add),
              reads=["ssq%d" % q], writes=["rstd%d" % q])
        P.add("act", lambda e: e.activation(out=st_rstd[q][0:nrow, :], in_=st_rstd[q][0:nrow, :], func=AF.Sqrt),
              reads=["rstd%d" % q], writes=["rstd%d" % q])
        P.add("dve", lambda e: e.reciprocal(st_rstd[q][0:nrow, :], st_rstd[q][0:nrow, :]), reads=["rstd%d" % q], writes=["rstd%d" % q])
        P.add("dve", lambda e: e.tensor_scalar_mul(ub[b][0:nrow, :], xt[a][0:nrow, :], st_rstd[q][0:nrow, :]),
              reads=[xk, "rstd%d" % q], writes=[uk])
        for c in range(8):
            P.add("pe", lambda e, c=c: e.transpose(pbT[:, c * 128:c * 128 + nrow], ub[b][0:nrow, c * 128:(c + 1) * 128], ident_b[0:nrow, 0:nrow]),
                  reads=[uk, "ident_b"], writes=bk(7))
        P.add("dve", lambda e: e.tensor_tensor(out=dst[:, :, col0:col0 + nrow],
                                               in0=pbT[:, :].rearrange("p (c n) -> p c n", c=8)[:, :, 0:nrow],
                                               in1=gB[:, :, 0:nrow], op=ALU.mult),
              reads=bk(7) + ["gB"], writes=[dst_key])

    ARENA_BYTES = 73 * 1024
    arena = sb("arena", [128, ARENA_BYTES // 4], F32)
    carve_state = {}

    def carve(group, name, shape, dt):
        off = carve_state.get(group, 0)
        nel = 1
        for d_ in shape[1:]:
            nel *= d_
        nbytes = nel * (4 if dt == F32 else 2)
        nbytes_al = (nbytes + 31) // 32 * 32
        assert off + nbytes_al <= ARENA_BYTES, (group, name, off, nbytes_al)
        carve_state[group] = off + nbytes_al
        v = arena[0:shape[0], off // 4:(off + nbytes) // 4]
        if dt != F32:
            v = v.bitcast(dt)
        if len(shape) == 3:
            v = v.rearrange("p (a b) -> p a b", a=shape[1])
        elif len(shape) == 4:
            v = v.rearrange("p (a b c) -> p a b c", a=shape[1], b=shape[2])
        elif len(shape) == 5:
            v = v.rearrange("p (a b c d) -> p a b c d", a=shape[1], b=shape[2], c=shape[3])
        groups.setdefault(group, []).append(name)
        return v

    groups = {}
    tmp4 = [sb("tmp%d" % i, [128, 512], F32) for i in range(4)]
    carve_state["att"] = 40 * 1024
    qT = carve("att", "qT", [128, 4, SEG], BF16)
    kT = carve("att", "kT", [128, NTH], BF16)
    vtok = carve("att", "vtok", [128, NBLK, 128], BF16)
    ptb = [carve("att", "ptb%d" % i, [128, 512], BF16) for i in range(3)]
    rden = carve("att", "rden", [128, 512], F32)
    fones = sb("fones", [128, 2, 64], BF16)
    oattn = sb("oattn", [128, 4, SEG], BF16)
    zs = sb("zs", [128, 16, SEG], BF16)
    ztmp = [tmp4[0], tmp4[1]]
    ztmp2 = [tmp4[2], tmp4[3]]
    ps_rot = {"d": 0}

    def nbank():
        b = ps_rot["d"] % 2
        ps_rot["d"] += 1
        return b

    def slot_load_norm(S):
        for b in range(NBLK):
            norm_block(xh[S, b * 128:(b + 1) * 128, :], 128, 0, uT, "uT", b * 128)

    def proj_fm_tile(wb, wkey, kc, n, col, src, skey, tok_lo, ntok, bank):
        wv = wb[:, 0:kc * n].rearrange("p (c n) -> p c n", c=kc)
        for c in range(kc):
            mm(PB(bank, 0, ntok), wv[:, c, col:col + 128], src[:, c, tok_lo:tok_lo + ntok], c == 0, c == kc - 1,
               [wkey, skey], bk(bank))

    def qkv_project(S):
        wb, wkey = w_take("QG", S)
        for i in range(4):
            bank = nbank()
            proj_fm_tile(wb, wkey, 8, 512, i * 128, uT, "uT", HALO, SEG, bank)
            act_copy(qT[:, i, :], PB(bank), bk(bank), ["qT"])
        w_done()
        wb, wkey = w_take("KVG", S)
        for (lo, n) in ((0, 512), (512, 256)):
            bank = nbank()
            proj_fm_tile(wb, wkey, 8, 256, 0, uT, "uT", lo, n, bank)
            act_copy(kT[:, lo:lo + n], PB(bank, 0, n), bk(bank), ["kT"])
        wv = wb[:, 0:8 * 256].rearrange("p (c n) -> p c n", c=8)
        for half in range(2):
            bank = nbank()
            for bb in range(3):
                b = half * 3 + bb
                for c in range(8):
                    mm(PB(bank, bb * 128, (bb + 1) * 128), uT[:, c, b * 128:(b + 1) * 128], wv[:, c, 128:256], c == 0, c == 7,
                       [wkey, "uT"], bk(bank))
            dve_copy(vtok[:, half * 3:half * 3 + 3, :].rearrange("p a b -> p (a b)"), PB(bank, 0, 384), bk(bank), ["vtok"])
        w_done()

    def attention(S):
        fp = flags[:, 2 * S:2 * S + 1]
        tsmul("dve", fones[:, 0, :], ones_b[:, 0:64], fp, ["ones_b", "flags"], ["fones"])
        tsmul("dve", vtok[:, 1, :], vtok[:, 1, :], fp, ["vtok", "flags"], ["vtok"])
        nqb = SEG // 128
        for qb in range(nqb):
            nb, db = 3 + (qb % 2), 5 + (qb % 2)
            for g in range(2):
                pr = slice(g * 64, (g + 1) * 64)
                for j in range(3):
                    kb = qb + j
                    sbk = (qb * 6 + g * 3 + j) % 3
                    pk = "ptb%d" % sbk
                    mm(PB(sbk), kT[pr, kb * 128:(kb + 1) * 128], qT[pr, :, qb * 128:(qb + 1) * 128], True, False,
                       ["kT", "qT"], bk(sbk), tp=(g * 64, 0))
                    mm(PB(sbk), ident_b[:], PEN[:, g * 3 + j, :], False, True, ["ident_b", "PEN"], bk(sbk))
                    actf(ptb[sbk][:], PB(sbk), AF.Exp, bk(sbk), [pk], scale=0.125)
                    mm(PB(nb)[pr, :], vtok[:, kb, pr], ptb[sbk][:], j == 0, j == 2, ["vtok", pk], bk(nb), tp=(0, g * 64))
                    if kb <= 1:
                        dl, dk = fones[:, 0, :], "fones"
                    else:
                        dl, dk = ones_b[:, 0:64], "ones_b"
                    mm(PB(db)[pr, :], dl, ptb[sbk][:], j == 0, False, [dk, pk], bk(db), tp=(0, g * 64))
                mm(PB(db)[pr, :], ones_b[0:1, 0:64], rb[0:1, 2 + g, :], False, True, ["ones_b", "rb"], bk(db), tp=(0, g * 64))
            recip(rden[:], PB(db), bk(db), ["rden"])
            tt("dve", oattn[:, :, qb * 128:(qb + 1) * 128], PB(nb).rearrange("p (a b) -> p a b", a=4),
               rden[:].rearrange("p (a b) -> p a b", a=4), ALU.mult, bk(nb) + ["rden"], ["oattn"])

    ZSUB = ((0, 510), (510, 2))

    def z_project(S, tiles):
        tasks = [(zt, o, n) for zt in tiles for (o, n) in ZSUB]
        zb = [tmp4[0], tmp4[1], tmp4[2]]
        zk = ["ztmp0", "ztmp1", "ztmq0"]
        cur = [None]

        def stage_mm(i):
            zt, o, n = tasks[i]
            g = zt // 4
            if cur[0] is None or cur[0][0] != g:
                if cur[0] is not None:
                    w_done()
                wb, wkey = w_take("ZG%d" % g, S)
                cur[0] = (g, wb, wkey)
            _, wb, wkey = cur[0]
            bank = i % 2
            proj_fm_tile(wb, wkey, 8, 512, (zt % 4) * 128, uT, "uT", HALO + o - 1, n + 2, bank)
            actf(zb[i % 3][:, 0:n], PB(bank, 1, n + 1), AF.Copy, bk(bank) + ["c0"], [zk[i % 3]], scale=c0[:, zt:zt + 1])

        def stage_taps(i):
            zt, o, n = tasks[i]
            bank = i % 2
            stt(tmp4[3][:, 0:n], PB(bank, 0, n), vecs[:, V_MUP + zt:V_MUP + zt + 1], zb[i % 3][:, 0:n], ALU.mult, ALU.add,
                bk(bank) + ["vecs", zk[i % 3]], ["ztmq1"])
            stt(zs[:, zt, o:o + n], PB(bank, 2, n + 2), vecs[:, V_MUN + zt:V_MUN + zt + 1], tmp4[3][:, 0:n], ALU.mult, ALU.add,
                bk(bank) + ["vecs", "ztmq1"], ["zs%d" % zt])

        nt_ = len(tasks)
        for i in range(nt_ + 1):
            if i < nt_:
                stage_mm(i)
            if i >= 1:
                stage_taps(i - 1)
        w_done()

    def rw(name, shape, dt):
        return carve("rw", name, shape, dt)

    twd = rw("twd", [128, GT], BF16)
    sigtok = [rw("sigtok%d" % i, [128, 512], F32) for i in range(2)]
    E = [rw("E%d" % i, [128, 4, GT], F32) for i in range(2)]
    kq = rw("kq", [128, GT], F32)
    ksq = rw("ksq", [128, GT], F32)
    nrm = rw("nrm", [128, GT], F32)
    kk = rw("kk", [128, GT], F32)
    asig = [rw("asig%d" % i, [128, GT], F32) for i in range(2)]
    kdir = [rw("kdir%d" % i, [128, GT], F32) for i in range(2)]
    t1 = rw("t1", [128, GT], F32)
    bbv = rw("bbv", [128, GT], F32)
    bbar = rw("bbar", [128, GT], BF16)
    kbar = rw("kbar", [128, GT], BF16)
    AR = [rw("AR%d" % i, [128, 4, GRP, 2, C], BF16) for i in range(2)]
    BT = [rw("BT%d" % i, [128, 4, GT], BF16) for i in range(2)]
    KTl = [rw("KTl%d" % i, [128, 4, GT], BF16) for i in range(2)]
    bbt = [rw("bbt%d" % i, [128, 4, GRP, C], BF16) for i in range(2)]
    kbt = [rw("kbt%d" % i, [128, 4, GRP, C], BF16) for i in range(2)]
    vt = [rw("vt%d" % i, [128, 4, GRP, C], BF16) for i in range(2)]
    GC = [rw("GC%d" % i, [128, 4, GRP], F32) for i in range(2)]
    GB1 = rw("GB1", [128, 16, 128], BF16)
    GB2 = rw("GB2", [128, 16, 128], BF16)
    Pk = [rw("Pk%d" % i, [128, 16, 2, C], BF16) for i in range(2)]
    Zk = [rw("Zk%d" % i, [128, 16, C], BF16) for i in range(2)]
    groups["rw"] += ["Pk0a", "Pk0b", "Pk1a", "Pk1b", "Zk0a", "Zk0b", "Zk1a", "Zk1b"]
    Wb = rw("Wb", [128, 4, C], BF16)
    Ub = rw("Ub", [128, 4, C], BF16)
    Tf = sb("Tf", [128, 4, C], F32)
    Tst = sb("Tst", [128, 4, C], BF16)
    yT = sb("yT", [128, 4, SEG], F32)
    bp = sb("bp", [128, 4, SEG], BF16)

    def flat(ap3):
        return ap3.rearrange("p a b -> p (a b)")

    def rwkv_state_init():
        P.add("dve", lambda e: e.memset(flat(Tf[:]), 0.0), writes=["Tf"])

    def rwkv_slot_begin(S, d):
        col = 2 * S + (0 if d == 0 else 1)
        tsmul("dve", flat(Tf[:]), flat(Tf[:]), flags[:, col:col + 1], ["Tf", "flags"], ["Tf"])
        act_copy(flat(Tst[:]), flat(Tf[:]), ["Tf"], ["Tst"])

    def rwkv_group(S, d, gi, passB, parts=("pre", "minv", "scan")):
        gb = gi % 2
        t0 = gi * GT
        dr = slice(d * 64, (d + 1) * 64)
        ARk, BTk, KTk, bbtk, kbtk, vtk, GCk = ("AR%d" % gb, "BT%d" % gb, "KTl%d" % gb, "bbt%d" % gb, "kbt%d" % gb,
                                               "vt%d" % gb, "GC%d" % gb)
        Zf, Zfk = Zk[1], "Zk1"
        if "pre" in parts:
            actf(twd[dr, :], zs[dr, 12, t0:t0 + GT], AF.Tanh, ["zs12"], ["twd"])
            for blk in range(2):
                mm(PB(blk), twd[dr, blk * 128:(blk + 1) * 128], w2b[dr, :], True, False, ["twd", "w2b"], bk(blk), tp=(d * 64, 0))
                mm(PB(blk), ones_b[0:2, 0:128], rb[0:2, d, :], False, True, ["ones_b", "rb"], bk(blk))
                actf(sigtok[blk][:], PB(blk), AF.Sigmoid, bk(blk), ["sigtok%d" % blk])
            for ct in range(4):
                eb = ct % 2
                Et, Ek = E[eb], "E%d" % eb
                for blk in range(2):
                    bank = 2 + blk
                    mm(PB(bank, 0, 384), sigtok[blk][:, ct * 128:(ct + 1) * 128], TRI[:, d, :], True, True,
                       ["sigtok%d" % blk, "TRI"], bk(bank))
                    actf(Et[:, 0:3, blk * 128:(blk + 1) * 128], PB(bank, 0, 384).rearrange("p (a b) -> p a b", a=3), AF.Exp,
                         bk(bank), [Ek], scale=-KAPPA)
                    actf(Et[:, 3, blk * 128:(blk + 1) * 128], PB(bank, 0, 128), AF.Exp, bk(bank), [Ek], scale=KAPPA)
                kz, kzk = zs[:, 4 + ct, t0:t0 + GT], "zs%d" % (4 + ct)
                tsmul("dve", kq[:], kz, vecs[:, V_KK + ct:V_KK + ct + 1], [kzk, "vecs"], ["kq"])
                actf(ksq[:], kq[:], AF.Square, ["kq"], ["ksq"])
                mm(PB(4, 0, 256), bones_f[:], ksq[:], True, True, ["bones_f", "ksq"], bk(4))
                actf(nrm[:], PB(4, 0, 256), AF.Sqrt, bk(4), ["nrm"])
                P.add("dve", lambda e: e.tensor_scalar_max(nrm[:], nrm[:], 1e-12), reads=["nrm"], writes=["nrm"])
                recip(nrm[:], nrm[:], ["nrm"], ["nrm"])
                tt("dve", kk[:], kq[:], nrm[:], ALU.mult, ["kq", "nrm"], ["kk"])
                dirs = (0, 1) if passB else (d,)
                for dd in dirs:
                    ddr = slice(dd * 64, (dd + 1) * 64)
                    psa = PB(4, 256, 512)
                    mm(psa, a2b[ddr, ct * 128:(ct + 1) * 128], zs[ddr, 13, t0:t0 + GT], True, True, ["a2b", "zs13"], bk(4), tp=(dd * 64, 0))
                    actf(asig[dd][:], psa, AF.Sigmoid, bk(4) + ["vecs"], ["asig%d" % dd],
                         bias=vecs[:, V_A0 + dd * 4 + ct:V_A0 + dd * 4 + ct + 1])
                    tsc("dve", t1[:], asig[dd][:], vecs[:, V_KA + ct:V_KA + ct + 1], omka[:, ct:ct + 1], ALU.mult, ALU.add,
                        ["asig%d" % dd, "vecs", "omka"], ["t1"])
                    tt("dve", kdir[dd][:], kz, t1[:], ALU.mult, [kzk, "t1"], ["kdir%d" % dd])
                if passB:
                    tt("pool", t1[:], kdir[0][:], kdir[1][:], ALU.add, ["kdir0", "kdir1"], ["t1"])
                    stt(bp[:, ct, t0:t0 + GT], zs[:, ct, t0:t0 + GT], vecs[:, V_RK + ct:V_RK + ct + 1], t1[:], ALU.mult, ALU.mult,
                        ["zs%d" % ct, "vecs", "t1"], ["bp"])
                tt("dve", bbv[:], kk[:], asig[d][:], ALU.mult, ["kk", "asig%d" % d], ["bbv"])
                v4 = lambda ap: ap.rearrange("p (c t) -> p c t", c=GRP)
                tt("dve", AR[gb][:, ct, :, 1, :], v4(zs[:, ct, t0:t0 + GT]), v4(Et[:, 0, :]), ALU.mult, ["zs%d" % ct, Ek], [ARk])
                stt(AR[gb][:, ct, :, 0, :], v4(kk[:]), -1.0, v4(Et[:, 1, :]), ALU.mult, ALU.mult, ["kk", Ek], [ARk])
                tt("pool", BT[gb][:, ct, :], bbv[:], Et[:, 3, :], ALU.mult, ["bbv", Ek], [BTk])
                tt("pool", KTl[gb][:, ct, :], kdir[d][:], Et[:, 3, :], ALU.mult, ["kdir%d" % d, Ek], [KTk])
                tt("pool", bbar[:], bbv[:], Et[:, 2, :], ALU.mult, ["bbv", Ek], ["bbar"])
                tt("pool", kbar[:], kdir[d][:], Et[:, 2, :], ALU.mult, ["kdir%d" % d, Ek], ["kbar"])
                tend = (C - 1) if d == 0 else 0
                dve_copy(GC[gb][:, ct, :], v4(Et[:, 0, :])[:, :, tend], [Ek], [GCk])
                vz, vzk = zs[:, 8 + ct, t0:t0 + GT], "zs%d" % (8 + ct)
                for e_ in range(2):
                    er = slice(e_ * 64, (e_ + 1) * 64)
                    tp = (e_ * 64, e_ * 64)
                    for c in range(GRP):
                        cs = slice(c * C, (c + 1) * C)
                        mm(PB(0)[er, c * C:(c + 1) * C], bbar[er, cs], ident_b[er, er], True, True, ["bbar", "ident_b"], bk(0), tp=tp)
                        mm(PB(0)[er, 256 + c * C:256 + (c + 1) * C], kbar[er, cs], ident_b[er, er], True, True, ["kbar", "ident_b"], bk(0), tp=tp)
                        mm(PB(1)[er, c * C:(c + 1) * C], vz[er, cs], ident_b[er, er], True, True, [vzk, "ident_b"], bk(1), tp=tp)
                act_copy(flat(bbt[gb][:, ct, :, :]), PB(0, 0, 256), bk(0), [bbtk])
                act_copy(flat(kbt[gb][:, ct, :, :]), PB(0, 256, 512), bk(0), [kbtk])
                dve_copy(flat(vt[gb][:, ct, :, :]), PB(1, 0, 256), bk(1), [vtk])
        if "minv" in parts:
            m1 = M1[:, d, :].unsqueeze(1).to_broadcast([128, 4, 128])
            m2 = M2[:, d, :].unsqueeze(1).to_broadcast([128, 4, C])
            for c in range(GRP):
                b1, b2 = c % 2, 2 + c % 2
                b3 = 4 + c % 2
                cs = slice(c * C, (c + 1) * C)
                for hp in range(4):
                    for e_ in range(2):
                        er = slice(e_ * 64, (e_ + 1) * 64)
                        tp = (e_ * 64, e_ * 64)
                        arv = AR[gb][er, hp, c, :, :]
                        mm(PB(b1)[er, hp * 128:(hp + 1) * 128], BT[gb][er, hp, cs], arv, True, True, [BTk, ARk], bk(b1), tp=tp)
                        mm(PB(b2)[er, hp * 128:(hp + 1) * 128], KTl[gb][er, hp, cs], arv, True, True, [KTk, ARk], bk(b2), tp=tp)
                        mm(PB(b3)[er, hp * C:(hp + 1) * C], AR[gb][er, hp, c, 0, :], BT[gb][er, hp, cs], True, True,
                           [ARk, BTk], bk(b3), tp=tp)
                tt("dve", GB1[:, c * 4:(c + 1) * 4, :], PB(b1).rearrange("p (a b) -> p a b", a=4), m1, ALU.mult, bk(b1) + ["M1"], ["GB1"])
                tt("dve", GB2[:, c * 4:(c + 1) * 4, :], PB(b2).rearrange("p (a b) -> p a b", a=4), m1, ALU.mult, bk(b2) + ["M1"], ["GB2"])
                tt("dve", Pk[0][:, c * 4:(c + 1) * 4, 0, :], PB(b3, 0, 256).rearrange("p (a b) -> p a b", a=4), m2, ALU.mult,
                   bk(b3) + ["M2"], ["Pk0a" if c < 2 else "Pk0b"])
            tt("dve", Zk[0][:], GB1[:, :, 0:C], I2[:].unsqueeze(1).to_broadcast([128, 16, C]), ALU.add, ["GB1", "I2"], ["Zk0a", "Zk0b"])
            psP = psum[:, 0:2048]
            psZ = psum[:, 2048:3072]
            HS = ((0, "a"), (1, "b"))

            def mmP(k, h, hs):
                cur = k % 2
                for slot in range(h * 8, h * 8 + 8):
                    for e_ in range(2):
                        er = slice(e_ * 64, (e_ + 1) * 64)
                        tp = (e_ * 64, e_ * 64)
                        Pv = Pk[cur][er, slot, 0, :]
                        if k == 0:
                            PTv, ptk = GB1[er, slot, 0:C], "GB1"
                        else:
                            PTv, ptk = Pk[cur][er, slot, 1, :], "Pk%d%s" % (cur, hs)
                        rk_ = [ptk, "Pk%d%s" % (cur, hs)]
                        mm(psP[er, slot * 128:slot * 128 + C], PTv, Pv, True, True, rk_, bk(2 * h, 2 * h + 1), tp=tp)
                        if k < 4:
                            mm(psP[er, slot * 128 + C:slot * 128 + 2 * C], Pv, PTv, True, True, rk_, bk(2 * h, 2 * h + 1), tp=tp)

            def evP(k, h, hs):
                nxt = (k + 1) % 2
                src = psP[:, h * 1024:(h + 1) * 1024]
                if k < 4:
                    act_copy(Pk[nxt][:, h * 8:(h + 1) * 8, :, :].rearrange("p s o n -> p (s o n)"), src, bk(2 * h, 2 * h + 1),
                             ["Pk%d%s" % (nxt, hs)])
                else:
                    act_copy(Pk[nxt][:, h * 8:(h + 1) * 8, 0, :], src.rearrange("p (s o n) -> p s o n", s=8, o=2)[:, :, 0, :],
                             bk(2 * h, 2 * h + 1), ["Pk%d%s" % (nxt, hs)])

            def mmZ(k, h, hs):
                cur, nxt = k % 2, (k + 1) % 2
                for slot in range(h * 8, h * 8 + 8):
                    for e_ in range(2):
                        er = slice(e_ * 64, (e_ + 1) * 64)
                        tp = (e_ * 64, e_ * 64)
                        mm(psZ[er, slot * C:(slot + 1) * C], Pk[nxt][er, slot, 0, :], Zk[cur][er, slot, :], True, True,
                           ["Pk%d%s" % (nxt, hs), "Zk%d%s" % (cur, hs)], bk(4 + h), tp=tp)

            def evZ(k, h, hs):
                cur, nxt = k % 2, (k + 1) % 2
                tt("dve", flat(Zk[nxt][:, h * 8:(h + 1) * 8, :]), psZ[:, h * 512:(h + 1) * 512], flat(Zk[cur][:, h * 8:(h + 1) * 8, :]),
                   ALU.add, bk(4 + h) + ["Zk%d%s" % (cur, hs)], ["Zk%d%s" % (nxt, hs)])

            for k in range(5):
                for h, hs in HS:
                    mmP(k, h, hs)
                for h, hs in HS:
                    evP(k, h, hs)
                for h, hs in HS:
                    mmZ(k, h, hs)
                for h, hs in HS:
                    evZ(k, h, hs)
        if "scan" in parts:
            order = range(GRP) if d == 0 else range(GRP - 1, -1, -1)
            for c in order:
                tok = t0 + c * C
                for hp in range(4):
                    for e_ in range(2):
                        er = slice(e_ * 64, (e_ + 1) * 64)
                        tp = (e_ * 64, e_ * 64)
                        slot = c * 4 + hp
                        o = PB(5)[er, hp * C:(hp + 1) * C]
                        mm(o, AR[gb][er, hp, c, 0, :], Tst[er, hp, :], True, False, [ARk, "Tst"], bk(5), tp=tp)
                        mm(o, GB2[er, slot, 0:C], vt[gb][er, hp, c, :], False, True, ["GB2", vtk], bk(5), tp=tp)
                act_copy(flat(Wb[:]), PB(5, 0, 256), bk(5), ["Wb"])
                for hp in range(4):
                    for e_ in range(2):
                        er = slice(e_ * 64, (e_ + 1) * 64)
                        tp = (e_ * 64, e_ * 64)
                        slot = c * 4 + hp
                        mm(PB(5)[er, 256 + hp * C:256 + (hp + 1) * C], Zf[er, slot, :], Wb[er, hp, :], True, True,
                           ["Zk1a" if c < 2 else "Zk1b", "Wb"], bk(5), tp=tp)
                dve_copy(flat(Ub[:]), PB(5, 256, 512), bk(5), ["Ub"])
                for hp in range(4):
                    for e_ in range(2):
                        er = slice(e_ * 64, (e_ + 1) * 64)
                        tp = (e_ * 64, e_ * 64)
                        slot = c * 4 + hp
                        oy = PB(6)[er, hp * C:(hp + 1) * C]
                        mm(oy, Tst[er, hp, :], AR[gb][er, hp, c, 1, :], True, False, ["Tst", ARk], bk(6), tp=tp)
                        mm(oy, Ub[er, hp, :], GB1[er, slot, C:2 * C], False, False, ["Ub", "GB1"], bk(6), tp=tp)
                        mm(oy, vt[gb][er, hp, c, :], GB2[er, slot, C:2 * C], False, True, [vtk, "GB2"], bk(6), tp=tp)
                        ot = PB(6)[er, 256 + hp * C:256 + (hp + 1) * C]
                        mm(ot, bbt[gb][er, hp, c, :], Ub[er, hp, :], True, False, [bbtk, "Ub"], bk(6), tp=tp)
                        mm(ot, kbt[gb][er, hp, c, :], vt[gb][er, hp, c, :], False, True, [kbtk, vtk], bk(6), tp=tp)
                py = PB(6, 0, 256).rearrange("p (a b) -> p a b", a=4)
                if passB:
                    tt("dve", yT[:, :, tok:tok + C], py, yT[:, :, tok:tok + C], ALU.add, bk(6) + ["yT"], ["yT"])
                else:
                    dve_copy(yT[:, :, tok:tok + C], py, bk(6), ["yT"])
                tt("dve", Tf[:], Tf[:], GC[gb][:, :, c:c + 1].to_broadcast([128, 4, C]), ALU.mult, ["Tf", GCk], ["Tf"])
                tt("dve", flat(Tf[:]), flat(Tf[:]), PB(6, 256, 512), ALU.add, bk(6) + ["Tf"], ["Tf"])
                act_copy(flat(Tst[:]), flat(Tf[:]), ["Tf"], ["Tst"])

    yc, ysq, sd, bon = tmp4
    sg = sb("sg", [128, 2, SEG], BF16)
    orw = sb("orw", [128, 4, SEG], BF16)
    epsc = sb("epsc", [128, 1], F32)

    def rwkv_epilogue(S):
        actf(sg[:, 0, :], zs[:, 14, :], AF.Sigmoid, ["zs14"], ["sg"])
        actf(sg[0:32, 1, :], zs[0:32, 15, :], AF.Sigmoid, ["zs15"], ["sg"])
        for ct in range(4):
            b0 = nbank()
            mm(PB(b0), bones_f[:], yT[:, ct, :], True, True, ["bones_f", "yT"], bk(b0))
            stt(yc[:], PB(b0), -1.0 / C, yT[:, ct, :], ALU.mult, ALU.add, bk(b0) + ["yT"], ["yc"])
            actf(ysq[:], yc[:], AF.Square, ["yc"], ["ysq"])
            b1 = nbank()
            mm(PB(b1), bones_f[:], ysq[:], True, True, ["bones_f", "ysq"], bk(b1))
            actf(sd[:], PB(b1), AF.Sqrt, bk(b1) + ["epsc"], ["sd"], bias=epsc[:, 0:1], scale=1.0 / C)
            recip(sd[:], sd[:], ["sd"], ["sd"])
            tt("dve", yc[:], yc[:], sd[:], ALU.mult, ["yc", "sd"], ["yc"])
            tsc("dve", yc[:], yc[:], vecs[:, V_LNW + ct:V_LNW + ct + 1], vecs[:, V_LNB + ct:V_LNB + ct + 1], ALU.mult, ALU.add,
                ["yc", "vecs"], ["yc"])
            b2 = nbank()
            mm(PB(b2), bones_b[:], bp[:, ct, :], True, True, ["bones_b", "bp"], bk(b2))
            tt("dve", bon[:], PB(b2), zs[:, 8 + ct, :], ALU.mult, bk(b2) + ["zs%d" % (8 + ct)], ["bon"])
            tt("pool", yc[:], yc[:], bon[:], ALU.add, ["yc", "bon"], ["yc"])
            b3 = nbank()
            mm(PB(b3), g2b[:, 0, ct * 128:(ct + 1) * 128], sg[:, 0, :], True, False, ["g2b", "sg"], bk(b3))
            mm(PB(b3), g2b[0:32, 1, ct * 128:(ct + 1) * 128], sg[0:32, 1, :], False, True, ["g2b", "sg"], bk(b3))
            tt("dve", orw[:, ct, :], PB(b3), yc[:], ALU.mult, bk(b3) + ["yc"], ["orw"])

    mergedT = zs
    sga, sgr, tma, tmr = tmp4
    for grp_ in (("ztmp0", "yc", "sga", "hrA"), ("ztmp1", "ysq", "sgr", "hrB"), ("ztmq0", "sd", "tma"), ("ztmq1", "bon", "tmr")):
        for k_ in grp_:
            P.alias[k_] = tuple(x for x in grp_ if x != k_)

    def merge_branches(S):
        for h in range(2):
            wga, kga = w_take("GA%d" % h, S)
            wgr, kgr = w_take("GR%d" % h, S)
            wbr_, kbr = w_take("BAR%d" % h, S)
            wbv = wbr_[:, :].rearrange("p (a c n) -> p a c n", a=2, c=4)
            for mi in range(4):
                m = h * 4 + mi
                proj_fm_tile(wga, kga, 8, 512, mi * 128, uT, "uT", HALO, SEG, 0)
                actf(sga[:], PB(0), AF.Sigmoid, bk(0), ["sga"])
                proj_fm_tile(wgr, kgr, 8, 512, mi * 128, uT, "uT", HALO, SEG, 1)
                actf(sgr[:], PB(1), AF.Sigmoid, bk(1), ["sgr"])
                for c in range(4):
                    mm(PB(2), wbv[:, 0, c, mi * 128:(mi + 1) * 128], oattn[:, c, :], c == 0, c == 3, [kbr, "oattn"], bk(2))
                for c in range(4):
                    mm(PB(3), wbv[:, 1, c, mi * 128:(mi + 1) * 128], orw[:, c, :], c == 0, c == 3, [kbr, "orw"], bk(3))
                tt("dve", tma[:], PB(2), sga[:], ALU.mult, bk(2) + ["sga"], ["tma"])
                tt("dve", tmr[:], PB(3), sgr[:], ALU.mult, bk(3) + ["sgr"], ["tmr"])
                tt("pool", mergedT[:, m, :], tma[:], tmr[:], ALU.add, ["tma", "tmr"], ["zs%d" % m])
            w_done(3)

    def out_proj(S):
        w0_, k0_ = w_take("WO0", S)
        w1_, k1_ = w_take("WO1", S)
        wv = [w0_[:, :].rearrange("p (c n) -> p c n", c=8), w1_[:, :].rearrange("p (c n) -> p c n", c=8)]
        wk = [k0_, k1_]
        for tb in range(SEG // 128):
            i = cnt["st"]
            cnt["st"] += 1
            q = i % 4
            a = cnt["x"] % 3
            cnt["x"] += 1
            xk = "xt%d" % a
            dma("pool", xt[a][:], xh[S, HALO + tb * 128:HALO + (tb + 1) * 128, :], [], [xk], "xin%d" % a)
            for n2 in range(2):
                bank = 4 + n2
                for c in range(8):
                    mm(PB(bank), mergedT[:, c, tb * 128:(tb + 1) * 128], wv[n2][:, c, :], c == 0, c == 7, ["zs%d" % c, wk[n2]], bk(bank))
            post_norm_residual(q, (4, 5), xt[a], xk)
            dma("pool", hbuf[1 + S * SEG + tb * 128:1 + S * SEG + (tb + 1) * 128, :], xt[a][:], [xk], ["hb%d" % S], "xst%d" % a)
        w_done(2)

    def post_norm_residual(q, banks, res, rkey):
        sk, rk_ = "ssq%d" % q, "rstd%d" % q
        for n2 in range(2):
            actf(junk[:, n2 * 512:(n2 + 1) * 512], PB(banks[n2]), AF.Square, bk(banks[n2]), ["junk", sk], accum=st_ssq[q][:, n2:n2 + 1])
        tt("dve", st_rstd[q][:], st_ssq[q][:, 0:1], st_ssq[q][:, 1:2], ALU.add, [sk], [rk_])
        tsc("dve", st_rstd[q][:], st_rstd[q][:], 1.0 / D, NORM_EPS, ALU.mult, ALU.add, [rk_], [rk_])
        actf(st_rstd[q][:], st_rstd[q][:], AF.Sqrt, [rk_], [rk_])
        recip(st_rstd[q][:], st_rstd[q][:], [rk_], [rk_])
        for n2 in range(2):
            hk = "hrA" if n2 == 0 else "hrB"
            stt(tmp4[n2][:], PB(banks[n2]), st_rstd[q][:, 0:1], rows[:, n2 * 512:(n2 + 1) * 512], ALU.mult, ALU.mult,
                bk(banks[n2]) + [rk_, "rows"], [hk])
            tt("pool", res[:, n2 * 512:(n2 + 1) * 512], res[:, n2 * 512:(n2 + 1) * 512], tmp4[n2][:], ALU.add, [hk, rkey], [rkey])

    uT2 = carve("ffn", "uT2", [128, 8, SEG + 2], BF16)
    actT = [carve("ffn", "actT%d" % i, [128, NFT, 256], BF16) for i in range(2)]
    cg = [carve("ffn", "cg%d" % i, [128, 256], F32) for i in range(3)]
    cu = [carve("ffn", "cu%d" % i, [128, 256], F32) for i in range(3)]
    gl = [carve("ffn", "gl%d" % i, [128, 256], F32) for i in range(2)]
    def phase_barrier(G):
        others = [k_ for g2_ in groups if g2_ != G for k_ in groups[g2_]]
        P.barrier(others)

    def ffn_slot(S):
        base = S * SEG
        for (r0, n) in ((0, 128), (128, 128), (256, 128), (384, 128), (512, 2)):
            norm_block(hbuf[base + r0:base + r0 + n, :], n, 1, uT2, "uT2", r0,
                       src_reads=["hb_first", "hb_last"] + ["hb%d" % k for k in (S - 1, S, S + 1) if 0 <= k < nslot])
        for side, col in ((0, 0), (1, SEG + 1)):
            tsmul("dve", uT2[:, :, col:col + 1], uT2[:, :, col:col + 1], flags[:, 2 * S + side:2 * S + side + 1],
                  ["uT2", "flags"], ["uT2"])
        tasks = []
        for g in range(11):
            for ti in range(2):
                for st in range(2):
                    tasks.append((g, ti, st, 2))
        held = [None]

        def st_mm(i):
            g, ti, st, nt = tasks[i]
            if held[0] is None or held[0][0] != g:
                if held[0] is not None:
                    w_done()
                wf_, kf_ = w_take("UF%d" % g, S)
                held[0] = (g, wf_, kf_)
            _, wf_, kf_ = held[0]
            f = g * 2 + ti
            x, y = i % 2, i % 3
            bg, bu = 2 * x, 2 * x + 1
            proj_fm_tile(wf_[:, 0:2048], kf_, 8, 256, ti * 128, uT2, "uT2", st * 256, 258, bg)
            proj_fm_tile(wf_[:, 2048:4096], kf_, 8, 256, ti * 128, uT2, "uT2", st * 256, 258, bu)
            for (bank, dst, dk, ft) in ((bg, cg[y], "cg%d" % y, f), (bu, cu[y], "cu%d" % y, NFT + f)):
                actf(dst[:], PB(bank, 1, 257), AF.Identity, bk(bank) + ["vecs"], [dk],
                     bias=vecs[:, V_CB + ft:V_CB + ft + 1], scale=vecs[:, V_CW + 44 + ft:V_CW + 44 + ft + 1])

        def st_taps(i):
            g, ti, st, nt = tasks[i]
            f = g * 2 + ti
            x, y = i % 2, i % 3
            bg, bu = 2 * x, 2 * x + 1
            for (lo, hi, wo) in ((0, 256, 0), (2, 258, 88)):
                for (bank, dst, dk, ft) in ((bg, cg[y], "cg%d" % y, f), (bu, cu[y], "cu%d" % y, NFT + f)):
                    stt(dst[:], PB(bank, lo, hi), vecs[:, V_CW + wo + ft:V_CW + wo + ft + 1], dst[:], ALU.mult, ALU.add,
                        bk(bank) + ["vecs", dk], [dk])

        def st_glu(i):
            g, ti, st, nt = tasks[i]
            f = g * 2 + ti
            x, y = i % 2, i % 3
            actf(gl[x][:], cg[y][:], AF.Gelu_apprx_tanh, ["cg%d" % y], ["gl%d" % x])
            tt("pool", actT[st][:, f, :], gl[x][:], cu[y][:], ALU.mult, ["gl%d" % x, "cu%d" % y], ["actT%d" % st])

        nt_ = len(tasks)
        import os
        kffn = os.environ.get("K_FFN", "")
        for i in range(nt_ + 2):
            if i < nt_:
                st_mm(i)
            if 0 <= i - 1 < nt_ and "notaps" not in kffn:
                st_taps(i - 1)
            if 0 <= i - 2 < nt_ and "noglu" not in kffn:
                st_glu(i - 2)
        w_done()
        for g in range(6):
            wd_, kd_ = w_take("DN%d" % g, S)
            kc = 4 if g < 5 else 2
            wv = wd_[:, 0:kc * 1024].rearrange("p (c n) -> p c n", c=kc)
            for c in range(kc):
                f = g * 4 + c
                for tb in range(4 if "nodown" not in kffn else 0):
                    st, o = tb // 2, (tb % 2) * 128
                    for n2 in range(2):
                        bank = tb * 2 + n2
                        mm(PB(bank), actT[st][:, f, o:o + 128], wv[:, c, n2 * 512:(n2 + 1) * 512], f == 0, f == NFT - 1,
                           ["actT%d" % st, kd_], bk(bank))
            w_done()
        for tb in range(4):
            i = cnt["st"]
            cnt["st"] += 1
            q = i % 4
            a = cnt["x"] % 3
            cnt["x"] += 1
            xk = "xt%d" % a
            dma("pool", xt[a][:], hbuf[1 + base + tb * 128:1 + base + (tb + 1) * 128, :], ["hb%d" % S], [xk], "xin%d" % a)
            post_norm_residual(q, (tb * 2, tb * 2 + 1), xt[a], xk)
            dma("pool", y_out[S * SEG + tb * 128:S * SEG + (tb + 1) * 128, :], xt[a][:], [xk], ["yout"], "xst%d" % a)

    def dump(name, ap, keys):
        if name in dbg_out:
            dma("pool", dbg_out[name], ap, keys, ["dbg_" + name], "dbg_" + name)

    setup()
    prepass()
    load_gains(0)
    def cap(fn):
        P.begin_capture()
        fn()
        return P.end_capture()

    if "A" in passes:
        rwkv_state_init()
        orderA = list(reversed(range(nslot)))

        def stage_x(S):
            slot_load_norm(S)
            z_project(S, range(16))
            dma("pool", us_d[S], flat(uT[:]), ["uT"], ["us%d" % S], "uTst")
            dma("pool", zs_d[S], flat(zs[:]), ["zs%d" % i for i in range(16)], ["zsd%d" % S], "zsst")

        stage_x(orderA[0])
        for idx, S in enumerate(orderA):
            rwkv_slot_begin(S, 1)
            rwkv_group(S, 1, 1, False, parts=("pre", "minv"))
            P.interleave(cap(lambda: rwkv_group(S, 1, 1, False, parts=("scan",))),
                         cap(lambda: rwkv_group(S, 1, 0, False, parts=("pre",))))
            rwkv_group(S, 1, 0, False, parts=("minv",))
            nxt = cap(lambda: stage_x(orderA[idx + 1])) if idx + 1 < len(orderA) else []
            P.interleave(cap(lambda: rwkv_group(S, 1, 0, False, parts=("scan",))), nxt)
            dma("pool", ybwd_d[S], flat(yT[:]), ["yT"], ["ybwd%d" % S], "yTst")
            if S == 0:
                dump("ybwd", flat(yT[:]), ["yT"])
    if "B" in passes:
        rwkv_state_init()
        import os
        kstop = int(os.environ.get("K_STOP", "99"))

        def drain(tags, S):
            for t_ in tags:
                if t_.startswith("ZG") and "A" in passes:
                    continue
                w_take(t_, S)
                w_done()

        for S in range(nslot):
            if "A" in passes:
                dma("pool", flat(uT[:]), us_d[S], ["us%d" % S], ["uT"], "uTld")
            else:
                slot_load_norm(S)
            if kstop <= 1:
                drain(["QG", "KVG", "ZG0", "ZG1", "ZG2", "ZG3", "GA0", "GR0", "BAR0", "GA1", "GR1", "BAR1", "WO0", "WO1"], S)
                continue
            phase_barrier("att")
            qkv_project(S)
            if kstop <= 2:
                drain(["ZG0", "ZG1", "ZG2", "ZG3", "GA0", "GR0", "BAR0", "GA1", "GR1", "BAR1", "WO0", "WO1"], S)
                continue
            attention(S)
            if kstop <= 3:
                drain(["ZG0", "ZG1", "ZG2", "ZG3", "GA0", "GR0", "BAR0", "GA1", "GR1", "BAR1", "WO0", "WO1"], S)
                continue
            if "A" in passes:
                dma("pool", flat(zs[:]), zs_d[S], ["zsd%d" % S], ["zs%d" % i for i in range(16)], "zsld")
            else:
                z_project(S, range(16))
            if kstop <= 4:
                drain(["GA0", "GR0", "BAR0", "GA1", "GR1", "BAR1", "WO0", "WO1"], S)
                continue
            if "A" in passes:
                dma("pool", flat(yT[:]), ybwd_d[S], ["ybwd%d" % S], ["yT"], "yTld")
            else:
                P.add("dve", lambda e: e.memset(flat(yT[:]), 0.0), writes=["yT"])
            rwkv_slot_begin(S, 0)
            phase_barrier("rw")
            rwkv_group(S, 0, 0, True, parts=("pre", "minv"))
            P.interleave(cap(lambda: rwkv_group(S, 0, 0, True, parts=("scan",))),
                         cap(lambda: rwkv_group(S, 0, 1, True, parts=("pre",))))
            rwkv_group(S, 0, 1, True, parts=("minv", "scan"))
            if S == 0:
                dump("uT", flat(uT[:]), ["uT"])
                dump("oattn", flat(oattn[:]), ["oattn"])
                dump("zs", flat(zs[:]), ["zs%d" % i for i in range(16)])
                dump("yT", flat(yT[:]), ["yT"])
            if kstop <= 5:
                drain(["GA0", "GR0", "BAR0", "GA1", "GR1", "BAR1", "WO0", "WO1"], S)
                continue
            rwkv_epilogue(S)
            if kstop <= 6:
                drain(["GA0", "GR0", "BAR0", "GA1", "GR1", "BAR1", "WO0", "WO1"], S)
                continue
            merge_branches(S)
            if S == 0:
                dump("orw", flat(orw[:]), ["orw"])
                dump("mergedT", flat(mergedT[:, 0:8, :]), ["zs%d" % i for i in range(8)])
            if kstop <= 7:
                drain(["WO0", "WO1"], S)
                continue
            out_proj(S)
    if "C" in passes:
        load_gains(1)
        phase_barrier("ffn")
        for S in range(nslot):
            ffn_slot(S)
    P.add("pool", None, reads=["yout", "wscr"] + ["hb%d" % k for k in range(nslot)] + ["dbg_" + n for n in dbg_out]
          + ["ybwd%d" % k for k in range(nslot)])
    assert wstate["taken"] == len(wq) == wstate["released"], (wstate, len(wq))

    sems = P.assign(nc, es)
    with nc.Block() as block:
        @block.sync
        def _(e):
            P.run_engine("sp", e, sems)

        @block.tensor
        def _(e):
            P.run_engine("pe", e, sems)

        @block.scalar
        def _(e):
            P.run_engine("act", e, sems)

        @block.vector
        def _(e):
            P.run_engine("dve", e, sems)

        @block.gpsimd
        def _(e):
            P.run_engine("pool", e, sems)
    es.close()
    nc._n_ops = P.n
    return nc


def _assign_sequences():
    plan = [[("p", 0)]]
    counts = [5, 5, 5, 5, 4, 4, 4]
    k = 0
    for c in counts:
        plan.append([("s", k + i) for i in range(c)])
        k += c
    return plan


def kernel(**inputs):
    xp = np.asarray(inputs["x_prompt"], np.float32)
    xs = np.asarray(inputs["x_sample"], np.float32)
    wl = _layout_weights(inputs)
    plan = _assign_sequences()
    in_maps, places = [], []
    for core in range(NCORES):
        seqs = [xp[0] if kind == "p" else xs[i] for (kind, i) in plan[core]]
        xh, fl, place = _layout_core(seqs, SLOTS_FULL)
        m = dict(wl)
        m["xh"] = xh
        m["flags"] = fl
        in_maps.append(m)
        places.append(place)
    nc = build(SLOTS_FULL, "ABC")
    res = run_bass_kernel_spmd(nc, in_maps, core_ids=list(range(NCORES)))
    y_prompt = np.zeros_like(xp)
    y_sample = np.zeros_like(xs)
    for core in range(NCORES):
        y = np.asarray(res.results[core]["y"], np.float32)
        for (kind, i), (s0, n) in zip(plan[core], places[core]):
            blk = y[s0 * SEG:(s0 + n) * SEG]
            if kind == "p":
                y_prompt[0] = blk
            else:
                y_sample[i] = blk
    return (y_prompt, y_sample)
```

```python
import math
from contextlib import ExitStack

import numpy as np
import concourse.bass as bass
import concourse.mybir as mybir
from concourse.bass_utils import run_bass_kernel_spmd

F32 = mybir.dt.float32
BF16 = mybir.dt.bfloat16
AF = mybir.ActivationFunctionType
ALU = mybir.AluOpType

D = 1024
SEG = 512
HALO = 128
NTH = SEG + 2 * HALO
NBLK = NTH // 128
C = 64
GRP = 4
NGRP = SEG // (C * GRP)
GT = C * GRP
KAPPA = math.exp(-0.5)
NORM_EPS = 1e-6
GN_EPS = 64e-5
NW = 3
NCORES = 8
SLOTS_FULL = 32
D_FF = 2816
NFT = D_FF // 128

V_G1, V_G2, V_MUP, V_MUN, V_A0, V_KK, V_KA, V_RK, V_LNW, V_LNB, V_CW, V_CB = (
    0, 8, 16, 32, 48, 56, 60, 64, 68, 72, 76, 76 + 132)
NV = V_CB + 44
C_ID, C_BONES, C_M1, C_M2, C_TRI, C_I2, C_PEN = 0, 128, 256, 512, 640, 1408, 1472
NCST = C_PEN + 6 * 512

SEM_LIMIT = 12000


class Op:
    __slots__ = ("eng", "fn", "is_dma", "semkey", "waits", "needs_inc", "sem", "count")

    def __init__(self, eng, fn, is_dma, semkey):
        self.eng = eng
        self.fn = fn
        self.is_dma = is_dma
        self.semkey = semkey
        self.waits = []
        self.needs_inc = is_dma
        self.sem = None
        self.count = 0


class Prog:
    ENGS = ("pe", "act", "dve", "pool", "sp")

    def __init__(self):
        self.ops = {e: [] for e in self.ENGS}
        self.last_w = {}
        self.readers = {}
        self.dma_cnt = {}
        self.n = 0
        self.alias = {}

    def begin_capture(self):
        self._cap = []

    def end_capture(self):
        c, self._cap = self._cap, None
        return c

    def replay(self, items):
        for it in items:
            self.add(*it)

    def interleave(self, a, b):
        ia = ib = 0
        na, nb = len(a), len(b)
        while ia < na or ib < nb:
            if ib >= nb or (ia < na and ia * nb <= ib * na):
                self.add(*a[ia])
                ia += 1
            else:
                self.add(*b[ib])
                ib += 1

    def add(self, eng, fn, reads=(), writes=(), dma=False, semkey=None):
        if getattr(self, "_cap", None) is not None:
            self._cap.append((eng, fn, tuple(reads), tuple(writes), dma, semkey))
            return None
        op = Op(eng, fn, dma, semkey if dma else None)
        self.n += 1
        if self.alias:
            reads = list(reads) + [a for k in reads for a in self.alias.get(k, ())]
            writes = list(writes) + [a for k in writes for a in self.alias.get(k, ())]
        deps = []
        for k in reads:
            lw = self.last_w.get(k)
            if lw is not None:
                deps.append((lw, "raw"))
        for k in writes:
            lw = self.last_w.get(k)
            if lw is not None:
                deps.append((lw, "waw"))
            for r in self.readers.get(k, ()):
                deps.append((r, "war"))
        seen = set()
        for d, kind in deps:
            if d is op or id(d) in seen:
                continue
            if not d.is_dma and not dma and d.eng == eng:
                if eng == "pe":
                    continue
            seen.add(id(d))
            if d.is_dma:
                op.waits.append((d.sem, self.dma_cnt[d.semkey]))
            else:
                d.needs_inc = True
                op.waits.append(d)
        if dma:
            self.dma_cnt[semkey] = self.dma_cnt.get(semkey, 0) + 16
            op.sem = "d_" + str(semkey)
            op.count = self.dma_cnt[semkey]
        for k in reads:
            self.readers.setdefault(k, []).append(op)
        for k in writes:
            self.last_w[k] = op
            self.readers[k] = []
        self.ops[eng].append(op)
        return op

    def barrier(self, keys, engines=("pe", "act", "dve", "pool")):
        deps, seen = [], set()
        for k in keys:
            cand = list(self.readers.get(k, ()))
            if self.last_w.get(k) is not None:
                cand.append(self.last_w[k])
            for d in cand:
                if id(d) not in seen and d.fn is not None:
                    seen.add(id(d))
                    deps.append(d)
        for e in engines:
            op = Op(e, None, False, None)
            for d in deps:
                if d.is_dma:
                    op.waits.append((d.sem, self.dma_cnt[d.semkey]))
                elif not (d.eng == e and e == "pe"):
                    d.needs_inc = True
                    op.waits.append(d)
            self.ops[e].append(op)
        for k in keys:
            self.last_w.pop(k, None)
            self.readers[k] = []

    def assign(self, nc, es):
        sems = {}
        for e in self.ENGS:
            epoch, cnt = 0, 0
            for op in self.ops[e]:
                if not op.is_dma and op.needs_inc:
                    if cnt >= SEM_LIMIT:
                        epoch += 1
                        cnt = 0
                    cnt += 1
                    op.sem = "c_%s_%d" % (e, epoch)
                    op.count = cnt
        for e in self.ENGS:
            for op in self.ops[e]:
                if op.sem is not None and op.sem not in sems:
                    sems[op.sem] = es.enter_context(nc.semaphore(op.sem))
        return sems

    def run_engine(self, ename, eng, sems):
        seen = {}
        for op in self.ops[ename]:
            for d in op.waits:
                sname, cnt = d if isinstance(d, tuple) else (d.sem, d.count)
                if seen.get(sname, 0) >= cnt:
                    continue
                seen[sname] = cnt
                eng.wait_ge(sems[sname], cnt)
            if op.fn is None:
                continue
            ins = op.fn(eng)
            if op.is_dma:
                ins.then_inc(sems[op.sem], 16)
            elif op.needs_inc:
                ins.then_inc(sems[op.sem], 1)


def _fm(vec, ntile):
    return np.ascontiguousarray(np.asarray(vec, np.float32).reshape(ntile, 128).T)


def _constants():
    cst = np.zeros((128, NCST), np.float32)
    p = np.arange(128)
    cst[:, C_ID:C_ID + 128] = np.eye(128, dtype=np.float32)
    cst[:, C_BONES:C_BONES + 128] = (p[:, None] // 64 == p[None, :] // 64).astype(np.float32)
    s = (p % 64)[:, None]
    t = np.arange(64)[None, :]
    m1f = np.concatenate([(t > s), (t >= s)], axis=1).astype(np.float32)
    m1b = np.concatenate([(t < s), (t <= s)], axis=1).astype(np.float32)
    cst[:, C_M1:C_M1 + 128] = m1f
    cst[:, C_M1 + 128:C_M1 + 256] = m1b
    cst[:, C_M2:C_M2 + 64] = (t < s).astype(np.float32)
    cst[:, C_M2 + 64:C_M2 + 128] = (t > s).astype(np.float32)
    ps_, pt_ = p[:, None], p[None, :]
    same = (ps_ // 64 == pt_ // 64)
    for d in range(2):
        if d == 0:
            incl, excl, rest = (ps_ <= pt_), (ps_ < pt_), (ps_ > pt_)
        else:
            incl, excl, rest = (ps_ >= pt_), (ps_ > pt_), (ps_ < pt_)
        base = C_TRI + d * 384
        cst[:, base:base + 128] = (incl & same)
        cst[:, base + 128:base + 256] = (excl & same)
        cst[:, base + 256:base + 384] = (rest & same)
    cst[:, C_I2:C_I2 + 64] = (t == s).astype(np.float32)
    for g in range(2):
        for j in range(3):
            blk = np.zeros((128, 4, 128), np.float32)
            sk = p[:, None] + (j - 1) * 128
            qq = p[None, :]
            dist = np.abs(qq - sk).astype(np.float32)
            for i in range(4):
                h = g * 4 + i
                slope = 2.0 ** (-(h + 1))
                v = -8.0 * slope * dist
                v = np.where(dist <= 128, v, -240000.0)
                blk[:, i, :] = v
            base = C_PEN + (g * 3 + j) * 512
            cst[:, base:base + 512] = blk.reshape(128, 512)
    return cst


def _layout_weights(inp):
    L = 0
    w_in = np.asarray(inp["w_in"][L], np.float32)
    cols = []
    for i in range(4):
        cols += list(range(i * 64, (i + 1) * 64)) + list(range((4 + i) * 64, (5 + i) * 64))
    cols += list(range(512, 768))
    zc = list(range(768, 768 + 1952))
    w_in_p = np.zeros((1024, 38 * 128), np.float32)
    w_in_p[:, 0:768] = w_in[:, cols]
    w_in_p[:, 768:768 + 1952] = w_in[:, zc]
    w_in_p[:, 22 * 128:38 * 128] = w_in[:, 2720:4768]
    wba = np.asarray(inp["w_branch_attn"][L], np.float32)
    rows = []
    for i in range(4):
        rows += list(range(i * 64, (i + 1) * 64)) + list(range((4 + i) * 64, (5 + i) * 64))
    wba_p = np.ascontiguousarray(wba[rows, :])
    vecs = np.zeros((128, NV), np.float32)
    vecs[:, V_G1:V_G1 + 8] = _fm(inp["norm_mix_pre"][L], 8)
    vecs[:, V_G2:V_G2 + 8] = _fm(inp["norm_ffn_pre"][L], 8)
    mup = np.zeros(2048, np.float32)
    mun = np.zeros(2048, np.float32)
    mup[:1952] = np.asarray(inp["rw_mu_prev"][L])
    mun[:1952] = np.asarray(inp["rw_mu_next"][L])
    vecs[:, V_MUP:V_MUP + 16] = _fm(mup, 16)
    vecs[:, V_MUN:V_MUN + 16] = _fm(mun, 16)
    a0 = np.asarray(inp["rw_a0"][L], np.float32)
    for d in range(2):
        vecs[:, V_A0 + d * 4:V_A0 + d * 4 + 4] = _fm(a0[d], 4)
    vecs[:, V_KK:V_KK + 4] = _fm(inp["rw_k_k"][L], 4)
    vecs[:, V_KA:V_KA + 4] = _fm(inp["rw_k_a"][L], 4)
    vecs[:, V_RK:V_RK + 4] = _fm(np.asarray(inp["rw_r_k"][L]).reshape(512), 4)
    vecs[:, V_LNW:V_LNW + 4] = _fm(inp["rw_ln_w"][L], 4)
    vecs[:, V_LNB:V_LNB + 4] = _fm(inp["rw_ln_b"][L], 4)
    cw = np.asarray(inp["ffn_conv_w"][L], np.float32)
    for j in range(3):
        vecs[:, V_CW + j * 44:V_CW + (j + 1) * 44] = _fm(cw[j], 44)
    vecs[:, V_CB:V_CB + 44] = _fm(inp["ffn_conv_b"][L], 44)
    rows128 = np.zeros((128, 2, 1024), np.float32)
    rows128[:, 0, :] = np.asarray(inp["norm_mix_post"][L], np.float32)[None, :]
    rows128[:, 1, :] = np.asarray(inp["norm_ffn_post"][L], np.float32)[None, :]
    w0rows = np.ascontiguousarray(np.asarray(inp["rw_w0"][L], np.float32).reshape(1, 2, 512))
    sink = np.asarray(inp["attn_sink"][L], np.float32)
    sinkrows = np.zeros((1, 2, 4, 128), np.float32)
    for g in range(2):
        for i in range(4):
            sinkrows[0, g, i, :] = sink[g * 4 + i]
    sinkrows = sinkrows.reshape(1, 2, 512)
    lora = np.zeros((128, 4, 512), np.float32)
    lora[:, 0, :] = np.asarray(inp["rw_w2"][L], np.float32).reshape(128, 512)
    lora[:, 1, :] = np.asarray(inp["rw_a2"][L], np.float32).reshape(128, 512)
    g2 = np.asarray(inp["rw_g2"][L], np.float32)
    lora[:, 2, :] = g2[0:128]
    lora[0:32, 3, :] = g2[128:160]
    return {
        "w_in": w_in_p, "wba": wba_p,
        "wbr": np.ascontiguousarray(np.asarray(inp["w_branch_rwkv"][L], np.float32)),
        "wout": np.ascontiguousarray(np.asarray(inp["w_out"][L], np.float32)),
        "wup": np.ascontiguousarray(np.asarray(inp["w_ffn_up"][L], np.float32)),
        "wdn": np.ascontiguousarray(np.asarray(inp["w_ffn_down"][L], np.float32)),
        "vecs": vecs, "rows": rows128, "w0rows": w0rows, "sinkrows": sinkrows, "lora": lora,
        "cst": _constants(),
    }


def _layout_core(seqs, nslot):
    xh = np.zeros((nslot, NTH, D), np.float32)
    flags = np.zeros((nslot, 2), np.float32)
    s = 0
    place = []
    for x in seqs:
        T = x.shape[0]
        n = T // SEG
        xp = np.zeros((T + 2 * HALO, D), np.float32)
        xp[HALO:HALO + T] = x
        for k in range(n):
            xh[s + k] = xp[k * SEG:k * SEG + NTH]
            flags[s + k, 0] = 1.0 if k > 0 else 0.0
            flags[s + k, 1] = 1.0 if k < n - 1 else 0.0
        place.append((s, n))
        s += n
    fl = np.ascontiguousarray(np.broadcast_to(flags.reshape(1, nslot * 2), (128, nslot * 2))).astype(np.float32)
    return xh, fl, place


def weight_schedule(nslot, passes):
    q = []
    if "A" in passes:
        for s in reversed(range(nslot)):
            q += [("ZG0", s), ("ZG1", s), ("ZG2", s), ("ZG3", s)]
    if "B" in passes:
        for s in range(nslot):
            q += [("QG", s), ("KVG", s)]
            if "A" not in passes:
                q += [("ZG0", s), ("ZG1", s), ("ZG2", s), ("ZG3", s)]
            for h in range(2):
                q += [("GA%d" % h, s), ("GR%d" % h, s), ("BAR%d" % h, s)]
            q += [("WO0", s), ("WO1", s)]
    if "C" in passes:
        for s in range(nslot):
            for g in range(11):
                q += [("UF%d" % g, s)]
            for g in range(6):
                q += [("DN%d" % g, s)]
    return q


def build(nslot, passes="ABC", dbg=()):
    nc = bass.Bass("TRN2", target_bir_lowering=False)

    def din(name, shape, dt=F32):
        return nc.dram_tensor(name, list(shape), dt, kind="ExternalInput").ap()

    def dout(name, shape, dt=F32):
        return nc.dram_tensor(name, list(shape), dt, kind="ExternalOutput").ap()

    def dscr(name, shape, dt):
        return nc.dram_tensor(name, list(shape), dt).ap()

    xh = din("xh", [nslot, NTH, D])
    flags_d = din("flags", [128, 2 * nslot])
    w_in_d = din("w_in", [1024, 4864])
    wba_d = din("wba", [512, 1024])
    wbr_d = din("wbr", [512, 1024])
    wout_d = din("wout", [1024, 1024])
    wup_d = din("wup", [1024, 5632])
    wdn_d = din("wdn", [2816, 1024])
    vecs_d = din("vecs", [128, NV])
    rows_d = din("rows", [128, 2, 1024])
    w0_d = din("w0rows", [1, 2, 512])
    sink_d = din("sinkrows", [1, 2, 512])
    lora_d = din("lora", [128, 4, 512])
    cst_d = din("cst", [128, NCST])
    y_out = dout("y", [nslot * SEG, D])
    dbg_out = {}
    for name, shape in dbg:
        dbg_out[name] = dout("dbg_" + name, shape)

    w_in_b = dscr("w_in_b", [1024, 4864], BF16)
    wba_b = dscr("wba_b", [512, 1024], BF16)
    wbr_b = dscr("wbr_b", [512, 1024], BF16)
    wout_b = dscr("wout_b", [1024, 1024], BF16)
    wup_b = dscr("wup_b", [1024, 5632], BF16)
    wdn_b = dscr("wdn_b", [2816, 1024], BF16)
    ybwd_d = dscr("ybwd", [nslot, 128, 4 * SEG], F32)
    us_d = dscr("us_scr", [nslot, 128, 8 * NTH], BF16)
    zs_d = dscr("zs_scr", [nslot, 128, 16 * SEG], BF16)
    hbuf = dscr("hbuf", [nslot * SEG + 2, D], F32)

    P = Prog()
    es = ExitStack()

    def sb(name, shape, dt):
        return es.enter_context(nc.sbuf_tensor("s_" + name, list(shape), dt))

    psum = es.enter_context(nc.psum_tensor("psum", [128, 4096], F32))

    def PB(b, lo=0, hi=512):
        return psum[:, b * 512 + lo:b * 512 + hi]

    def bk(*banks):
        r = []
        for b in banks:
            r += ["pb%da" % b, "pb%db" % b]
        return r

    pbT = psum[:, 7 * 512:8 * 512].bitcast(BF16)

    ident_b = sb("ident_b", [128, 128], BF16)
    ones_b = sb("ones_b", [128, 128], BF16)
    bones_f = sb("bones_f", [128, 128], F32)
    bones_b = sb("bones_b", [128, 128], BF16)
    M1 = sb("M1", [128, 2, 128], F32)
    M2 = sb("M2", [128, 2, 64], F32)
    TRI = sb("TRI", [128, 2, 384], F32)
    I2 = sb("I2", [128, 64], BF16)
    PEN = sb("PEN", [128, 6, 512], BF16)
    vecs = sb("vecs", [128, NV], F32)
    c0 = sb("c0", [128, 16], F32)
    omka = sb("omka", [128, 4], F32)
    rows = sb("rows", [128, 1024], F32)
    gB = sb("gB", [128, 8, 128], F32)
    flags = sb("flags", [128, 2 * nslot], F32)
    rb = sb("rb", [2, 4, 512], BF16)
    w2b = sb("w2b", [128, 512], BF16)
    a2b = sb("a2b", [128, 512], BF16)
    g2b = sb("g2b", [128, 2, 512], BF16)
    onesf = sb("onesf", [128, 128], F32)

    xt = [sb("xt%d" % i, [128, 1024], F32) for i in range(3)]
    ub = [sb("ub%d" % i, [128, 1024], BF16) for i in range(2)]
    st_ssq = [sb("ssq%d" % i, [128, 2], F32) for i in range(4)]
    st_rstd = [sb("rstd%d" % i, [128, 1], F32) for i in range(4)]
    junk = sb("junk", [128, 1024], BF16)
    uT = sb("uT", [128, 8, NTH], BF16)
    wbuf = [sb("wbuf%d" % i, [128, 4096], BF16) for i in range(NW)]

    cnt = {"x": 0, "st": 0}

    def dma(eng, out, in_, reads, writes, semkey):
        P.add(eng, lambda e: e.dma_start(out=out, in_=in_), reads=reads, writes=writes, dma=True, semkey=semkey + "_" + eng)

    def act_copy(out, in_, reads, writes):
        P.add("act", lambda e: e.copy(out, in_), reads=reads, writes=writes)

    def dve_copy(out, in_, reads, writes):
        P.add("dve", lambda e: e.tensor_copy(out, in_), reads=reads, writes=writes)

    def mm(out, lhsT, rhs, start, stop, r, w, tp=None):
        if tp is None:
            P.add("pe", lambda e: e.matmul(out, lhsT, rhs, start=start, stop=stop), reads=r, writes=w)
        else:
            P.add("pe", lambda e: e.matmul(out, lhsT, rhs, start=start, stop=stop, tile_position=tp), reads=r, writes=w)

    def actf(out, in_, func, r, w, bias=None, scale=None, accum=None):
        kw = {}
        if bias is not None:
            kw["bias"] = bias
        if scale is not None:
            kw["scale"] = scale
        if accum is not None:
            kw["accum_out"] = accum
        P.add("act", lambda e: e.activation(out=out, in_=in_, func=func, **kw), reads=r, writes=w)

    def tt(eng, out, in0, in1, op, r, w):
        P.add(eng, lambda e: e.tensor_tensor(out=out, in0=in0, in1=in1, op=op), reads=r, writes=w)

    def stt(out, in0, scalar, in1, op0, op1, r, w):
        P.add("dve", lambda e: e.scalar_tensor_tensor(out=out, in0=in0, scalar=scalar, in1=in1, op0=op0, op1=op1), reads=r, writes=w)

    def tsc(eng, out, in0, s1, s2, op0, op1, r, w):
        P.add(eng, lambda e: e.tensor_scalar(out, in0, s1, s2, op0, op1), reads=r, writes=w)

    def tsmul(eng, out, in0, s, r, w):
        P.add(eng, lambda e: e.tensor_scalar_mul(out, in0, s), reads=r, writes=w)

    def recip(out, in_, r, w):
        P.add("dve", lambda e: e.reciprocal(out, in_), reads=r, writes=w)

    def load_gains(which):
        dma("pool", rows[:], rows_d[:, which, :], [], ["rows"], "ld_rows")
        for c in range(8):
            col = (V_G1 if which == 0 else V_G2) + c
            actf(gB[:, c, :], onesf[:], AF.Copy, ["vecs", "onesf"], ["gB"], scale=vecs[:, col:col + 1])

    def setup():
        stg = xt[0]
        dma("pool", vecs[:], vecs_d, [], ["vecs"], "ld_vecs")
        dma("pool", flags[:], flags_d, [], ["flags"], "ld_flags")
        dma("pool", stg[:, 0:1024], cst_d[:, 0:1024], [], ["xt0"], "xin0")
        dve_copy(ident_b[:], stg[:, C_ID:C_ID + 128], ["xt0"], ["ident_b"])
        dve_copy(bones_f[:], stg[:, C_BONES:C_BONES + 128], ["xt0"], ["bones_f"])
        dve_copy(bones_b[:], stg[:, C_BONES:C_BONES + 128], ["xt0"], ["bones_b"])
        dve_copy(M1[:].rearrange("p a b -> p (a b)"), stg[:, C_M1:C_M1 + 256], ["xt0"], ["M1"])
        dve_copy(M2[:].rearrange("p a b -> p (a b)"), stg[:, C_M2:C_M2 + 128], ["xt0"], ["M2"])
        dve_copy(TRI[:, 0, :], stg[:, C_TRI:C_TRI + 384], ["xt0"], ["TRI"])
        dma("pool", xt[1][:, 0:448], cst_d[:, 1024:1472], [], ["xt1"], "xin1")
        dve_copy(TRI[:, 1, :], xt[1][:, 0:384], ["xt1"], ["TRI"])
        dve_copy(I2[:], xt[1][:, 384:448], ["xt1"], ["I2"])
        for k in range(3):
            t = xt[(k + 2) % 3]
            key = "xt%d" % ((k + 2) % 3)
            dma("pool", t[:], cst_d[:, C_PEN + k * 1024:C_PEN + (k + 1) * 1024], [], [key], "xin%d" % ((k + 2) % 3))
            dve_copy(PEN[:, 2 * k:2 * k + 2, :].rearrange("p a b -> p (a b)"), t[:], [key], ["PEN"])
        P.add("dve", lambda e: e.memset(ones_b[:], 1.0), writes=["ones_b"])
        P.add("dve", lambda e: e.memset(onesf[:], 1.0), writes=["onesf"])
        P.add("dve", lambda e: e.memset(epsc[:], GN_EPS), writes=["epsc"])
        dma("pool", xt[0][:, 0:1024], lora_d[:, 0:2, :].rearrange("p a b -> p (a b)"), [], ["xt0"], "xin0")
        dve_copy(w2b[:], xt[0][:, 0:512], ["xt0"], ["w2b"])
        dve_copy(a2b[:], xt[0][:, 512:1024], ["xt0"], ["a2b"])
        dma("pool", xt[1][:, 0:1024], lora_d[:, 2:4, :].rearrange("p a b -> p (a b)"), [], ["xt1"], "xin1")
        dve_copy(g2b[:].rearrange("p a b -> p (a b)"), xt[1][:, 0:1024], ["xt1"], ["g2b"])
        tt("dve", c0[:], vecs[:, V_MUP:V_MUP + 16], vecs[:, V_MUN:V_MUN + 16], ALU.add, ["vecs"], ["c0"])
        tsc("dve", c0[:], c0[:], -1.0, 1.0, ALU.mult, ALU.add, ["c0"], ["c0"])
        tsc("dve", omka[:], vecs[:, V_KA:V_KA + 4], -1.0, 1.0, ALU.mult, ALU.add, ["vecs"], ["omka"])
        w0f = xt[2][0:1, 0:1024].rearrange("p (a b) -> p a b", a=2)
        w0t = xt[0][0:1, 0:1024].rearrange("p (a b) -> p a b", a=2)
        sinkf = xt[1][0:1, 0:1024].rearrange("p (a b) -> p a b", a=2)
        lo_b = ub[0][0:1, 0:1024].rearrange("p (a b) -> p a b", a=2)
        dma("pool", w0f, w0_d, [], ["xt2"], "xin2")
        dma("pool", sinkf, sink_d, [], ["xt1"], "xin1")
        dve_copy(rb[0:1, 0:2, :], w0f, ["xt2"], ["rb"])
        dve_copy(w0t, rb[0:1, 0:2, :], ["rb"], ["xt0"])
        tt("dve", w0t, w0f, w0t, ALU.subtract, ["xt2", "xt0"], ["xt0"])
        dve_copy(lo_b, w0t, ["xt0"], ["ub0"])
        dma("pool", rb[1:2, 0:2, :], lo_b, ["ub0"], ["rb"], "ubst0")
        actf(rb[0:1, 2:4, :], sinkf, AF.Exp, ["xt1"], ["rb"])
        P.add("dve", lambda e: e.memset(xt[2][0:1, :], 0.0), reads=["xt2"], writes=["xt2"])
        dma("pool", hbuf[0:1, :], xt[2][0:1, :], ["xt2"], ["hb_first"], "xst2")
        dma("pool", hbuf[nslot * SEG + 1:nslot * SEG + 2, :], xt[2][0:1, :], ["xt2"], ["hb_last"], "xst2")

    def prepass():
        jobs = []
        for (src, dst, R, Cc) in ((w_in_d, w_in_b, 1024, 4864), (wba_d, wba_b, 512, 1024), (wbr_d, wbr_b, 512, 1024),
                                  (wout_d, wout_b, 1024, 1024), (wup_d, wup_b, 1024, 5632), (wdn_d, wdn_b, 2816, 1024)):
            for rc in range(R // 128):
                for c0_ in range(0, Cc, 1024):
                    w = min(1024, Cc - c0_)
                    jobs.append((src[rc * 128:(rc + 1) * 128, c0_:c0_ + w], dst[rc * 128:(rc + 1) * 128, c0_:c0_ + w], w))
        for i, (s_ap, d_ap, w) in enumerate(jobs):
            a = i % 3
            b = i % 2
            dma("sp", xt[a][:, 0:w], s_ap, [], ["xt%d" % a], "xin%d" % a)
            if i % 2 == 0:
                dve_copy(ub[b][:, 0:w], xt[a][:, 0:w], ["xt%d" % a], ["ub%d" % b])
            else:
                act_copy(ub[b][:, 0:w], xt[a][:, 0:w], ["xt%d" % a], ["ub%d" % b])
            dma("pool", d_ap, ub[b][:, 0:w], ["ub%d" % b], ["wscr"], "ubst%d" % b)

    wq = weight_schedule(nslot, passes)
    wstate = {"issued": 0, "taken": 0, "released": 0}
    wfm = lambda t: t.rearrange("(c p) n -> p c n", p=128)

    def wsrc(tag):
        if tag == "QG":
            return wfm(w_in_b)[:, :, 0:512], 8, 512
        if tag == "KVG":
            return wfm(w_in_b)[:, :, 512:768], 8, 256
        if tag.startswith("ZG"):
            g = int(tag[2])
            return wfm(w_in_b)[:, :, 768 + g * 512:768 + (g + 1) * 512], 8, 512
        if tag.startswith("GA"):
            h = int(tag[2])
            return wfm(w_in_b)[:, :, 2816 + h * 512:2816 + (h + 1) * 512], 8, 512
        if tag.startswith("GR"):
            h = int(tag[2])
            return wfm(w_in_b)[:, :, 3840 + h * 512:3840 + (h + 1) * 512], 8, 512
        if tag.startswith("WO"):
            h = int(tag[2])
            return wfm(wout_b)[:, :, h * 512:(h + 1) * 512], 8, 512
        if tag.startswith("UG"):
            g = int(tag[2])
            n = 512 if g < 5 else 256
            return wfm(wup_b)[:, :, g * 512:g * 512 + n], 8, n
        if tag.startswith("UU"):
            g = int(tag[2])
            n = 512 if g < 5 else 256
            return wfm(wup_b)[:, :, 2816 + g * 512:2816 + g * 512 + n], 8, n
        if tag.startswith("DN"):
            g = int(tag[2])
            kc = 4 if g < 5 else 2
            return wfm(wdn_b)[:, g * 4:g * 4 + kc, :], kc, 1024
        raise KeyError(tag)

    def w_issue():
        i = wstate["issued"]
        tag, _ = wq[i]
        b = i % NW
        if tag.startswith("UF"):
            g = int(tag[2:])
            v = wbuf[b][:, :].rearrange("p (a c n) -> p a c n", a=2, c=8)
            dma("sp", v[:, 0, :, :], wfm(wup_b)[:, :, g * 256:(g + 1) * 256], ["wscr"], ["wbuf%d" % b], "w%d" % b)
            dma("sp", v[:, 1, :, :], wfm(wup_b)[:, :, 2816 + g * 256:2816 + (g + 1) * 256], ["wscr"], ["wbuf%d" % b], "w%d" % b)
        elif tag.startswith("BAR"):
            h = int(tag[3])
            v = wbuf[b][:, :].rearrange("p (a c n) -> p a c n", a=2, c=4)
            dma("sp", v[:, 0, :, :], wfm(wba_b)[:, :, h * 512:(h + 1) * 512], ["wscr"], ["wbuf%d" % b], "w%d" % b)
            dma("sp", v[:, 1, :, :], wfm(wbr_b)[:, :, h * 512:(h + 1) * 512], ["wscr"], ["wbuf%d" % b], "w%d" % b)
        else:
            src, kc, n = wsrc(tag)
            v = wbuf[b][:, 0:kc * n].rearrange("p (c n) -> p c n", c=kc)
            dma("sp", v, src, ["wscr"], ["wbuf%d" % b], "w%d" % b)
        wstate["issued"] += 1

    def w_pump():
        while wstate["issued"] < len(wq) and wstate["issued"] - wstate["released"] < NW:
            w_issue()

    def w_take(tag, slot):
        i = wstate["taken"]
        assert wq[i] == (tag, slot), (wq[i], tag, slot)
        if wstate["issued"] <= i:
            assert wstate["issued"] - wstate["released"] < NW, "too many weight groups held"
            w_issue()
        wstate["taken"] += 1
        b = i % NW
        return wbuf[b], "wbuf%d" % b

    def w_done(k=1):
        wstate["released"] += k
        assert wstate["released"] <= wstate["taken"]
        w_pump()

    def norm_block(src_ap, nrow, which, dst, dst_key, col0, src_reads=()):
        i = cnt["x"]
        cnt["x"] += 1
        a, b, q = i % 3, i % 2, i % 4
        xk, uk = "xt%d" % a, "ub%d" % b
        dma("pool", xt[a][0:nrow, :], src_ap, list(src_reads), [xk], "xin%d" % a)
        P.add("act", lambda e: e.activation(out=junk[0:nrow, :], in_=xt[a][0:nrow, :], func=AF.Square, accum_out=st_ssq[q][0:nrow, 0:1]),
              reads=[xk], writes=["junk", "ssq%d" % q])
        P.add("dve", lambda e: e.tensor_scalar(st_rstd[q][0:nrow, :], st_ssq[q][0:nrow, 0:1], 1.0 / D, NORM_EPS, ALU.mult, ALU.add),
              reads=["ssq%d" % q], writes=["rstd%d" % q])
        P.add("act", lambda e: e.activation(out=st_rstd[q][0:nrow, :], in_=st_rstd[q][0:nrow, :], func=AF.Sqrt),
              reads=["rstd%d" % q], writes=["rstd%d" % q])
        P.add("dve", lambda e: e.reciprocal(st_rstd[q][0:nrow, :], st_rstd[q][0:nrow, :]), reads=["rstd%d" % q], writes=["rstd%d" % q])
        P.add("dve", lambda e: e.tensor_scalar_mul(ub[b][0:nrow, :], xt[a][0:nrow, :], st_rstd[q][0:nrow, :]),
              reads=[xk, "rstd%d" % q], writes=[uk])
        for c in range(8):
            P.add("pe", lambda e, c=c: e.transpose(pbT[:, c * 128:c * 128 + nrow], ub[b][0:nrow, c * 128:(c + 1) * 128], ident_b[0:nrow, 0:nrow]),
                  reads=[uk, "ident_b"], writes=bk(7))
        P.add("dve", lambda e: e.tensor_tensor(out=dst[:, :, col0:col0 + nrow],
                                               in0=pbT[:, :].rearrange("p (c n) -> p c n", c=8)[:, :, 0:nrow],
                                               in1=gB[:, :, 0:nrow], op=ALU.mult),
              reads=bk(7) + ["gB"], writes=[dst_key])

    ARENA_BYTES = 73 * 1024
    arena = sb("arena", [128, ARENA_BYTES // 4], F32)
    carve_state = {}

    def carve(group, name, shape, dt):
        off = carve_state.get(group, 0)
        nel = 1
        for d_ in shape[1:]:
            nel *= d_
        nbytes = nel * (4 if dt == F32 else 2)
        nbytes_al = (nbytes + 31) // 32 * 32
        assert off + nbytes_al <= ARENA_BYTES, (group, name, off, nbytes_al)
        carve_state[group] = off + nbytes_al
        v = arena[0:shape[0], off // 4:(off + nbytes) // 4]
        if dt != F32:
            v = v.bitcast(dt)
        if len(shape) == 3:
            v = v.rearrange("p (a b) -> p a b", a=shape[1])
        elif len(shape) == 4:
            v = v.rearrange("p (a b c) -> p a b c", a=shape[1], b=shape[2])
        elif len(shape) == 5:
            v = v.rearrange("p (a b c d) -> p a b c d", a=shape[1], b=shape[2], c=shape[3])
        groups.setdefault(group, []).append(name)
        return v

    groups = {}
    tmp4 = [sb("tmp%d" % i, [128, 512], F32) for i in range(4)]
    carve_state["att"] = 40 * 1024
    qT = carve("att", "qT", [128, 4, SEG], BF16)
    kT = carve("att", "kT", [128, NTH], BF16)
    vtok = carve("att", "vtok", [128, NBLK, 128], BF16)
    ptb = [carve("att", "ptb%d" % i, [128, 512], BF16) for i in range(3)]
    rden = carve("att", "rden", [128, 512], F32)
    fones = sb("fones", [128, 2, 64], BF16)
    oattn = sb("oattn", [128, 4, SEG], BF16)
    zs = sb("zs", [128, 16, SEG], BF16)
    ztmp = [tmp4[0], tmp4[1]]
    ztmp2 = [tmp4[2], tmp4[3]]
    ps_rot = {"d": 0}

    def nbank():
        b = ps_rot["d"] % 2
        ps_rot["d"] += 1
        return b

    def slot_load_norm(S):
        for b in range(NBLK):
            norm_block(xh[S, b * 128:(b + 1) * 128, :], 128, 0, uT, "uT", b * 128)

    def proj_fm_tile(wb, wkey, kc, n, col, src, skey, tok_lo, ntok, bank):
        wv = wb[:, 0:kc * n].rearrange("p (c n) -> p c n", c=kc)
        for c in range(kc):
            mm(PB(bank, 0, ntok), wv[:, c, col:col + 128], src[:, c, tok_lo:tok_lo + ntok], c == 0, c == kc - 1,
               [wkey, skey], bk(bank))

    def qkv_project(S):
        wb, wkey = w_take("QG", S)
        for i in range(4):
            bank = nbank()
            proj_fm_tile(wb, wkey, 8, 512, i * 128, uT, "uT", HALO, SEG, bank)
            act_copy(qT[:, i, :], PB(bank), bk(bank), ["qT"])
        w_done()
        wb, wkey = w_take("KVG", S)
        for (lo, n) in ((0, 512), (512, 256)):
            bank = nbank()
            proj_fm_tile(wb, wkey, 8, 256, 0, uT, "uT", lo, n, bank)
            act_copy(kT[:, lo:lo + n], PB(bank, 0, n), bk(bank), ["kT"])
        wv = wb[:, 0:8 * 256].rearrange("p (c n) -> p c n", c=8)
        for half in range(2):
            bank = nbank()
            for bb in range(3):
                b = half * 3 + bb
                for c in range(8):
                    mm(PB(bank, bb * 128, (bb + 1) * 128), uT[:, c, b * 128:(b + 1) * 128], wv[:, c, 128:256], c == 0, c == 7,
                       [wkey, "uT"], bk(bank))
            dve_copy(vtok[:, half * 3:half * 3 + 3, :].rearrange("p a b -> p (a b)"), PB(bank, 0, 384), bk(bank), ["vtok"])
        w_done()

    def attention(S):
        fp = flags[:, 2 * S:2 * S + 1]
        tsmul("dve", fones[:, 0, :], ones_b[:, 0:64], fp, ["ones_b", "flags"], ["fones"])
        tsmul("dve", vtok[:, 1, :], vtok[:, 1, :], fp, ["vtok", "flags"], ["vtok"])
        nqb = SEG // 128
        pending = []

        def flush():
            for f_ in pending:
                f_()
            del pending[:]

        for qb in range(nqb):
            nb, db = 3 + (qb % 2), 5 + (qb % 2)
            for g in range(2):
                pr = slice(g * 64, (g + 1) * 64)
                for j in range(3):
                    kb = qb + j
                    sbk = (qb * 6 + g * 3 + j) % 3
                    pk = "ptb%d" % sbk
                    mm(PB(sbk), kT[pr, kb * 128:(kb + 1) * 128], qT[pr, :, qb * 128:(qb + 1) * 128], True, False,
                       ["kT", "qT"], bk(sbk), tp=(g * 64, 0))
                    mm(PB(sbk), ident_b[:], PEN[:, g * 3 + j, :], False, True, ["ident_b", "PEN"], bk(sbk))
                    actf(ptb[sbk][:], PB(sbk), AF.Exp, bk(sbk), [pk], scale=0.125)
                    flush()
                    if kb <= 1:
                        dl, dk = fones[:, 0, :], "fones"
                    else:
                        dl, dk = ones_b[:, 0:64], "ones_b"

                    def tail(qb=qb, g=g, j=j, kb=kb, sbk=sbk, pk=pk, pr=pr, nb=nb, db=db, dl=dl, dk=dk):
                        mm(PB(nb)[pr, :], vtok[:, kb, pr], ptb[sbk][:], j == 0, j == 2, ["vtok", pk], bk(nb), tp=(0, g * 64))
                        mm(PB(db)[pr, :], dl, ptb[sbk][:], j == 0, False, [dk, pk], bk(db), tp=(0, g * 64))
                        if j == 2:
                            mm(PB(db)[pr, :], ones_b[0:1, 0:64], rb[0:1, 2 + g, :], False, True, ["ones_b", "rb"], bk(db), tp=(0, g * 64))
                            if g == 1:
                                recip(rden[:], PB(db), bk(db), ["rden"])
                                tt("dve", oattn[:, :, qb * 128:(qb + 1) * 128], PB(nb).rearrange("p (a b) -> p a b", a=4),
                                   rden[:].rearrange("p (a b) -> p a b", a=4), ALU.mult, bk(nb) + ["rden"], ["oattn"])
                    pending.append(tail)
        flush()

    ZSUB = ((0, 510), (510, 2))

    def z_project(S, tiles):
        tasks = [(zt, o, n) for zt in tiles for (o, n) in ZSUB]
        zb = [tmp4[0], tmp4[1], tmp4[2]]
        zk = ["ztmp0", "ztmp1", "ztmq0"]
        cur = [None]

        def stage_mm(i):
            zt, o, n = tasks[i]
            g = zt // 4
            if cur[0] is None or cur[0][0] != g:
                if cur[0] is not None:
                    w_done()
                wb, wkey = w_take("ZG%d" % g, S)
                cur[0] = (g, wb, wkey)
            _, wb, wkey = cur[0]
            bank = i % 2
            proj_fm_tile(wb, wkey, 8, 512, (zt % 4) * 128, uT, "uT", HALO + o - 1, n + 2, bank)
            actf(zb[i % 3][:, 0:n], PB(bank, 1, n + 1), AF.Copy, bk(bank) + ["c0"], [zk[i % 3]], scale=c0[:, zt:zt + 1])

        def stage_taps(i):
            zt, o, n = tasks[i]
            bank = i % 2
            stt(tmp4[3][:, 0:n], PB(bank, 0, n), vecs[:, V_MUP + zt:V_MUP + zt + 1], zb[i % 3][:, 0:n], ALU.mult, ALU.add,
                bk(bank) + ["vecs", zk[i % 3]], ["ztmq1"])
            stt(zs[:, zt, o:o + n], PB(bank, 2, n + 2), vecs[:, V_MUN + zt:V_MUN + zt + 1], tmp4[3][:, 0:n], ALU.mult, ALU.add,
                bk(bank) + ["vecs", "ztmq1"], ["zs%d" % zt])

        nt_ = len(tasks)
        for i in range(nt_ + 1):
            if i < nt_:
                stage_mm(i)
            if i >= 1:
                stage_taps(i - 1)
        w_done()

    def rw(name, shape, dt):
        return carve("rw", name, shape, dt)

    twd = rw("twd", [128, GT], BF16)
    sigtok = [rw("sigtok%d" % i, [128, 512], F32) for i in range(2)]
    E = [rw("E%d" % i, [128, 4, GT], F32) for i in range(2)]
    kq = rw("kq", [128, GT], F32)
    ksq = rw("ksq", [128, GT], F32)
    nrm = rw("nrm", [128, GT], F32)
    kk = rw("kk", [128, GT], F32)
    asig = [rw("asig%d" % i, [128, GT], F32) for i in range(2)]
    kdir = [rw("kdir%d" % i, [128, GT], F32) for i in range(2)]
    t1 = rw("t1", [128, GT], F32)
    bbv = rw("bbv", [128, GT], F32)
    bbar = rw("bbar", [128, GT], BF16)
    kbar = rw("kbar", [128, GT], BF16)
    AR = [rw("AR%d" % i, [128, 4, GRP, 2, C], BF16) for i in range(2)]
    BT = [rw("BT%d" % i, [128, 4, GT], BF16) for i in range(2)]
    KTl = [rw("KTl%d" % i, [128, 4, GT], BF16) for i in range(2)]
    bbt = [rw("bbt%d" % i, [128, 4, GRP, C], BF16) for i in range(2)]
    kbt = [rw("kbt%d" % i, [128, 4, GRP, C], BF16) for i in range(2)]
    vt = [rw("vt%d" % i, [128, 4, GRP, C], BF16) for i in range(2)]
    GC = [rw("GC%d" % i, [128, 4, GRP], F32) for i in range(2)]
    GB1 = rw("GB1", [128, 16, 128], BF16)
    GB2 = rw("GB2", [128, 16, 128], BF16)
    Pk = [rw("Pk%d" % i, [128, 16, 2, C], BF16) for i in range(2)]
    Zk = [rw("Zk%d" % i, [128, 16, C], BF16) for i in range(2)]
    groups["rw"] += ["Pk0a", "Pk0b", "Pk1a", "Pk1b", "Zk0a", "Zk0b", "Zk1a", "Zk1b"]
    Wb = rw("Wb", [128, 4, C], BF16)
    Ub = rw("Ub", [128, 4, C], BF16)
    Tf = sb("Tf", [128, 4, C], F32)
    Tst = sb("Tst", [128, 4, C], BF16)
    yT = sb("yT", [128, 4, SEG], F32)
    bp = sb("bp", [128, 4, SEG], BF16)

    def flat(ap3):
        return ap3.rearrange("p a b -> p (a b)")

    def rwkv_state_init():
        P.add("dve", lambda e: e.memset(flat(Tf[:]), 0.0), writes=["Tf"])

    def rwkv_slot_begin(S, d):
        col = 2 * S + (0 if d == 0 else 1)
        tsmul("dve", flat(Tf[:]), flat(Tf[:]), flags[:, col:col + 1], ["Tf", "flags"], ["Tf"])
        act_copy(flat(Tst[:]), flat(Tf[:]), ["Tf"], ["Tst"])

    def rwkv_group(S, d, gi, passB, parts=("pre", "minv", "scan")):
        gb = gi % 2
        t0 = gi * GT
        dr = slice(d * 64, (d + 1) * 64)
        ARk, BTk, KTk, bbtk, kbtk, vtk, GCk = ("AR%d" % gb, "BT%d" % gb, "KTl%d" % gb, "bbt%d" % gb, "kbt%d" % gb,
                                               "vt%d" % gb, "GC%d" % gb)
        Zf, Zfk = Zk[1], "Zk1"
        if "pre" in parts:
            actf(twd[dr, :], zs[dr, 12, t0:t0 + GT], AF.Tanh, ["zs12"], ["twd"])
            for blk in range(2):
                mm(PB(blk), twd[dr, blk * 128:(blk + 1) * 128], w2b[dr, :], True, False, ["twd", "w2b"], bk(blk), tp=(d * 64, 0))
                mm(PB(blk), ones_b[0:2, 0:128], rb[0:2, d, :], False, True, ["ones_b", "rb"], bk(blk))
                actf(sigtok[blk][:], PB(blk), AF.Sigmoid, bk(blk), ["sigtok%d" % blk])
            for ct in range(4):
                eb = ct % 2
                Et, Ek = E[eb], "E%d" % eb
                for blk in range(2):
                    bank = 2 + blk
                    mm(PB(bank, 0, 384), sigtok[blk][:, ct * 128:(ct + 1) * 128], TRI[:, d, :], True, True,
                       ["sigtok%d" % blk, "TRI"], bk(bank))
                    actf(Et[:, 0:3, blk * 128:(blk + 1) * 128], PB(bank, 0, 384).rearrange("p (a b) -> p a b", a=3), AF.Exp,
                         bk(bank), [Ek], scale=-KAPPA)
                    actf(Et[:, 3, blk * 128:(blk + 1) * 128], PB(bank, 0, 128), AF.Exp, bk(bank), [Ek], scale=KAPPA)
                kz, kzk = zs[:, 4 + ct, t0:t0 + GT], "zs%d" % (4 + ct)
                tsmul("dve", kq[:], kz, vecs[:, V_KK + ct:V_KK + ct + 1], [kzk, "vecs"], ["kq"])
                actf(ksq[:], kq[:], AF.Square, ["kq"], ["ksq"])
                mm(PB(4, 0, 256), bones_f[:], ksq[:], True, True, ["bones_f", "ksq"], bk(4))
                actf(nrm[:], PB(4, 0, 256), AF.Sqrt, bk(4), ["nrm"])
                P.add("dve", lambda e: e.tensor_scalar_max(nrm[:], nrm[:], 1e-12), reads=["nrm"], writes=["nrm"])
                recip(nrm[:], nrm[:], ["nrm"], ["nrm"])
                tt("dve", kk[:], kq[:], nrm[:], ALU.mult, ["kq", "nrm"], ["kk"])
                dirs = (0, 1) if passB else (d,)
                for dd in dirs:
                    ddr = slice(dd * 64, (dd + 1) * 64)
                    psa = PB(4, 256, 512)
                    mm(psa, a2b[ddr, ct * 128:(ct + 1) * 128], zs[ddr, 13, t0:t0 + GT], True, True, ["a2b", "zs13"], bk(4), tp=(dd * 64, 0))
                    actf(asig[dd][:], psa, AF.Sigmoid, bk(4) + ["vecs"], ["asig%d" % dd],
                         bias=vecs[:, V_A0 + dd * 4 + ct:V_A0 + dd * 4 + ct + 1])
                    tsc("dve", t1[:], asig[dd][:], vecs[:, V_KA + ct:V_KA + ct + 1], omka[:, ct:ct + 1], ALU.mult, ALU.add,
                        ["asig%d" % dd, "vecs", "omka"], ["t1"])
                    tt("dve", kdir[dd][:], kz, t1[:], ALU.mult, [kzk, "t1"], ["kdir%d" % dd])
                if passB:
                    tt("pool", t1[:], kdir[0][:], kdir[1][:], ALU.add, ["kdir0", "kdir1"], ["t1"])
                    stt(bp[:, ct, t0:t0 + GT], zs[:, ct, t0:t0 + GT], vecs[:, V_RK + ct:V_RK + ct + 1], t1[:], ALU.mult, ALU.mult,
                        ["zs%d" % ct, "vecs", "t1"], ["bp"])
                tt("dve", bbv[:], kk[:], asig[d][:], ALU.mult, ["kk", "asig%d" % d], ["bbv"])
                v4 = lambda ap: ap.rearrange("p (c t) -> p c t", c=GRP)
                tt("dve", AR[gb][:, ct, :, 1, :], v4(zs[:, ct, t0:t0 + GT]), v4(Et[:, 0, :]), ALU.mult, ["zs%d" % ct, Ek], [ARk])
                stt(AR[gb][:, ct, :, 0, :], v4(kk[:]), -1.0, v4(Et[:, 1, :]), ALU.mult, ALU.mult, ["kk", Ek], [ARk])
                tt("pool", BT[gb][:, ct, :], bbv[:], Et[:, 3, :], ALU.mult, ["bbv", Ek], [BTk])
                tt("pool", KTl[gb][:, ct, :], kdir[d][:], Et[:, 3, :], ALU.mult, ["kdir%d" % d, Ek], [KTk])
                tt("pool", bbar[:], bbv[:], Et[:, 2, :], ALU.mult, ["bbv", Ek], ["bbar"])
                tt("pool", kbar[:], kdir[d][:], Et[:, 2, :], ALU.mult, ["kdir%d" % d, Ek], ["kbar"])
                tend = (C - 1) if d == 0 else 0
                dve_copy(GC[gb][:, ct, :], v4(Et[:, 0, :])[:, :, tend], [Ek], [GCk])
                vz, vzk = zs[:, 8 + ct, t0:t0 + GT], "zs%d" % (8 + ct)
                for e_ in range(2):
                    er = slice(e_ * 64, (e_ + 1) * 64)
                    tp = (e_ * 64, e_ * 64)
                    for c in range(GRP):
                        cs = slice(c * C, (c + 1) * C)
                        mm(PB(0)[er, c * C:(c + 1) * C], bbar[er, cs], ident_b[er, er], True, True, ["bbar", "ident_b"], bk(0), tp=tp)
                        mm(PB(0)[er, 256 + c * C:256 + (c + 1) * C], kbar[er, cs], ident_b[er, er], True, True, ["kbar", "ident_b"], bk(0), tp=tp)
                        mm(PB(1)[er, c * C:(c + 1) * C], vz[er, cs], ident_b[er, er], True, True, [vzk, "ident_b"], bk(1), tp=tp)
                act_copy(flat(bbt[gb][:, ct, :, :]), PB(0, 0, 256), bk(0), [bbtk])
                act_copy(flat(kbt[gb][:, ct, :, :]), PB(0, 256, 512), bk(0), [kbtk])
                dve_copy(flat(vt[gb][:, ct, :, :]), PB(1, 0, 256), bk(1), [vtk])
        if "minv" in parts:
            m1 = M1[:, d, :].unsqueeze(1).to_broadcast([128, 4, 128])
            m2 = M2[:, d, :].unsqueeze(1).to_broadcast([128, 4, C])
            for c in range(GRP):
                b1, b2 = c % 2, 2 + c % 2
                b3 = 4 + c % 2
                cs = slice(c * C, (c + 1) * C)
                for hp in range(4):
                    for e_ in range(2):
                        er = slice(e_ * 64, (e_ + 1) * 64)
                        tp = (e_ * 64, e_ * 64)
                        arv = AR[gb][er, hp, c, :, :]
                        mm(PB(b1)[er, hp * 128:(hp + 1) * 128], BT[gb][er, hp, cs], arv, True, True, [BTk, ARk], bk(b1), tp=tp)
                        mm(PB(b2)[er, hp * 128:(hp + 1) * 128], KTl[gb][er, hp, cs], arv, True, True, [KTk, ARk], bk(b2), tp=tp)
                        mm(PB(b3)[er, hp * C:(hp + 1) * C], AR[gb][er, hp, c, 0, :], BT[gb][er, hp, cs], True, True,
                           [ARk, BTk], bk(b3), tp=tp)
                tt("dve", GB1[:, c * 4:(c + 1) * 4, :], PB(b1).rearrange("p (a b) -> p a b", a=4), m1, ALU.mult, bk(b1) + ["M1"], ["GB1"])
                tt("dve", GB2[:, c * 4:(c + 1) * 4, :], PB(b2).rearrange("p (a b) -> p a b", a=4), m1, ALU.mult, bk(b2) + ["M1"], ["GB2"])
                tt("dve", Pk[0][:, c * 4:(c + 1) * 4, 0, :], PB(b3, 0, 256).rearrange("p (a b) -> p a b", a=4), m2, ALU.mult,
                   bk(b3) + ["M2"], ["Pk0a" if c < 2 else "Pk0b"])
            tt("dve", Zk[0][:], GB1[:, :, 0:C], I2[:].unsqueeze(1).to_broadcast([128, 16, C]), ALU.add, ["GB1", "I2"], ["Zk0a", "Zk0b"])
            psP = psum[:, 0:2048]
            psZ = psum[:, 2048:3072]
            HS = ((0, "a"), (1, "b"))

            def mmP(k, h, hs):
                cur = k % 2
                for slot in range(h * 8, h * 8 + 8):
                    for e_ in range(2):
                        er = slice(e_ * 64, (e_ + 1) * 64)
                        tp = (e_ * 64, e_ * 64)
                        Pv = Pk[cur][er, slot, 0, :]
                        if k == 0:
                            PTv, ptk = GB1[er, slot, 0:C], "GB1"
                        else:
                            PTv, ptk = Pk[cur][er, slot, 1, :], "Pk%d%s" % (cur, hs)
                        rk_ = [ptk, "Pk%d%s" % (cur, hs)]
                        mm(psP[er, slot * 128:slot * 128 + C], PTv, Pv, True, True, rk_, bk(2 * h, 2 * h + 1), tp=tp)
                        if k < 4:
                            mm(psP[er, slot * 128 + C:slot * 128 + 2 * C], Pv, PTv, True, True, rk_, bk(2 * h, 2 * h + 1), tp=tp)

            def evP(k, h, hs):
                nxt = (k + 1) % 2
                src = psP[:, h * 1024:(h + 1) * 1024]
                if k < 4:
                    act_copy(Pk[nxt][:, h * 8:(h + 1) * 8, :, :].rearrange("p s o n -> p (s o n)"), src, bk(2 * h, 2 * h + 1),
                             ["Pk%d%s" % (nxt, hs)])
                else:
                    act_copy(Pk[nxt][:, h * 8:(h + 1) * 8, 0, :], src.rearrange("p (s o n) -> p s o n", s=8, o=2)[:, :, 0, :],
                             bk(2 * h, 2 * h + 1), ["Pk%d%s" % (nxt, hs)])

            def mmZ(k, h, hs):
                cur, nxt = k % 2, (k + 1) % 2
                for slot in range(h * 8, h * 8 + 8):
                    for e_ in range(2):
                        er = slice(e_ * 64, (e_ + 1) * 64)
                        tp = (e_ * 64, e_ * 64)
                        mm(psZ[er, slot * C:(slot + 1) * C], Pk[nxt][er, slot, 0, :], Zk[cur][er, slot, :], True, True,
                           ["Pk%d%s" % (nxt, hs), "Zk%d%s" % (cur, hs)], bk(4 + h), tp=tp)

            def evZ(k, h, hs):
                cur, nxt = k % 2, (k + 1) % 2
                tt("dve", flat(Zk[nxt][:, h * 8:(h + 1) * 8, :]), psZ[:, h * 512:(h + 1) * 512], flat(Zk[cur][:, h * 8:(h + 1) * 8, :]),
                   ALU.add, bk(4 + h) + ["Zk%d%s" % (cur, hs)], ["Zk%d%s" % (nxt, hs)])

            for k in range(5):
                for h, hs in HS:
                    mmP(k, h, hs)
                for h, hs in HS:
                    evP(k, h, hs)
                for h, hs in HS:
                    mmZ(k, h, hs)
                for h, hs in HS:
                    evZ(k, h, hs)
        if "scan" in parts:
            order = range(GRP) if d == 0 else range(GRP - 1, -1, -1)
            for c in order:
                tok = t0 + c * C
                for hp in range(4):
                    for e_ in range(2):
                        er = slice(e_ * 64, (e_ + 1) * 64)
                        tp = (e_ * 64, e_ * 64)
                        slot = c * 4 + hp
                        o = PB(5)[er, hp * C:(hp + 1) * C]
                        mm(o, AR[gb][er, hp, c, 0, :], Tst[er, hp, :], True, False, [ARk, "Tst"], bk(5), tp=tp)
                        mm(o, GB2[er, slot, 0:C], vt[gb][er, hp, c, :], False, True, ["GB2", vtk], bk(5), tp=tp)
                act_copy(flat(Wb[:]), PB(5, 0, 256), bk(5), ["Wb"])
                for hp in range(4):
                    for e_ in range(2):
                        er = slice(e_ * 64, (e_ + 1) * 64)
                        tp = (e_ * 64, e_ * 64)
                        slot = c * 4 + hp
                        mm(PB(5)[er, 256 + hp * C:256 + (hp + 1) * C], Zf[er, slot, :], Wb[er, hp, :], True, True,
                           ["Zk1a" if c < 2 else "Zk1b", "Wb"], bk(5), tp=tp)
                dve_copy(flat(Ub[:]), PB(5, 256, 512), bk(5), ["Ub"])
                for hp in range(4):
                    for e_ in range(2):
                        er = slice(e_ * 64, (e_ + 1) * 64)
                        tp = (e_ * 64, e_ * 64)
                        slot = c * 4 + hp
                        oy = PB(6)[er, hp * C:(hp + 1) * C]
                        mm(oy, Tst[er, hp, :], AR[gb][er, hp, c, 1, :], True, False, ["Tst", ARk], bk(6), tp=tp)
                        mm(oy, Ub[er, hp, :], GB1[er, slot, C:2 * C], False, False, ["Ub", "GB1"], bk(6), tp=tp)
                        mm(oy, vt[gb][er, hp, c, :], GB2[er, slot, C:2 * C], False, True, [vtk, "GB2"], bk(6), tp=tp)
                        ot = PB(6)[er, 256 + hp * C:256 + (hp + 1) * C]
                        mm(ot, bbt[gb][er, hp, c, :], Ub[er, hp, :], True, False, [bbtk, "Ub"], bk(6), tp=tp)
                        mm(ot, kbt[gb][er, hp, c, :], vt[gb][er, hp, c, :], False, True, [kbtk, vtk], bk(6), tp=tp)
                py = PB(6, 0, 256).rearrange("p (a b) -> p a b", a=4)
                if passB:
                    tt("dve", yT[:, :, tok:tok + C], py, yT[:, :, tok:tok + C], ALU.add, bk(6) + ["yT"], ["yT"])
                else:
                    dve_copy(yT[:, :, tok:tok + C], py, bk(6), ["yT"])
                tt("dve", Tf[:], Tf[:], GC[gb][:, :, c:c + 1].to_broadcast([128, 4, C]), ALU.mult, ["Tf", GCk], ["Tf"])
                tt("dve", flat(Tf[:]), flat(Tf[:]), PB(6, 256, 512), ALU.add, bk(6) + ["Tf"], ["Tf"])
                act_copy(flat(Tst[:]), flat(Tf[:]), ["Tf"], ["Tst"])

    yc, ysq, sd, bon = tmp4
    sg = sb("sg", [128, 2, SEG], BF16)
    orw = sb("orw", [128, 4, SEG], BF16)
    epsc = sb("epsc", [128, 1], F32)

    def rwkv_epilogue(S):
        actf(sg[:, 0, :], zs[:, 14, :], AF.Sigmoid, ["zs14"], ["sg"])
        actf(sg[0:32, 1, :], zs[0:32, 15, :], AF.Sigmoid, ["zs15"], ["sg"])
        for ct in range(4):
            b0 = nbank()
            mm(PB(b0), bones_f[:], yT[:, ct, :], True, True, ["bones_f", "yT"], bk(b0))
            stt(yc[:], PB(b0), -1.0 / C, yT[:, ct, :], ALU.mult, ALU.add, bk(b0) + ["yT"], ["yc"])
            actf(ysq[:], yc[:], AF.Square, ["yc"], ["ysq"])
            b1 = nbank()
            mm(PB(b1), bones_f[:], ysq[:], True, True, ["bones_f", "ysq"], bk(b1))
            actf(sd[:], PB(b1), AF.Sqrt, bk(b1) + ["epsc"], ["sd"], bias=epsc[:, 0:1], scale=1.0 / C)
            recip(sd[:], sd[:], ["sd"], ["sd"])
            tt("dve", yc[:], yc[:], sd[:], ALU.mult, ["yc", "sd"], ["yc"])
            tsc("dve", yc[:], yc[:], vecs[:, V_LNW + ct:V_LNW + ct + 1], vecs[:, V_LNB + ct:V_LNB + ct + 1], ALU.mult, ALU.add,
                ["yc", "vecs"], ["yc"])
            b2 = nbank()
            mm(PB(b2), bones_b[:], bp[:, ct, :], True, True, ["bones_b", "bp"], bk(b2))
            tt("dve", bon[:], PB(b2), zs[:, 8 + ct, :], ALU.mult, bk(b2) + ["zs%d" % (8 + ct)], ["bon"])
            tt("pool", yc[:], yc[:], bon[:], ALU.add, ["yc", "bon"], ["yc"])
            b3 = nbank()
            mm(PB(b3), g2b[:, 0, ct * 128:(ct + 1) * 128], sg[:, 0, :], True, False, ["g2b", "sg"], bk(b3))
            mm(PB(b3), g2b[0:32, 1, ct * 128:(ct + 1) * 128], sg[0:32, 1, :], False, True, ["g2b", "sg"], bk(b3))
            tt("dve", orw[:, ct, :], PB(b3), yc[:], ALU.mult, bk(b3) + ["yc"], ["orw"])

    mergedT = zs
    sga, sgr, tma, tmr = tmp4
    for grp_ in (("ztmp0", "yc", "sga", "hrA"), ("ztmp1", "ysq", "sgr", "hrB"), ("ztmq0", "sd", "tma"), ("ztmq1", "bon", "tmr")):
        for k_ in grp_:
            P.alias[k_] = tuple(x for x in grp_ if x != k_)

    def merge_branches(S):
        for h in range(2):
            wga, kga = w_take("GA%d" % h, S)
            wgr, kgr = w_take("GR%d" % h, S)
            wbr_, kbr = w_take("BAR%d" % h, S)
            wbv = wbr_[:, :].rearrange("p (a c n) -> p a c n", a=2, c=4)
            for mi in range(4):
                m = h * 4 + mi
                proj_fm_tile(wga, kga, 8, 512, mi * 128, uT, "uT", HALO, SEG, 0)
                actf(sga[:], PB(0), AF.Sigmoid, bk(0), ["sga"])
                proj_fm_tile(wgr, kgr, 8, 512, mi * 128, uT, "uT", HALO, SEG, 1)
                actf(sgr[:], PB(1), AF.Sigmoid, bk(1), ["sgr"])
                for c in range(4):
                    mm(PB(2), wbv[:, 0, c, mi * 128:(mi + 1) * 128], oattn[:, c, :], c == 0, c == 3, [kbr, "oattn"], bk(2))
                for c in range(4):
                    mm(PB(3), wbv[:, 1, c, mi * 128:(mi + 1) * 128], orw[:, c, :], c == 0, c == 3, [kbr, "orw"], bk(3))
                tt("dve", tma[:], PB(2), sga[:], ALU.mult, bk(2) + ["sga"], ["tma"])
                tt("dve", tmr[:], PB(3), sgr[:], ALU.mult, bk(3) + ["sgr"], ["tmr"])
                tt("pool", mergedT[:, m, :], tma[:], tmr[:], ALU.add, ["tma", "tmr"], ["zs%d" % m])
            w_done(3)

    def out_proj(S):
        w0_, k0_ = w_take("WO0", S)
        w1_, k1_ = w_take("WO1", S)
        wv = [w0_[:, :].rearrange("p (c n) -> p c n", c=8), w1_[:, :].rearrange("p (c n) -> p c n", c=8)]
        wk = [k0_, k1_]
        for tb in range(SEG // 128):
            i = cnt["st"]
            cnt["st"] += 1
            q = i % 4
            a = cnt["x"] % 3
            cnt["x"] += 1
            xk = "xt%d" % a
            dma("pool", xt[a][:], xh[S, HALO + tb * 128:HALO + (tb + 1) * 128, :], [], [xk], "xin%d" % a)
            for n2 in range(2):
                bank = 4 + n2
                for c in range(8):
                    mm(PB(bank), mergedT[:, c, tb * 128:(tb + 1) * 128], wv[n2][:, c, :], c == 0, c == 7, ["zs%d" % c, wk[n2]], bk(bank))
            post_norm_residual(q, (4, 5), xt[a], xk)
            dma("pool", hbuf[1 + S * SEG + tb * 128:1 + S * SEG + (tb + 1) * 128, :], xt[a][:], [xk], ["hb%d" % S], "xst%d" % a)
        w_done(2)

    def post_norm_residual(q, banks, res, rkey):
        sk, rk_ = "ssq%d" % q, "rstd%d" % q
        for n2 in range(2):
            actf(junk[:, n2 * 512:(n2 + 1) * 512], PB(banks[n2]), AF.Square, bk(banks[n2]), ["junk", sk], accum=st_ssq[q][:, n2:n2 + 1])
        tt("dve", st_rstd[q][:], st_ssq[q][:, 0:1], st_ssq[q][:, 1:2], ALU.add, [sk], [rk_])
        tsc("dve", st_rstd[q][:], st_rstd[q][:], 1.0 / D, NORM_EPS, ALU.mult, ALU.add, [rk_], [rk_])
        actf(st_rstd[q][:], st_rstd[q][:], AF.Sqrt, [rk_], [rk_])
        recip(st_rstd[q][:], st_rstd[q][:], [rk_], [rk_])
        for n2 in range(2):
            hk = "hrA" if n2 == 0 else "hrB"
            stt(tmp4[n2][:], PB(banks[n2]), st_rstd[q][:, 0:1], rows[:, n2 * 512:(n2 + 1) * 512], ALU.mult, ALU.mult,
                bk(banks[n2]) + [rk_, "rows"], [hk])
            tt("pool", res[:, n2 * 512:(n2 + 1) * 512], res[:, n2 * 512:(n2 + 1) * 512], tmp4[n2][:], ALU.add, [hk, rkey], [rkey])

    uT2 = carve("ffn", "uT2", [128, 8, SEG + 2], BF16)
    actT = [carve("ffn", "actT%d" % i, [128, NFT, 256], BF16) for i in range(2)]
    cg = [carve("ffn", "cg%d" % i, [128, 256], F32) for i in range(3)]
    cu = [carve("ffn", "cu%d" % i, [128, 256], F32) for i in range(3)]
    gl = [carve("ffn", "gl%d" % i, [128, 256], F32) for i in range(2)]
    def phase_barrier(G):
        others = [k_ for g2_ in groups if g2_ != G for k_ in groups[g2_]]
        P.barrier(others)

    def ffn_slot(S):
        base = S * SEG
        for (r0, n) in ((0, 128), (128, 128), (256, 128), (384, 128), (512, 2)):
            norm_block(hbuf[base + r0:base + r0 + n, :], n, 1, uT2, "uT2", r0,
                       src_reads=["hb_first", "hb_last"] + ["hb%d" % k for k in (S - 1, S, S + 1) if 0 <= k < nslot])
        for side, col in ((0, 0), (1, SEG + 1)):
            tsmul("dve", uT2[:, :, col:col + 1], uT2[:, :, col:col + 1], flags[:, 2 * S + side:2 * S + side + 1],
                  ["uT2", "flags"], ["uT2"])
        tasks = []
        for g in range(11):
            for ti in range(2):
                for st in range(2):
                    tasks.append((g, ti, st, 2))
        held = [None]

        def st_mm(i):
            g, ti, st, nt = tasks[i]
            if held[0] is None or held[0][0] != g:
                if held[0] is not None:
                    w_done()
                wf_, kf_ = w_take("UF%d" % g, S)
                held[0] = (g, wf_, kf_)
            _, wf_, kf_ = held[0]
            f = g * 2 + ti
            x, y = i % 2, i % 3
            bg, bu = 2 * x, 2 * x + 1
            proj_fm_tile(wf_[:, 0:2048], kf_, 8, 256, ti * 128, uT2, "uT2", st * 256, 258, bg)
            proj_fm_tile(wf_[:, 2048:4096], kf_, 8, 256, ti * 128, uT2, "uT2", st * 256, 258, bu)
            for (bank, dst, dk, ft) in ((bg, cg[y], "cg%d" % y, f), (bu, cu[y], "cu%d" % y, NFT + f)):
                actf(dst[:], PB(bank, 1, 257), AF.Identity, bk(bank) + ["vecs"], [dk],
                     bias=vecs[:, V_CB + ft:V_CB + ft + 1], scale=vecs[:, V_CW + 44 + ft:V_CW + 44 + ft + 1])

        def st_taps(i):
            g, ti, st, nt = tasks[i]
            f = g * 2 + ti
            x, y = i % 2, i % 3
            bg, bu = 2 * x, 2 * x + 1
            for (lo, hi, wo) in ((0, 256, 0), (2, 258, 88)):
                for (bank, dst, dk, ft) in ((bg, cg[y], "cg%d" % y, f), (bu, cu[y], "cu%d" % y, NFT + f)):
                    stt(dst[:], PB(bank, lo, hi), vecs[:, V_CW + wo + ft:V_CW + wo + ft + 1], dst[:], ALU.mult, ALU.add,
                        bk(bank) + ["vecs", dk], [dk])

        def st_glu(i):
            g, ti, st, nt = tasks[i]
            f = g * 2 + ti
            x, y = i % 2, i % 3
            actf(gl[x][:], cg[y][:], AF.Gelu_apprx_tanh, ["cg%d" % y], ["gl%d" % x])
            tt("pool", actT[st][:, f, :], gl[x][:], cu[y][:], ALU.mult, ["gl%d" % x, "cu%d" % y], ["actT%d" % st])

        nt_ = len(tasks)
        import os
        kffn = os.environ.get("K_FFN", "")
        for i in range(nt_ + 2):
            if i < nt_:
                st_mm(i)
            if 0 <= i - 1 < nt_ and "notaps" not in kffn:
                st_taps(i - 1)
            if 0 <= i - 2 < nt_ and "noglu" not in kffn:
                st_glu(i - 2)
        w_done()
        for g in range(6):
            wd_, kd_ = w_take("DN%d" % g, S)
            kc = 4 if g < 5 else 2
            wv = wd_[:, 0:kc * 1024].rearrange("p (c n) -> p c n", c=kc)
            for c in range(kc):
                f = g * 4 + c
                for tb in range(4 if "nodown" not in kffn else 0):
                    st, o = tb // 2, (tb % 2) * 128
                    for n2 in range(2):
                        bank = tb * 2 + n2
                        mm(PB(bank), actT[st][:, f, o:o + 128], wv[:, c, n2 * 512:(n2 + 1) * 512], f == 0, f == NFT - 1,
                           ["actT%d" % st, kd_], bk(bank))
            w_done()
        for tb in range(4):
            i = cnt["st"]
            cnt["st"] += 1
            q = i % 4
            a = cnt["x"] % 3
            cnt["x"] += 1
            xk = "xt%d" % a
            dma("pool", xt[a][:], hbuf[1 + base + tb * 128:1 + base + (tb + 1) * 128, :], ["hb%d" % S], [xk], "xin%d" % a)
            post_norm_residual(q, (tb * 2, tb * 2 + 1), xt[a], xk)
            dma("pool", y_out[S * SEG + tb * 128:S * SEG + (tb + 1) * 128, :], xt[a][:], [xk], ["yout"], "xst%d" % a)

    def dump(name, ap, keys):
        if name in dbg_out:
            dma("pool", dbg_out[name], ap, keys, ["dbg_" + name], "dbg_" + name)

    setup()
    prepass()
    load_gains(0)
    def cap(fn):
        P.begin_capture()
        fn()
        return P.end_capture()

    if "A" in passes:
        rwkv_state_init()
        orderA = list(reversed(range(nslot)))

        def stage_x(S):
            slot_load_norm(S)
            z_project(S, range(16))
            dma("pool", us_d[S], flat(uT[:]), ["uT"], ["us%d" % S], "uTst")
            dma("pool", zs_d[S], flat(zs[:]), ["zs%d" % i for i in range(16)], ["zsd%d" % S], "zsst")

        stage_x(orderA[0])
        for idx, S in enumerate(orderA):
            rwkv_slot_begin(S, 1)
            rwkv_group(S, 1, 1, False, parts=("pre", "minv"))
            P.interleave(cap(lambda: rwkv_group(S, 1, 1, False, parts=("scan",))),
                         cap(lambda: rwkv_group(S, 1, 0, False, parts=("pre",))))
            rwkv_group(S, 1, 0, False, parts=("minv",))
            nxt = cap(lambda: stage_x(orderA[idx + 1])) if idx + 1 < len(orderA) else []
            P.interleave(cap(lambda: rwkv_group(S, 1, 0, False, parts=("scan",))), nxt)
            dma("pool", ybwd_d[S], flat(yT[:]), ["yT"], ["ybwd%d" % S], "yTst")
            if S == 0:
                dump("ybwd", flat(yT[:]), ["yT"])
    if "B" in passes:
        rwkv_state_init()
        import os
        kstop = int(os.environ.get("K_STOP", "99"))

        def drain(tags, S):
            for t_ in tags:
                if t_.startswith("ZG") and "A" in passes:
                    continue
                w_take(t_, S)
                w_done()

        for S in range(nslot):
            if "A" in passes:
                dma("pool", flat(uT[:]), us_d[S], ["us%d" % S], ["uT"], "uTld")
            else:
                slot_load_norm(S)
            if kstop <= 1:
                drain(["QG", "KVG", "ZG0", "ZG1", "ZG2", "ZG3", "GA0", "GR0", "BAR0", "GA1", "GR1", "BAR1", "WO0", "WO1"], S)
                continue
            phase_barrier("att")
            qkv_project(S)
            if kstop <= 2:
                drain(["ZG0", "ZG1", "ZG2", "ZG3", "GA0", "GR0", "BAR0", "GA1", "GR1", "BAR1", "WO0", "WO1"], S)
                continue
            attention(S)
            if kstop <= 3:
                drain(["ZG0", "ZG1", "ZG2", "ZG3", "GA0", "GR0", "BAR0", "GA1", "GR1", "BAR1", "WO0", "WO1"], S)
                continue
            if "A" in passes:
                dma("pool", flat(zs[:]), zs_d[S], ["zsd%d" % S], ["zs%d" % i for i in range(16)], "zsld")
            else:
                z_project(S, range(16))
            if kstop <= 4:
                drain(["GA0", "GR0", "BAR0", "GA1", "GR1", "BAR1", "WO0", "WO1"], S)
                continue
            if "A" in passes:
                dma("pool", flat(yT[:]), ybwd_d[S], ["ybwd%d" % S], ["yT"], "yTld")
            else:
                P.add("dve", lambda e: e.memset(flat(yT[:]), 0.0), writes=["yT"])
            rwkv_slot_begin(S, 0)
            phase_barrier("rw")
            rwkv_group(S, 0, 0, True, parts=("pre", "minv"))
            P.interleave(cap(lambda: rwkv_group(S, 0, 0, True, parts=("scan",))),
                         cap(lambda: rwkv_group(S, 0, 1, True, parts=("pre",))))
            rwkv_group(S, 0, 1, True, parts=("minv", "scan"))
            if S == 0:
                dump("uT", flat(uT[:]), ["uT"])
                dump("oattn", flat(oattn[:]), ["oattn"])
                dump("zs", flat(zs[:]), ["zs%d" % i for i in range(16)])
                dump("yT", flat(yT[:]), ["yT"])
            if kstop <= 5:
                drain(["GA0", "GR0", "BAR0", "GA1", "GR1", "BAR1", "WO0", "WO1"], S)
                continue
            rwkv_epilogue(S)
            if kstop <= 6:
                drain(["GA0", "GR0", "BAR0", "GA1", "GR1", "BAR1", "WO0", "WO1"], S)
                continue
            merge_branches(S)
            if S == 0:
                dump("orw", flat(orw[:]), ["orw"])
                dump("mergedT", flat(mergedT[:, 0:8, :]), ["zs%d" % i for i in range(8)])
            if kstop <= 7:
                drain(["WO0", "WO1"], S)
                continue
            out_proj(S)
    if "C" in passes:
        load_gains(1)
        phase_barrier("ffn")
        for S in range(nslot):
            ffn_slot(S)
    P.add("pool", None, reads=["yout", "wscr"] + ["hb%d" % k for k in range(nslot)] + ["dbg_" + n for n in dbg_out]
          + ["ybwd%d" % k for k in range(nslot)])
    assert wstate["taken"] == len(wq) == wstate["released"], (wstate, len(wq))

    sems = P.assign(nc, es)
    with nc.Block() as block:
        @block.sync
        def _(e):
            P.run_engine("sp", e, sems)

        @block.tensor
        def _(e):
            P.run_engine("pe", e, sems)

        @block.scalar
        def _(e):
            P.run_engine("act", e, sems)

        @block.vector
        def _(e):
            P.run_engine("dve", e, sems)

        @block.gpsimd
        def _(e):
            P.run_engine("pool", e, sems)
    es.close()
    nc._n_ops = P.n
    return nc


def _assign_sequences():
    plan = [[("p", 0)]]
    counts = [5, 5, 5, 5, 4, 4, 4]
    k = 0
    for c in counts:
        plan.append([("s", k + i) for i in range(c)])
        k += c
    return plan


def kernel(**inputs):
    xp = np.asarray(inputs["x_prompt"], np.float32)
    xs = np.asarray(inputs["x_sample"], np.float32)
    wl = _layout_weights(inputs)
    plan = _assign_sequences()
    in_maps, places = [], []
    for core in range(NCORES):
        seqs = [xp[0] if kind == "p" else xs[i] for (kind, i) in plan[core]]
        xh, fl, place = _layout_core(seqs, SLOTS_FULL)
        m = dict(wl)
        m["xh"] = xh
        m["flags"] = fl
        in_maps.append(m)
        places.append(place)
    nc = build(SLOTS_FULL, "ABC")
    res = run_bass_kernel_spmd(nc, in_maps, core_ids=list(range(NCORES)))
    y_prompt = np.zeros_like(xp)
    y_sample = np.zeros_like(xs)
    for core in range(NCORES):
        y = np.asarray(res.results[core]["y"], np.float32)
        for (kind, i), (s0, n) in zip(plan[core], places[core]):
            blk = y[s0 * SEG:(s0 + n) * SEG]
            if kind == "p":
                y_prompt[0] = blk
            else:
                y_sample[i] = blk
    return (y_prompt, y_sample)
```
